# Optimizing a Trainium2 kernel written in Bass

```python
import jax, jax.numpy as jnp
from jax import lax
import numpy as np

D_MODEL = 1024
BATCH = 4
SEQ = 4096
DEPTH = 2
DEC_BATCH = 128
DEC_SEQ = 1
PAST_LEN = 16384
PAGE_SIZE = 128

HEAD_DIM = 64
ATTN_WIDTH = D_MODEL // 2
N_HEADS = ATTN_WIDTH // HEAD_DIM
N_KV_HEADS = N_HEADS // 4
GQA_GROUP = N_HEADS // N_KV_HEADS
KV_WIDTH = N_KV_HEADS * HEAD_DIM
WINDOW = 128
ATTN_BLOCK = WINDOW
LRU_WIDTH = D_MODEL // 4
LRU_BLOCKS = 4
LRU_BLOCK_W = LRU_WIDTH // LRU_BLOCKS
LRU_C = 8.0
CONV_W = 4
POOL_WINDOWS = (2, 4, 8, 16)
POOL_WIDTH = D_MODEL // 4
POOL_GROUPS = len(POOL_WINDOWS)
POOL_GROUP_W = POOL_WIDTH // POOL_GROUPS
POOL_CTX = max(POOL_WINDOWS) - 1
MIX_WIDTH = ATTN_WIDTH + LRU_WIDTH + POOL_WIDTH
IN_WIDTH = ATTN_WIDTH + 2 * KV_WIDTH + 2 * LRU_WIDTH + POOL_WIDTH
SPLIT_POINTS = (ATTN_WIDTH, ATTN_WIDTH + KV_WIDTH, ATTN_WIDTH + 2 * KV_WIDTH,
                ATTN_WIDTH + 2 * KV_WIDTH + LRU_WIDTH, ATTN_WIDTH + 2 * KV_WIDTH + 2 * LRU_WIDTH)
D_FF = 4 * D_MODEL
LN_EPS = 1e-5
NEG_INF = -1e30

kernel_name = 'hybrid_swa_rglru_pool_deepnorm_step'


def layer_norm(x, g, b):
    xf = x.astype(jnp.float32)
    mu = jnp.mean(xf, axis=-1, keepdims=True)
    xc = xf - mu
    var = jnp.mean(jnp.square(xc), axis=-1, keepdims=True)
    return (xc * lax.rsqrt(var + LN_EPS) * g.astype(jnp.float32) + b.astype(jnp.float32)).astype(x.dtype)


def alibi_slopes():
    return jnp.exp2(-8.0 * (jnp.arange(N_HEADS, dtype=jnp.float32) + 1.0) / N_HEADS)


def window_attention(q, k, v, q_pos, k_pos, sinks):
    scores = jnp.einsum('...qkgd,...skd->...kgqs', q, k).astype(jnp.float32) * (HEAD_DIM ** -0.5)
    delta = q_pos[..., :, None] - k_pos[..., None, :]
    mask = (delta >= 0) & (delta <= WINDOW) & (k_pos[..., None, :] >= 0)
    slopes = alibi_slopes().reshape(N_KV_HEADS, GQA_GROUP, 1, 1)
    scores = scores - slopes * delta[..., None, None, :, :].astype(jnp.float32)
    scores = jnp.where(mask[..., None, None, :, :], scores, NEG_INF)
    sink = jnp.broadcast_to(sinks.astype(jnp.float32).reshape(N_KV_HEADS, GQA_GROUP, 1, 1),
                            scores.shape[:-1] + (1,))
    probs = jax.nn.softmax(jnp.concatenate([scores, sink], axis=-1), axis=-1)[..., :-1]
    return jnp.einsum('...kgqs,...skd->...qkgd', probs.astype(v.dtype), v)


def attention_prompt(q, k, v, sinks):
    B, L = q.shape[:2]
    nblk = L // ATTN_BLOCK
    qb = q.reshape(B, nblk, ATTN_BLOCK, N_KV_HEADS, GQA_GROUP, HEAD_DIM)

    def band(t):
        tp = jnp.concatenate([jnp.zeros((B, WINDOW) + t.shape[2:], t.dtype), t], axis=1)
        tp = tp.reshape(B, nblk + 1, ATTN_BLOCK, N_KV_HEADS, HEAD_DIM)
        return jnp.concatenate([tp[:, :-1], tp[:, 1:]], axis=2)

    q_pos = jnp.arange(L).reshape(nblk, ATTN_BLOCK)
    k_pos = (jnp.arange(nblk) * ATTN_BLOCK - WINDOW)[:, None] + jnp.arange(2 * ATTN_BLOCK)[None, :]
    out = window_attention(qb, band(k), band(v), q_pos, k_pos, sinks)
    return out.reshape(B, L, ATTN_WIDTH), k[:, -WINDOW:], v[:, -WINDOW:]


def attention_sample(q, k, v, k_buf, v_buf, sinks, pos0):
    B, L = q.shape[:2]
    qg = q.reshape(B, L, N_KV_HEADS, GQA_GROUP, HEAD_DIM)
    k_all = jnp.concatenate([k_buf.astype(k.dtype), k], axis=1)
    v_all = jnp.concatenate([v_buf.astype(v.dtype), v], axis=1)
    q_pos = pos0 + jnp.arange(L)
    k_pos = pos0 - WINDOW + jnp.arange(WINDOW + L)
    out = window_attention(qg, k_all, v_all, q_pos, k_pos, sinks)
    return out.reshape(B, L, ATTN_WIDTH), k_all[:, -WINDOW:], v_all[:, -WINDOW:]


def recurrent_branch(xr, gr, conv_ctx, h0, lp):
    B, L, _ = xr.shape
    x_ext = jnp.concatenate([conv_ctx.astype(xr.dtype), xr], axis=1)
    xc = lp['conv_b']
    for tap in range(CONV_W):
        xc = xc + x_ext[:, tap:tap + L] * lp['conv_w'][tap]
    xb = xc.reshape(B, L, LRU_BLOCKS, LRU_BLOCK_W)
    r = jax.nn.sigmoid(jnp.einsum('blnc,ncd->blnd', xb, lp['gate_a_w']).reshape(B, L, LRU_WIDTH) + lp['gate_a_b'])
    i = jax.nn.sigmoid(jnp.einsum('blnc,ncd->blnd', xb, lp['gate_x_w']).reshape(B, L, LRU_WIDTH) + lp['gate_x_b'])
    log_a = (-LRU_C * r.astype(jnp.float32)) * jax.nn.softplus(-lp['lru_lambda'].astype(jnp.float32))
    a = jnp.exp(log_a)
    b = jnp.sqrt(-jnp.expm1(2.0 * log_a)) * (i * xc).astype(jnp.float32)

    def step(h, ab):
        h = ab[0] * h + ab[1]
        return h, h

    h_last, hs = lax.scan(step, h0.astype(jnp.float32), (jnp.swapaxes(a, 0, 1), jnp.swapaxes(b, 0, 1)))
    out = jnp.swapaxes(hs, 0, 1).astype(xr.dtype) * jax.nn.gelu(gr)
    return out, h_last.astype(h0.dtype), x_ext[:, -(CONV_W - 1):]


def pool_branch(zp, pool_ctx, pos0, lp):
    B, L, _ = zp.shape
    z_ext = jnp.concatenate([pool_ctx.astype(zp.dtype), zp], axis=1)
    cs = jnp.cumsum(z_ext.astype(jnp.float32), axis=1)
    cs = jnp.concatenate([jnp.zeros((B, 1, POOL_WIDTH), jnp.float32), cs], axis=1)
    pos = (pos0 + jnp.arange(L)).astype(jnp.float32)
    means = []
    for g, w in enumerate(POOL_WINDOWS):
        ch = slice(g * POOL_GROUP_W, (g + 1) * POOL_GROUP_W)
        win_sum = cs[:, POOL_CTX + 1:POOL_CTX + 1 + L, ch] - cs[:, POOL_CTX + 1 - w:POOL_CTX + 1 - w + L, ch]
        count = jnp.minimum(pos + 1.0, float(w))
        means.append(win_sum / count[None, :, None])
    diff = (jnp.concatenate(means, axis=-1) - zp.astype(jnp.float32)).astype(zp.dtype)
    diff = diff.reshape(B, L, POOL_GROUPS, POOL_GROUP_W)
    out = jnp.einsum('blgc,gcd->blgd', diff, lp['pool_w']).reshape(B, L, POOL_WIDTH) * lp['pool_scale']
    return out, z_ext[:, -POOL_CTX:]


def trunk_layer(x, pos0, state, lp, alpha):
    B, L, _ = x.shape
    u = x @ lp['w_in']
    q, k, v, xr, gr, zp = jnp.split(u, SPLIT_POINTS, axis=-1)
    k = k.reshape(B, L, N_KV_HEADS, HEAD_DIM)
    v = v.reshape(B, L, N_KV_HEADS, HEAD_DIM)
    if state is None:
        attn, k_new, v_new = attention_prompt(q, k, v, lp['attn_sinks'])
        h0 = jnp.zeros((B, LRU_WIDTH), x.dtype)
        conv_ctx = jnp.zeros((B, CONV_W - 1, LRU_WIDTH), x.dtype)
        pool_ctx = jnp.zeros((B, POOL_CTX, POOL_WIDTH), x.dtype)
    else:
        k_buf, v_buf, h0, conv_ctx, pool_ctx = state
        attn, k_new, v_new = attention_sample(q, k, v, k_buf, v_buf, lp['attn_sinks'], pos0)
    rec, h_last, conv_new = recurrent_branch(xr, gr, conv_ctx, h0, lp)
    pool, pool_new = pool_branch(zp, pool_ctx, pos0, lp)
    mix = jnp.concatenate([attn, rec, pool], axis=-1) @ lp['w_out']
    x = layer_norm(alpha * x + mix, lp['ln1_g'], lp['ln1_b'])
    hid = jnp.square(jax.nn.relu(x @ lp['w_ff1']))
    x = layer_norm(alpha * x + hid @ lp['w_ff2'], lp['ln2_g'], lp['ln2_b'])
    return x, (k_new, v_new, h_last, conv_new, pool_new)


def setup_inputs(seed: int = 0) -> dict:
    key = jax.random.key(seed)
    ks = iter(jax.random.split(key, 40))
    f32 = jnp.float32

    def nrm(shape, scale):
        return scale * jax.random.normal(next(ks), shape, f32)

    beta = (8.0 * DEPTH) ** -0.25
    a0 = jax.random.uniform(next(ks), (DEPTH, LRU_WIDTH), f32, minval=0.9, maxval=0.999)
    return {
        'x_prompt': nrm((BATCH, SEQ, D_MODEL), 1.0),
        'x_sample': nrm((DEC_BATCH, DEC_SEQ, D_MODEL), 1.0),
        'cache_k': nrm((DEPTH, DEC_BATCH, WINDOW, N_KV_HEADS, HEAD_DIM), 1.0),
        'cache_v': nrm((DEPTH, DEC_BATCH, WINDOW, N_KV_HEADS, HEAD_DIM), 1.0),
        'state_h': nrm((DEPTH, DEC_BATCH, LRU_WIDTH), 0.5),
        'state_conv': nrm((DEPTH, DEC_BATCH, CONV_W - 1, LRU_WIDTH), 1.0),
        'state_pool': nrm((DEPTH, DEC_BATCH, POOL_CTX, POOL_WIDTH), 1.0),
        'w_in': nrm((DEPTH, D_MODEL, IN_WIDTH), D_MODEL ** -0.5),
        'attn_sinks': nrm((DEPTH, N_HEADS), 1.0),
        'conv_w': nrm((DEPTH, CONV_W, LRU_WIDTH), CONV_W ** -0.5),
        'conv_b': nrm((DEPTH, LRU_WIDTH), 0.02),
        'gate_a_w': nrm((DEPTH, LRU_BLOCKS, LRU_BLOCK_W, LRU_BLOCK_W), LRU_BLOCK_W ** -0.5),
        'gate_a_b': nrm((DEPTH, LRU_WIDTH), 0.02),
        'gate_x_w': nrm((DEPTH, LRU_BLOCKS, LRU_BLOCK_W, LRU_BLOCK_W), LRU_BLOCK_W ** -0.5),
        'gate_x_b': nrm((DEPTH, LRU_WIDTH), 0.02),
        'lru_lambda': jnp.log(a0) - jnp.log1p(-a0),
        'pool_w': nrm((DEPTH, POOL_GROUPS, POOL_GROUP_W, POOL_GROUP_W), POOL_GROUP_W ** -0.5),
        'pool_scale': 1.0 + nrm((DEPTH, POOL_WIDTH), 0.1),
        'w_out': nrm((DEPTH, MIX_WIDTH, D_MODEL), beta * MIX_WIDTH ** -0.5),
        'ln1_g': 1.0 + nrm((DEPTH, D_MODEL), 0.05),
        'ln1_b': nrm((DEPTH, D_MODEL), 0.02),
        'w_ff1': nrm((DEPTH, D_MODEL, D_FF), D_MODEL ** -0.5),
        'w_ff2': nrm((DEPTH, D_FF, D_MODEL), beta * D_FF ** -0.5),
        'ln2_g': 1.0 + nrm((DEPTH, D_MODEL), 0.05),
        'ln2_b': nrm((DEPTH, D_MODEL), 0.02),
    }


def reference(x_prompt, x_sample, cache_k, cache_v, state_h, state_conv, state_pool,
              w_in, attn_sinks, conv_w, conv_b, gate_a_w, gate_a_b, gate_x_w, gate_x_b, lru_lambda,
              pool_w, pool_scale, w_out, ln1_g, ln1_b, w_ff1, w_ff2, ln2_g, ln2_b):
    alpha = (2.0 * DEPTH) ** 0.25
    yp, ys = x_prompt, x_sample
    p_k, p_v, p_h, p_conv, p_pool = [], [], [], [], []
    s_k, s_v, s_h, s_conv, s_pool = [], [], [], [], []
    for l in range(DEPTH):
        lp = {
            'w_in': w_in[l], 'attn_sinks': attn_sinks[l], 'conv_w': conv_w[l], 'conv_b': conv_b[l],
            'gate_a_w': gate_a_w[l], 'gate_a_b': gate_a_b[l], 'gate_x_w': gate_x_w[l], 'gate_x_b': gate_x_b[l],
            'lru_lambda': lru_lambda[l], 'pool_w': pool_w[l], 'pool_scale': pool_scale[l], 'w_out': w_out[l],
            'ln1_g': ln1_g[l], 'ln1_b': ln1_b[l], 'w_ff1': w_ff1[l], 'w_ff2': w_ff2[l],
            'ln2_g': ln2_g[l], 'ln2_b': ln2_b[l],
        }
        yp, (k1, v1, h1, c1, q1) = trunk_layer(yp, 0, None, lp, alpha)
        ys, (k2, v2, h2, c2, q2) = trunk_layer(
            ys, PAST_LEN, (cache_k[l], cache_v[l], state_h[l], state_conv[l], state_pool[l]), lp, alpha)
        p_k.append(k1); p_v.append(v1); p_h.append(h1); p_conv.append(c1); p_pool.append(q1)
        s_k.append(k2); s_v.append(v2); s_h.append(h2); s_conv.append(c2); s_pool.append(q2)
    return (yp, ys,
            jnp.stack(p_k), jnp.stack(p_v), jnp.stack(p_h), jnp.stack(p_conv), jnp.stack(p_pool),
            jnp.stack(s_k), jnp.stack(s_v), jnp.stack(s_h), jnp.stack(s_conv), jnp.stack(s_pool))
```

```python
import contextlib
import numpy as np
import concourse.bass as bass
import concourse.mybir as mybir
from concourse.bass_utils import run_bass_kernel_spmd

F32 = mybir.dt.float32
BF16 = mybir.dt.bfloat16
AF = mybir.ActivationFunctionType
ALU = mybir.AluOpType
AX = mybir.AxisListType

NCORES = 8
T = 2048
NS = 16
TT = T + NS
L = 2
ALPHA = float((2.0 * L) ** 0.25)
EPS = 1e-5
NEG = -1e30
NPV = 50
USE_CC = False
STOP = None
NOSELF = False
XMODE = 2


class _Stop(Exception):
    pass


class Ctx:
    def __init__(self, nc):
        self.nc = nc
        self.engs = {"pe": nc.tensor, "act": nc.scalar, "dve": nc.vector, "pool": nc.gpsimd, "sp": nc.sync}
        self.esem = {e: nc.alloc_semaphore(name="sem_" + e) for e in ["pe", "act", "dve", "pool"]}
        self.ecnt = {e: 0 for e in self.esem}
        self.seen = {e: {} for e in self.engs}
        self.res = {}
        self.dsem = {}
        self.nbank = 0
        self.bank_pool = list(range(8))
        self.ccs = []

    def bank(self):
        b = self.bank_pool[self.nbank % len(self.bank_pool)]
        self.nbank += 1
        return b

    def _wait(self, eng, tok):
        sem, val, src = tok
        if src == "pe" and eng == "pe":
            return
        if NOSELF and src == eng:
            return
        d = self.seen[eng]
        if d.get(id(sem), 0) >= val:
            return
        self.engs[eng].wait_ge(sem, val)
        d[id(sem)] = val

    def _deps(self, eng, reads, writes):
        for k in reads:
            st = self.res.get(k)
            if st and st["w"]:
                self._wait(eng, st["w"])
        for k in writes:
            st = self.res.get(k)
            if st:
                if st["w"]:
                    self._wait(eng, st["w"])
                for r in st["r"]:
                    self._wait(eng, r)

    def _reg(self, tok, reads, writes):
        for k in reads:
            self.res.setdefault(k, {"w": None, "r": []})["r"].append(tok)
        for k in writes:
            self.res[k] = {"w": tok, "r": []}

    def op(self, eng, fn, reads=(), writes=(), signal=True):
        psr = [k for k in reads if isinstance(k, tuple) and k[0] == "ps"]
        if psr:
            reads = [k for k in reads if k not in psr]
            writes = list(writes) + psr
        self._deps(eng, reads, writes)
        ins = fn()
        if signal:
            self.ecnt[eng] += 1
            ins.then_inc(self.esem[eng], 1)
            val = self.ecnt[eng]
        else:
            val = self.ecnt[eng] + 1
        self._reg((self.esem[eng], val, eng), reads, writes)

    def dma(self, q, pairs, reads=(), writes=(), key=None, **kw):
        if not isinstance(pairs, list):
            pairs = [pairs]
        self._deps(q, reads, writes)
        if key not in self.dsem:
            self.dsem[key] = [self.nc.alloc_semaphore(name="d_" + str(key)), 0]
        ent = self.dsem[key]
        if ent[1] > 0:
            self._wait(q, (ent[0], ent[1], "dma"))
        for out, in_ in pairs:
            self.engs[q].dma_start(out=out, in_=in_, **kw).then_inc(ent[0], 16)
            ent[1] += 16
        self._reg((ent[0], ent[1], "dma"), reads, writes)

    def collective(self, in_ap, out_ap, reads=(), writes=()):
        self._deps("pool", reads, writes)
        sem = self.nc.alloc_semaphore(name="cc%d" % len(self.ccs))
        self.ccs.append(sem)
        self.nc.gpsimd.collective_compute("AllGather", ALU.bypass, replica_groups=[[0, 1], [2, 3], [4, 5], [6, 7]],
                                          ins=[in_ap], outs=[out_ap]).then_inc(sem, 1)
        self._reg((sem, 1, "cc"), reads, writes)

    def barrier(self):
        toks = [(self.esem[e], self.ecnt[e], e) for e in self.esem if self.ecnt[e] > 0]
        toks += [(s, c, "dma") for s, c in self.dsem.values() if c > 0]
        for e in self.engs:
            for t in toks:
                if t[2] == e:
                    continue
                self._wait(e, t)
        self.res = {}

    def finish(self):
        for s, c in self.dsem.values():
            if c > 0:
                self._wait("sp", (s, c, "dma"))
        for e in self.esem:
            if self.ecnt[e] > 0:
                self._wait("sp", (self.esem[e], self.ecnt[e], e))


def build():
    nc = bass.Bass("TRN2", target_bir_lowering=False)
    C = Ctx(nc)

    def din(name, shape):
        return nc.dram_tensor(name, shape, F32, kind="ExternalInput").ap()

    def dout(name, shape):
        return nc.dram_tensor(name, shape, F32, kind="ExternalOutput").ap()

    xp = din("xp", [T, 1024]); xs = din("xs", [NS, 1024])
    ck = din("ck", [L, NS, 128, 128]); cv = din("cv", [L, NS, 128, 128])
    sh = din("sh", [L, NS, 256]); sc = din("sc", [L, NS * 3, 256]); spool = din("spool", [L, NS * 15, 256])
    w_in = din("w_in", [L, 1024, 1536]); sinks = din("sinks", [L, 8])
    conv_w = din("conv_w", [L, 4, 256]); conv_b = din("conv_b", [L, 256])
    ga_w = din("ga_w", [L, 4, 64, 64]); ga_b = din("ga_b", [L, 256])
    gx_w = din("gx_w", [L, 4, 64, 64]); gx_b = din("gx_b", [L, 256])
    lam = din("lam", [L, 256]); pool_w = din("pool_w", [L, 4, 64, 64]); pool_s = din("pool_s", [L, 256])
    w_out = din("w_out", [L, 1024, 1024]); ln1_g = din("ln1_g", [L, 1024]); ln1_b = din("ln1_b", [L, 1024])
    w_ff1 = din("w_ff1", [L, 1024, 4096]); w_ff2 = din("w_ff2", [L, 4096, 1024])
    ln2_g = din("ln2_g", [L, 1024]); ln2_b = din("ln2_b", [L, 1024])
    ident_d = din("ident", [128, 128]); bcur_d = din("bcur", [2, 128, 512]); bprev_d = din("bprev", [2, 128, 512])
    sbias_d = din("sbias", [128, 128]); nm0_d = din("nm0", [128, 1]); flag_d = din("flag", [128, 1])
    pcorr_d = din("pcorr", [128, 2, 16]); st_in = din("st_in", [L, 147, 256])
    emask_d = din("emask", [128, 2]); sel_d = din("sel", [128, 64])

    cc1_in = nc.dram_tensor("cc1_in", [L, 146, 256], F32, kind="Internal").ap()
    cc1_out = nc.dram_tensor("cc1_out", [L, 292, 256], F32, kind="Internal").ap()
    cc2_in = nc.dram_tensor("cc2_in", [L, 1, 256], F32, kind="Internal").ap()
    cc2_out = nc.dram_tensor("cc2_out", [L, 2, 256], F32, kind="Internal").ap()
    yp = dout("yp", [T, 1024]); ys = dout("ys", [NS, 1024]); st_out = dout("st_out", [L, 147, 256])
    s_k = dout("s_k", [L, NS, 128, 128]); s_v = dout("s_v", [L, NS, 128, 128])
    s_h = dout("s_h", [L, NS, 256]); s_conv = dout("s_conv", [L, NS, 3, 256]); s_pool = dout("s_pool", [L, NS, 15, 256])

    op = C.op
    dma = C.dma
    pe, act, dve, pool = nc.tensor, nc.scalar, nc.vector, nc.gpsimd

    top = contextlib.ExitStack()

    uid = [0]

    def sbt(stack, name, shape, dt=F32):
        uid[0] += 1
        return stack.enter_context(nc.sbuf_tensor("sb%d_%s" % (uid[0], name), shape, dt))

    ps = top.enter_context(nc.psum_tensor("psum_all", [128, 8, 512], F32))
    xf = sbt(top, "xf", [128, 8, TT]); xb = sbt(top, "xb", [128, 8, TT], BF16)
    ident = sbt(top, "ident", [128, 128]); ones = sbt(top, "ones", [128, 128], BF16)
    bcur = sbt(top, "bcur", [128, 2, 512], BF16); bprev = sbt(top, "bprev", [128, 2, 512], BF16)
    sbias = sbt(top, "sbias", [128, 128]); nm0 = sbt(top, "nm0", [128, 1]); flag = sbt(top, "flag", [128, 1])
    pcorr = sbt(top, "pcorr", [128, 2, 16]); pv = sbt(top, "pv", [128, 2 * NPV]); der = sbt(top, "der", [128, L, 8])
    es64 = sbt(top, "es64", [128, 16]); essamp = sbt(top, "essamp", [128, 2])
    wbd = sbt(top, "wbd", [128, L, 6, 128], BF16)
    cneg = sbt(top, "cneg", [128, 256], BF16)
    emask = sbt(top, "emask", [128, 2]); selm = sbt(top, "selm", [128, 64], BF16)
    winb = sbt(top, "winb", [128, 8, 1536], BF16)

    def load_winb(l):
        for kk in range(4):
            dma("pool", (winb[:, 2 * kk:2 * kk + 2, :], w_in[l, 256 * kk:256 * kk + 256, :].rearrange("(k p) n -> p k n", p=128)),
                writes=["winb"], key="winb%d" % kk)

    def PS(b, n=512, p0=0, p1=128, o=0):
        return ps[p0:p1, b, o:o + n]

    dma("sp", (ident[:], ident_d[:, :]), writes=["ident"], key="c0")
    dma("pool", (bcur[:], bcur_d.rearrange("k s n -> s k n")), writes=["bcur"], key="c1")
    dma("pool", (bprev[:], bprev_d.rearrange("k s n -> s k n")), writes=["bprev"], key="c2")
    dma("sp", [(sbias[:], sbias_d[:, :]), (nm0[:], nm0_d[:, :]), (flag[:], flag_d[:, :]), (pcorr[:], pcorr_d[:, :, :])],
        writes=["smallc"], key="c3")
    dma("sp", (emask[:], emask_d[:, :]), writes=["emask"], key="c8")
    dma("pool", (selm[:], sel_d[:, :]), writes=["selm"], key="c9")
    op("dve", lambda: dve.memset(ones[:], 1.0), writes=["ones"])
    op("dve", lambda: dve.memset(cneg[:], -0.5), writes=["cneg"])
    op("dve", lambda: dve.memset(wbd[:], 0.0), writes=["wbd"])
    if STOP == 'i1':
        C.finish(); top.close(); return nc
    with contextlib.ExitStack() as ph:
        prow = sbt(ph, "prow", [2 * NPV, 128])
        plist = [(conv_w, 0, 8), (conv_b, 8, 2), (ga_b, 10, 2), (gx_b, 12, 2), (lam, 14, 2), (pool_s, 16, 2),
                 (ln1_g, 18, 8), (ln1_b, 26, 8), (ln2_g, 34, 8), (ln2_b, 42, 8)]
        pairs = []
        for l in range(L):
            for (t, off, n) in plist:
                if t is conv_w:
                    src = t[l].rearrange("t (c p) -> (t c) p", p=128)
                else:
                    src = t[l].rearrange("(c p) -> c p", p=128)
                pairs.append((prow[NPV * l + off:NPV * l + off + n, :], src))
        dma("sp", pairs, writes=["prow"], key="c4")
        b0 = C.bank()
        op("pe", lambda: pe.transpose(PS(b0, 2 * NPV), prow[:], ident[0:2 * NPV, 0:2 * NPV]),
           reads=["prow", "ident"], writes=[("ps", b0)])
        op("dve", lambda: dve.tensor_copy(out=pv[:], in_=PS(b0, 2 * NPV)), reads=[("ps", b0)], writes=["pv"])
        if STOP == 'i2':
            C.finish(); ph.close(); top.close(); return nc
        tl = sbt(ph, "tl", [128, 2])
        for l in range(L):
            pb = NPV * l
            op("act", lambda: act.activation(out=tl[:], in_=pv[:, pb + 14:pb + 16], func=AF.Exp, scale=-1.0),
               reads=["pv"], writes=["tl"])
            op("dve", lambda: dve.tensor_scalar_add(tl[:], tl[:], 1.0), reads=["tl"], writes=["tl"])
            op("act", lambda: act.activation(out=tl[:], in_=tl[:], func=AF.Ln), reads=["tl"], writes=["tl"])
            op("dve", lambda: dve.tensor_scalar_mul(der[:, l, 0:2], tl[:], -4.0), reads=["tl"], writes=["der"])
            op("dve", lambda: dve.tensor_scalar_mul(der[:, l, 2:4], tl[:], -8.0), reads=["tl"], writes=["der"])
            op("dve", lambda: dve.tensor_scalar_mul(der[:, l, 4:6], pv[:, pb + 10:pb + 12], 0.5), reads=["pv"], writes=["der"])
            op("dve", lambda: dve.tensor_scalar_mul(der[:, l, 6:8], pv[:, pb + 12:pb + 14], 0.5), reads=["pv"], writes=["der"])
        if STOP == 'i3':
            C.finish(); ph.close(); top.close(); return nc
        dma("sp", (es64[:], sinks.rearrange("l h -> (l h)").partition_broadcast(128)), writes=["es64"], key="c5")
        if STOP == 'i3a':
            C.finish(); ph.close(); top.close(); return nc
        op("act", lambda: act.activation(out=es64[:], in_=es64[:], func=AF.Exp), reads=["es64"], writes=["es64"])
        pairs = []
        for l in range(L):
            for h in range(8):
                pairs.append((essamp[16 * h:16 * h + 16, l:l + 1], sinks[l, h:h + 1].partition_broadcast(16)))
        dma("sp", pairs, writes=["essamp"], key="c6")
        op("act", lambda: act.activation(out=essamp[:], in_=essamp[:], func=AF.Exp), reads=["essamp"], writes=["essamp"])
        if STOP == 'i4':
            C.finish(); ph.close(); top.close(); return nc
        pairs = []
        for l in range(L):
            for n in range(4):
                c, e = n // 2, n % 2
                for si, wt in ((0, ga_w), (2, gx_w), (4, pool_w)):
                    pairs.append((wbd[64 * e:64 * e + 64, l, si + c, 64 * e:64 * e + 64], wt[l, n]))
        dma("pool", pairs, writes=["wbd"], key="c7")
        if STOP == 'i5':
            C.finish(); ph.close(); top.close(); return nc

        for l in range(L):
            dma("sp", [(s_k[l, :, 0:127, :], ck[l, :, 1:128, :]), (s_v[l, :, 0:127, :], cv[l, :, 1:128, :]),
                       (s_conv[l, :, 0:2, :], sc[l].rearrange("(b t) n -> b t n", t=3)[:, 1:3, :]),
                       (s_pool[l, :, 0:14, :], spool[l].rearrange("(b t) n -> b t n", t=15)[:, 1:15, :])],
                key="dd%d" % l)
        if STOP == 'i6':
            C.finish(); ph.close(); top.close(); return nc
        if STOP == 'i6b':
            C.barrier(); C.finish(); ph.close(); top.close(); return nc

        xst = [sbt(ph, "xst%d" % i, [128, 1024]) for i in range(2)]
        for ti in range(16 if STOP == 'initA' else (1 if STOP == 'initB' else 17)):
            st = xst[ti % 2]
            sk = "xst%d" % (ti % 2)
            n = 128 if ti < 16 else NS
            src = xp[128 * ti:128 * ti + 128, :] if ti < 16 else xs[:, :]
            dma("sp", (st[0:n, :], src), writes=[sk], key=sk)
            for hb in range(0 if XMODE == 0 else 2):
                b = C.bank()
                for mm in range(4):
                    m = 4 * hb + mm
                    op("pe", lambda: pe.transpose(PS(b, n, o=n * mm), st[0:n, 128 * m:128 * m + 128], ident[0:n, 0:n]),
                       reads=[sk, "ident"], writes=[("ps", b)], signal=(mm == 3))
                src_ps = ps[:, b, 0:4 * n].rearrange("p (m n) -> p m n", n=n)
                op("dve", lambda: dve.tensor_copy(out=xf[:, 4 * hb:4 * hb + 4, 128 * ti:128 * ti + n], in_=src_ps),
                   reads=[("ps", b)], writes=[("xf", ti)])
                if XMODE >= 2:
                    op("act", lambda: act.activation(out=xb[:, 4 * hb:4 * hb + 4, 128 * ti:128 * ti + n], in_=src_ps, func=AF.Copy),
                       reads=[("ps", b)], writes=[("xb", ti)])
    C.barrier()
    if STOP in ('init', 'initA', 'initB'):
        C.finish(); top.close(); return nc

    def layer_norm(W, l, which, c0, N, xkey):
        for _ in layer_norm_g(W, l, which, c0, N, xkey):
            pass

    def layer_norm_g(W, l, which, c0, N, xkey, banks=None, slack=0):
        gcol = NPV * l + (18 if which == 1 else 34)
        bcol = gcol + 8
        ybf, ybk = W["ybf"], W["ybfk"]
        k0_, k1_, k2_, k3_ = W["stk"]
        cap = ybf.shape[1]
        b1, b2 = banks if banks is not None else (C.bank(), C.bank())
        for (bb_, fn_) in ((b1, AF.Copy), (b2, AF.Square)):
            for h0 in range(0, 8, cap):
                op("act", lambda: act.activation(out=ybf[:, 0:cap, 0:N], in_=xf[:, h0:h0 + cap, c0:c0 + N], func=fn_), reads=[xkey], writes=ybk)
                yield
                for m in range(h0, h0 + cap):
                    op("pe", lambda: pe.matmul(PS(bb_, N), lhsT=ones[:], rhs=ybf[:, m - h0, 0:N], start=(m == 0), stop=(m == 7)),
                       reads=ybk + ["ones"], writes=[("ps", bb_)], signal=(m == h0 + cap - 1))
            yield
        for _ in range(slack):
            yield
        mean, msq, rstd, nmr = [a_[:, 0:N] for a_ in W["st"]]
        op("dve", lambda: dve.tensor_scalar_mul(mean, PS(b1, N), 1.0 / 1024), reads=[("ps", b1)], writes=[k0_])
        op("dve", lambda: dve.tensor_tensor(out=msq, in0=mean, in1=mean, op=ALU.mult), reads=[k0_], writes=[k1_])
        op("dve", lambda: dve.scalar_tensor_tensor(out=msq, in0=PS(b2, N), scalar=1.0 / 1024, in1=msq, op0=ALU.mult, op1=ALU.subtract),
           reads=[("ps", b2), k1_], writes=[k1_])
        op("dve", lambda: dve.tensor_scalar_add(msq, msq, EPS), reads=[k1_], writes=[k1_])
        op("act", lambda: act.activation(out=rstd, in_=msq, func=AF.Ln), reads=[k1_], writes=[k2_])
        op("act", lambda: act.activation(out=rstd, in_=rstd, func=AF.Exp, scale=-0.5), reads=[k2_], writes=[k2_])
        yield
        for _ in range(slack):
            yield
        op("dve", lambda: dve.scalar_tensor_tensor(out=nmr, in0=mean, scalar=-1.0, in1=rstd, op0=ALU.mult, op1=ALU.mult),
           reads=[k0_, k2_], writes=[k3_])
        xblk = xf[:, :, c0:c0 + N]
        op("dve", lambda: dve.tensor_tensor(out=xblk, in0=xblk, in1=rstd.unsqueeze(1).to_broadcast([128, 8, N]), op=ALU.mult), reads=[xkey, k2_], writes=[xkey])
        yield
        op("dve", lambda: dve.tensor_tensor(out=xblk, in0=xblk, in1=nmr.unsqueeze(1).to_broadcast([128, 8, N]), op=ALU.add), reads=[xkey, k3_], writes=[xkey])
        yield
        for m in range(8):
            xs_ = xf[:, m, c0:c0 + N]
            op("dve", lambda: dve.tensor_scalar(out=xs_, in0=xs_, scalar1=pv[:, gcol + m:gcol + m + 1], scalar2=pv[:, bcol + m:bcol + m + 1],
                                                op0=ALU.mult, op1=ALU.add), reads=[xkey, "pv"], writes=[xkey])
            if m % 4 == 3:
                yield
        op("act", lambda: act.activation(out=xb[:, :, c0:c0 + N], in_=xblk, func=AF.Copy), reads=[xkey], writes=[xkey + "b"])

    def pool_part(W, l, N, zwin, zcur, mixb, first_corr=None):
        pb = NPV * l
        diffb = W["diffb"]
        for (p0, p1, c, win, w, wkey) in zwin:
            if first_corr is not None:
                first_corr(p0, p1, c, win, wkey)
            op("dve", lambda: dve.scalar_tensor_tensor(out=diffb[p0:p1, c, 0:N], in0=win, scalar=1.0 / w, in1=zcur(c, p0, p1),
                                                       op0=ALU.mult, op1=ALU.subtract), reads=[wkey, "zp"], writes=["diffb"])
        bp = C.bank()
        for c in range(2):
            op("pe", lambda: pe.matmul(PS(bp, N, o=256 * c), lhsT=wbd[:, l, 4 + c, :], rhs=diffb[:, c, 0:N], start=True, stop=True),
               reads=["diffb", "wbd"], writes=[("ps", bp)])
        for c in range(2):
            op("act", lambda: act.activation(out=mixb[:, 2 + c, :], in_=PS(bp, N, o=256 * c), func=AF.Identity, scale=pv[:, pb + 16 + c:pb + 17 + c]),
               reads=[("ps", bp), "pv"], writes=["mixb"])

    def lru_part_g(W, l, N, xc_taps, xc, gr, h_apply, finish, sfx=""):
        pb = NPV * l
        wk = W["wk"]
        xcb = W["xcb"]
        for c in range(2):
            op("act", lambda: act.activation(out=xc[:, c, :], in_=xc_taps(c, 0), func=AF.Identity, scale=pv[:, pb + c:pb + c + 1],
                                             bias=pv[:, pb + 8 + c:pb + 9 + c]),
               reads=["xr" + sfx, "pv"], writes=["xc" + sfx])
            for tap in range(1, 4):
                op("dve", lambda: dve.scalar_tensor_tensor(out=xc[:, c, :], in0=xc_taps(c, tap), scalar=pv[:, pb + 2 * tap + c:pb + 2 * tap + c + 1],
                                                           in1=xc[:, c, :], op0=ALU.mult, op1=ALU.add),
                   reads=["xr" + sfx, "pv", "xc" + sfx], writes=["xc" + sfx])
        yield
        op("act", lambda: act.activation(out=xcb[:, :, 0:N], in_=xc, func=AF.Copy), reads=["xc" + sfx], writes=["xcb" + sfx])
        bg, bh = C.bank(), C.bank()
        for c in range(2):
            op("pe", lambda: pe.matmul(PS(bg, N, o=256 * c), lhsT=wbd[:, l, 0 + c, :], rhs=xcb[:, c, 0:N], start=True, stop=True),
               reads=["xcb" + sfx, "wbd"], writes=[("ps", bg)])
            op("pe", lambda: pe.matmul(PS(bh, N, o=256 * c), lhsT=wbd[:, l, 2 + c, :], rhs=xcb[:, c, 0:N], start=True, stop=True),
               reads=["xcb" + sfx, "wbd"], writes=[("ps", bh)])
        tha, thx, a_, a2 = [wk[i][:, :, 0:N] for i in range(4)]
        hs = a2
        for c in range(2):
            op("act", lambda: act.activation(out=tha[:, c, :], in_=PS(bg, N, o=256 * c), func=AF.Tanh, scale=0.5, bias=der[:, l, 4 + c:5 + c]),
               reads=[("ps", bg), "der"], writes=["wk0" + sfx])
            op("act", lambda: act.activation(out=thx[:, c, :], in_=PS(bh, N, o=256 * c), func=AF.Tanh, scale=0.5, bias=der[:, l, 6 + c:7 + c]),
               reads=[("ps", bh), "der"], writes=["wk1" + sfx])
            op("act", lambda: act.activation(out=a_[:, c, :], in_=tha[:, c, :], func=AF.Exp, scale=der[:, l, c:c + 1], bias=der[:, l, c:c + 1]),
               reads=["wk0" + sfx, "der"], writes=["wk2" + sfx])
            op("act", lambda: act.activation(out=a2[:, c, :], in_=tha[:, c, :], func=AF.Exp, scale=der[:, l, 2 + c:3 + c], bias=der[:, l, 2 + c:3 + c]),
               reads=["wk0" + sfx, "der"], writes=["wk3" + sfx])
        yield
        op("dve", lambda: dve.tensor_scalar(out=a2, in0=a2, scalar1=-1.0, scalar2=1.0, op0=ALU.mult, op1=ALU.add), reads=["wk3" + sfx], writes=["wk3" + sfx])
        op("act", lambda: act.activation(out=tha, in_=a2, func=AF.Ln), reads=["wk3" + sfx], writes=["wk0" + sfx])
        op("act", lambda: act.activation(out=tha, in_=tha, func=AF.Exp, scale=0.5), reads=["wk0" + sfx], writes=["wk0" + sfx])
        op("dve", lambda: dve.scalar_tensor_tensor(out=thx, in0=thx, scalar=1.0, in1=xc, op0=ALU.add, op1=ALU.mult),
           reads=["wk1" + sfx, "xc" + sfx], writes=["wk1" + sfx])
        op("dve", lambda: dve.scalar_tensor_tensor(out=thx, in0=thx, scalar=0.5, in1=tha, op0=ALU.mult, op1=ALU.mult),
           reads=["wk1" + sfx, "wk0" + sfx], writes=["wk1" + sfx])
        yield
        h_apply(a_, thx, hs)
        yield
        op("act", lambda: act.activation(out=tha, in_=gr, func=AF.Square), reads=["gr" + sfx], writes=["wk0" + sfx])
        op("dve", lambda: dve.tensor_scalar(out=tha, in0=tha, scalar1=0.044715, scalar2=1.0, op0=ALU.mult, op1=ALU.add), reads=["wk0" + sfx], writes=["wk0" + sfx])
        op("dve", lambda: dve.tensor_tensor(out=tha, in0=tha, in1=gr, op=ALU.mult), reads=["wk0" + sfx, "gr" + sfx], writes=["wk0" + sfx])
        op("act", lambda: act.activation(out=tha, in_=tha, func=AF.Tanh, scale=0.7978845608028654), reads=["wk0" + sfx], writes=["wk0" + sfx])
        op("dve", lambda: dve.scalar_tensor_tensor(out=tha, in0=tha, scalar=1.0, in1=gr, op0=ALU.add, op1=ALU.mult), reads=["wk0" + sfx, "gr" + sfx], writes=["wk0" + sfx])
        yield
        finish(hs, tha, thx)

    def lru_part(W, l, N, xc_taps, xc, gr, h_apply, finish):
        for _ in lru_part_g(W, l, N, xc_taps, xc, gr, h_apply, finish):
            pass

    def lru_pool_common(W, l, N, xc_taps, xc, gr, zwin, zcur, h_apply, mixb, first_corr=None):
        pool_part(W, l, N, zwin, zcur, mixb, first_corr)

        def fin(hs, ge, _):
            op("dve", lambda: dve.scalar_tensor_tensor(out=mixb[:, 0:2, :], in0=hs, scalar=0.5, in1=ge, op0=ALU.mult, op1=ALU.mult),
               reads=["wk3", "wk0"], writes=["mixb"])
        lru_part(W, l, N, xc_taps, xc, gr, h_apply, fin)

    def wout_ln(W, l, c0, N, attn, mixb, xkey, ln=True):
        for _ in wout_g(W, l, c0, N, attn, mixb, xkey):
            pass
        if ln:
            layer_norm(W, l, 1, c0, N, xkey)

    def wout_g(W, l, c0, N, attn, mixb, xkey):
        woa, wob = W["woa"], W["wob"]
        for m in range(8):
            b = C.bank()
            for h in range(4):
                op("pe", lambda: pe.matmul(PS(b, N), lhsT=woa[:, h, 128 * m:128 * m + 128], rhs=attn[:, h, :], start=(h == 0), stop=False),
                   reads=["attnT", "woa"], writes=[("ps", b)], signal=False)
            for j in range(4):
                op("pe", lambda: pe.matmul(PS(b, N), lhsT=wob[:, j, 128 * m:128 * m + 128], rhs=mixb[:, j, :], start=False, stop=(j == 3)),
                   reads=["mixb", "wob"], writes=[("ps", b)], signal=(j == 3))
            op("dve", lambda: dve.scalar_tensor_tensor(out=xf[:, m, c0:c0 + N], in0=xf[:, m, c0:c0 + N], scalar=ALPHA, in1=PS(b, N),
                                                       op0=ALU.mult, op1=ALU.add), reads=[("ps", b), xkey], writes=[xkey])
            if m % 2 == 1:
                yield

    def chk(stage):
        if STOP == stage:
            raise _Stop()

    def run_layers():
      for l in range(L):
          pb = NPV * l
          with contextlib.ExitStack() as ph:
              W = {}
              W["woa"] = woa = sbt(ph, "woa", [128, 4, 1024], BF16)
              W["wob"] = wob = sbt(ph, "wob", [128, 4, 1024], BF16)
              W["wk"] = [sbt(ph, "wk%d" % i, [128, 2, 272]) for i in range(4)]
              W["st"] = [W["wk"][2][:, 0, 0:256], W["wk"][2][:, 1, 0:256], W["wk"][3][:, 0, 0:256], W["wk"][3][:, 1, 0:256]]
              W["stk"] = ["wk2", "wk2", "wk3", "wk3"]
              st_ph, stk_ph = W["st"], W["stk"]
              W["diffb"] = sbt(ph, "diffb", [128, 2, 256], BF16)
              W["tA"] = W["wk"][0][:, 0, 0:256]; W["tAk"] = "wk0"
              W["tB"] = W["wk"][1][:, 0, 0:256]; W["tBk"] = "wk1"
              if l == 0:
                  load_winb(0)
              dma("pool", (woa[:], w_out[l, 0:512, :].rearrange("(j p) n -> p j n", p=128)), writes=["woa"], key="woa")
              dma("pool", (wob[:], w_out[l, 512:1024, :].rearrange("(j p) n -> p j n", p=128)), writes=["wob"], key="wob")

              with contextlib.ExitStack() as pp:
                  rec0 = sbt(pp, "rec0", [128, 2, T], BF16)
                  corr = sbt(pp, "corr", [128, 2, T], BF16)
                  kTb = sbt(pp, "kTb", [64, 2, 384], BF16)
                  Vb = sbt(pp, "Vb", [128, 3, 128], BF16)
                  xr_ext = sbt(pp, "xr_ext", [128, 2, 259])
                  zp_ext = sbt(pp, "zp_ext", [128, 2, 271])
                  hcar = sbt(pp, "hcar", [128, 2]); Acar = sbt(pp, "Acar", [128, 2]); hst = sbt(pp, "hst", [128, 2]); hfin = sbt(pp, "hfin", [128, 2])
                  wkt = W["wk"]

                  with contextlib.ExitStack() as pa:
                      stq = rec0[:].rearrange("p c n -> p (c n)")[:, 0:1536].bitcast(F32)
                      cview = corr[:].rearrange("p c n -> p (c n)")[:, 0:1024].bitcast(F32)
                      sth = cview[:, 0:256]; stc = cview[0:18, 256:512]
                      b1, b2 = C.bank(), C.bank()
                      for k in range(8):
                          op("pe", lambda: pe.matmul(PS(b1), lhsT=xb[:, k, T - 128:T], rhs=winb[:, k, 512:1024], start=(k == 0), stop=(k == 7)),
                             reads=["winb"], writes=[("ps", b1)], signal=(k == 7))
                      for k in range(8):
                          op("pe", lambda: pe.matmul(PS(b2, 256), lhsT=xb[:, k, T - 128:T], rhs=winb[:, k, 1280:1536], start=(k == 0), stop=(k == 7)),
                             reads=["winb"], writes=[("ps", b2)], signal=(k == 7))
                      op("dve", lambda: dve.tensor_copy(out=stq[:, 0:512], in_=PS(b1)), reads=[("ps", b1)], writes=["rec0"])
                      op("dve", lambda: dve.tensor_copy(out=stq[:, 512:768], in_=PS(b2, 256)), reads=[("ps", b2)], writes=["rec0"])
                      dma("sp", [(st_out[l, 0:128, :], stq[:, 0:256]), (st_out[l, 128:131, :], stq[125:128, 256:512]),
                                 (st_out[l, 131:146, :], stq[113:128, 512:768])], reads=["rec0"], key="stq")
                      dma("sp", [(cc1_in[l, 0:128, :], stq[:, 0:256]), (cc1_in[l, 128:131, :], stq[125:128, 256:512]),
                                 (cc1_in[l, 131:146, :], stq[113:128, 512:768])], reads=["rec0"], writes=["cc1in"], key="stq2")
                      C.collective(cc1_in[l], cc1_out[l], reads=["cc1in"], writes=["cc1out"])
                      dma("sp", [(sth[:], cc1_out[l, 0:128, :]), (stc[:], cc1_out[l, 128:146, :])], reads=["cc1out"], writes=["corr"], key="sth")
                      bk = C.bank()
                      for kv in range(2):
                          op("pe", lambda: pe.transpose(PS(bk, 128, 0, 64, 128 * kv), sth[:, 64 * kv:64 * kv + 64], ident[:, :]),
                             reads=["corr", "ident"], writes=[("ps", bk)], signal=(kv == 1))
                      op("dve", lambda: dve.tensor_scalar(out=kTb[:, :, 0:128], in0=ps[0:64, bk, 0:256].rearrange("p (k n) -> p k n", n=128),
                                                          scalar1=flag[0:64, 0:1], scalar2=None, op0=ALU.mult),
                         reads=[("ps", bk), "smallc"], writes=["kTb"])
                      op("dve", lambda: dve.tensor_scalar(out=Vb[:, 0, :], in0=sth[:, 128:256], scalar1=flag[:, 0:1], scalar2=None, op0=ALU.mult),
                         reads=["corr", "smallc"], writes=["Vb"])
                      bk2 = C.bank()
                      for c in range(2):
                          op("pe", lambda: pe.transpose(PS(bk2, 18, o=32 * c), stc[0:18, 128 * c:128 * c + 128], ident[0:18, 0:18]),
                             reads=["corr", "ident"], writes=[("ps", bk2)], signal=(c == 1))
                      for c in range(2):
                          op("dve", lambda: dve.tensor_scalar(out=xr_ext[:, c, 0:3], in0=PS(bk2, 3, o=32 * c), scalar1=flag[:, 0:1], scalar2=None, op0=ALU.mult),
                             reads=[("ps", bk2), "smallc"], writes=["xr0"])
                          op("dve", lambda: dve.tensor_scalar(out=zp_ext[:, c, 0:15], in0=PS(bk2, 15, o=32 * c + 3), scalar1=flag[:, 0:1], scalar2=None, op0=ALU.mult),
                             reads=[("ps", bk2), "smallc"], writes=["zp"])
                  op("dve", lambda: dve.memset(hcar[:], 0.0), writes=["hcar"])
                  op("dve", lambda: dve.memset(Acar[:], 1.0), writes=["Acar"])

                  with contextlib.ExitStack() as pb_:
                      sets = []
                      for i in range(2):
                          d_ = {"gr": sbt(pb_, "gr%d" % i, [128, 2, 256]), "xc": sbt(pb_, "xc%d" % i, [128, 2, 256]),
                                "xcb": sbt(pb_, "xcb%d" % i, [128, 2, 256], BF16)}
                          d_["wk"] = W["wk"] if i == 0 else [sbt(pb_, "wkB%d" % q_, [128, 2, 272]) for q_ in range(4)]
                          d_["xr"] = xr_ext if i == 0 else sbt(pb_, "xr_extB", [128, 2, 259])
                          sets.append(d_)

                      def pre(bi):
                          c0 = 256 * bi
                          N = 256
                          i = bi % 2
                          sx = str(i)
                          S_ = sets[i]
                          xr_i, gr_i, xc_i = S_["xr"], S_["gr"], S_["xc"]
                          for (cb, dst, key) in ((768, xr_i[:, :, 3:259], "xr" + sx), (1024, gr_i[:, :, :], "gr" + sx)):
                              b = C.bank()
                              for c in range(2):
                                  for k in range(8):
                                      op("pe", lambda: pe.matmul(PS(b, N, o=256 * c), lhsT=winb[:, k, cb + 128 * c:cb + 128 * c + 128], rhs=xb[:, k, c0:c0 + N],
                                                                 start=(k == 0), stop=(k == 7)),
                                         reads=["winb"], writes=[("ps", b)], signal=(k == 7 and c == 1))
                              if key.startswith("gr"):
                                  op("act", lambda: act.activation(out=dst, in_=ps[:, b, :].rearrange("p (e n) -> p e n", n=256), func=AF.Copy),
                                     reads=[("ps", b)], writes=[key])
                              else:
                                  op("dve", lambda: dve.tensor_copy(out=dst, in_=ps[:, b, :].rearrange("p (e n) -> p e n", n=256)),
                                     reads=[("ps", b)], writes=[key])
                          if bi > 0:
                              xr_p = sets[1 - i]["xr"]
                              op("dve", lambda: dve.tensor_copy(out=xr_i[:, :, 0:3], in_=xr_p[:, :, 256:259]), reads=["xr" + str(1 - i)], writes=["xr" + sx])
                          yield

                          def h_apply(a_, bb, hs):
                              for c in range(2):
                                  op("dve", lambda: dve.tensor_tensor_scan(out=hs[:, c, :], data0=a_[:, c, :], data1=bb[:, c, :], initial=hcar[:, c:c + 1],
                                                                           op0=ALU.mult, op1=ALU.add), reads=["wk2" + sx, "wk1" + sx, "hcar"], writes=["wk3" + sx])
                              op("dve", lambda: dve.tensor_copy(out=hcar[:, :], in_=hs[:, :, 255]), reads=["wk3" + sx], writes=["hcar"])
                              for c in range(2):
                                  op("dve", lambda: dve.tensor_tensor_scan(out=bb[:, c, :], data0=a_[:, c, :], data1=cneg[:, 0:256], initial=Acar[:, c:c + 1],
                                                                           op0=ALU.mult, op1=ALU.max), reads=["wk2" + sx, "cneg", "Acar"], writes=["wk1" + sx])
                              op("dve", lambda: dve.tensor_copy(out=Acar[:, :], in_=bb[:, :, 255]), reads=["wk1" + sx], writes=["Acar"])

                          def fin(hs, ge, Acum):
                              op("dve", lambda: dve.scalar_tensor_tensor(out=rec0[:, :, c0:c0 + 256], in0=hs, scalar=0.5, in1=ge, op0=ALU.mult, op1=ALU.mult),
                                 reads=["wk3" + sx, "wk0" + sx], writes=["rec0"])
                              op("dve", lambda: dve.scalar_tensor_tensor(out=corr[:, :, c0:c0 + 256], in0=Acum, scalar=0.5, in1=ge, op0=ALU.mult, op1=ALU.mult),
                                 reads=["wk1" + sx, "wk0" + sx], writes=["corr"])
                          yield from lru_part_g(S_, l, N, lambda c, tap: xr_i[:, c, tap:tap + 256], xc_i[:], gr_i[:], h_apply, fin, sfx=sx)

                      pend = [pre(bi) for bi in range(8)]
                      active = []
                      while pend or active:
                          while len(active) < 2 and pend:
                              active.append(pend.pop(0))
                          for g_ in list(active):
                              try:
                                  next(g_)
                              except StopIteration:
                                  active.remove(g_)
                  C.barrier()
                  with nc.allow_non_contiguous_dma(reason="tiny h state"):
                      dma("sp", (cc2_in[l, 0, :].rearrange("(c p) -> p c", p=128), hcar[:, :]), reads=["hcar"], writes=["cc2in"], key="hst")
                  C.collective(cc2_in[l], cc2_out[l], reads=["cc2in"], writes=["cc2out"])
                  with nc.allow_non_contiguous_dma(reason="tiny h state"):
                      dma("sp", (hst[:, :], cc2_out[l, 0, :].rearrange("(c p) -> p c", p=128)), reads=["cc2out"], writes=["hst"], key="hst2")
                  op("dve", lambda: dve.tensor_scalar(out=hst[:], in0=hst[:], scalar1=flag[:, 0:1], scalar2=None, op0=ALU.mult), reads=["hst", "smallc"], writes=["hst"])
                  op("dve", lambda: dve.tensor_tensor(out=hfin[:], in0=Acar[:], in1=hst[:], op=ALU.mult), reads=["Acar", "hst"], writes=["hfin"])
                  op("dve", lambda: dve.tensor_tensor(out=hfin[:], in0=hfin[:], in1=hcar[:], op=ALU.add), reads=["hfin", "hcar"], writes=["hfin"])
                  with nc.allow_non_contiguous_dma(reason="tiny h state"):
                      dma("sp", (st_out[l, 146, :].rearrange("(c p) -> p c", p=128), hfin[:, :]), reads=["hfin"], key="hst3")

                  with contextlib.ExitStack() as pc:
                      qT = sbt(pc, "qT", [64, 8, 256], BF16)
                      attnT = sbt(pc, "attnT", [128, 4, 256], BF16)
                      mixb = sbt(pc, "mixb", [128, 4, 256], BF16)
                      tPy = [sbt(pc, "tPy%d" % i, [128, 1024]) for i in range(2)]
                      tP = [[tPy[i][:, 0:512], tPy[i][:, 512:1024]] for i in range(2)]
                      PT = [[sbt(pc, "PT%d%d" % (i, j_), [128, 512], BF16) for j_ in range(2)] for i in range(2)]
                      W["ybf"] = sbt(pc, "ybfL", [128, 4, 256], BF16)
                      W["ybfk"] = ["ybfL"]
                      stL = sbt(pc, "stL", [128, 3, 256])
                      W["st"] = [stL[:, 0, :], stL[:, 1, :], stL[:, 2, :], stL[:, 1, :]]
                      W["stk"] = ["stL0", "stL1", "stL2", "stL1"]
                      dd = wkt[3][:].rearrange("p c n -> p (c n)")[:, 0:256]
                      S2 = wkt[0][:, :, 0:270]; S4 = wkt[1][:, :, 0:268]
                      def front(bi):
                          c0 = 256 * bi
                          N = 256
                          xkey = "xP%d" % bi
                          xin = [xkey + "b"]

                          def rhs_x(k):
                              return xb[:, k, c0:c0 + N]
                          for j in range(4):
                              b = C.bank()
                              for e in range(2):
                                  h = 2 * j + e
                                  for k in range(8):
                                      op("pe", lambda: pe.matmul(PS(b, N, 0, 64, 256 * e), lhsT=winb[:, k, 64 * h:64 * h + 64], rhs=rhs_x(k),
                                                                 start=(k == 0), stop=(k == 7)),
                                         reads=["winb"] + xin, writes=[("ps", b)], signal=(k == 7 and e == 1))
                              op("act", lambda: act.activation(out=qT[:, 2 * j:2 * j + 2, :], in_=ps[0:64, b, :].rearrange("p (e n) -> p e n", n=256), func=AF.Copy),
                                 reads=[("ps", b)], writes=["qT"])
                              if j % 2 == 1:
                                  yield
                          b = C.bank()
                          for kv in range(2):
                              for k in range(8):
                                  op("pe", lambda: pe.matmul(PS(b, N, 0, 64, 256 * kv), lhsT=winb[:, k, 512 + 64 * kv:512 + 64 * kv + 64], rhs=rhs_x(k),
                                                             start=(k == 0), stop=(k == 7)),
                                     reads=["winb"] + xin, writes=[("ps", b)], signal=(k == 7 and kv == 1))
                          op("act", lambda: act.activation(out=kTb[:, :, 128:384], in_=ps[0:64, b, :].rearrange("p (e n) -> p e n", n=256), func=AF.Copy),
                             reads=[("ps", b)], writes=["kTb"])
                          yield
                          b = C.bank()
                          for c in range(2):
                              for k in range(8):
                                  op("pe", lambda: pe.matmul(PS(b, N, o=256 * c), lhsT=winb[:, k, 1280 + 128 * c:1280 + 128 * c + 128], rhs=rhs_x(k),
                                                             start=(k == 0), stop=(k == 7)),
                                     reads=["winb"] + xin, writes=[("ps", b)], signal=(k == 7 and c == 1))
                          op("dve", lambda: dve.tensor_copy(out=zp_ext[:, :, 15:271], in_=ps[:, b, :].rearrange("p (e n) -> p e n", n=256)),
                             reads=[("ps", b)], writes=["zp"])
                          yield
                          b = C.bank()
                          for i in range(2):
                              for k in range(8):
                                  op("pe", lambda: pe.matmul(PS(b, 128, o=128 * i), lhsT=xb[:, k, c0 + 128 * i:c0 + 128 * i + 128], rhs=winb[:, k, 640:768],
                                                             start=(k == 0), stop=(k == 7)),
                                     reads=["winb"] + xin, writes=[("ps", b)], signal=(k == 7 and i == 1))
                          op("act", lambda: act.activation(out=Vb[:, 1:3, :], in_=ps[:, b, 0:256].rearrange("p (e n) -> p e n", n=128), func=AF.Copy),
                             reads=[("ps", b)], writes=["Vb"])
                          yield
                          iters = [(qi, kv) for qi in range(2) for kv in range(2)]

                          def s1(it):
                              qi, kv = iters[it]
                              sx = it % 2
                              bs = [C.bank(), C.bank()]
                              for pc_ in range(2):
                                  ko = 128 * qi + 128 * pc_
                                  tk, pk = "tP%d%d" % (sx, pc_), "PT%d%d" % (sx, pc_)
                                  op("pe", lambda: pe.matmul(PS(bs[pc_]), lhsT=kTb[:, kv, ko:ko + 128], rhs=qT[:, 4 * kv:4 * kv + 4, 128 * qi:128 * qi + 128],
                                                             start=True, stop=True),
                                     reads=["kTb", "qT"], writes=[("ps", bs[pc_])])
                                  bias_t = (bprev if pc_ == 0 else bcur)[:, kv, :]
                                  op("dve", lambda: dve.scalar_tensor_tensor(out=tP[sx][pc_], in0=PS(bs[pc_]), scalar=0.125, in1=bias_t, op0=ALU.mult, op1=ALU.add),
                                     reads=[("ps", bs[pc_]), "bcur", "bprev"], writes=[tk])
                                  if pc_ == 0 and bi == 0 and qi == 0:
                                      op("act", lambda: act.activation(out=PT[sx][pc_][:], in_=tP[sx][pc_], func=AF.Exp, bias=nm0[:, 0:1]),
                                         reads=[tk, "smallc"], writes=[pk])
                                  else:
                                      op("act", lambda: act.activation(out=PT[sx][pc_][:], in_=tP[sx][pc_], func=AF.Exp),
                                         reads=[tk], writes=[pk])

                          def s2(it):
                              qi, kv = iters[it]
                              sx = it % 2
                              bo, bd = C.bank(), C.bank()
                              for e in range(2):
                                  for pc_ in range(2):
                                      rhs_ = PT[sx][pc_][:].rearrange("p (gg e n) -> p e gg n", e=2, n=128)[:, e, :, :]
                                      op("pe", lambda: pe.matmul(PS(bo, 256, 64 * e, 64 * e + 64), lhsT=Vb[:, qi + pc_, 64 * kv:64 * kv + 64], rhs=rhs_,
                                                                 start=(pc_ == 0), stop=(pc_ == 1)),
                                         reads=["Vb", "PT%d%d" % (sx, pc_)], writes=[("ps", bo)], signal=(pc_ == 1 and e == 1))
                              for e in range(2):
                                  for pc_ in range(2):
                                      rhs_ = PT[sx][pc_][:].rearrange("p (gg e n) -> p e gg n", e=2, n=128)[:, e, :, :]
                                      op("pe", lambda: pe.matmul(PS(bd, 256, 64 * e, 64 * e + 64), lhsT=ones[:, 0:64], rhs=rhs_, start=(pc_ == 0), stop=(pc_ == 1)),
                                         reads=["ones", "PT%d%d" % (sx, pc_)], writes=[("ps", bd)], signal=(pc_ == 1 and e == 1))
                              for e in range(2):
                                  for gg in range(2):
                                      hcol = 8 * l + 4 * kv + 2 * gg + e
                                      op("act", lambda: act.activation(out=dd[64 * e:64 * e + 64, 128 * gg:128 * gg + 128], in_=ps[64 * e:64 * e + 64, bd, 128 * gg:128 * gg + 128],
                                                                       func=AF.Ln, bias=es64[64 * e:64 * e + 64, hcol:hcol + 1]),
                                         reads=[("ps", bd), "es64"], writes=["wk3"])
                              op("act", lambda: act.activation(out=dd, in_=dd, func=AF.Exp, scale=-1.0), reads=["wk3"], writes=["wk3"])
                              op("dve", lambda: dve.tensor_tensor(out=attnT[:, 2 * kv:2 * kv + 2, 128 * qi:128 * qi + 128],
                                                                  in0=ps[:, bo, 0:256].rearrange("p (g n) -> p g n", n=128),
                                                                  in1=dd.rearrange("p (g n) -> p g n", n=128), op=ALU.mult),
                                 reads=[("ps", bo), "wk3"], writes=["attnT"])

                          s1(0)
                          for it in range(4):
                              if it + 1 < 4:
                                  s1(it + 1)
                              s2(it)
                              yield
                          op("dve", lambda: dve.tensor_tensor(out=S2, in0=zp_ext[:, :, 1:271], in1=zp_ext[:, :, 0:270], op=ALU.add), reads=["zp"], writes=["wk0"])
                          op("dve", lambda: dve.tensor_tensor(out=S4, in0=S2[:, :, 2:270], in1=S2[:, :, 0:268], op=ALU.add), reads=["wk0"], writes=["wk1"])
                          S8a = wkt[2][:, 0, 0:264]; S8b = wkt[2][:, 1, 0:264]
                          op("dve", lambda: dve.tensor_tensor(out=S8a, in0=S4[:, 1, 4:268], in1=S4[:, 1, 0:264], op=ALU.add),
                             reads=["wk1"], writes=["wk2"])
                          op("dve", lambda: dve.tensor_tensor(out=S8b[:, 0:256], in0=S8a[:, 8:264], in1=S8a[:, 0:256], op=ALU.add),
                             reads=["wk2"], writes=["wk2"])
                          zwin = [(0, 64, 0, S2[0:64, 0, 14:270], 2, "wk0"), (64, 128, 0, S4[64:128, 0, 12:268], 4, "wk1"),
                                  (0, 64, 1, S8a[0:64, 8:264], 8, "wk2"), (64, 128, 1, S8b[64:128, 0:256], 16, "wk2")]

                          def first_corr(p0, p1, c, win, wkey, bi=bi):
                              if bi != 0:
                                  return
                              w16 = win[:, 0:16]
                              op("dve", lambda: dve.tensor_tensor(out=w16, in0=w16, in1=pcorr[p0:p1, c, :], op=ALU.mult),
                                 reads=[wkey, "smallc"], writes=[wkey])
                          yield
                          pool_part(W, l, N, zwin, lambda c, p0, p1: zp_ext[p0:p1, c, 15:271], mixb[:], first_corr)
                          yield
                          for c in range(2):
                              op("dve", lambda: dve.scalar_tensor_tensor(out=mixb[:, c, :], in0=corr[:, c, c0:c0 + N], scalar=hst[:, c:c + 1], in1=rec0[:, c, c0:c0 + N],
                                                                         op0=ALU.mult, op1=ALU.add), reads=["corr", "rec0", "hst"], writes=["mixb"])
                          yield
                          yield from wout_g(W, l, c0, N, attnT, mixb, xkey)
                          yield
                          if bi < 7:
                              op("dve", lambda: dve.tensor_copy(out=zp_ext[:, :, 0:15], in_=zp_ext[:, :, 256:271]), reads=["zp"], writes=["zp"])
                              op("act", lambda: act.activation(out=kTb[:, :, 0:128], in_=kTb[:, :, 256:384], func=AF.Copy), reads=["kTb"], writes=["kTb"])
                              op("act", lambda: act.activation(out=Vb[:, 0, :], in_=Vb[:, 2, :], func=AF.Copy), reads=["Vb"], writes=["Vb"])

                      def drive(gA, gB, delay=0):
                          a_alive, b_alive = gA is not None, gB is not None
                          rnd = 0
                          while a_alive or b_alive:
                              if a_alive:
                                  try:
                                      next(gA)
                                  except StopIteration:
                                      a_alive = False
                              rnd += 1
                              if b_alive and (rnd > delay or not a_alive):
                                  try:
                                      next(gB)
                                  except StopIteration:
                                      b_alive = False

                      C.bank_pool = list(range(6))
                      drive(front(0), None)
                      for bi in range(8):
                          drive(front(bi + 1) if bi + 1 < 8 else None, layer_norm_g(W, l, 1, 256 * bi, 256, "xP%d" % bi, banks=(6, 7)), delay=3)
                      C.bank_pool = list(range(8))

              C.barrier()
              if STOP == 'prompt%d' % l: return

              with contextlib.ExitStack() as pp:
                  N = NS
                  c0 = T
                  xkey = "xS"
                  W["ybf"] = sbt(pp, "ybfS", [128, 8, NS], BF16); W["ybfk"] = ["ybf"]
                  W["xcb"] = sbt(pp, "xcbS", [128, 2, NS], BF16)
                  W["st"], W["stk"] = st_ph, stk_ph
                  xin = ["xSb"]
                  qTs = sbt(pp, "qTs", [64, 8, NS]); kTs = sbt(pp, "kTs", [64, 2, NS]); vTs = sbt(pp, "vTs", [64, 2, NS])
                  xrs = sbt(pp, "xrs", [128, 2, NS, 4]); grs = sbt(pp, "grs", [128, 2, NS]); zps = sbt(pp, "zps", [128, 2, NS, 16])
                  xcs = sbt(pp, "xcs", [128, 2, NS]); h0s = sbt(pp, "h0s", [128, 2, NS])
                  attnTs = sbt(pp, "attnTs", [128, 4, NS], BF16); accw = sbt(pp, "accw", [128, 128], BF16); mixbs = sbt(pp, "mixbs", [128, 4, NS], BF16)
                  sts = sbt(pp, "sts", [NS, 768]); scs = sbt(pp, "scs", [48, 256]); sps = sbt(pp, "sps", [120, 2, 256]); shs = sbt(pp, "shs", [NS, 256])
                  sths = sbt(pp, "sths", [NS, 256])
                  qs128 = sbt(pp, "qs128", [128, 64]); kn128 = sbt(pp, "kn128", [128, 64]); vn128 = sbt(pp, "vn128", [128, 64])
                  krep = sbt(pp, "krep", [64, 128]); vrep = sbt(pp, "vrep", [64, 128])
                  Kcs = [sbt(pp, "Kc%d" % i, [128, 16, 64]) for i in range(2)]; Vcs = [sbt(pp, "Vc%d" % i, [128, 16, 64]) for i in range(2)]
                  tmpc = sbt(pp, "tmpc", [128, 16, 64])

                  def load_kv(buf, src, ch, key):
                      pairs = [(buf[16 * h:16 * h + 16, :, :], src[l, :, 16 * ch:16 * ch + 16, 64 * (h // 4):64 * (h // 4) + 64]) for h in range(8)]
                      dma("sp", pairs, writes=[key], key=key)
                  scr = sbt(pp, "scr", [128, 128]); Pm = sbt(pp, "Pm", [128, 128]); sm = sbt(pp, "sm", [128, 8])
                  acc = sbt(pp, "acc", [128, 64]); part = sbt(pp, "part", [128, 64]); wins = sbt(pp, "wins", [128, 2, NS])

                  def rhs_x(k):
                      return xb[:, k, c0:c0 + N]
                  dma("sp", [(scs[:], sc[l]), (sps[:], spool[l].rearrange("(i r) n -> r i n", i=2)), (shs[:], sh[l])], writes=["sst"], key="sst")
                  for ch in range(2):
                      load_kv(Kcs[ch], ck, ch, "Kc%d" % ch)
                  for ch in range(2):
                      load_kv(Vcs[ch], cv, ch, "Vc%d" % ch)
                  b = C.bank()
                  for c in range(2):
                      op("pe", lambda: pe.transpose(PS(b, 48, o=64 * c), scs[:, 128 * c:128 * c + 128], ident[0:48, 0:48]),
                         reads=["sst", "ident"], writes=[("ps", b)], signal=(c == 1))
                  for c in range(2):
                      op("dve", lambda: dve.tensor_copy(out=xrs[:, c, :, 0:3], in_=ps[:, b, 64 * c:64 * c + 48].rearrange("p (b t) -> p b t", t=3)),
                         reads=[("ps", b)], writes=["xr"])
                  b = C.bank()
                  for i in range(2):
                      for c in range(2):
                          op("pe", lambda: pe.transpose(PS(b, 120, o=120 * (2 * i + c)), sps[:, i, 128 * c:128 * c + 128], ident[0:120, 0:120]),
                             reads=["sst", "ident"], writes=[("ps", b)], signal=(i == 1 and c == 1))
                  for i in range(2):
                      for c in range(2):
                          op("dve", lambda: dve.tensor_copy(out=zps[:, c, 8 * i:8 * i + 8, 0:15],
                                                            in_=ps[:, b, 120 * (2 * i + c):120 * (2 * i + c) + 120].rearrange("p (b t) -> p b t", t=15)),
                             reads=[("ps", b)], writes=["zp"])
                  b = C.bank()
                  for c in range(2):
                      op("pe", lambda: pe.transpose(PS(b, NS, o=NS * c), shs[:, 128 * c:128 * c + 128], ident[0:NS, 0:NS]),
                         reads=["sst", "ident"], writes=[("ps", b)], signal=(c == 1))
                  op("dve", lambda: dve.tensor_copy(out=h0s[:], in_=ps[:, b, 0:2 * NS].rearrange("p (c n) -> p c n", n=NS)), reads=[("ps", b)], writes=["h0s"])
                  b = C.bank()
                  for h in range(8):
                      for k in range(8):
                          op("pe", lambda: pe.matmul(PS(b, N, 0, 64, NS * h), lhsT=winb[:, k, 64 * h:64 * h + 64], rhs=rhs_x(k), start=(k == 0), stop=(k == 7)),
                             reads=["winb"] + xin, writes=[("ps", b)], signal=(k == 7 and h == 7))
                  op("dve", lambda: dve.tensor_copy(out=qTs[:], in_=ps[0:64, b, 0:8 * NS].rearrange("p (h n) -> p h n", n=NS)), reads=[("ps", b)], writes=["qTs"])
                  b = C.bank()
                  for e in range(4):
                      for k in range(8):
                          op("pe", lambda: pe.matmul(PS(b, N, 0, 64, NS * e), lhsT=winb[:, k, 512 + 64 * e:512 + 64 * e + 64], rhs=rhs_x(k), start=(k == 0), stop=(k == 7)),
                             reads=["winb"] + xin, writes=[("ps", b)], signal=(k == 7 and e == 3))
                  op("dve", lambda: dve.tensor_copy(out=kTs[:], in_=ps[0:64, b, 0:2 * NS].rearrange("p (h n) -> p h n", n=NS)), reads=[("ps", b)], writes=["kTs"])
                  op("dve", lambda: dve.tensor_copy(out=vTs[:], in_=ps[0:64, b, 2 * NS:4 * NS].rearrange("p (h n) -> p h n", n=NS)), reads=[("ps", b)], writes=["vTs"])
                  b = C.bank()
                  for e in range(6):
                      for k in range(8):
                          op("pe", lambda: pe.matmul(PS(b, N, o=NS * e), lhsT=winb[:, k, 768 + 128 * e:768 + 128 * e + 128], rhs=rhs_x(k), start=(k == 0), stop=(k == 7)),
                             reads=["winb"] + xin, writes=[("ps", b)], signal=(k == 7 and e == 5))
                  op("dve", lambda: dve.tensor_copy(out=xrs[:, :, :, 3], in_=ps[:, b, 0:2 * NS].rearrange("p (c n) -> p c n", n=NS)), reads=[("ps", b)], writes=["xr"])
                  op("dve", lambda: dve.tensor_copy(out=grs[:], in_=ps[:, b, 2 * NS:4 * NS].rearrange("p (c n) -> p c n", n=NS)), reads=[("ps", b)], writes=["gr"])
                  op("dve", lambda: dve.tensor_copy(out=zps[:, :, :, 15], in_=ps[:, b, 4 * NS:6 * NS].rearrange("p (c n) -> p c n", n=NS)), reads=[("ps", b)], writes=["zp"])
                  b1, b2 = C.bank(), C.bank()
                  for k in range(8):
                      op("pe", lambda: pe.matmul(PS(b1, 512, 0, NS), lhsT=xb[:, k, c0:c0 + NS], rhs=winb[:, k, 512:1024], start=(k == 0), stop=(k == 7)),
                         reads=["winb"] + xin, writes=[("ps", b1)], signal=(k == 7))
                  for k in range(8):
                      op("pe", lambda: pe.matmul(PS(b2, 256, 0, NS), lhsT=xb[:, k, c0:c0 + NS], rhs=winb[:, k, 1280:1536], start=(k == 0), stop=(k == 7)),
                         reads=["winb"] + xin, writes=[("ps", b2)], signal=(k == 7))
                  op("dve", lambda: dve.tensor_copy(out=sts[:, 0:512], in_=PS(b1, 512, 0, NS)), reads=[("ps", b1)], writes=["sts"])
                  op("dve", lambda: dve.tensor_copy(out=sts[:, 512:768], in_=PS(b2, 256, 0, NS)), reads=[("ps", b2)], writes=["sts"])
                  dma("sp", [(s_k[l, :, 127, :], sts[:, 0:128]), (s_v[l, :, 127, :], sts[:, 128:256]),
                             (s_conv[l, :, 2, :], sts[:, 256:512]), (s_pool[l, :, 14, :], sts[:, 512:768])], reads=["sts"], key="sts")
                  b = C.bank()
                  op("pe", lambda: pe.transpose(PS(b, 64), qTs[:].rearrange("p h n -> p (h n)"), ident[0:64, 0:64]), reads=["qTs", "ident"], writes=[("ps", b)])
                  op("dve", lambda: dve.tensor_copy(out=qs128[:], in_=PS(b, 64)), reads=[("ps", b)], writes=["qs128"])
                  op("dve", lambda: dve.tensor_copy(out=krep[:].rearrange("p (k g n) -> p k g n", k=2, g=4),
                                                    in_=kTs[:].unsqueeze(2).to_broadcast([64, 2, 4, NS])), reads=["kTs"], writes=["krep"])
                  op("dve", lambda: dve.tensor_copy(out=vrep[:].rearrange("p (k g n) -> p k g n", k=2, g=4),
                                                    in_=vTs[:].unsqueeze(2).to_broadcast([64, 2, 4, NS])), reads=["vTs"], writes=["vrep"])
                  b = C.bank()
                  op("pe", lambda: pe.transpose(PS(b, 64), krep[:], ident[0:64, 0:64]), reads=["krep", "ident"], writes=[("ps", b)], signal=False)
                  op("pe", lambda: pe.transpose(PS(b, 64, o=64), vrep[:], ident[0:64, 0:64]), reads=["vrep", "ident"], writes=[("ps", b)])
                  op("dve", lambda: dve.tensor_copy(out=kn128[:], in_=PS(b, 64)), reads=[("ps", b)], writes=["kn128"])
                  op("dve", lambda: dve.tensor_copy(out=vn128[:], in_=PS(b, 64, o=64)), reads=[("ps", b)], writes=["vn128"])
                  for ch in range(8):
                      Kc = Kcs[ch % 2]
                      op("dve", lambda: dve.tensor_tensor(out=tmpc[:], in0=Kc[:], in1=qs128[:].unsqueeze(1).to_broadcast([128, 16, 64]), op=ALU.mult),
                         reads=["Kc%d" % (ch % 2), "qs128"], writes=["tmpc"])
                      op("dve", lambda: dve.tensor_reduce(out=scr[:, 16 * ch:16 * ch + 16], in_=tmpc[:], op=ALU.add, axis=AX.X), reads=["tmpc"], writes=["scr"])
                      if ch + 2 < 8:
                          load_kv(Kcs[ch % 2], ck, ch + 2, "Kc%d" % (ch % 2))
                  op("dve", lambda: dve.tensor_tensor(out=part[:], in0=kn128[:], in1=qs128[:], op=ALU.mult), reads=["kn128", "qs128"], writes=["part"])
                  op("dve", lambda: dve.tensor_reduce(out=sm[:, 0:1], in_=part[:], op=ALU.add, axis=AX.X), reads=["part"], writes=["sm0"])
                  op("dve", lambda: dve.scalar_tensor_tensor(out=scr[:], in0=scr[:], scalar=0.125, in1=sbias[:], op0=ALU.mult, op1=ALU.add),
                     reads=["scr", "smallc"], writes=["scr"])
                  op("act", lambda: act.activation(out=Pm[:], in_=scr[:], func=AF.Exp), reads=["scr"], writes=["Pm"])
                  op("act", lambda: act.activation(out=sm[:, 1:2], in_=sm[:, 0:1], func=AF.Exp, scale=0.125), reads=["sm0"], writes=["sm1"])
                  op("dve", lambda: dve.tensor_reduce(out=sm[:, 2:3], in_=Pm[:], op=ALU.add, axis=AX.X), reads=["Pm"], writes=["sm2"])
                  op("dve", lambda: dve.tensor_tensor(out=sm[:, 2:3], in0=sm[:, 2:3], in1=sm[:, 1:2], op=ALU.add), reads=["sm2", "sm1"], writes=["sm2"])
                  op("dve", lambda: dve.tensor_tensor(out=sm[:, 2:3], in0=sm[:, 2:3], in1=essamp[:, l:l + 1], op=ALU.add), reads=["sm2", "essamp"], writes=["sm2"])
                  op("dve", lambda: dve.reciprocal(out=sm[:, 3:4], in_=sm[:, 2:3]), reads=["sm2"], writes=["sm3"])
                  op("dve", lambda: dve.tensor_scalar(out=acc[:], in0=vn128[:], scalar1=sm[:, 1:2], scalar2=None, op0=ALU.mult), reads=["vn128", "sm1"], writes=["acc"])
                  for ch in range(8):
                      Vc = Vcs[ch % 2]
                      op("dve", lambda: dve.tensor_tensor(out=tmpc[:], in0=Vc[:], in1=Pm[:, 16 * ch:16 * ch + 16].unsqueeze(2).to_broadcast([128, 16, 64]), op=ALU.mult),
                         reads=["Vc%d" % (ch % 2), "Pm"], writes=["tmpc"])
                      op("dve", lambda: dve.tensor_reduce(out=part[:], in_=tmpc[:].rearrange("p s d -> p d s"), op=ALU.add, axis=AX.X), reads=["tmpc"], writes=["part"])
                      op("dve", lambda: dve.tensor_tensor(out=acc[:], in0=acc[:], in1=part[:], op=ALU.add), reads=["acc", "part"], writes=["acc"])
                      if ch + 2 < 8:
                          load_kv(Vcs[ch % 2], cv, ch + 2, "Vc%d" % (ch % 2))
                  op("dve", lambda: dve.tensor_scalar(out=acc[:], in0=acc[:], scalar1=sm[:, 3:4], scalar2=None, op0=ALU.mult), reads=["acc", "sm3"], writes=["acc"])
                  b = C.bank()
                  for e in range(2):
                      op("dve", lambda: dve.tensor_scalar(out=accw[:, 64 * e:64 * e + 64], in0=acc[:], scalar1=emask[:, e:e + 1], scalar2=None, op0=ALU.mult),
                         reads=["acc", "emask"], writes=["accw"])
                  op("pe", lambda: pe.matmul(PS(b, 64), lhsT=accw[:], rhs=selm[:], start=True, stop=True), reads=["accw", "selm"], writes=[("ps", b)])
                  op("dve", lambda: dve.tensor_copy(out=attnTs[:], in_=ps[:, b, 0:64].rearrange("p (j n) -> p j n", n=NS)), reads=[("ps", b)], writes=["attnT"])

                  def h_apply_s(a_, bb, hs):
                      op("dve", lambda: dve.tensor_tensor(out=hs, in0=a_, in1=h0s[:], op=ALU.mult), reads=["wk2", "h0s"], writes=["wk3"])
                      op("dve", lambda: dve.tensor_tensor(out=hs, in0=hs, in1=bb, op=ALU.add), reads=["wk3", "wk1"], writes=["wk3"])
                      bt = C.bank()
                      for c in range(2):
                          op("pe", lambda: pe.transpose(PS(bt, 128, 0, NS, 128 * c), hs[:, c, :], ident[:, :]), reads=["wk3", "ident"], writes=[("ps", bt)], signal=(c == 1))
                      op("dve", lambda: dve.tensor_copy(out=sths[:], in_=PS(bt, 256, 0, NS)), reads=[("ps", bt)], writes=["sths"])
                      dma("sp", (s_h[l], sths[:]), reads=["sths"], key="sths")

                  zwin = []
                  for gi, (p0, p1, c, w) in enumerate([(0, 64, 0, 2), (64, 128, 0, 4), (0, 64, 1, 8), (64, 128, 1, 16)]):
                      op("dve", lambda: dve.tensor_reduce(out=wins[p0:p1, c, :], in_=zps[p0:p1, c, :, 16 - w:16], op=ALU.add, axis=AX.X), reads=["zp"], writes=["zw"])
                      zwin.append((p0, p1, c, wins[p0:p1, c, :], w, "zw"))
                  lru_pool_common(W, l, N, lambda c, tap: xrs[:, c, :, tap], xcs[:], grs[:], zwin,
                                  lambda c, p0, p1: zps[p0:p1, c, :, 15], h_apply_s, mixbs[:])
                  wout_ln(W, l, c0, N, attnTs, mixbs, xkey)
              C.barrier()
              if STOP == 'sample%d' % l: return

          with contextlib.ExitStack() as ph:
              W = {}
              NSLOT = 3
              w1s = [sbt(ph, "w1s%d" % i, [128, 8, 512], BF16) for i in range(NSLOT)]
              w2s = [sbt(ph, "w2s%d" % i, [128, 4, 1024], BF16) for i in range(NSLOT)]
              hT = [sbt(ph, "hT%d" % i, [128, 4, 512], BF16) for i in range(2)]
              rt = [sbt(ph, "rt0", [128, 512])] * 2
              Wl = []
              for q_ in range(2):
                  d_ = {"ybf": sbt(ph, "ybfF%d" % q_, [128, 4, 256], BF16), "ybfk": ["ybfF%d" % q_]}
                  st_ = sbt(ph, "stF%d" % q_, [128, 3, 256])
                  d_["st"] = [st_[:, 0, :], st_[:, 1, :], st_[:, 2, :], st_[:, 1, :]]
                  d_["stk"] = ["stF%d_0" % q_, "stF%d_1" % q_, "stF%d_2" % q_, "stF%d_1" % q_]
                  d_["banks"] = (6, 7) if q_ == 0 else (4, 5)
                  Wl.append(d_)
              ost = [sbt(ph, "ost%d" % i, [128, 1024]) for i in range(2)]

              def load_slice(j):
                  s = j % NSLOT
                  for i_ in range(4):
                      dma("pool", (w1s[s][:, :, 128 * i_:128 * i_ + 128], w_ff1[l, :, 512 * j + 128 * i_:512 * j + 128 * i_ + 128].rearrange("(k p) n -> p k n", p=128)),
                          writes=["w1s%d_%d" % (s, i_)], key="w1s%d_%d" % (s, i_))
                  for h_ in range(2):
                      dma("pool", (w2s[s][:, 2 * h_:2 * h_ + 2, :], w_ff2[l, 512 * j + 256 * h_:512 * j + 256 * h_ + 256, :].rearrange("(i p) n -> p i n", p=128)),
                          writes=["w2s%d_%d" % (s, h_)], key="w2s%d_%d" % (s, h_))

              for j in range(NSLOT):
                  load_slice(j)
              blocks = [(512 * i, 512) for i in range(4)]
              items = [(j, c0, N) for j in range(5) for (c0, N) in blocks]
              items += [(j, c0, N) for (c0, N) in blocks for j in (5, 6, 7)]
              last_of_slice = {}
              for ii, (j_, _c, _n) in enumerate(items):
                  last_of_slice[j_] = ii

              hTs = [sbt(ph, "hTs%d" % i, [128, 4, NS], BF16) for i in range(2)]
              rts = sbt(ph, "rts", [128, NS])

              def ff1(idx):
                  j, c0, N = items[idx]
                  s = j % NSLOT
                  hb = idx % 2
                  xk = "xF%d" % c0
                  mrg = (c0 == 1536)
                  for i in range(4):
                      b = C.bank()
                      b2 = C.bank() if mrg else None
                      for k in range(8):
                          op("pe", lambda: pe.matmul(PS(b, N), lhsT=w1s[s][:, k, 128 * i:128 * i + 128], rhs=xb[:, k, c0:c0 + N], start=(k == 0), stop=(k == 7)),
                             reads=["w1s%d_%d" % (s, i), xk + "_0b", xk + "_1b"], writes=[("ps", b)], signal=(k == 7))
                          if mrg:
                              op("pe", lambda: pe.matmul(PS(b2, NS), lhsT=w1s[s][:, k, 128 * i:128 * i + 128], rhs=xb[:, k, T:T + NS], start=(k == 0), stop=(k == 7)),
                                 reads=["w1s%d_%d" % (s, i), "xF2048_0b"], writes=[("ps", b2)], signal=(k == 7))
                      op("act", lambda: act.activation(out=rt[0][:, 0:N], in_=PS(b, N), func=AF.Relu), reads=[("ps", b)], writes=["rt0"])
                      op("act", lambda: act.activation(out=hT[hb][:, i, 0:N], in_=rt[0][:, 0:N], func=AF.Square), reads=["rt0"], writes=["hT%d" % hb])
                      if mrg:
                          op("act", lambda: act.activation(out=rts[:, :], in_=PS(b2, NS), func=AF.Relu), reads=[("ps", b2)], writes=["rts"])
                          op("act", lambda: act.activation(out=hTs[hb][:, i, :], in_=rts[:, :], func=AF.Square), reads=["rts"], writes=["hTs%d" % hb])
                      yield

              def ff2(idx):
                  j, c0, N = items[idx]
                  s = j % NSLOT
                  hb = idx % 2
                  xk = "xF%d" % c0
                  mrg = (c0 == 1536)
                  for m in range(8):
                      b = C.bank()
                      b2 = C.bank() if mrg else None
                      for i in range(4):
                          op("pe", lambda: pe.matmul(PS(b, N), lhsT=w2s[s][:, i, 128 * m:128 * m + 128], rhs=hT[hb][:, i, 0:N], start=(i == 0), stop=(i == 3)),
                             reads=["w2s%d_%d" % (s, i // 2), "hT%d" % hb], writes=[("ps", b)], signal=(i == 3))
                          if mrg:
                              op("pe", lambda: pe.matmul(PS(b2, NS), lhsT=w2s[s][:, i, 128 * m:128 * m + 128], rhs=hTs[hb][:, i, :], start=(i == 0), stop=(i == 3)),
                                 reads=["w2s%d_%d" % (s, i // 2), "hTs%d" % hb], writes=[("ps", b2)], signal=(i == 3))
                      for (bq, cq, nq, kq) in ([(b, c0, N, [xk + "_0", xk + "_1"])] + ([(b2, T, NS, ["xF2048_0"])] if mrg else [])):
                          xs_ = xf[:, m, cq:cq + nq]
                          if j == 0:
                              op("dve", lambda: dve.scalar_tensor_tensor(out=xs_, in0=xs_, scalar=ALPHA, in1=PS(bq, nq), op0=ALU.mult, op1=ALU.add),
                                 reads=[("ps", bq)] + kq, writes=kq)
                          else:
                              op("dve", lambda: dve.tensor_tensor(out=xs_, in0=xs_, in1=PS(bq, nq), op=ALU.add), reads=[("ps", bq)] + kq, writes=kq)
                      yield

              otile = [0]

              def ln2_sub(cc, nn, xk, Wq):
                  yield from layer_norm_g(Wq, l, 2, cc, nn, xk, banks=Wq["banks"], slack=1)
                  if l == L - 1:
                      ntile = 2 if nn == 256 else 1
                      for ti in range(ntile):
                          n = 128 if nn == 256 else NS
                          t0 = cc + 128 * ti
                          o = ost[otile[0] % 2]
                          ok = "ost%d" % (otile[0] % 2)
                          otile[0] += 1
                          for hb in range(2):
                              b = C.bank()
                              for mm in range(4):
                                  m = 4 * hb + mm
                                  op("pe", lambda: pe.transpose(PS(b, 128, 0, n, 128 * mm), xf[:, m, t0:t0 + n], ident[:, :]),
                                     reads=[xk, "ident"], writes=[("ps", b)], signal=(mm == 3))
                              if hb == 0:
                                  op("act", lambda: act.activation(out=o[0:n, 0:512], in_=PS(b, 512, 0, n), func=AF.Copy), reads=[("ps", b)], writes=[ok])
                              else:
                                  op("dve", lambda: dve.tensor_copy(out=o[0:n, 512:1024], in_=PS(b, 512, 0, n)), reads=[("ps", b)], writes=[ok])
                          dst = yp[t0:t0 + 128, :] if nn == 256 else ys[:, :]
                          dma("sp", (dst, o[0:n, :]), reads=[ok], key=ok)
                          yield

              d_ = {"ybf": w1s[2][:].rearrange("p k n -> p (k n)")[:, 0:1024].rearrange("p (m n) -> p m n", n=256),
                    "ybfk": ["w1s2_0", "w1s2_1", "w1s2_2", "w1s2_3"]}
              st_ = w2s[2][:].rearrange("p k n -> p (k n)")[:, 0:1536].bitcast(F32).rearrange("p (m n) -> p m n", n=256)
              d_["st"] = [st_[:, 0, :], st_[:, 1, :], st_[:, 2, :], st_[:, 1, :]]
              d_["stk"] = ["w2s2_0", "w2s2_0", "w2s2_0", "w2s2_0"]
              d_["banks"] = (2, 3)
              Wl.append(d_)

              lnq = []
              nln = [0]

              def delayed(g_, k_):
                  for _ in range(k_):
                      yield
                  yield from g_

              def ffn_main():
                  yield from ff1(0)
                  for idx in range(len(items)):
                      if idx + 1 < len(items):
                          yield from ff1(idx + 1)
                      yield from ff2(idx)
                      j, c0, N = items[idx]
                      if idx + 3 < len(items) and items[idx + 3][0] == 7 and items[idx + 2][0] == 6 and items[idx + 1][0] == 5 and items[idx][0] == 4:
                          C.bank_pool = list(range(4))
                      if j == 7:
                          lnq.append((c0, 256, "xF%d_0" % c0))
                          lnq.append((c0 + 256, 256, "xF%d_1" % c0))
                          if c0 == 1536:
                              lnq.append((T, NS, "xF2048_0"))
                      if last_of_slice[j] == idx and j + NSLOT < 8:
                          load_slice(j + NSLOT)
                      if j == 3 and last_of_slice[j] == idx and l + 1 < L:
                          load_winb(l + 1)

              C.bank_pool = list(range(8))
              gmain = ffn_main()
              alive = True
              slots = [None, None, None]
              while alive or lnq or any(g_ is not None for g_ in slots):
                  if alive:
                      try:
                          next(gmain)
                      except StopIteration:
                          alive = False
                          C.bank_pool = [0, 1]
                  nslots = 2 if alive else 3
                  for q_ in range(nslots):
                      if slots[q_] is None and lnq:
                          cc_, nn_, xk_ = lnq.pop(0)
                          slots[q_] = delayed(ln2_sub(cc_, nn_, xk_, Wl[q_]), 3 if alive else 0)
                      if slots[q_] is not None:
                          try:
                              next(slots[q_])
                          except StopIteration:
                              slots[q_] = None
              C.bank_pool = list(range(8))
          C.barrier()
          if STOP == 'ln2%d' % l: return
    run_layers()
    C.finish()
    top.close()
    return nc


def _consts():
    slopes = np.exp2(-8.0 * (np.arange(8, dtype=np.float32) + 1.0) / 8).astype(np.float32)
    s = np.arange(128)[:, None]
    q = np.arange(128)[None, :]
    bcur = np.full((2, 128, 4, 128), NEG, np.float32)
    bprev = np.full((2, 128, 4, 128), NEG, np.float32)
    for kv in range(2):
        for g in range(4):
            sl = slopes[4 * kv + g]
            d = (q - s).astype(np.float32)
            bcur[kv, :, g, :] = np.where(s <= q, -sl * d, NEG)
            d2 = (q - s + 128).astype(np.float32)
            bprev[kv, :, g, :] = np.where(s >= q, -sl * d2, NEG)
    sbias = np.zeros((128, 128), np.float32)
    for h in range(8):
        sbias[16 * h:16 * h + 16, :] = -slopes[h] * (128 - np.arange(128, dtype=np.float32))[None, :]
    return bcur.reshape(2, 128, 512), bprev.reshape(2, 128, 512), sbias


_NC = None


def kernel(x_prompt, x_sample, cache_k, cache_v, state_h, state_conv, state_pool,
           w_in, attn_sinks, conv_w, conv_b, gate_a_w, gate_a_b, gate_x_w, gate_x_b, lru_lambda,
           pool_w, pool_scale, w_out, ln1_g, ln1_b, w_ff1, w_ff2, ln2_g, ln2_b):
    global _NC
    f = lambda a: np.ascontiguousarray(np.asarray(a, dtype=np.float32))
    bcur, bprev, sbias = _consts()
    ident = np.eye(128, dtype=np.float32)
    shared = dict(w_in=f(w_in), sinks=f(attn_sinks), conv_w=f(conv_w), conv_b=f(conv_b), ga_w=f(gate_a_w), ga_b=f(gate_a_b),
                  gx_w=f(gate_x_w), gx_b=f(gate_x_b), lam=f(lru_lambda), pool_w=f(pool_w), pool_s=f(pool_scale),
                  w_out=f(w_out), ln1_g=f(ln1_g), ln1_b=f(ln1_b), w_ff1=f(w_ff1), w_ff2=f(w_ff2), ln2_g=f(ln2_g), ln2_b=f(ln2_b),
                  ident=ident, bcur=bcur, bprev=bprev, sbias=sbias)
    emask = np.zeros((128, 2), np.float32); sel = np.zeros((128, 64), np.float32)
    for p in range(128):
        h, b_ = p // 16, p % 16
        emask[p, h % 2] = 1.0
        sel[p, (h // 2) * 16 + b_] = 1.0
    xpr = f(x_prompt); xsa = f(x_sample)
    ckk = f(cache_k).reshape(L, 128, 128, 128); cvv = f(cache_v).reshape(L, 128, 128, 128)
    shh = f(state_h); scc = f(state_conv); spp = f(state_pool)
    in_maps = []
    for c in range(NCORES):
        seq, half = c // 2, c % 2
        b0 = NS * c
        pcorr = np.ones((128, 2, 16), np.float32)
        if half == 0:
            for gi, w in enumerate((2, 4, 8, 16)):
                cc, p0 = gi // 2, 64 * (gi % 2)
                t = np.arange(16, dtype=np.float32)
                pcorr[p0:p0 + 64, cc, :] = (w / np.minimum(t + 1.0, float(w)))[None, :]
        m = dict(shared)
        m.update(xp=np.ascontiguousarray(xpr[seq, T * half:T * half + T]), xs=np.ascontiguousarray(xsa[b0:b0 + NS, 0]),
                 ck=np.ascontiguousarray(ckk[:, b0:b0 + NS]), cv=np.ascontiguousarray(cvv[:, b0:b0 + NS]),
                 sh=np.ascontiguousarray(shh[:, b0:b0 + NS]),
                 sc=np.ascontiguousarray(scc[:, b0:b0 + NS].reshape(L, NS * 3, 256)),
                 spool=np.ascontiguousarray(spp[:, b0:b0 + NS].reshape(L, NS * 15, 256)),
                 nm0=np.full((128, 1), NEG if half == 0 else 0.0, np.float32),
                 flag=np.full((128, 1), 0.0 if half == 0 else 1.0, np.float32),
                 pcorr=pcorr, st_in=np.zeros((L, 147, 256), np.float32), emask=emask, sel=sel)
        in_maps.append(m)
    if _NC is None:
        _NC = build()
    res = run_bass_kernel_spmd(_NC, in_maps, core_ids=list(range(NCORES))).results
    y_prompt = np.zeros((4, 4096, 1024), np.float32)
    for c in range(NCORES):
        y_prompt[c // 2, T * (c % 2):T * (c % 2) + T] = res[c]["yp"]
    y_sample = np.concatenate([res[c]["ys"] for c in range(NCORES)], 0).reshape(128, 1, 1024)
    sto = np.stack([res[2 * s + 1]["st_out"] for s in range(4)], 1)
    p_k = np.ascontiguousarray(sto[:, :, 0:128, 0:128]).reshape(L, 4, 128, 2, 64)
    p_v = np.ascontiguousarray(sto[:, :, 0:128, 128:256]).reshape(L, 4, 128, 2, 64)
    p_conv = np.ascontiguousarray(sto[:, :, 128:131, :])
    p_pool = np.ascontiguousarray(sto[:, :, 131:146, :])
    p_h = np.ascontiguousarray(sto[:, :, 146, :])
    cat = lambda k: np.concatenate([res[c][k] for c in range(NCORES)], 1)
    s_k = cat("s_k").reshape(L, 128, 128, 2, 64); s_v = cat("s_v").reshape(L, 128, 128, 2, 64)
    return (y_prompt, y_sample, p_k, p_v, p_h, p_conv, p_pool, s_k, s_v, cat("s_h"), cat("s_conv"), cat("s_pool"))
```

```python
import contextlib
import numpy as np
import concourse.bass as bass
import concourse.mybir as mybir
from concourse.bass_utils import run_bass_kernel_spmd

F32 = mybir.dt.float32
BF16 = mybir.dt.bfloat16
AF = mybir.ActivationFunctionType
ALU = mybir.AluOpType
AX = mybir.AxisListType

NCORES = 8
T = 2048
NS = 16
TT = T + NS
L = 2
ALPHA = float((2.0 * L) ** 0.25)
EPS = 1e-5
NEG = -1e30
NPV = 50
USE_CC = False
STOP = None
NOSELF = False
XMODE = 2


class _Stop(Exception):
    pass


class Ctx:
    def __init__(self, nc):
        self.nc = nc
        self.engs = {"pe": nc.tensor, "act": nc.scalar, "dve": nc.vector, "pool": nc.gpsimd, "sp": nc.sync}
        self.esem = {e: nc.alloc_semaphore(name="sem_" + e) for e in ["pe", "act", "dve", "pool"]}
        self.ecnt = {e: 0 for e in self.esem}
        self.seen = {e: {} for e in self.engs}
        self.res = {}
        self.dsem = {}
        self.nbank = 0
        self.bank_pool = list(range(8))
        self.ccs = []

    def bank(self):
        b = self.bank_pool[self.nbank % len(self.bank_pool)]
        self.nbank += 1
        return b

    def _wait(self, eng, tok):
        sem, val, src = tok
        if src == "pe" and eng == "pe":
            return
        if NOSELF and src == eng:
            return
        d = self.seen[eng]
        if d.get(id(sem), 0) >= val:
            return
        self.engs[eng].wait_ge(sem, val)
        d[id(sem)] = val

    def _deps(self, eng, reads, writes):
        for k in reads:
            st = self.res.get(k)
            if st and st["w"]:
                self._wait(eng, st["w"])
        for k in writes:
            st = self.res.get(k)
            if st:
                if st["w"]:
                    self._wait(eng, st["w"])
                for r in st["r"]:
                    self._wait(eng, r)

    def _reg(self, tok, reads, writes):
        for k in reads:
            self.res.setdefault(k, {"w": None, "r": []})["r"].append(tok)
        for k in writes:
            self.res[k] = {"w": tok, "r": []}

    def op(self, eng, fn, reads=(), writes=(), signal=True):
        psr = [k for k in reads if isinstance(k, tuple) and k[0] == "ps"]
        if psr:
            reads = [k for k in reads if k not in psr]
            writes = list(writes) + psr
        self._deps(eng, reads, writes)
        ins = fn()
        if signal:
            self.ecnt[eng] += 1
            ins.then_inc(self.esem[eng], 1)
            val = self.ecnt[eng]
        else:
            val = self.ecnt[eng] + 1
        self._reg((self.esem[eng], val, eng), reads, writes)

    def dma(self, q, pairs, reads=(), writes=(), key=None, **kw):
        if not isinstance(pairs, list):
            pairs = [pairs]
        self._deps(q, reads, writes)
        if key not in self.dsem:
            self.dsem[key] = [self.nc.alloc_semaphore(name="d_" + str(key)), 0]
        ent = self.dsem[key]
        if ent[1] > 0:
            self._wait(q, (ent[0], ent[1], "dma"))
        for out, in_ in pairs:
            self.engs[q].dma_start(out=out, in_=in_, **kw).then_inc(ent[0], 16)
            ent[1] += 16
        self._reg((ent[0], ent[1], "dma"), reads, writes)

    def collective(self, in_ap, out_ap, reads=(), writes=()):
        self._deps("pool", reads, writes)
        sem = self.nc.alloc_semaphore(name="cc%d" % len(self.ccs))
        self.ccs.append(sem)
        self.nc.gpsimd.collective_compute("AllGather", ALU.bypass, replica_groups=[[0, 1], [2, 3], [4, 5], [6, 7]],
                                          ins=[in_ap], outs=[out_ap]).then_inc(sem, 1)
        self._reg((sem, 1, "cc"), reads, writes)

    def barrier(self):
        toks = [(self.esem[e], self.ecnt[e], e) for e in self.esem if self.ecnt[e] > 0]
        toks += [(s, c, "dma") for s, c in self.dsem.values() if c > 0]
        for e in self.engs:
            for t in toks:
                if t[2] == e:
                    continue
                self._wait(e, t)
        self.res = {}

    def finish(self):
        for s, c in self.dsem.values():
            if c > 0:
                self._wait("sp", (s, c, "dma"))
        for e in self.esem:
            if self.ecnt[e] > 0:
                self._wait("sp", (self.esem[e], self.ecnt[e], e))


def build():
    nc = bass.Bass("TRN2", target_bir_lowering=False)
    C = Ctx(nc)

    def din(name, shape):
        return nc.dram_tensor(name, shape, F32, kind="ExternalInput").ap()

    def dout(name, shape):
        return nc.dram_tensor(name, shape, F32, kind="ExternalOutput").ap()

    xp = din("xp", [T, 1024]); xs = din("xs", [NS, 1024])
    ck = din("ck", [L, NS, 128, 128]); cv = din("cv", [L, NS, 128, 128])
    sh = din("sh", [L, NS, 256]); sc = din("sc", [L, NS * 3, 256]); spool = din("spool", [L, NS * 15, 256])
    w_in = din("w_in", [L, 1024, 1536]); sinks = din("sinks", [L, 8])
    conv_w = din("conv_w", [L, 4, 256]); conv_b = din("conv_b", [L, 256])
    ga_w = din("ga_w", [L, 4, 64, 64]); ga_b = din("ga_b", [L, 256])
    gx_w = din("gx_w", [L, 4, 64, 64]); gx_b = din("gx_b", [L, 256])
    lam = din("lam", [L, 256]); pool_w = din("pool_w", [L, 4, 64, 64]); pool_s = din("pool_s", [L, 256])
    w_out = din("w_out", [L, 1024, 1024]); ln1_g = din("ln1_g", [L, 1024]); ln1_b = din("ln1_b", [L, 1024])
    w_ff1 = din("w_ff1", [L, 1024, 4096]); w_ff2 = din("w_ff2", [L, 4096, 1024])
    ln2_g = din("ln2_g", [L, 1024]); ln2_b = din("ln2_b", [L, 1024])
    ident_d = din("ident", [128, 128]); bcur_d = din("bcur", [2, 128, 512]); bprev_d = din("bprev", [2, 128, 512])
    sbias_d = din("sbias", [128, 128]); nm0_d = din("nm0", [128, 1]); flag_d = din("flag", [128, 1])
    pcorr_d = din("pcorr", [128, 2, 16]); st_in = din("st_in", [L, 147, 256])
    emask_d = din("emask", [128, 2]); sel_d = din("sel", [128, 64])

    cc1_in = nc.dram_tensor("cc1_in", [L, 146, 256], F32, kind="Internal").ap()
    cc1_out = nc.dram_tensor("cc1_out", [L, 292, 256], F32, kind="Internal").ap()
    cc2_in = nc.dram_tensor("cc2_in", [L, 1, 256], F32, kind="Internal").ap()
    cc2_out = nc.dram_tensor("cc2_out", [L, 2, 256], F32, kind="Internal").ap()
    yp = dout("yp", [T, 1024]); ys = dout("ys", [NS, 1024]); st_out = dout("st_out", [L, 147, 256])
    s_k = dout("s_k", [L, NS, 128, 128]); s_v = dout("s_v", [L, NS, 128, 128])
    s_h = dout("s_h", [L, NS, 256]); s_conv = dout("s_conv", [L, NS, 3, 256]); s_pool = dout("s_pool", [L, NS, 15, 256])

    op = C.op
    dma = C.dma
    pe, act, dve, pool = nc.tensor, nc.scalar, nc.vector, nc.gpsimd

    top = contextlib.ExitStack()

    uid = [0]

    def sbt(stack, name, shape, dt=F32):
        uid[0] += 1
        return stack.enter_context(nc.sbuf_tensor("sb%d_%s" % (uid[0], name), shape, dt))

    ps = top.enter_context(nc.psum_tensor("psum_all", [128, 8, 512], F32))
    xf = sbt(top, "xf", [128, 8, TT]); xb = sbt(top, "xb", [128, 8, TT], BF16)
    ident = sbt(top, "ident", [128, 128]); ones = sbt(top, "ones", [128, 128], BF16)
    bcur = sbt(top, "bcur", [128, 2, 512], BF16); bprev = sbt(top, "bprev", [128, 2, 512], BF16)
    sbias = sbt(top, "sbias", [128, 128]); nm0 = sbt(top, "nm0", [128, 1]); flag = sbt(top, "flag", [128, 1])
    pcorr = sbt(top, "pcorr", [128, 2, 16]); pv = sbt(top, "pv", [128, 2 * NPV]); der = sbt(top, "der", [128, L, 8])
    es64 = sbt(top, "es64", [128, 16]); essamp = sbt(top, "essamp", [128, 2])
    wbd = sbt(top, "wbd", [128, L, 6, 128], BF16)
    cneg = sbt(top, "cneg", [128, 256], BF16)
    emask = sbt(top, "emask", [128, 2]); selm = sbt(top, "selm", [128, 64], BF16)
    winb = sbt(top, "winb", [128, 8, 1536], BF16)

    def load_winb(l):
        for kk in range(4):
            dma("pool", (winb[:, 2 * kk:2 * kk + 2, :], w_in[l, 256 * kk:256 * kk + 256, :].rearrange("(k p) n -> p k n", p=128)),
                writes=["winb"], key="winb%d" % kk)

    def PS(b, n=512, p0=0, p1=128, o=0):
        return ps[p0:p1, b, o:o + n]

    dma("sp", (ident[:], ident_d[:, :]), writes=["ident"], key="c0")
    dma("pool", (bcur[:], bcur_d.rearrange("k s n -> s k n")), writes=["bcur"], key="c1")
    dma("pool", (bprev[:], bprev_d.rearrange("k s n -> s k n")), writes=["bprev"], key="c2")
    dma("sp", [(sbias[:], sbias_d[:, :]), (nm0[:], nm0_d[:, :]), (flag[:], flag_d[:, :]), (pcorr[:], pcorr_d[:, :, :])],
        writes=["smallc"], key="c3")
    dma("sp", (emask[:], emask_d[:, :]), writes=["emask"], key="c8")
    dma("pool", (selm[:], sel_d[:, :]), writes=["selm"], key="c9")
    op("dve", lambda: dve.memset(ones[:], 1.0), writes=["ones"])
    op("dve", lambda: dve.memset(cneg[:], -0.5), writes=["cneg"])
    op("dve", lambda: dve.memset(wbd[:], 0.0), writes=["wbd"])
    if STOP == 'i1':
        C.finish(); top.close(); return nc
    with contextlib.ExitStack() as ph:
        prow = sbt(ph, "prow", [2 * NPV, 128])
        plist = [(conv_w, 0, 8), (conv_b, 8, 2), (ga_b, 10, 2), (gx_b, 12, 2), (lam, 14, 2), (pool_s, 16, 2),
                 (ln1_g, 18, 8), (ln1_b, 26, 8), (ln2_g, 34, 8), (ln2_b, 42, 8)]
        pairs = []
        for l in range(L):
            for (t, off, n) in plist:
                if t is conv_w:
                    src = t[l].rearrange("t (c p) -> (t c) p", p=128)
                else:
                    src = t[l].rearrange("(c p) -> c p", p=128)
                pairs.append((prow[NPV * l + off:NPV * l + off + n, :], src))
        dma("sp", pairs, writes=["prow"], key="c4")
        b0 = C.bank()
        op("pe", lambda: pe.transpose(PS(b0, 2 * NPV), prow[:], ident[0:2 * NPV, 0:2 * NPV]),
           reads=["prow", "ident"], writes=[("ps", b0)])
        op("dve", lambda: dve.tensor_copy(out=pv[:], in_=PS(b0, 2 * NPV)), reads=[("ps", b0)], writes=["pv"])
        if STOP == 'i2':
            C.finish(); ph.close(); top.close(); return nc
        tl = sbt(ph, "tl", [128, 2])
        for l in range(L):
            pb = NPV * l
            op("act", lambda: act.activation(out=tl[:], in_=pv[:, pb + 14:pb + 16], func=AF.Exp, scale=-1.0),
               reads=["pv"], writes=["tl"])
            op("dve", lambda: dve.tensor_scalar_add(tl[:], tl[:], 1.0), reads=["tl"], writes=["tl"])
            op("act", lambda: act.activation(out=tl[:], in_=tl[:], func=AF.Ln), reads=["tl"], writes=["tl"])
            op("dve", lambda: dve.tensor_scalar_mul(der[:, l, 0:2], tl[:], -4.0), reads=["tl"], writes=["der"])
            op("dve", lambda: dve.tensor_scalar_mul(der[:, l, 2:4], tl[:], -8.0), reads=["tl"], writes=["der"])
            op("dve", lambda: dve.tensor_scalar_mul(der[:, l, 4:6], pv[:, pb + 10:pb + 12], 0.5), reads=["pv"], writes=["der"])
            op("dve", lambda: dve.tensor_scalar_mul(der[:, l, 6:8], pv[:, pb + 12:pb + 14], 0.5), reads=["pv"], writes=["der"])
        if STOP == 'i3':
            C.finish(); ph.close(); top.close(); return nc
        dma("sp", (es64[:], sinks.rearrange("l h -> (l h)").partition_broadcast(128)), writes=["es64"], key="c5")
        if STOP == 'i3a':
            C.finish(); ph.close(); top.close(); return nc
        op("act", lambda: act.activation(out=es64[:], in_=es64[:], func=AF.Exp), reads=["es64"], writes=["es64"])
        pairs = []
        for l in range(L):
            for h in range(8):
                pairs.append((essamp[16 * h:16 * h + 16, l:l + 1], sinks[l, h:h + 1].partition_broadcast(16)))
        dma("sp", pairs, writes=["essamp"], key="c6")
        op("act", lambda: act.activation(out=essamp[:], in_=essamp[:], func=AF.Exp), reads=["essamp"], writes=["essamp"])
        if STOP == 'i4':
            C.finish(); ph.close(); top.close(); return nc
        pairs = []
        for l in range(L):
            for n in range(4):
                c, e = n // 2, n % 2
                for si, wt in ((0, ga_w), (2, gx_w), (4, pool_w)):
                    pairs.append((wbd[64 * e:64 * e + 64, l, si + c, 64 * e:64 * e + 64], wt[l, n]))
        dma("pool", pairs, writes=["wbd"], key="c7")
        if STOP == 'i5':
            C.finish(); ph.close(); top.close(); return nc

        for l in range(L):
            dma("sp", [(s_k[l, :, 0:127, :], ck[l, :, 1:128, :]), (s_v[l, :, 0:127, :], cv[l, :, 1:128, :]),
                       (s_conv[l, :, 0:2, :], sc[l].rearrange("(b t) n -> b t n", t=3)[:, 1:3, :]),
                       (s_pool[l, :, 0:14, :], spool[l].rearrange("(b t) n -> b t n", t=15)[:, 1:15, :])],
                key="dd%d" % l)
        if STOP == 'i6':
            C.finish(); ph.close(); top.close(); return nc
        if STOP == 'i6b':
            C.barrier(); C.finish(); ph.close(); top.close(); return nc

        xst = [sbt(ph, "xst%d" % i, [128, 1024]) for i in range(2)]
        for ti in range(16 if STOP == 'initA' else (1 if STOP == 'initB' else 17)):
            st = xst[ti % 2]
            sk = "xst%d" % (ti % 2)
            n = 128 if ti < 16 else NS
            src = xp[128 * ti:128 * ti + 128, :] if ti < 16 else xs[:, :]
            dma("sp", (st[0:n, :], src), writes=[sk], key=sk)
            for hb in range(0 if XMODE == 0 else 2):
                b = C.bank()
                for mm in range(4):
                    m = 4 * hb + mm
                    op("pe", lambda: pe.transpose(PS(b, n, o=n * mm), st[0:n, 128 * m:128 * m + 128], ident[0:n, 0:n]),
                       reads=[sk, "ident"], writes=[("ps", b)], signal=(mm == 3))
                src_ps = ps[:, b, 0:4 * n].rearrange("p (m n) -> p m n", n=n)
                op("dve", lambda: dve.tensor_copy(out=xf[:, 4 * hb:4 * hb + 4, 128 * ti:128 * ti + n], in_=src_ps),
                   reads=[("ps", b)], writes=[("xf", ti)])
                if XMODE >= 2:
                    op("act", lambda: act.activation(out=xb[:, 4 * hb:4 * hb + 4, 128 * ti:128 * ti + n], in_=src_ps, func=AF.Copy),
                       reads=[("ps", b)], writes=[("xb", ti)])
    C.barrier()
    if STOP in ('init', 'initA', 'initB'):
        C.finish(); top.close(); return nc

    def layer_norm(W, l, which, c0, N, xkey):
        for _ in layer_norm_g(W, l, which, c0, N, xkey):
            pass

    def layer_norm_g(W, l, which, c0, N, xkey, banks=None, slack=0):
        gcol = NPV * l + (18 if which == 1 else 34)
        bcol = gcol + 8
        ybf, ybk = W["ybf"], W["ybfk"]
        k0_, k1_, k2_, k3_ = W["stk"]
        cap = ybf.shape[1]
        b1, b2 = banks if banks is not None else (C.bank(), C.bank())
        for (bb_, fn_) in ((b1, AF.Copy), (b2, AF.Square)):
            for h0 in range(0, 8, cap):
                op("act", lambda: act.activation(out=ybf[:, 0:cap, 0:N], in_=xf[:, h0:h0 + cap, c0:c0 + N], func=fn_), reads=[xkey], writes=ybk)
                yield
                for m in range(h0, h0 + cap):
                    op("pe", lambda: pe.matmul(PS(bb_, N), lhsT=ones[:], rhs=ybf[:, m - h0, 0:N], start=(m == 0), stop=(m == 7)),
                       reads=ybk + ["ones"], writes=[("ps", bb_)], signal=(m == h0 + cap - 1))
            yield
        for _ in range(slack):
            yield
        mean, msq, rstd, nmr = [a_[:, 0:N] for a_ in W["st"]]
        op("dve", lambda: dve.tensor_scalar_mul(mean, PS(b1, N), 1.0 / 1024), reads=[("ps", b1)], writes=[k0_])
        op("dve", lambda: dve.tensor_tensor(out=msq, in0=mean, in1=mean, op=ALU.mult), reads=[k0_], writes=[k1_])
        op("dve", lambda: dve.scalar_tensor_tensor(out=msq, in0=PS(b2, N), scalar=1.0 / 1024, in1=msq, op0=ALU.mult, op1=ALU.subtract),
           reads=[("ps", b2), k1_], writes=[k1_])
        op("dve", lambda: dve.tensor_scalar_add(msq, msq, EPS), reads=[k1_], writes=[k1_])
        op("act", lambda: act.activation(out=rstd, in_=msq, func=AF.Ln), reads=[k1_], writes=[k2_])
        op("act", lambda: act.activation(out=rstd, in_=rstd, func=AF.Exp, scale=-0.5), reads=[k2_], writes=[k2_])
        yield
        for _ in range(slack):
            yield
        op("dve", lambda: dve.scalar_tensor_tensor(out=nmr, in0=mean, scalar=-1.0, in1=rstd, op0=ALU.mult, op1=ALU.mult),
           reads=[k0_, k2_], writes=[k3_])
        xblk = xf[:, :, c0:c0 + N]
        op("dve", lambda: dve.tensor_tensor(out=xblk, in0=xblk, in1=rstd.unsqueeze(1).to_broadcast([128, 8, N]), op=ALU.mult), reads=[xkey, k2_], writes=[xkey])
        yield
        op("dve", lambda: dve.tensor_tensor(out=xblk, in0=xblk, in1=nmr.unsqueeze(1).to_broadcast([128, 8, N]), op=ALU.add), reads=[xkey, k3_], writes=[xkey])
        yield
        for m in range(8):
            xs_ = xf[:, m, c0:c0 + N]
            op("dve", lambda: dve.tensor_scalar(out=xs_, in0=xs_, scalar1=pv[:, gcol + m:gcol + m + 1], scalar2=pv[:, bcol + m:bcol + m + 1],
                                                op0=ALU.mult, op1=ALU.add), reads=[xkey, "pv"], writes=[xkey])
            if m % 4 == 3:
                yield
        op("act", lambda: act.activation(out=xb[:, :, c0:c0 + N], in_=xblk, func=AF.Copy), reads=[xkey], writes=[xkey + "b"])

    def pool_part(W, l, N, zwin, zcur, mixb, first_corr=None):
        pb = NPV * l
        diffb = W["diffb"]
        for (p0, p1, c, win, w, wkey) in zwin:
            if first_corr is not None:
                first_corr(p0, p1, c, win, wkey)
            op("dve", lambda: dve.scalar_tensor_tensor(out=diffb[p0:p1, c, 0:N], in0=win, scalar=1.0 / w, in1=zcur(c, p0, p1),
                                                       op0=ALU.mult, op1=ALU.subtract), reads=[wkey, "zp"], writes=["diffb"])
        bp = C.bank()
        for c in range(2):
            op("pe", lambda: pe.matmul(PS(bp, N, o=256 * c), lhsT=wbd[:, l, 4 + c, :], rhs=diffb[:, c, 0:N], start=True, stop=True),
               reads=["diffb", "wbd"], writes=[("ps", bp)])
        for c in range(2):
            op("act", lambda: act.activation(out=mixb[:, 2 + c, :], in_=PS(bp, N, o=256 * c), func=AF.Identity, scale=pv[:, pb + 16 + c:pb + 17 + c]),
               reads=[("ps", bp), "pv"], writes=["mixb"])

    def lru_part_g(W, l, N, xc_taps, xc, gr, h_apply, finish, sfx=""):
        pb = NPV * l
        wk = W["wk"]
        xcb = W["xcb"]
        for c in range(2):
            op("act", lambda: act.activation(out=xc[:, c, :], in_=xc_taps(c, 0), func=AF.Identity, scale=pv[:, pb + c:pb + c + 1],
                                             bias=pv[:, pb + 8 + c:pb + 9 + c]),
               reads=["xr" + sfx, "pv"], writes=["xc" + sfx])
            for tap in range(1, 4):
                op("dve", lambda: dve.scalar_tensor_tensor(out=xc[:, c, :], in0=xc_taps(c, tap), scalar=pv[:, pb + 2 * tap + c:pb + 2 * tap + c + 1],
                                                           in1=xc[:, c, :], op0=ALU.mult, op1=ALU.add),
                   reads=["xr" + sfx, "pv", "xc" + sfx], writes=["xc" + sfx])
        yield
        op("act", lambda: act.activation(out=xcb[:, :, 0:N], in_=xc, func=AF.Copy), reads=["xc" + sfx], writes=["xcb" + sfx])
        bg, bh = C.bank(), C.bank()
        for c in range(2):
            op("pe", lambda: pe.matmul(PS(bg, N, o=256 * c), lhsT=wbd[:, l, 0 + c, :], rhs=xcb[:, c, 0:N], start=True, stop=True),
               reads=["xcb" + sfx, "wbd"], writes=[("ps", bg)])
            op("pe", lambda: pe.matmul(PS(bh, N, o=256 * c), lhsT=wbd[:, l, 2 + c, :], rhs=xcb[:, c, 0:N], start=True, stop=True),
               reads=["xcb" + sfx, "wbd"], writes=[("ps", bh)])
        tha, thx, a_, a2 = [wk[i][:, :, 0:N] for i in range(4)]
        hs = a2
        for c in range(2):
            op("act", lambda: act.activation(out=tha[:, c, :], in_=PS(bg, N, o=256 * c), func=AF.Tanh, scale=0.5, bias=der[:, l, 4 + c:5 + c]),
               reads=[("ps", bg), "der"], writes=["wk0" + sfx])
            op("act", lambda: act.activation(out=thx[:, c, :], in_=PS(bh, N, o=256 * c), func=AF.Tanh, scale=0.5, bias=der[:, l, 6 + c:7 + c]),
               reads=[("ps", bh), "der"], writes=["wk1" + sfx])
            op("act", lambda: act.activation(out=a_[:, c, :], in_=tha[:, c, :], func=AF.Exp, scale=der[:, l, c:c + 1], bias=der[:, l, c:c + 1]),
               reads=["wk0" + sfx, "der"], writes=["wk2" + sfx])
            op("act", lambda: act.activation(out=a2[:, c, :], in_=tha[:, c, :], func=AF.Exp, scale=der[:, l, 2 + c:3 + c], bias=der[:, l, 2 + c:3 + c]),
               reads=["wk0" + sfx, "der"], writes=["wk3" + sfx])
        yield
        op("dve", lambda: dve.tensor_scalar(out=a2, in0=a2, scalar1=-1.0, scalar2=1.0, op0=ALU.mult, op1=ALU.add), reads=["wk3" + sfx], writes=["wk3" + sfx])
        op("act", lambda: act.activation(out=tha, in_=a2, func=AF.Ln), reads=["wk3" + sfx], writes=["wk0" + sfx])
        op("act", lambda: act.activation(out=tha, in_=tha, func=AF.Exp, scale=0.5), reads=["wk0" + sfx], writes=["wk0" + sfx])
        op("dve", lambda: dve.scalar_tensor_tensor(out=thx, in0=thx, scalar=1.0, in1=xc, op0=ALU.add, op1=ALU.mult),
           reads=["wk1" + sfx, "xc" + sfx], writes=["wk1" + sfx])
        op("dve", lambda: dve.scalar_tensor_tensor(out=thx, in0=thx, scalar=0.5, in1=tha, op0=ALU.mult, op1=ALU.mult),
           reads=["wk1" + sfx, "wk0" + sfx], writes=["wk1" + sfx])
        yield
        h_apply(a_, thx, hs)
        yield
        op("act", lambda: act.activation(out=tha, in_=gr, func=AF.Square), reads=["gr" + sfx], writes=["wk0" + sfx])
        op("dve", lambda: dve.tensor_scalar(out=tha, in0=tha, scalar1=0.044715, scalar2=1.0, op0=ALU.mult, op1=ALU.add), reads=["wk0" + sfx], writes=["wk0" + sfx])
        op("dve", lambda: dve.tensor_tensor(out=tha, in0=tha, in1=gr, op=ALU.mult), reads=["wk0" + sfx, "gr" + sfx], writes=["wk0" + sfx])
        op("act", lambda: act.activation(out=tha, in_=tha, func=AF.Tanh, scale=0.7978845608028654), reads=["wk0" + sfx], writes=["wk0" + sfx])
        op("dve", lambda: dve.scalar_tensor_tensor(out=tha, in0=tha, scalar=1.0, in1=gr, op0=ALU.add, op1=ALU.mult), reads=["wk0" + sfx, "gr" + sfx], writes=["wk0" + sfx])
        yield
        finish(hs, tha, thx)

    def lru_part(W, l, N, xc_taps, xc, gr, h_apply, finish):
        for _ in lru_part_g(W, l, N, xc_taps, xc, gr, h_apply, finish):
            pass

    def lru_pool_common(W, l, N, xc_taps, xc, gr, zwin, zcur, h_apply, mixb, first_corr=None):
        pool_part(W, l, N, zwin, zcur, mixb, first_corr)

        def fin(hs, ge, _):
            op("dve", lambda: dve.scalar_tensor_tensor(out=mixb[:, 0:2, :], in0=hs, scalar=0.5, in1=ge, op0=ALU.mult, op1=ALU.mult),
               reads=["wk3", "wk0"], writes=["mixb"])
        lru_part(W, l, N, xc_taps, xc, gr, h_apply, fin)

    def wout_ln(W, l, c0, N, attn, mixb, xkey, ln=True):
        for _ in wout_g(W, l, c0, N, attn, mixb, xkey):
            pass
        if ln:
            layer_norm(W, l, 1, c0, N, xkey)

    def wout_g(W, l, c0, N, attn, mixb, xkey):
        woa, wob = W["woa"], W["wob"]
        for m in range(8):
            b = C.bank()
            for h in range(4):
                op("pe", lambda: pe.matmul(PS(b, N), lhsT=woa[:, h, 128 * m:128 * m + 128], rhs=attn[:, h, :], start=(h == 0), stop=False),
                   reads=["attnT", "woa"], writes=[("ps", b)], signal=False)
            for j in range(4):
                op("pe", lambda: pe.matmul(PS(b, N), lhsT=wob[:, j, 128 * m:128 * m + 128], rhs=mixb[:, j, :], start=False, stop=(j == 3)),
                   reads=["mixb", "wob"], writes=[("ps", b)], signal=(j == 3))
            op("dve", lambda: dve.scalar_tensor_tensor(out=xf[:, m, c0:c0 + N], in0=xf[:, m, c0:c0 + N], scalar=ALPHA, in1=PS(b, N),
                                                       op0=ALU.mult, op1=ALU.add), reads=[("ps", b), xkey], writes=[xkey])
            if m % 2 == 1:
                yield

    def chk(stage):
        if STOP == stage:
            raise _Stop()

    def run_layers():
      for l in range(L):
          pb = NPV * l
          with contextlib.ExitStack() as ph:
              W = {}
              W["woa"] = woa = sbt(ph, "woa", [128, 4, 1024], BF16)
              W["wob"] = wob = sbt(ph, "wob", [128, 4, 1024], BF16)
              W["wk"] = [sbt(ph, "wk%d" % i, [128, 2, 272]) for i in range(4)]
              W["st"] = [W["wk"][2][:, 0, 0:256], W["wk"][2][:, 1, 0:256], W["wk"][3][:, 0, 0:256], W["wk"][3][:, 1, 0:256]]
              W["stk"] = ["wk2", "wk2", "wk3", "wk3"]
              st_ph, stk_ph = W["st"], W["stk"]
              W["diffb"] = sbt(ph, "diffb", [128, 2, 256], BF16)
              W["tA"] = W["wk"][0][:, 0, 0:256]; W["tAk"] = "wk0"
              W["tB"] = W["wk"][1][:, 0, 0:256]; W["tBk"] = "wk1"
              if l == 0:
                  load_winb(0)
              dma("pool", (woa[:], w_out[l, 0:512, :].rearrange("(j p) n -> p j n", p=128)), writes=["woa"], key="woa")
              dma("pool", (wob[:], w_out[l, 512:1024, :].rearrange("(j p) n -> p j n", p=128)), writes=["wob"], key="wob")

              with contextlib.ExitStack() as pp:
                  rec0 = sbt(pp, "rec0", [128, 2, T], BF16)
                  corr = sbt(pp, "corr", [128, 2, T], BF16)
                  kTb = sbt(pp, "kTb", [64, 2, 384], BF16)
                  Vb = sbt(pp, "Vb", [128, 3, 128], BF16)
                  xr_ext = sbt(pp, "xr_ext", [128, 2, 259])
                  zp_ext = sbt(pp, "zp_ext", [128, 2, 271])
                  hcar = sbt(pp, "hcar", [128, 2]); Acar = sbt(pp, "Acar", [128, 2]); hst = sbt(pp, "hst", [128, 2]); hfin = sbt(pp, "hfin", [128, 2])
                  wkt = W["wk"]

                  with contextlib.ExitStack() as pa:
                      stq = rec0[:].rearrange("p c n -> p (c n)")[:, 0:1536].bitcast(F32)
                      cview = corr[:].rearrange("p c n -> p (c n)")[:, 0:1024].bitcast(F32)
                      sth = cview[:, 0:256]; stc = cview[0:18, 256:512]
                      b1, b2 = C.bank(), C.bank()
                      for k in range(8):
                          op("pe", lambda: pe.matmul(PS(b1), lhsT=xb[:, k, T - 128:T], rhs=winb[:, k, 512:1024], start=(k == 0), stop=(k == 7)),
                             reads=["winb"], writes=[("ps", b1)], signal=(k == 7))
                      for k in range(8):
                          op("pe", lambda: pe.matmul(PS(b2, 256), lhsT=xb[:, k, T - 128:T], rhs=winb[:, k, 1280:1536], start=(k == 0), stop=(k == 7)),
                             reads=["winb"], writes=[("ps", b2)], signal=(k == 7))
                      op("dve", lambda: dve.tensor_copy(out=stq[:, 0:512], in_=PS(b1)), reads=[("ps", b1)], writes=["rec0"])
                      op("dve", lambda: dve.tensor_copy(out=stq[:, 512:768], in_=PS(b2, 256)), reads=[("ps", b2)], writes=["rec0"])
                      dma("sp", [(st_out[l, 0:128, :], stq[:, 0:256]), (st_out[l, 128:131, :], stq[125:128, 256:512]),
                                 (st_out[l, 131:146, :], stq[113:128, 512:768])], reads=["rec0"], key="stq")
                      dma("sp", [(cc1_in[l, 0:128, :], stq[:, 0:256]), (cc1_in[l, 128:131, :], stq[125:128, 256:512]),
                                 (cc1_in[l, 131:146, :], stq[113:128, 512:768])], reads=["rec0"], writes=["cc1in"], key="stq2")
                      C.collective(cc1_in[l], cc1_out[l], reads=["cc1in"], writes=["cc1out"])
                      dma("sp", [(sth[:], cc1_out[l, 0:128, :]), (stc[:], cc1_out[l, 128:146, :])], reads=["cc1out"], writes=["corr"], key="sth")
                      bk = C.bank()
                      for kv in range(2):
                          op("pe", lambda: pe.transpose(PS(bk, 128, 0, 64, 128 * kv), sth[:, 64 * kv:64 * kv + 64], ident[:, :]),
                             reads=["corr", "ident"], writes=[("ps", bk)], signal=(kv == 1))
                      op("dve", lambda: dve.tensor_scalar(out=kTb[:, :, 0:128], in0=ps[0:64, bk, 0:256].rearrange("p (k n) -> p k n", n=128),
                                                          scalar1=flag[0:64, 0:1], scalar2=None, op0=ALU.mult),
                         reads=[("ps", bk), "smallc"], writes=["kTb"])
                      op("dve", lambda: dve.tensor_scalar(out=Vb[:, 0, :], in0=sth[:, 128:256], scalar1=flag[:, 0:1], scalar2=None, op0=ALU.mult),
                         reads=["corr", "smallc"], writes=["Vb"])
                      bk2 = C.bank()
                      for c in range(2):
                          op("pe", lambda: pe.transpose(PS(bk2, 18, o=32 * c), stc[0:18, 128 * c:128 * c + 128], ident[0:18, 0:18]),
                             reads=["corr", "ident"], writes=[("ps", bk2)], signal=(c == 1))
                      for c in range(2):
                          op("dve", lambda: dve.tensor_scalar(out=xr_ext[:, c, 0:3], in0=PS(bk2, 3, o=32 * c), scalar1=flag[:, 0:1], scalar2=None, op0=ALU.mult),
                             reads=[("ps", bk2), "smallc"], writes=["xr0"])
                          op("dve", lambda: dve.tensor_scalar(out=zp_ext[:, c, 0:15], in0=PS(bk2, 15, o=32 * c + 3), scalar1=flag[:, 0:1], scalar2=None, op0=ALU.mult),
                             reads=[("ps", bk2), "smallc"], writes=["zp"])
                  op("dve", lambda: dve.memset(hcar[:], 0.0), writes=["hcar"])
                  op("dve", lambda: dve.memset(Acar[:], 1.0), writes=["Acar"])

                  with contextlib.ExitStack() as pb_:
                      sets = []
                      for i in range(2):
                          d_ = {"gr": sbt(pb_, "gr%d" % i, [128, 2, 256]), "xc": sbt(pb_, "xc%d" % i, [128, 2, 256]),
                                "xcb": sbt(pb_, "xcb%d" % i, [128, 2, 256], BF16)}
                          d_["wk"] = W["wk"] if i == 0 else [sbt(pb_, "wkB%d" % q_, [128, 2, 272]) for q_ in range(4)]
                          d_["xr"] = xr_ext if i == 0 else sbt(pb_, "xr_extB", [128, 2, 259])
                          sets.append(d_)

                      def pre(bi):
                          c0 = 256 * bi
                          N = 256
                          i = bi % 2
                          sx = str(i)
                          S_ = sets[i]
                          xr_i, gr_i, xc_i = S_["xr"], S_["gr"], S_["xc"]
                          for (cb, dst, key) in ((768, xr_i[:, :, 3:259], "xr" + sx), (1024, gr_i[:, :, :], "gr" + sx)):
                              b = C.bank()
                              for c in range(2):
                                  for k in range(8):
                                      op("pe", lambda: pe.matmul(PS(b, N, o=256 * c), lhsT=winb[:, k, cb + 128 * c:cb + 128 * c + 128], rhs=xb[:, k, c0:c0 + N],
                                                                 start=(k == 0), stop=(k == 7)),
                                         reads=["winb"], writes=[("ps", b)], signal=(k == 7 and c == 1))
                              if key.startswith("gr"):
                                  op("act", lambda: act.activation(out=dst, in_=ps[:, b, :].rearrange("p (e n) -> p e n", n=256), func=AF.Copy),
                                     reads=[("ps", b)], writes=[key])
                              else:
                                  op("dve", lambda: dve.tensor_copy(out=dst, in_=ps[:, b, :].rearrange("p (e n) -> p e n", n=256)),
                                     reads=[("ps", b)], writes=[key])
                          if bi > 0:
                              xr_p = sets[1 - i]["xr"]
                              op("dve", lambda: dve.tensor_copy(out=xr_i[:, :, 0:3], in_=xr_p[:, :, 256:259]), reads=["xr" + str(1 - i)], writes=["xr" + sx])
                          yield

                          def h_apply(a_, bb, hs):
                              for c in range(2):
                                  op("dve", lambda: dve.tensor_tensor_scan(out=hs[:, c, :], data0=a_[:, c, :], data1=bb[:, c, :], initial=hcar[:, c:c + 1],
                                                                           op0=ALU.mult, op1=ALU.add), reads=["wk2" + sx, "wk1" + sx, "hcar"], writes=["wk3" + sx])
                              op("dve", lambda: dve.tensor_copy(out=hcar[:, :], in_=hs[:, :, 255]), reads=["wk3" + sx], writes=["hcar"])
                              for c in range(2):
                                  op("dve", lambda: dve.tensor_tensor_scan(out=bb[:, c, :], data0=a_[:, c, :], data1=cneg[:, 0:256], initial=Acar[:, c:c + 1],
                                                                           op0=ALU.mult, op1=ALU.max), reads=["wk2" + sx, "cneg", "Acar"], writes=["wk1" + sx])
                              op("dve", lambda: dve.tensor_copy(out=Acar[:, :], in_=bb[:, :, 255]), reads=["wk1" + sx], writes=["Acar"])

                          def fin(hs, ge, Acum):
                              op("dve", lambda: dve.scalar_tensor_tensor(out=rec0[:, :, c0:c0 + 256], in0=hs, scalar=0.5, in1=ge, op0=ALU.mult, op1=ALU.mult),
                                 reads=["wk3" + sx, "wk0" + sx], writes=["rec0"])
                              op("dve", lambda: dve.scalar_tensor_tensor(out=corr[:, :, c0:c0 + 256], in0=Acum, scalar=0.5, in1=ge, op0=ALU.mult, op1=ALU.mult),
                                 reads=["wk1" + sx, "wk0" + sx], writes=["corr"])
                          yield from lru_part_g(S_, l, N, lambda c, tap: xr_i[:, c, tap:tap + 256], xc_i[:], gr_i[:], h_apply, fin, sfx=sx)

                      pend = [pre(bi) for bi in range(8)]
                      active = []
                      while pend or active:
                          while len(active) < 2 and pend:
                              active.append(pend.pop(0))
                          for g_ in list(active):
                              try:
                                  next(g_)
                              except StopIteration:
                                  active.remove(g_)
                  C.barrier()
                  with nc.allow_non_contiguous_dma(reason="tiny h state"):
                      dma("sp", (cc2_in[l, 0, :].rearrange("(c p) -> p c", p=128), hcar[:, :]), reads=["hcar"], writes=["cc2in"], key="hst")
                  C.collective(cc2_in[l], cc2_out[l], reads=["cc2in"], writes=["cc2out"])
                  with nc.allow_non_contiguous_dma(reason="tiny h state"):
                      dma("sp", (hst[:, :], cc2_out[l, 0, :].rearrange("(c p) -> p c", p=128)), reads=["cc2out"], writes=["hst"], key="hst2")
                  op("dve", lambda: dve.tensor_scalar(out=hst[:], in0=hst[:], scalar1=flag[:, 0:1], scalar2=None, op0=ALU.mult), reads=["hst", "smallc"], writes=["hst"])
                  op("dve", lambda: dve.tensor_tensor(out=hfin[:], in0=Acar[:], in1=hst[:], op=ALU.mult), reads=["Acar", "hst"], writes=["hfin"])
                  op("dve", lambda: dve.tensor_tensor(out=hfin[:], in0=hfin[:], in1=hcar[:], op=ALU.add), reads=["hfin", "hcar"], writes=["hfin"])
                  with nc.allow_non_contiguous_dma(reason="tiny h state"):
                      dma("sp", (st_out[l, 146, :].rearrange("(c p) -> p c", p=128), hfin[:, :]), reads=["hfin"], key="hst3")

                  with contextlib.ExitStack() as pc:
                      qT = sbt(pc, "qT", [64, 8, 256], BF16)
                      attnT = sbt(pc, "attnT", [128, 4, 256], BF16)
                      mixb = sbt(pc, "mixb", [128, 4, 256], BF16)
                      tPy = [sbt(pc, "tPy%d" % i, [128, 1024]) for i in range(2)]
                      tP = [[tPy[i][:, 0:512], tPy[i][:, 512:1024]] for i in range(2)]
                      PT = [[sbt(pc, "PT%d%d" % (i, j_), [128, 512], BF16) for j_ in range(2)] for i in range(2)]
                      W["ybf"] = sbt(pc, "ybfL", [128, 4, 256], BF16)
                      W["ybfk"] = ["ybfL"]
                      stL = sbt(pc, "stL", [128, 3, 256])
                      W["st"] = [stL[:, 0, :], stL[:, 1, :], stL[:, 2, :], stL[:, 1, :]]
                      W["stk"] = ["stL0", "stL1", "stL2", "stL1"]
                      dd = wkt[3][:].rearrange("p c n -> p (c n)")[:, 0:256]
                      S2 = wkt[0][:, :, 0:270]; S4 = wkt[1][:, :, 0:268]
                      def front(bi):
                          c0 = 256 * bi
                          N = 256
                          xkey = "xP%d" % bi
                          xin = [xkey + "b"]

                          def rhs_x(k):
                              return xb[:, k, c0:c0 + N]
                          for j in range(4):
                              b = C.bank()
                              for e in range(2):
                                  h = 2 * j + e
                                  for k in range(8):
                                      op("pe", lambda: pe.matmul(PS(b, N, 0, 64, 256 * e), lhsT=winb[:, k, 64 * h:64 * h + 64], rhs=rhs_x(k),
                                                                 start=(k == 0), stop=(k == 7)),
                                         reads=["winb"] + xin, writes=[("ps", b)], signal=(k == 7 and e == 1))
                              op("act", lambda: act.activation(out=qT[:, 2 * j:2 * j + 2, :], in_=ps[0:64, b, :].rearrange("p (e n) -> p e n", n=256), func=AF.Copy),
                                 reads=[("ps", b)], writes=["qT"])
                              if j % 2 == 1:
                                  yield
                          b = C.bank()
                          for kv in range(2):
                              for k in range(8):
                                  op("pe", lambda: pe.matmul(PS(b, N, 0, 64, 256 * kv), lhsT=winb[:, k, 512 + 64 * kv:512 + 64 * kv + 64], rhs=rhs_x(k),
                                                             start=(k == 0), stop=(k == 7)),
                                     reads=["winb"] + xin, writes=[("ps", b)], signal=(k == 7 and kv == 1))
                          op("act", lambda: act.activation(out=kTb[:, :, 128:384], in_=ps[0:64, b, :].rearrange("p (e n) -> p e n", n=256), func=AF.Copy),
                             reads=[("ps", b)], writes=["kTb"])
                          yield
                          b = C.bank()
                          for c in range(2):
                              for k in range(8):
                                  op("pe", lambda: pe.matmul(PS(b, N, o=256 * c), lhsT=winb[:, k, 1280 + 128 * c:1280 + 128 * c + 128], rhs=rhs_x(k),
                                                             start=(k == 0), stop=(k == 7)),
                                     reads=["winb"] + xin, writes=[("ps", b)], signal=(k == 7 and c == 1))
                          op("dve", lambda: dve.tensor_copy(out=zp_ext[:, :, 15:271], in_=ps[:, b, :].rearrange("p (e n) -> p e n", n=256)),
                             reads=[("ps", b)], writes=["zp"])
                          yield
                          b = C.bank()
                          for i in range(2):
                              for k in range(8):
                                  op("pe", lambda: pe.matmul(PS(b, 128, o=128 * i), lhsT=xb[:, k, c0 + 128 * i:c0 + 128 * i + 128], rhs=winb[:, k, 640:768],
                                                             start=(k == 0), stop=(k == 7)),
                                     reads=["winb"] + xin, writes=[("ps", b)], signal=(k == 7 and i == 1))
                          op("act", lambda: act.activation(out=Vb[:, 1:3, :], in_=ps[:, b, 0:256].rearrange("p (e n) -> p e n", n=128), func=AF.Copy),
                             reads=[("ps", b)], writes=["Vb"])
                          yield
                          iters = [(qi, kv) for qi in range(2) for kv in range(2)]

                          def s1(it):
                              qi, kv = iters[it]
                              sx = it % 2
                              bs = [C.bank(), C.bank()]
                              for pc_ in range(2):
                                  ko = 128 * qi + 128 * pc_
                                  tk, pk = "tP%d%d" % (sx, pc_), "PT%d%d" % (sx, pc_)
                                  op("pe", lambda: pe.matmul(PS(bs[pc_]), lhsT=kTb[:, kv, ko:ko + 128], rhs=qT[:, 4 * kv:4 * kv + 4, 128 * qi:128 * qi + 128],
                                                             start=True, stop=True),
                                     reads=["kTb", "qT"], writes=[("ps", bs[pc_])])
                                  bias_t = (bprev if pc_ == 0 else bcur)[:, kv, :]
                                  op("dve", lambda: dve.scalar_tensor_tensor(out=tP[sx][pc_], in0=PS(bs[pc_]), scalar=0.125, in1=bias_t, op0=ALU.mult, op1=ALU.add),
                                     reads=[("ps", bs[pc_]), "bcur", "bprev"], writes=[tk])
                                  if pc_ == 0 and bi == 0 and qi == 0:
                                      op("act", lambda: act.activation(out=PT[sx][pc_][:], in_=tP[sx][pc_], func=AF.Exp, bias=nm0[:, 0:1]),
                                         reads=[tk, "smallc"], writes=[pk])
                                  else:
                                      op("act", lambda: act.activation(out=PT[sx][pc_][:], in_=tP[sx][pc_], func=AF.Exp),
                                         reads=[tk], writes=[pk])

                          def s2(it):
                              qi, kv = iters[it]
                              sx = it % 2
                              bo, bd = C.bank(), C.bank()
                              for e in range(2):
                                  for pc_ in range(2):
                                      rhs_ = PT[sx][pc_][:].rearrange("p (gg e n) -> p e gg n", e=2, n=128)[:, e, :, :]
                                      op("pe", lambda: pe.matmul(PS(bo, 256, 64 * e, 64 * e + 64), lhsT=Vb[:, qi + pc_, 64 * kv:64 * kv + 64], rhs=rhs_,
                                                                 start=(pc_ == 0), stop=(pc_ == 1)),
                                         reads=["Vb", "PT%d%d" % (sx, pc_)], writes=[("ps", bo)], signal=(pc_ == 1 and e == 1))
                              for e in range(2):
                                  for pc_ in range(2):
                                      rhs_ = PT[sx][pc_][:].rearrange("p (gg e n) -> p e gg n", e=2, n=128)[:, e, :, :]
                                      op("pe", lambda: pe.matmul(PS(bd, 256, 64 * e, 64 * e + 64), lhsT=ones[:, 0:64], rhs=rhs_, start=(pc_ == 0), stop=(pc_ == 1)),
                                         reads=["ones", "PT%d%d" % (sx, pc_)], writes=[("ps", bd)], signal=(pc_ == 1 and e == 1))
                              for e in range(2):
                                  for gg in range(2):
                                      hcol = 8 * l + 4 * kv + 2 * gg + e
                                      op("act", lambda: act.activation(out=dd[64 * e:64 * e + 64, 128 * gg:128 * gg + 128], in_=ps[64 * e:64 * e + 64, bd, 128 * gg:128 * gg + 128],
                                                                       func=AF.Ln, bias=es64[64 * e:64 * e + 64, hcol:hcol + 1]),
                                         reads=[("ps", bd), "es64"], writes=["wk3"])
                              op("act", lambda: act.activation(out=dd, in_=dd, func=AF.Exp, scale=-1.0), reads=["wk3"], writes=["wk3"])
                              op("dve", lambda: dve.tensor_tensor(out=attnT[:, 2 * kv:2 * kv + 2, 128 * qi:128 * qi + 128],
                                                                  in0=ps[:, bo, 0:256].rearrange("p (g n) -> p g n", n=128),
                                                                  in1=dd.rearrange("p (g n) -> p g n", n=128), op=ALU.mult),
                                 reads=[("ps", bo), "wk3"], writes=["attnT"])

                          s1(0)
                          for it in range(4):
                              if it + 1 < 4:
                                  s1(it + 1)
                              s2(it)
                              yield
                          op("dve", lambda: dve.tensor_tensor(out=S2, in0=zp_ext[:, :, 1:271], in1=zp_ext[:, :, 0:270], op=ALU.add), reads=["zp"], writes=["wk0"])
                          op("dve", lambda: dve.tensor_tensor(out=S4, in0=S2[:, :, 2:270], in1=S2[:, :, 0:268], op=ALU.add), reads=["wk0"], writes=["wk1"])
                          S8a = wkt[2][:, 0, 0:264]; S8b = wkt[2][:, 1, 0:264]
                          op("dve", lambda: dve.tensor_tensor(out=S8a, in0=S4[:, 1, 4:268], in1=S4[:, 1, 0:264], op=ALU.add),
                             reads=["wk1"], writes=["wk2"])
                          op("dve", lambda: dve.tensor_tensor(out=S8b[:, 0:256], in0=S8a[:, 8:264], in1=S8a[:, 0:256], op=ALU.add),
                             reads=["wk2"], writes=["wk2"])
                          zwin = [(0, 64, 0, S2[0:64, 0, 14:270], 2, "wk0"), (64, 128, 0, S4[64:128, 0, 12:268], 4, "wk1"),
                                  (0, 64, 1, S8a[0:64, 8:264], 8, "wk2"), (64, 128, 1, S8b[64:128, 0:256], 16, "wk2")]

                          def first_corr(p0, p1, c, win, wkey, bi=bi):
                              if bi != 0:
                                  return
                              w16 = win[:, 0:16]
                              op("dve", lambda: dve.tensor_tensor(out=w16, in0=w16, in1=pcorr[p0:p1, c, :], op=ALU.mult),
                                 reads=[wkey, "smallc"], writes=[wkey])
                          yield
                          pool_part(W, l, N, zwin, lambda c, p0, p1: zp_ext[p0:p1, c, 15:271], mixb[:], first_corr)
                          yield
                          for c in range(2):
                              op("dve", lambda: dve.scalar_tensor_tensor(out=mixb[:, c, :], in0=corr[:, c, c0:c0 + N], scalar=hst[:, c:c + 1], in1=rec0[:, c, c0:c0 + N],
                                                                         op0=ALU.mult, op1=ALU.add), reads=["corr", "rec0", "hst"], writes=["mixb"])
                          yield
                          yield from wout_g(W, l, c0, N, attnT, mixb, xkey)
                          yield
                          if bi < 7:
                              op("dve", lambda: dve.tensor_copy(out=zp_ext[:, :, 0:15], in_=zp_ext[:, :, 256:271]), reads=["zp"], writes=["zp"])
                              op("act", lambda: act.activation(out=kTb[:, :, 0:128], in_=kTb[:, :, 256:384], func=AF.Copy), reads=["kTb"], writes=["kTb"])
                              op("act", lambda: act.activation(out=Vb[:, 0, :], in_=Vb[:, 2, :], func=AF.Copy), reads=["Vb"], writes=["Vb"])

                      def drive(gA, gB, delay=0):
                          a_alive, b_alive = gA is not None, gB is not None
                          rnd = 0
                          while a_alive or b_alive:
                              if a_alive:
                                  try:
                                      next(gA)
                                  except StopIteration:
                                      a_alive = False
                              rnd += 1
                              if b_alive and (rnd > delay or not a_alive):
                                  try:
                                      next(gB)
                                  except StopIteration:
                                      b_alive = False

                      C.bank_pool = list(range(6))
                      drive(front(0), None)
                      for bi in range(8):
                          drive(front(bi + 1) if bi + 1 < 8 else None, layer_norm_g(W, l, 1, 256 * bi, 256, "xP%d" % bi, banks=(6, 7)), delay=4)
                      C.bank_pool = list(range(8))

              C.barrier()
              if STOP == 'prompt%d' % l: return

              with contextlib.ExitStack() as pp:
                  N = NS
                  c0 = T
                  xkey = "xS"
                  W["ybf"] = sbt(pp, "ybfS", [128, 8, NS], BF16); W["ybfk"] = ["ybf"]
                  W["xcb"] = sbt(pp, "xcbS", [128, 2, NS], BF16)
                  W["st"], W["stk"] = st_ph, stk_ph
                  xin = ["xSb"]
                  qTs = sbt(pp, "qTs", [64, 8, NS]); kTs = sbt(pp, "kTs", [64, 2, NS]); vTs = sbt(pp, "vTs", [64, 2, NS])
                  xrs = sbt(pp, "xrs", [128, 2, NS, 4]); grs = sbt(pp, "grs", [128, 2, NS]); zps = sbt(pp, "zps", [128, 2, NS, 16])
                  xcs = sbt(pp, "xcs", [128, 2, NS]); h0s = sbt(pp, "h0s", [128, 2, NS])
                  attnTs = sbt(pp, "attnTs", [128, 4, NS], BF16); accw = sbt(pp, "accw", [128, 128], BF16); mixbs = sbt(pp, "mixbs", [128, 4, NS], BF16)
                  sts = sbt(pp, "sts", [NS, 768]); scs = sbt(pp, "scs", [48, 256]); sps = sbt(pp, "sps", [120, 2, 256]); shs = sbt(pp, "shs", [NS, 256])
                  sths = sbt(pp, "sths", [NS, 256])
                  qs128 = sbt(pp, "qs128", [128, 64]); kn128 = sbt(pp, "kn128", [128, 64]); vn128 = sbt(pp, "vn128", [128, 64])
                  krep = sbt(pp, "krep", [64, 128]); vrep = sbt(pp, "vrep", [64, 128])
                  Kcs = [sbt(pp, "Kc%d" % i, [128, 16, 64]) for i in range(2)]; Vcs = [sbt(pp, "Vc%d" % i, [128, 16, 64]) for i in range(2)]
                  tmpc = sbt(pp, "tmpc", [128, 16, 64])

                  def load_kv(buf, src, ch, key):
                      pairs = [(buf[16 * h:16 * h + 16, :, :], src[l, :, 16 * ch:16 * ch + 16, 64 * (h // 4):64 * (h // 4) + 64]) for h in range(8)]
                      dma("sp", pairs, writes=[key], key=key)
                  scr = sbt(pp, "scr", [128, 128]); Pm = sbt(pp, "Pm", [128, 128]); sm = sbt(pp, "sm", [128, 8])
                  acc = sbt(pp, "acc", [128, 64]); part = sbt(pp, "part", [128, 64]); wins = sbt(pp, "wins", [128, 2, NS])

                  def rhs_x(k):
                      return xb[:, k, c0:c0 + N]
                  dma("sp", [(scs[:], sc[l]), (sps[:], spool[l].rearrange("(i r) n -> r i n", i=2)), (shs[:], sh[l])], writes=["sst"], key="sst")
                  for ch in range(2):
                      load_kv(Kcs[ch], ck, ch, "Kc%d" % ch)
                  for ch in range(2):
                      load_kv(Vcs[ch], cv, ch, "Vc%d" % ch)
                  b = C.bank()
                  for c in range(2):
                      op("pe", lambda: pe.transpose(PS(b, 48, o=64 * c), scs[:, 128 * c:128 * c + 128], ident[0:48, 0:48]),
                         reads=["sst", "ident"], writes=[("ps", b)], signal=(c == 1))
                  for c in range(2):
                      op("dve", lambda: dve.tensor_copy(out=xrs[:, c, :, 0:3], in_=ps[:, b, 64 * c:64 * c + 48].rearrange("p (b t) -> p b t", t=3)),
                         reads=[("ps", b)], writes=["xr"])
                  b = C.bank()
                  for i in range(2):
                      for c in range(2):
                          op("pe", lambda: pe.transpose(PS(b, 120, o=120 * (2 * i + c)), sps[:, i, 128 * c:128 * c + 128], ident[0:120, 0:120]),
                             reads=["sst", "ident"], writes=[("ps", b)], signal=(i == 1 and c == 1))
                  for i in range(2):
                      for c in range(2):
                          op("dve", lambda: dve.tensor_copy(out=zps[:, c, 8 * i:8 * i + 8, 0:15],
                                                            in_=ps[:, b, 120 * (2 * i + c):120 * (2 * i + c) + 120].rearrange("p (b t) -> p b t", t=15)),
                             reads=[("ps", b)], writes=["zp"])
                  b = C.bank()
                  for c in range(2):
                      op("pe", lambda: pe.transpose(PS(b, NS, o=NS * c), shs[:, 128 * c:128 * c + 128], ident[0:NS, 0:NS]),
                         reads=["sst", "ident"], writes=[("ps", b)], signal=(c == 1))
                  op("dve", lambda: dve.tensor_copy(out=h0s[:], in_=ps[:, b, 0:2 * NS].rearrange("p (c n) -> p c n", n=NS)), reads=[("ps", b)], writes=["h0s"])
                  b = C.bank()
                  for h in range(8):
                      for k in range(8):
                          op("pe", lambda: pe.matmul(PS(b, N, 0, 64, NS * h), lhsT=winb[:, k, 64 * h:64 * h + 64], rhs=rhs_x(k), start=(k == 0), stop=(k == 7)),
                             reads=["winb"] + xin, writes=[("ps", b)], signal=(k == 7 and h == 7))
                  op("dve", lambda: dve.tensor_copy(out=qTs[:], in_=ps[0:64, b, 0:8 * NS].rearrange("p (h n) -> p h n", n=NS)), reads=[("ps", b)], writes=["qTs"])
                  b = C.bank()
                  for e in range(4):
                      for k in range(8):
                          op("pe", lambda: pe.matmul(PS(b, N, 0, 64, NS * e), lhsT=winb[:, k, 512 + 64 * e:512 + 64 * e + 64], rhs=rhs_x(k), start=(k == 0), stop=(k == 7)),
                             reads=["winb"] + xin, writes=[("ps", b)], signal=(k == 7 and e == 3))
                  op("dve", lambda: dve.tensor_copy(out=kTs[:], in_=ps[0:64, b, 0:2 * NS].rearrange("p (h n) -> p h n", n=NS)), reads=[("ps", b)], writes=["kTs"])
                  op("dve", lambda: dve.tensor_copy(out=vTs[:], in_=ps[0:64, b, 2 * NS:4 * NS].rearrange("p (h n) -> p h n", n=NS)), reads=[("ps", b)], writes=["vTs"])
                  b = C.bank()
                  for e in range(6):
                      for k in range(8):
                          op("pe", lambda: pe.matmul(PS(b, N, o=NS * e), lhsT=winb[:, k, 768 + 128 * e:768 + 128 * e + 128], rhs=rhs_x(k), start=(k == 0), stop=(k == 7)),
                             reads=["winb"] + xin, writes=[("ps", b)], signal=(k == 7 and e == 5))
                  op("dve", lambda: dve.tensor_copy(out=xrs[:, :, :, 3], in_=ps[:, b, 0:2 * NS].rearrange("p (c n) -> p c n", n=NS)), reads=[("ps", b)], writes=["xr"])
                  op("dve", lambda: dve.tensor_copy(out=grs[:], in_=ps[:, b, 2 * NS:4 * NS].rearrange("p (c n) -> p c n", n=NS)), reads=[("ps", b)], writes=["gr"])
                  op("dve", lambda: dve.tensor_copy(out=zps[:, :, :, 15], in_=ps[:, b, 4 * NS:6 * NS].rearrange("p (c n) -> p c n", n=NS)), reads=[("ps", b)], writes=["zp"])
                  b1, b2 = C.bank(), C.bank()
                  for k in range(8):
                      op("pe", lambda: pe.matmul(PS(b1, 512, 0, NS), lhsT=xb[:, k, c0:c0 + NS], rhs=winb[:, k, 512:1024], start=(k == 0), stop=(k == 7)),
                         reads=["winb"] + xin, writes=[("ps", b1)], signal=(k == 7))
                  for k in range(8):
                      op("pe", lambda: pe.matmul(PS(b2, 256, 0, NS), lhsT=xb[:, k, c0:c0 + NS], rhs=winb[:, k, 1280:1536], start=(k == 0), stop=(k == 7)),
                         reads=["winb"] + xin, writes=[("ps", b2)], signal=(k == 7))
                  op("dve", lambda: dve.tensor_copy(out=sts[:, 0:512], in_=PS(b1, 512, 0, NS)), reads=[("ps", b1)], writes=["sts"])
                  op("dve", lambda: dve.tensor_copy(out=sts[:, 512:768], in_=PS(b2, 256, 0, NS)), reads=[("ps", b2)], writes=["sts"])
                  dma("sp", [(s_k[l, :, 127, :], sts[:, 0:128]), (s_v[l, :, 127, :], sts[:, 128:256]),
                             (s_conv[l, :, 2, :], sts[:, 256:512]), (s_pool[l, :, 14, :], sts[:, 512:768])], reads=["sts"], key="sts")
                  b = C.bank()
                  op("pe", lambda: pe.transpose(PS(b, 64), qTs[:].rearrange("p h n -> p (h n)"), ident[0:64, 0:64]), reads=["qTs", "ident"], writes=[("ps", b)])
                  op("dve", lambda: dve.tensor_copy(out=qs128[:], in_=PS(b, 64)), reads=[("ps", b)], writes=["qs128"])
                  op("dve", lambda: dve.tensor_copy(out=krep[:].rearrange("p (k g n) -> p k g n", k=2, g=4),
                                                    in_=kTs[:].unsqueeze(2).to_broadcast([64, 2, 4, NS])), reads=["kTs"], writes=["krep"])
                  op("dve", lambda: dve.tensor_copy(out=vrep[:].rearrange("p (k g n) -> p k g n", k=2, g=4),
                                                    in_=vTs[:].unsqueeze(2).to_broadcast([64, 2, 4, NS])), reads=["vTs"], writes=["vrep"])
                  b = C.bank()
                  op("pe", lambda: pe.transpose(PS(b, 64), krep[:], ident[0:64, 0:64]), reads=["krep", "ident"], writes=[("ps", b)], signal=False)
                  op("pe", lambda: pe.transpose(PS(b, 64, o=64), vrep[:], ident[0:64, 0:64]), reads=["vrep", "ident"], writes=[("ps", b)])
                  op("dve", lambda: dve.tensor_copy(out=kn128[:], in_=PS(b, 64)), reads=[("ps", b)], writes=["kn128"])
                  op("dve", lambda: dve.tensor_copy(out=vn128[:], in_=PS(b, 64, o=64)), reads=[("ps", b)], writes=["vn128"])
                  for ch in range(8):
                      Kc = Kcs[ch % 2]
                      op("dve", lambda: dve.tensor_tensor(out=tmpc[:], in0=Kc[:], in1=qs128[:].unsqueeze(1).to_broadcast([128, 16, 64]), op=ALU.mult),
                         reads=["Kc%d" % (ch % 2), "qs128"], writes=["tmpc"])
                      op("dve", lambda: dve.tensor_reduce(out=scr[:, 16 * ch:16 * ch + 16], in_=tmpc[:], op=ALU.add, axis=AX.X), reads=["tmpc"], writes=["scr"])
                      if ch + 2 < 8:
                          load_kv(Kcs[ch % 2], ck, ch + 2, "Kc%d" % (ch % 2))
                  op("dve", lambda: dve.tensor_tensor(out=part[:], in0=kn128[:], in1=qs128[:], op=ALU.mult), reads=["kn128", "qs128"], writes=["part"])
                  op("dve", lambda: dve.tensor_reduce(out=sm[:, 0:1], in_=part[:], op=ALU.add, axis=AX.X), reads=["part"], writes=["sm0"])
                  op("dve", lambda: dve.scalar_tensor_tensor(out=scr[:], in0=scr[:], scalar=0.125, in1=sbias[:], op0=ALU.mult, op1=ALU.add),
                     reads=["scr", "smallc"], writes=["scr"])
                  op("act", lambda: act.activation(out=Pm[:], in_=scr[:], func=AF.Exp), reads=["scr"], writes=["Pm"])
                  op("act", lambda: act.activation(out=sm[:, 1:2], in_=sm[:, 0:1], func=AF.Exp, scale=0.125), reads=["sm0"], writes=["sm1"])
                  op("dve", lambda: dve.tensor_reduce(out=sm[:, 2:3], in_=Pm[:], op=ALU.add, axis=AX.X), reads=["Pm"], writes=["sm2"])
                  op("dve", lambda: dve.tensor_tensor(out=sm[:, 2:3], in0=sm[:, 2:3], in1=sm[:, 1:2], op=ALU.add), reads=["sm2", "sm1"], writes=["sm2"])
                  op("dve", lambda: dve.tensor_tensor(out=sm[:, 2:3], in0=sm[:, 2:3], in1=essamp[:, l:l + 1], op=ALU.add), reads=["sm2", "essamp"], writes=["sm2"])
                  op("dve", lambda: dve.reciprocal(out=sm[:, 3:4], in_=sm[:, 2:3]), reads=["sm2"], writes=["sm3"])
                  op("dve", lambda: dve.tensor_scalar(out=acc[:], in0=vn128[:], scalar1=sm[:, 1:2], scalar2=None, op0=ALU.mult), reads=["vn128", "sm1"], writes=["acc"])
                  for ch in range(8):
                      Vc = Vcs[ch % 2]
                      op("dve", lambda: dve.tensor_tensor(out=tmpc[:], in0=Vc[:], in1=Pm[:, 16 * ch:16 * ch + 16].unsqueeze(2).to_broadcast([128, 16, 64]), op=ALU.mult),
                         reads=["Vc%d" % (ch % 2), "Pm"], writes=["tmpc"])
                      op("dve", lambda: dve.tensor_reduce(out=part[:], in_=tmpc[:].rearrange("p s d -> p d s"), op=ALU.add, axis=AX.X), reads=["tmpc"], writes=["part"])
                      op("dve", lambda: dve.tensor_tensor(out=acc[:], in0=acc[:], in1=part[:], op=ALU.add), reads=["acc", "part"], writes=["acc"])
                      if ch + 2 < 8:
                          load_kv(Vcs[ch % 2], cv, ch + 2, "Vc%d" % (ch % 2))
                  op("dve", lambda: dve.tensor_scalar(out=acc[:], in0=acc[:], scalar1=sm[:, 3:4], scalar2=None, op0=ALU.mult), reads=["acc", "sm3"], writes=["acc"])
                  b = C.bank()
                  for e in range(2):
                      op("dve", lambda: dve.tensor_scalar(out=accw[:, 64 * e:64 * e + 64], in0=acc[:], scalar1=emask[:, e:e + 1], scalar2=None, op0=ALU.mult),
                         reads=["acc", "emask"], writes=["accw"])
                  op("pe", lambda: pe.matmul(PS(b, 64), lhsT=accw[:], rhs=selm[:], start=True, stop=True), reads=["accw", "selm"], writes=[("ps", b)])
                  op("dve", lambda: dve.tensor_copy(out=attnTs[:], in_=ps[:, b, 0:64].rearrange("p (j n) -> p j n", n=NS)), reads=[("ps", b)], writes=["attnT"])

                  def h_apply_s(a_, bb, hs):
                      op("dve", lambda: dve.tensor_tensor(out=hs, in0=a_, in1=h0s[:], op=ALU.mult), reads=["wk2", "h0s"], writes=["wk3"])
                      op("dve", lambda: dve.tensor_tensor(out=hs, in0=hs, in1=bb, op=ALU.add), reads=["wk3", "wk1"], writes=["wk3"])
                      bt = C.bank()
                      for c in range(2):
                          op("pe", lambda: pe.transpose(PS(bt, 128, 0, NS, 128 * c), hs[:, c, :], ident[:, :]), reads=["wk3", "ident"], writes=[("ps", bt)], signal=(c == 1))
                      op("dve", lambda: dve.tensor_copy(out=sths[:], in_=PS(bt, 256, 0, NS)), reads=[("ps", bt)], writes=["sths"])
                      dma("sp", (s_h[l], sths[:]), reads=["sths"], key="sths")

                  zwin = []
                  for gi, (p0, p1, c, w) in enumerate([(0, 64, 0, 2), (64, 128, 0, 4), (0, 64, 1, 8), (64, 128, 1, 16)]):
                      op("dve", lambda: dve.tensor_reduce(out=wins[p0:p1, c, :], in_=zps[p0:p1, c, :, 16 - w:16], op=ALU.add, axis=AX.X), reads=["zp"], writes=["zw"])
                      zwin.append((p0, p1, c, wins[p0:p1, c, :], w, "zw"))
                  lru_pool_common(W, l, N, lambda c, tap: xrs[:, c, :, tap], xcs[:], grs[:], zwin,
                                  lambda c, p0, p1: zps[p0:p1, c, :, 15], h_apply_s, mixbs[:])
                  wout_ln(W, l, c0, N, attnTs, mixbs, xkey)
              C.barrier()
              if STOP == 'sample%d' % l: return

          with contextlib.ExitStack() as ph:
              W = {}
              NSLOT = 3
              w1s = [sbt(ph, "w1s%d" % i, [128, 8, 512], BF16) for i in range(NSLOT)]
              w2s = [sbt(ph, "w2s%d" % i, [128, 4, 1024], BF16) for i in range(NSLOT)]
              hT = [sbt(ph, "hT%d" % i, [128, 4, 512], BF16) for i in range(2)]
              rt = [sbt(ph, "rt0", [128, 512])] * 2
              Wl = []
              for q_ in range(2):
                  d_ = {"ybf": sbt(ph, "ybfF%d" % q_, [128, 4, 256], BF16), "ybfk": ["ybfF%d" % q_]}
                  st_ = sbt(ph, "stF%d" % q_, [128, 3, 256])
                  d_["st"] = [st_[:, 0, :], st_[:, 1, :], st_[:, 2, :], st_[:, 1, :]]
                  d_["stk"] = ["stF%d_0" % q_, "stF%d_1" % q_, "stF%d_2" % q_, "stF%d_1" % q_]
                  d_["banks"] = (6, 7) if q_ == 0 else (4, 5)
                  Wl.append(d_)
              ost = [sbt(ph, "ost%d" % i, [128, 1024]) for i in range(2)]

              def load_slice(j):
                  s = j % NSLOT
                  for i_ in range(4):
                      dma("pool", (w1s[s][:, :, 128 * i_:128 * i_ + 128], w_ff1[l, :, 512 * j + 128 * i_:512 * j + 128 * i_ + 128].rearrange("(k p) n -> p k n", p=128)),
                          writes=["w1s%d_%d" % (s, i_)], key="w1s%d_%d" % (s, i_))
                  for h_ in range(2):
                      dma("pool", (w2s[s][:, 2 * h_:2 * h_ + 2, :], w_ff2[l, 512 * j + 256 * h_:512 * j + 256 * h_ + 256, :].rearrange("(i p) n -> p i n", p=128)),
                          writes=["w2s%d_%d" % (s, h_)], key="w2s%d_%d" % (s, h_))

              for j in range(NSLOT):
                  load_slice(j)
              blocks = [(512 * i, 512) for i in range(4)]
              items = [(j, c0, N) for j in range(5) for (c0, N) in blocks]
              items += [(j, c0, N) for (c0, N) in blocks for j in (5, 6, 7)]
              last_of_slice = {}
              for ii, (j_, _c, _n) in enumerate(items):
                  last_of_slice[j_] = ii

              hTs = [sbt(ph, "hTs%d" % i, [128, 4, NS], BF16) for i in range(2)]
              rts = sbt(ph, "rts", [128, NS])

              def ff1(idx):
                  j, c0, N = items[idx]
                  s = j % NSLOT
                  hb = idx % 2
                  xk = "xF%d" % c0
                  mrg = (c0 == 1536)
                  for i in range(4):
                      b = C.bank()
                      b2 = C.bank() if mrg else None
                      for k in range(8):
                          op("pe", lambda: pe.matmul(PS(b, N), lhsT=w1s[s][:, k, 128 * i:128 * i + 128], rhs=xb[:, k, c0:c0 + N], start=(k == 0), stop=(k == 7)),
                             reads=["w1s%d_%d" % (s, i), xk + "_0b", xk + "_1b"], writes=[("ps", b)], signal=(k == 7))
                          if mrg:
                              op("pe", lambda: pe.matmul(PS(b2, NS), lhsT=w1s[s][:, k, 128 * i:128 * i + 128], rhs=xb[:, k, T:T + NS], start=(k == 0), stop=(k == 7)),
                                 reads=["w1s%d_%d" % (s, i), "xF2048_0b"], writes=[("ps", b2)], signal=(k == 7))
                      op("act", lambda: act.activation(out=rt[0][:, 0:N], in_=PS(b, N), func=AF.Relu), reads=[("ps", b)], writes=["rt0"])
                      op("act", lambda: act.activation(out=hT[hb][:, i, 0:N], in_=rt[0][:, 0:N], func=AF.Square), reads=["rt0"], writes=["hT%d" % hb])
                      if mrg:
                          op("act", lambda: act.activation(out=rts[:, :], in_=PS(b2, NS), func=AF.Relu), reads=[("ps", b2)], writes=["rts"])
                          op("act", lambda: act.activation(out=hTs[hb][:, i, :], in_=rts[:, :], func=AF.Square), reads=["rts"], writes=["hTs%d" % hb])
                      yield

              def ff2(idx):
                  j, c0, N = items[idx]
                  s = j % NSLOT
                  hb = idx % 2
                  xk = "xF%d" % c0
                  mrg = (c0 == 1536)
                  for m in range(8):
                      b = C.bank()
                      b2 = C.bank() if mrg else None
                      for i in range(4):
                          op("pe", lambda: pe.matmul(PS(b, N), lhsT=w2s[s][:, i, 128 * m:128 * m + 128], rhs=hT[hb][:, i, 0:N], start=(i == 0), stop=(i == 3)),
                             reads=["w2s%d_%d" % (s, i // 2), "hT%d" % hb], writes=[("ps", b)], signal=(i == 3))
                          if mrg:
                              op("pe", lambda: pe.matmul(PS(b2, NS), lhsT=w2s[s][:, i, 128 * m:128 * m + 128], rhs=hTs[hb][:, i, :], start=(i == 0), stop=(i == 3)),
                                 reads=["w2s%d_%d" % (s, i // 2), "hTs%d" % hb], writes=[("ps", b2)], signal=(i == 3))
                      for (bq, cq, nq, kq) in ([(b, c0, N, [xk + "_0", xk + "_1"])] + ([(b2, T, NS, ["xF2048_0"])] if mrg else [])):
                          xs_ = xf[:, m, cq:cq + nq]
                          if j == 0:
                              op("dve", lambda: dve.scalar_tensor_tensor(out=xs_, in0=xs_, scalar=ALPHA, in1=PS(bq, nq), op0=ALU.mult, op1=ALU.add),
                                 reads=[("ps", bq)] + kq, writes=kq)
                          else:
                              op("dve", lambda: dve.tensor_tensor(out=xs_, in0=xs_, in1=PS(bq, nq), op=ALU.add), reads=[("ps", bq)] + kq, writes=kq)
                      yield

              otile = [0]

              def ln2_sub(cc, nn, xk, Wq):
                  yield from layer_norm_g(Wq, l, 2, cc, nn, xk, banks=Wq["banks"], slack=0)
                  if l == L - 1:
                      ntile = 2 if nn == 256 else 1
                      for ti in range(ntile):
                          n = 128 if nn == 256 else NS
                          t0 = cc + 128 * ti
                          o = ost[otile[0] % 2]
                          ok = "ost%d" % (otile[0] % 2)
                          otile[0] += 1
                          for hb in range(2):
                              b = C.bank()
                              for mm in range(4):
                                  m = 4 * hb + mm
                                  op("pe", lambda: pe.transpose(PS(b, 128, 0, n, 128 * mm), xf[:, m, t0:t0 + n], ident[:, :]),
                                     reads=[xk, "ident"], writes=[("ps", b)], signal=(mm == 3))
                              if hb == 0:
                                  op("act", lambda: act.activation(out=o[0:n, 0:512], in_=PS(b, 512, 0, n), func=AF.Copy), reads=[("ps", b)], writes=[ok])
                              else:
                                  op("dve", lambda: dve.tensor_copy(out=o[0:n, 512:1024], in_=PS(b, 512, 0, n)), reads=[("ps", b)], writes=[ok])
                          dst = yp[t0:t0 + 128, :] if nn == 256 else ys[:, :]
                          dma("sp", (dst, o[0:n, :]), reads=[ok], key=ok)
                          yield

              d_ = {"ybf": w1s[2][:].rearrange("p k n -> p (k n)")[:, 0:1024].rearrange("p (m n) -> p m n", n=256),
                    "ybfk": ["w1s2_0", "w1s2_1", "w1s2_2", "w1s2_3"]}
              st_ = w2s[2][:].rearrange("p k n -> p (k n)")[:, 0:1536].bitcast(F32).rearrange("p (m n) -> p m n", n=256)
              d_["st"] = [st_[:, 0, :], st_[:, 1, :], st_[:, 2, :], st_[:, 1, :]]
              d_["stk"] = ["w2s2_0", "w2s2_0", "w2s2_0", "w2s2_0"]
              d_["banks"] = (2, 3)
              Wl.append(d_)

              lnq = []
              nln = [0]

              def delayed(g_, k_):
                  for _ in range(k_):
                      yield
                  yield from g_

              def ffn_main():
                  yield from ff1(0)
                  for idx in range(len(items)):
                      if idx + 1 < len(items):
                          yield from ff1(idx + 1)
                      yield from ff2(idx)
                      j, c0, N = items[idx]
                      if idx + 3 < len(items) and items[idx + 3][0] == 7 and items[idx + 2][0] == 6 and items[idx + 1][0] == 5 and items[idx][0] == 4:
                          C.bank_pool = list(range(4))
                      if j == 7:
                          lnq.append((c0, 256, "xF%d_0" % c0))
                          lnq.append((c0 + 256, 256, "xF%d_1" % c0))
                          if c0 == 1536:
                              lnq.append((T, NS, "xF2048_0"))
                      if last_of_slice[j] == idx and j + NSLOT < 8:
                          load_slice(j + NSLOT)
                      if j == 3 and last_of_slice[j] == idx and l + 1 < L:
                          load_winb(l + 1)

              C.bank_pool = list(range(8))
              gmain = ffn_main()
              alive = True
              slots = [None, None, None]
              while alive or lnq or any(g_ is not None for g_ in slots):
                  if alive:
                      try:
                          next(gmain)
                      except StopIteration:
                          alive = False
                          C.bank_pool = [0, 1]
                  nslots = 2 if alive else 3
                  for q_ in range(nslots):
                      if slots[q_] is None and lnq:
                          cc_, nn_, xk_ = lnq.pop(0)
                          slots[q_] = delayed(ln2_sub(cc_, nn_, xk_, Wl[q_]), 3 if alive else 0)
                      if slots[q_] is not None:
                          try:
                              next(slots[q_])
                          except StopIteration:
                              slots[q_] = None
              C.bank_pool = list(range(8))
          C.barrier()
          if STOP == 'ln2%d' % l: return
    run_layers()
    C.finish()
    top.close()
    return nc


def _consts():
    slopes = np.exp2(-8.0 * (np.arange(8, dtype=np.float32) + 1.0) / 8).astype(np.float32)
    s = np.arange(128)[:, None]
    q = np.arange(128)[None, :]
    bcur = np.full((2, 128, 4, 128), NEG, np.float32)
    bprev = np.full((2, 128, 4, 128), NEG, np.float32)
    for kv in range(2):
        for g in range(4):
            sl = slopes[4 * kv + g]
            d = (q - s).astype(np.float32)
            bcur[kv, :, g, :] = np.where(s <= q, -sl * d, NEG)
            d2 = (q - s + 128).astype(np.float32)
            bprev[kv, :, g, :] = np.where(s >= q, -sl * d2, NEG)
    sbias = np.zeros((128, 128), np.float32)
    for h in range(8):
        sbias[16 * h:16 * h + 16, :] = -slopes[h] * (128 - np.arange(128, dtype=np.float32))[None, :]
    return bcur.reshape(2, 128, 512), bprev.reshape(2, 128, 512), sbias


_NC = None


def kernel(x_prompt, x_sample, cache_k, cache_v, state_h, state_conv, state_pool,
           w_in, attn_sinks, conv_w, conv_b, gate_a_w, gate_a_b, gate_x_w, gate_x_b, lru_lambda,
           pool_w, pool_scale, w_out, ln1_g, ln1_b, w_ff1, w_ff2, ln2_g, ln2_b):
    global _NC
    f = lambda a: np.ascontiguousarray(np.asarray(a, dtype=np.float32))
    bcur, bprev, sbias = _consts()
    ident = np.eye(128, dtype=np.float32)
    shared = dict(w_in=f(w_in), sinks=f(attn_sinks), conv_w=f(conv_w), conv_b=f(conv_b), ga_w=f(gate_a_w), ga_b=f(gate_a_b),
                  gx_w=f(gate_x_w), gx_b=f(gate_x_b), lam=f(lru_lambda), pool_w=f(pool_w), pool_s=f(pool_scale),
                  w_out=f(w_out), ln1_g=f(ln1_g), ln1_b=f(ln1_b), w_ff1=f(w_ff1), w_ff2=f(w_ff2), ln2_g=f(ln2_g), ln2_b=f(ln2_b),
                  ident=ident, bcur=bcur, bprev=bprev, sbias=sbias)
    emask = np.zeros((128, 2), np.float32); sel = np.zeros((128, 64), np.float32)
    for p in range(128):
        h, b_ = p // 16, p % 16
        emask[p, h % 2] = 1.0
        sel[p, (h // 2) * 16 + b_] = 1.0
    xpr = f(x_prompt); xsa = f(x_sample)
    ckk = f(cache_k).reshape(L, 128, 128, 128); cvv = f(cache_v).reshape(L, 128, 128, 128)
    shh = f(state_h); scc = f(state_conv); spp = f(state_pool)
    in_maps = []
    for c in range(NCORES):
        seq, half = c // 2, c % 2
        b0 = NS * c
        pcorr = np.ones((128, 2, 16), np.float32)
        if half == 0:
            for gi, w in enumerate((2, 4, 8, 16)):
                cc, p0 = gi // 2, 64 * (gi % 2)
                t = np.arange(16, dtype=np.float32)
                pcorr[p0:p0 + 64, cc, :] = (w / np.minimum(t + 1.0, float(w)))[None, :]
        m = dict(shared)
        m.update(xp=np.ascontiguousarray(xpr[seq, T * half:T * half + T]), xs=np.ascontiguousarray(xsa[b0:b0 + NS, 0]),
                 ck=np.ascontiguousarray(ckk[:, b0:b0 + NS]), cv=np.ascontiguousarray(cvv[:, b0:b0 + NS]),
                 sh=np.ascontiguousarray(shh[:, b0:b0 + NS]),
                 sc=np.ascontiguousarray(scc[:, b0:b0 + NS].reshape(L, NS * 3, 256)),
                 spool=np.ascontiguousarray(spp[:, b0:b0 + NS].reshape(L, NS * 15, 256)),
                 nm0=np.full((128, 1), NEG if half == 0 else 0.0, np.float32),
                 flag=np.full((128, 1), 0.0 if half == 0 else 1.0, np.float32),
                 pcorr=pcorr, st_in=np.zeros((L, 147, 256), np.float32), emask=emask, sel=sel)
        in_maps.append(m)
    if _NC is None:
        _NC = build()
    res = run_bass_kernel_spmd(_NC, in_maps, core_ids=list(range(NCORES))).results
    y_prompt = np.zeros((4, 4096, 1024), np.float32)
    for c in range(NCORES):
        y_prompt[c // 2, T * (c % 2):T * (c % 2) + T] = res[c]["yp"]
    y_sample = np.concatenate([res[c]["ys"] for c in range(NCORES)], 0).reshape(128, 1, 1024)
    sto = np.stack([res[2 * s + 1]["st_out"] for s in range(4)], 1)
    p_k = np.ascontiguousarray(sto[:, :, 0:128, 0:128]).reshape(L, 4, 128, 2, 64)
    p_v = np.ascontiguousarray(sto[:, :, 0:128, 128:256]).reshape(L, 4, 128, 2, 64)
    p_conv = np.ascontiguousarray(sto[:, :, 128:131, :])
    p_pool = np.ascontiguousarray(sto[:, :, 131:146, :])
    p_h = np.ascontiguousarray(sto[:, :, 146, :])
    cat = lambda k: np.concatenate([res[c][k] for c in range(NCORES)], 1)
    s_k = cat("s_k").reshape(L, 128, 128, 2, 64); s_v = cat("s_v").reshape(L, 128, 128, 2, 64)
    return (y_prompt, y_sample, p_k, p_v, p_h, p_conv, p_pool, s_k, s_v, cat("s_h"), cat("s_conv"), cat("s_pool"))
```

```python
import contextlib
import numpy as np
import concourse.bass as bass
import concourse.mybir as mybir
from concourse.bass_utils import run_bass_kernel_spmd

F32 = mybir.dt.float32
BF16 = mybir.dt.bfloat16
AF = mybir.ActivationFunctionType
ALU = mybir.AluOpType
AX = mybir.AxisListType

NCORES = 8
T = 2048
NS = 16
TT = T + NS
L = 2
ALPHA = float((2.0 * L) ** 0.25)
EPS = 1e-5
NEG = -1e30
NPV = 50
USE_CC = False
STOP = None
NOSELF = False
XMODE = 2


class _Stop(Exception):
    pass


class Ctx:
    def __init__(self, nc):
        self.nc = nc
        self.engs = {"pe": nc.tensor, "act": nc.scalar, "dve": nc.vector, "pool": nc.gpsimd, "sp": nc.sync}
        self.esem = {e: nc.alloc_semaphore(name="sem_" + e) for e in ["pe", "act", "dve", "pool"]}
        self.ecnt = {e: 0 for e in self.esem}
        self.seen = {e: {} for e in self.engs}
        self.res = {}
        self.dsem = {}
        self.nbank = 0
        self.bank_pool = list(range(8))
        self.ccs = []

    def bank(self):
        b = self.bank_pool[self.nbank % len(self.bank_pool)]
        self.nbank += 1
        return b

    def _wait(self, eng, tok):
        sem, val, src = tok
        if src == "pe" and eng == "pe":
            return
        if NOSELF and src == eng:
            return
        d = self.seen[eng]
        if d.get(id(sem), 0) >= val:
            return
        self.engs[eng].wait_ge(sem, val)
        d[id(sem)] = val

    def _deps(self, eng, reads, writes):
        for k in reads:
            st = self.res.get(k)
            if st and st["w"]:
                self._wait(eng, st["w"])
        for k in writes:
            st = self.res.get(k)
            if st:
                if st["w"]:
                    self._wait(eng, st["w"])
                for r in st["r"]:
                    self._wait(eng, r)

    def _reg(self, tok, reads, writes):
        for k in reads:
            self.res.setdefault(k, {"w": None, "r": []})["r"].append(tok)
        for k in writes:
            self.res[k] = {"w": tok, "r": []}

    def op(self, eng, fn, reads=(), writes=(), signal=True):
        psr = [k for k in reads if isinstance(k, tuple) and k[0] == "ps"]
        if psr:
            reads = [k for k in reads if k not in psr]
            writes = list(writes) + psr
        self._deps(eng, reads, writes)
        ins = fn()
        if signal:
            self.ecnt[eng] += 1
            ins.then_inc(self.esem[eng], 1)
            val = self.ecnt[eng]
        else:
            val = self.ecnt[eng] + 1
        self._reg((self.esem[eng], val, eng), reads, writes)

    def dma(self, q, pairs, reads=(), writes=(), key=None, **kw):
        if not isinstance(pairs, list):
            pairs = [pairs]
        self._deps(q, reads, writes)
        if key not in self.dsem:
            self.dsem[key] = [self.nc.alloc_semaphore(name="d_" + str(key)), 0]
        ent = self.dsem[key]
        if ent[1] > 0:
            self._wait(q, (ent[0], ent[1], "dma"))
        for out, in_ in pairs:
            self.engs[q].dma_start(out=out, in_=in_, **kw).then_inc(ent[0], 16)
            ent[1] += 16
        self._reg((ent[0], ent[1], "dma"), reads, writes)

    def collective(self, in_ap, out_ap, reads=(), writes=()):
        self._deps("pool", reads, writes)
        sem = self.nc.alloc_semaphore(name="cc%d" % len(self.ccs))
        self.ccs.append(sem)
        self.nc.gpsimd.collective_compute("AllGather", ALU.bypass, replica_groups=[[0, 1], [2, 3], [4, 5], [6, 7]],
                                          ins=[in_ap], outs=[out_ap]).then_inc(sem, 1)
        self._reg((sem, 1, "cc"), reads, writes)

    def barrier(self):
        toks = [(self.esem[e], self.ecnt[e], e) for e in self.esem if self.ecnt[e] > 0]
        toks += [(s, c, "dma") for s, c in self.dsem.values() if c > 0]
        for e in self.engs:
            for t in toks:
                if t[2] == e:
                    continue
                self._wait(e, t)
        self.res = {}

    def finish(self):
        for s, c in self.dsem.values():
            if c > 0:
                self._wait("sp", (s, c, "dma"))
        for e in self.esem:
            if self.ecnt[e] > 0:
                self._wait("sp", (self.esem[e], self.ecnt[e], e))


def build():
    nc = bass.Bass("TRN2", target_bir_lowering=False)
    C = Ctx(nc)

    def din(name, shape):
        return nc.dram_tensor(name, shape, F32, kind="ExternalInput").ap()

    def dout(name, shape):
        return nc.dram_tensor(name, shape, F32, kind="ExternalOutput").ap()

    xp = din("xp", [T, 1024]); xs = din("xs", [NS, 1024])
    ck = din("ck", [L, NS, 128, 128]); cv = din("cv", [L, NS, 128, 128])
    sh = din("sh", [L, NS, 256]); sc = din("sc", [L, NS * 3, 256]); spool = din("spool", [L, NS * 15, 256])
    w_in = din("w_in", [L, 1024, 1536]); sinks = din("sinks", [L, 8])
    conv_w = din("conv_w", [L, 4, 256]); conv_b = din("conv_b", [L, 256])
    ga_w = din("ga_w", [L, 4, 64, 64]); ga_b = din("ga_b", [L, 256])
    gx_w = din("gx_w", [L, 4, 64, 64]); gx_b = din("gx_b", [L, 256])
    lam = din("lam", [L, 256]); pool_w = din("pool_w", [L, 4, 64, 64]); pool_s = din("pool_s", [L, 256])
    w_out = din("w_out", [L, 1024, 1024]); ln1_g = din("ln1_g", [L, 1024]); ln1_b = din("ln1_b", [L, 1024])
    w_ff1 = din("w_ff1", [L, 1024, 4096]); w_ff2 = din("w_ff2", [L, 4096, 1024])
    ln2_g = din("ln2_g", [L, 1024]); ln2_b = din("ln2_b", [L, 1024])
    ident_d = din("ident", [128, 128]); bcur_d = din("bcur", [2, 128, 512]); bprev_d = din("bprev", [2, 128, 512])
    sbias_d = din("sbias", [128, 128]); nm0_d = din("nm0", [128, 1]); flag_d = din("flag", [128, 1])
    pcorr_d = din("pcorr", [128, 2, 16]); st_in = din("st_in", [L, 147, 256])
    emask_d = din("emask", [128, 2]); sel_d = din("sel", [128, 64])

    cc1_in = nc.dram_tensor("cc1_in", [L, 146, 256], F32, kind="Internal").ap()
    cc1_out = nc.dram_tensor("cc1_out", [L, 292, 256], F32, kind="Internal").ap()
    cc2_in = nc.dram_tensor("cc2_in", [L, 1, 256], F32, kind="Internal").ap()
    cc2_out = nc.dram_tensor("cc2_out", [L, 2, 256], F32, kind="Internal").ap()
    yp = dout("yp", [T, 1024]); ys = dout("ys", [NS, 1024]); st_out = dout("st_out", [L, 147, 256])
    s_k = dout("s_k", [L, NS, 128, 128]); s_v = dout("s_v", [L, NS, 128, 128])
    s_h = dout("s_h", [L, NS, 256]); s_conv = dout("s_conv", [L, NS, 3, 256]); s_pool = dout("s_pool", [L, NS, 15, 256])

    op = C.op
    dma = C.dma
    pe, act, dve, pool = nc.tensor, nc.scalar, nc.vector, nc.gpsimd

    top = contextlib.ExitStack()

    uid = [0]

    def sbt(stack, name, shape, dt=F32):
        uid[0] += 1
        return stack.enter_context(nc.sbuf_tensor("sb%d_%s" % (uid[0], name), shape, dt))

    ps = top.enter_context(nc.psum_tensor("psum_all", [128, 8, 512], F32))
    xf = sbt(top, "xf", [128, 8, TT]); xb = sbt(top, "xb", [128, 8, TT], BF16)
    ident = sbt(top, "ident", [128, 128]); ones = sbt(top, "ones", [128, 128], BF16)
    bcur = sbt(top, "bcur", [128, 2, 512], BF16); bprev = sbt(top, "bprev", [128, 2, 512], BF16)
    sbias = sbt(top, "sbias", [128, 128]); nm0 = sbt(top, "nm0", [128, 1]); flag = sbt(top, "flag", [128, 1])
    pcorr = sbt(top, "pcorr", [128, 2, 16]); pv = sbt(top, "pv", [128, 2 * NPV]); der = sbt(top, "der", [128, L, 8])
    es64 = sbt(top, "es64", [128, 16]); essamp = sbt(top, "essamp", [128, 2])
    wbd = sbt(top, "wbd", [128, L, 6, 128], BF16)
    cneg = sbt(top, "cneg", [128, 256], BF16)
    emask = sbt(top, "emask", [128, 2]); selm = sbt(top, "selm", [128, 64], BF16)
    winb = sbt(top, "winb", [128, 8, 1536], BF16)

    def load_winb(l):
        for kk in range(4):
            dma("pool", (winb[:, 2 * kk:2 * kk + 2, :], w_in[l, 256 * kk:256 * kk + 256, :].rearrange("(k p) n -> p k n", p=128)),
                writes=["winb"], key="winb%d" % kk)

    def PS(b, n=512, p0=0, p1=128, o=0):
        return ps[p0:p1, b, o:o + n]

    dma("sp", (ident[:], ident_d[:, :]), writes=["ident"], key="c0")
    dma("pool", (bcur[:], bcur_d.rearrange("k s n -> s k n")), writes=["bcur"], key="c1")
    dma("pool", (bprev[:], bprev_d.rearrange("k s n -> s k n")), writes=["bprev"], key="c2")
    dma("sp", [(sbias[:], sbias_d[:, :]), (nm0[:], nm0_d[:, :]), (flag[:], flag_d[:, :]), (pcorr[:], pcorr_d[:, :, :])],
        writes=["smallc"], key="c3")
    dma("sp", (emask[:], emask_d[:, :]), writes=["emask"], key="c8")
    dma("pool", (selm[:], sel_d[:, :]), writes=["selm"], key="c9")
    op("dve", lambda: dve.memset(ones[:], 1.0), writes=["ones"])
    op("dve", lambda: dve.memset(cneg[:], -0.5), writes=["cneg"])
    op("dve", lambda: dve.memset(wbd[:], 0.0), writes=["wbd"])
    if STOP == 'i1':
        C.finish(); top.close(); return nc
    with contextlib.ExitStack() as ph:
        prow = sbt(ph, "prow", [2 * NPV, 128])
        plist = [(conv_w, 0, 8), (conv_b, 8, 2), (ga_b, 10, 2), (gx_b, 12, 2), (lam, 14, 2), (pool_s, 16, 2),
                 (ln1_g, 18, 8), (ln1_b, 26, 8), (ln2_g, 34, 8), (ln2_b, 42, 8)]
        pairs = []
        for l in range(L):
            for (t, off, n) in plist:
                if t is conv_w:
                    src = t[l].rearrange("t (c p) -> (t c) p", p=128)
                else:
                    src = t[l].rearrange("(c p) -> c p", p=128)
                pairs.append((prow[NPV * l + off:NPV * l + off + n, :], src))
        dma("sp", pairs, writes=["prow"], key="c4")
        b0 = C.bank()
        op("pe", lambda: pe.transpose(PS(b0, 2 * NPV), prow[:], ident[0:2 * NPV, 0:2 * NPV]),
           reads=["prow", "ident"], writes=[("ps", b0)])
        op("dve", lambda: dve.tensor_copy(out=pv[:], in_=PS(b0, 2 * NPV)), reads=[("ps", b0)], writes=["pv"])
        if STOP == 'i2':
            C.finish(); ph.close(); top.close(); return nc
        tl = sbt(ph, "tl", [128, 2])
        for l in range(L):
            pb = NPV * l
            op("act", lambda: act.activation(out=tl[:], in_=pv[:, pb + 14:pb + 16], func=AF.Exp, scale=-1.0),
               reads=["pv"], writes=["tl"])
            op("dve", lambda: dve.tensor_scalar_add(tl[:], tl[:], 1.0), reads=["tl"], writes=["tl"])
            op("act", lambda: act.activation(out=tl[:], in_=tl[:], func=AF.Ln), reads=["tl"], writes=["tl"])
            op("dve", lambda: dve.tensor_scalar_mul(der[:, l, 0:2], tl[:], -4.0), reads=["tl"], writes=["der"])
            op("dve", lambda: dve.tensor_scalar_mul(der[:, l, 2:4], tl[:], -8.0), reads=["tl"], writes=["der"])
            op("dve", lambda: dve.tensor_scalar_mul(der[:, l, 4:6], pv[:, pb + 10:pb + 12], 0.5), reads=["pv"], writes=["der"])
            op("dve", lambda: dve.tensor_scalar_mul(der[:, l, 6:8], pv[:, pb + 12:pb + 14], 0.5), reads=["pv"], writes=["der"])
        if STOP == 'i3':
            C.finish(); ph.close(); top.close(); return nc
        dma("sp", (es64[:], sinks.rearrange("l h -> (l h)").partition_broadcast(128)), writes=["es64"], key="c5")
        if STOP == 'i3a':
            C.finish(); ph.close(); top.close(); return nc
        op("act", lambda: act.activation(out=es64[:], in_=es64[:], func=AF.Exp), reads=["es64"], writes=["es64"])
        pairs = []
        for l in range(L):
            for h in range(8):
                pairs.append((essamp[16 * h:16 * h + 16, l:l + 1], sinks[l, h:h + 1].partition_broadcast(16)))
        dma("sp", pairs, writes=["essamp"], key="c6")
        op("act", lambda: act.activation(out=essamp[:], in_=essamp[:], func=AF.Exp), reads=["essamp"], writes=["essamp"])
        if STOP == 'i4':
            C.finish(); ph.close(); top.close(); return nc
        pairs = []
        for l in range(L):
            for n in range(4):
                c, e = n // 2, n % 2
                for si, wt in ((0, ga_w), (2, gx_w), (4, pool_w)):
                    pairs.append((wbd[64 * e:64 * e + 64, l, si + c, 64 * e:64 * e + 64], wt[l, n]))
        dma("pool", pairs, writes=["wbd"], key="c7")
        if STOP == 'i5':
            C.finish(); ph.close(); top.close(); return nc

        for l in range(L):
            dma("sp", [(s_k[l, :, 0:127, :], ck[l, :, 1:128, :]), (s_v[l, :, 0:127, :], cv[l, :, 1:128, :]),
                       (s_conv[l, :, 0:2, :], sc[l].rearrange("(b t) n -> b t n", t=3)[:, 1:3, :]),
                       (s_pool[l, :, 0:14, :], spool[l].rearrange("(b t) n -> b t n", t=15)[:, 1:15, :])],
                key="dd%d" % l)
        if STOP == 'i6':
            C.finish(); ph.close(); top.close(); return nc
        if STOP == 'i6b':
            C.barrier(); C.finish(); ph.close(); top.close(); return nc

        load_winb(0)
        xst = [sbt(ph, "xst%d" % i, [128, 1024]) for i in range(2)]
        for ti in range(16 if STOP == 'initA' else (1 if STOP == 'initB' else 17)):
            st = xst[ti % 2]
            sk = "xst%d" % (ti % 2)
            n = 128 if ti < 16 else NS
            src = xp[128 * ti:128 * ti + 128, :] if ti < 16 else xs[:, :]
            dma("sp", (st[0:n, :], src), writes=[sk], key=sk)
            for hb in range(0 if XMODE == 0 else 2):
                b = C.bank()
                for mm in range(4):
                    m = 4 * hb + mm
                    op("pe", lambda: pe.transpose(PS(b, n, o=n * mm), st[0:n, 128 * m:128 * m + 128], ident[0:n, 0:n]),
                       reads=[sk, "ident"], writes=[("ps", b)], signal=(mm == 3))
                src_ps = ps[:, b, 0:4 * n].rearrange("p (m n) -> p m n", n=n)
                op("dve", lambda: dve.tensor_copy(out=xf[:, 4 * hb:4 * hb + 4, 128 * ti:128 * ti + n], in_=src_ps),
                   reads=[("ps", b)], writes=[("xf", ti)])
                if XMODE >= 2:
                    op("act", lambda: act.activation(out=xb[:, 4 * hb:4 * hb + 4, 128 * ti:128 * ti + n], in_=src_ps, func=AF.Copy),
                       reads=[("ps", b)], writes=[("xb", ti)])
    C.barrier()
    if STOP in ('init', 'initA', 'initB'):
        C.finish(); top.close(); return nc

    def layer_norm(W, l, which, c0, N, xkey):
        for _ in layer_norm_g(W, l, which, c0, N, xkey):
            pass

    def layer_norm_g(W, l, which, c0, N, xkey, banks=None, slack=0):
        gcol = NPV * l + (18 if which == 1 else 34)
        bcol = gcol + 8
        ybf, ybk = W["ybf"], W["ybfk"]
        k0_, k1_, k2_, k3_ = W["stk"]
        cap = ybf.shape[1]
        b1, b2 = banks if banks is not None else (C.bank(), C.bank())
        for (bb_, fn_) in ((b1, AF.Copy), (b2, AF.Square)):
            for h0 in range(0, 8, cap):
                op("act", lambda: act.activation(out=ybf[:, 0:cap, 0:N], in_=xf[:, h0:h0 + cap, c0:c0 + N], func=fn_), reads=[xkey], writes=ybk)
                yield
                for m in range(h0, h0 + cap):
                    op("pe", lambda: pe.matmul(PS(bb_, N), lhsT=ones[:], rhs=ybf[:, m - h0, 0:N], start=(m == 0), stop=(m == 7)),
                       reads=ybk + ["ones"], writes=[("ps", bb_)], signal=(m == h0 + cap - 1))
            yield
        for _ in range(slack):
            yield
        mean, msq, rstd, nmr = [a_[:, 0:N] for a_ in W["st"]]
        op("dve", lambda: dve.tensor_scalar_mul(mean, PS(b1, N), 1.0 / 1024), reads=[("ps", b1)], writes=[k0_])
        op("dve", lambda: dve.tensor_tensor(out=msq, in0=mean, in1=mean, op=ALU.mult), reads=[k0_], writes=[k1_])
        op("dve", lambda: dve.scalar_tensor_tensor(out=msq, in0=PS(b2, N), scalar=1.0 / 1024, in1=msq, op0=ALU.mult, op1=ALU.subtract),
           reads=[("ps", b2), k1_], writes=[k1_])
        op("dve", lambda: dve.tensor_scalar_add(msq, msq, EPS), reads=[k1_], writes=[k1_])
        op("act", lambda: act.activation(out=rstd, in_=msq, func=AF.Ln), reads=[k1_], writes=[k2_])
        op("act", lambda: act.activation(out=rstd, in_=rstd, func=AF.Exp, scale=-0.5), reads=[k2_], writes=[k2_])
        yield
        for _ in range(slack):
            yield
        op("dve", lambda: dve.scalar_tensor_tensor(out=nmr, in0=mean, scalar=-1.0, in1=rstd, op0=ALU.mult, op1=ALU.mult),
           reads=[k0_, k2_], writes=[k3_])
        xblk = xf[:, :, c0:c0 + N]
        op("dve", lambda: dve.tensor_tensor(out=xblk, in0=xblk, in1=rstd.unsqueeze(1).to_broadcast([128, 8, N]), op=ALU.mult), reads=[xkey, k2_], writes=[xkey])
        yield
        op("dve", lambda: dve.tensor_tensor(out=xblk, in0=xblk, in1=nmr.unsqueeze(1).to_broadcast([128, 8, N]), op=ALU.add), reads=[xkey, k3_], writes=[xkey])
        yield
        for m in range(8):
            xs_ = xf[:, m, c0:c0 + N]
            op("dve", lambda: dve.tensor_scalar(out=xs_, in0=xs_, scalar1=pv[:, gcol + m:gcol + m + 1], scalar2=pv[:, bcol + m:bcol + m + 1],
                                                op0=ALU.mult, op1=ALU.add), reads=[xkey, "pv"], writes=[xkey])
            if m % 4 == 3:
                yield
        op("act", lambda: act.activation(out=xb[:, :, c0:c0 + N], in_=xblk, func=AF.Copy), reads=[xkey], writes=[xkey + "b"])

    def pool_part(W, l, N, zwin, zcur, mixb, first_corr=None):
        pb = NPV * l
        diffb = W["diffb"]
        for (p0, p1, c, win, w, wkey) in zwin:
            if first_corr is not None:
                first_corr(p0, p1, c, win, wkey)
            op("dve", lambda: dve.scalar_tensor_tensor(out=diffb[p0:p1, c, 0:N], in0=win, scalar=1.0 / w, in1=zcur(c, p0, p1),
                                                       op0=ALU.mult, op1=ALU.subtract), reads=[wkey, "zp"], writes=["diffb"])
        bp = C.bank()
        for c in range(2):
            op("pe", lambda: pe.matmul(PS(bp, N, o=256 * c), lhsT=wbd[:, l, 4 + c, :], rhs=diffb[:, c, 0:N], start=True, stop=True),
               reads=["diffb", "wbd"], writes=[("ps", bp)])
        for c in range(2):
            op("act", lambda: act.activation(out=mixb[:, 2 + c, :], in_=PS(bp, N, o=256 * c), func=AF.Identity, scale=pv[:, pb + 16 + c:pb + 17 + c]),
               reads=[("ps", bp), "pv"], writes=["mixb"])

    def lru_part_g(W, l, N, xc_taps, xc, gr, h_apply, finish, sfx=""):
        pb = NPV * l
        wk = W["wk"]
        xcb = W["xcb"]
        for c in range(2):
            op("act", lambda: act.activation(out=xc[:, c, :], in_=xc_taps(c, 0), func=AF.Identity, scale=pv[:, pb + c:pb + c + 1],
                                             bias=pv[:, pb + 8 + c:pb + 9 + c]),
               reads=["xr" + sfx, "pv"], writes=["xc" + sfx])
            for tap in range(1, 4):
                op("dve", lambda: dve.scalar_tensor_tensor(out=xc[:, c, :], in0=xc_taps(c, tap), scalar=pv[:, pb + 2 * tap + c:pb + 2 * tap + c + 1],
                                                           in1=xc[:, c, :], op0=ALU.mult, op1=ALU.add),
                   reads=["xr" + sfx, "pv", "xc" + sfx], writes=["xc" + sfx])
        yield
        op("act", lambda: act.activation(out=xcb[:, :, 0:N], in_=xc, func=AF.Copy), reads=["xc" + sfx], writes=["xcb" + sfx])
        bg, bh = C.bank(), C.bank()
        for c in range(2):
            op("pe", lambda: pe.matmul(PS(bg, N, o=256 * c), lhsT=wbd[:, l, 0 + c, :], rhs=xcb[:, c, 0:N], start=True, stop=True),
               reads=["xcb" + sfx, "wbd"], writes=[("ps", bg)])
            op("pe", lambda: pe.matmul(PS(bh, N, o=256 * c), lhsT=wbd[:, l, 2 + c, :], rhs=xcb[:, c, 0:N], start=True, stop=True),
               reads=["xcb" + sfx, "wbd"], writes=[("ps", bh)])
        tha, thx, a_, a2 = [wk[i][:, :, 0:N] for i in range(4)]
        hs = a2
        for c in range(2):
            op("act", lambda: act.activation(out=tha[:, c, :], in_=PS(bg, N, o=256 * c), func=AF.Tanh, scale=0.5, bias=der[:, l, 4 + c:5 + c]),
               reads=[("ps", bg), "der"], writes=["wk0" + sfx])
            op("act", lambda: act.activation(out=thx[:, c, :], in_=PS(bh, N, o=256 * c), func=AF.Tanh, scale=0.5, bias=der[:, l, 6 + c:7 + c]),
               reads=[("ps", bh), "der"], writes=["wk1" + sfx])
            op("act", lambda: act.activation(out=a_[:, c, :], in_=tha[:, c, :], func=AF.Exp, scale=der[:, l, c:c + 1], bias=der[:, l, c:c + 1]),
               reads=["wk0" + sfx, "der"], writes=["wk2" + sfx])
            op("act", lambda: act.activation(out=a2[:, c, :], in_=tha[:, c, :], func=AF.Exp, scale=der[:, l, 2 + c:3 + c], bias=der[:, l, 2 + c:3 + c]),
               reads=["wk0" + sfx, "der"], writes=["wk3" + sfx])
        yield
        op("dve", lambda: dve.tensor_scalar(out=a2, in0=a2, scalar1=-1.0, scalar2=1.0, op0=ALU.mult, op1=ALU.add), reads=["wk3" + sfx], writes=["wk3" + sfx])
        op("act", lambda: act.activation(out=tha, in_=a2, func=AF.Ln), reads=["wk3" + sfx], writes=["wk0" + sfx])
        op("act", lambda: act.activation(out=tha, in_=tha, func=AF.Exp, scale=0.5), reads=["wk0" + sfx], writes=["wk0" + sfx])
        op("dve", lambda: dve.scalar_tensor_tensor(out=thx, in0=thx, scalar=1.0, in1=xc, op0=ALU.add, op1=ALU.mult),
           reads=["wk1" + sfx, "xc" + sfx], writes=["wk1" + sfx])
        op("dve", lambda: dve.scalar_tensor_tensor(out=thx, in0=thx, scalar=0.5, in1=tha, op0=ALU.mult, op1=ALU.mult),
           reads=["wk1" + sfx, "wk0" + sfx], writes=["wk1" + sfx])
        yield
        h_apply(a_, thx, hs)
        yield
        op("act", lambda: act.activation(out=tha, in_=gr, func=AF.Square), reads=["gr" + sfx], writes=["wk0" + sfx])
        op("dve", lambda: dve.tensor_scalar(out=tha, in0=tha, scalar1=0.044715, scalar2=1.0, op0=ALU.mult, op1=ALU.add), reads=["wk0" + sfx], writes=["wk0" + sfx])
        op("dve", lambda: dve.tensor_tensor(out=tha, in0=tha, in1=gr, op=ALU.mult), reads=["wk0" + sfx, "gr" + sfx], writes=["wk0" + sfx])
        op("act", lambda: act.activation(out=tha, in_=tha, func=AF.Tanh, scale=0.7978845608028654), reads=["wk0" + sfx], writes=["wk0" + sfx])
        op("dve", lambda: dve.scalar_tensor_tensor(out=tha, in0=tha, scalar=1.0, in1=gr, op0=ALU.add, op1=ALU.mult), reads=["wk0" + sfx, "gr" + sfx], writes=["wk0" + sfx])
        yield
        finish(hs, tha, thx)

    def lru_part(W, l, N, xc_taps, xc, gr, h_apply, finish):
        for _ in lru_part_g(W, l, N, xc_taps, xc, gr, h_apply, finish):
            pass

    def lru_pool_common(W, l, N, xc_taps, xc, gr, zwin, zcur, h_apply, mixb, first_corr=None):
        pool_part(W, l, N, zwin, zcur, mixb, first_corr)

        def fin(hs, ge, _):
            op("dve", lambda: dve.scalar_tensor_tensor(out=mixb[:, 0:2, :], in0=hs, scalar=0.5, in1=ge, op0=ALU.mult, op1=ALU.mult),
               reads=["wk3", "wk0"], writes=["mixb"])
        lru_part(W, l, N, xc_taps, xc, gr, h_apply, fin)

    def wout_ln(W, l, c0, N, attn, mixb, xkey, ln=True):
        for _ in wout_g(W, l, c0, N, attn, mixb, xkey):
            pass
        if ln:
            layer_norm(W, l, 1, c0, N, xkey)

    def wout_g(W, l, c0, N, attn, mixb, xkey):
        woa, wob = W["woa"], W["wob"]
        for m in range(8):
            b = C.bank()
            for h in range(4):
                op("pe", lambda: pe.matmul(PS(b, N), lhsT=woa[:, h, 128 * m:128 * m + 128], rhs=attn[:, h, :], start=(h == 0), stop=False),
                   reads=["attnT", "woa"], writes=[("ps", b)], signal=False)
            for j in range(4):
                op("pe", lambda: pe.matmul(PS(b, N), lhsT=wob[:, j, 128 * m:128 * m + 128], rhs=mixb[:, j, :], start=False, stop=(j == 3)),
                   reads=["mixb", "wob"], writes=[("ps", b)], signal=(j == 3))
            op("dve", lambda: dve.scalar_tensor_tensor(out=xf[:, m, c0:c0 + N], in0=xf[:, m, c0:c0 + N], scalar=ALPHA, in1=PS(b, N),
                                                       op0=ALU.mult, op1=ALU.add), reads=[("ps", b), xkey], writes=[xkey])
            if m % 2 == 1:
                yield

    def chk(stage):
        if STOP == stage:
            raise _Stop()

    def run_layers():
      for l in range(L):
          pb = NPV * l
          with contextlib.ExitStack() as ph:
              W = {}
              W["woa"] = woa = sbt(ph, "woa", [128, 4, 1024], BF16)
              W["wob"] = wob = sbt(ph, "wob", [128, 4, 1024], BF16)
              W["wk"] = [sbt(ph, "wk%d" % i, [128, 2, 272]) for i in range(4)]
              W["st"] = [W["wk"][2][:, 0, 0:256], W["wk"][2][:, 1, 0:256], W["wk"][3][:, 0, 0:256], W["wk"][3][:, 1, 0:256]]
              W["stk"] = ["wk2", "wk2", "wk3", "wk3"]
              st_ph, stk_ph = W["st"], W["stk"]
              W["diffb"] = sbt(ph, "diffb", [128, 2, 256], BF16)
              W["tA"] = W["wk"][0][:, 0, 0:256]; W["tAk"] = "wk0"
              W["tB"] = W["wk"][1][:, 0, 0:256]; W["tBk"] = "wk1"
              dma("pool", (woa[:], w_out[l, 0:512, :].rearrange("(j p) n -> p j n", p=128)), writes=["woa"], key="woa")
              dma("pool", (wob[:], w_out[l, 512:1024, :].rearrange("(j p) n -> p j n", p=128)), writes=["wob"], key="wob")

              with contextlib.ExitStack() as pp:
                  rec0 = sbt(pp, "rec0", [128, 2, T], BF16)
                  corr = sbt(pp, "corr", [128, 2, T], BF16)
                  kTb = sbt(pp, "kTb", [64, 2, 384], BF16)
                  Vb = sbt(pp, "Vb", [128, 3, 128], BF16)
                  xr_ext = sbt(pp, "xr_ext", [128, 2, 259])
                  zp_ext = sbt(pp, "zp_ext", [128, 2, 271])
                  hcar = sbt(pp, "hcar", [128, 2]); Acar = sbt(pp, "Acar", [128, 2]); hst = sbt(pp, "hst", [128, 2]); hfin = sbt(pp, "hfin", [128, 2])
                  wkt = W["wk"]

                  with contextlib.ExitStack() as pa:
                      stq = rec0[:].rearrange("p c n -> p (c n)")[:, 0:1536].bitcast(F32)
                      cview = corr[:].rearrange("p c n -> p (c n)")[:, 0:1024].bitcast(F32)
                      sth = cview[:, 0:256]; stc = cview[0:18, 256:512]
                      b1, b2 = C.bank(), C.bank()
                      for k in range(8):
                          op("pe", lambda: pe.matmul(PS(b1), lhsT=xb[:, k, T - 128:T], rhs=winb[:, k, 512:1024], start=(k == 0), stop=(k == 7)),
                             reads=["winb"], writes=[("ps", b1)], signal=(k == 7))
                      for k in range(8):
                          op("pe", lambda: pe.matmul(PS(b2, 256), lhsT=xb[:, k, T - 128:T], rhs=winb[:, k, 1280:1536], start=(k == 0), stop=(k == 7)),
                             reads=["winb"], writes=[("ps", b2)], signal=(k == 7))
                      op("dve", lambda: dve.tensor_copy(out=stq[:, 0:512], in_=PS(b1)), reads=[("ps", b1)], writes=["rec0"])
                      op("dve", lambda: dve.tensor_copy(out=stq[:, 512:768], in_=PS(b2, 256)), reads=[("ps", b2)], writes=["rec0"])
                      dma("sp", [(st_out[l, 0:128, :], stq[:, 0:256]), (st_out[l, 128:131, :], stq[125:128, 256:512]),
                                 (st_out[l, 131:146, :], stq[113:128, 512:768])], reads=["rec0"], key="stq")
                      dma("sp", [(cc1_in[l, 0:128, :], stq[:, 0:256]), (cc1_in[l, 128:131, :], stq[125:128, 256:512]),
                                 (cc1_in[l, 131:146, :], stq[113:128, 512:768])], reads=["rec0"], writes=["cc1in"], key="stq2")
                      C.collective(cc1_in[l], cc1_out[l], reads=["cc1in"], writes=["cc1out"])
                      dma("sp", [(sth[:], cc1_out[l, 0:128, :]), (stc[:], cc1_out[l, 128:146, :])], reads=["cc1out"], writes=["corr"], key="sth")
                      bk = C.bank()
                      for kv in range(2):
                          op("pe", lambda: pe.transpose(PS(bk, 128, 0, 64, 128 * kv), sth[:, 64 * kv:64 * kv + 64], ident[:, :]),
                             reads=["corr", "ident"], writes=[("ps", bk)], signal=(kv == 1))
                      op("dve", lambda: dve.tensor_scalar(out=kTb[:, :, 0:128], in0=ps[0:64, bk, 0:256].rearrange("p (k n) -> p k n", n=128),
                                                          scalar1=flag[0:64, 0:1], scalar2=None, op0=ALU.mult),
                         reads=[("ps", bk), "smallc"], writes=["kTb"])
                      op("dve", lambda: dve.tensor_scalar(out=Vb[:, 0, :], in0=sth[:, 128:256], scalar1=flag[:, 0:1], scalar2=None, op0=ALU.mult),
                         reads=["corr", "smallc"], writes=["Vb"])
                      bk2 = C.bank()
                      for c in range(2):
                          op("pe", lambda: pe.transpose(PS(bk2, 18, o=32 * c), stc[0:18, 128 * c:128 * c + 128], ident[0:18, 0:18]),
                             reads=["corr", "ident"], writes=[("ps", bk2)], signal=(c == 1))
                      for c in range(2):
                          op("dve", lambda: dve.tensor_scalar(out=xr_ext[:, c, 0:3], in0=PS(bk2, 3, o=32 * c), scalar1=flag[:, 0:1], scalar2=None, op0=ALU.mult),
                             reads=[("ps", bk2), "smallc"], writes=["xr0"])
                          op("dve", lambda: dve.tensor_scalar(out=zp_ext[:, c, 0:15], in0=PS(bk2, 15, o=32 * c + 3), scalar1=flag[:, 0:1], scalar2=None, op0=ALU.mult),
                             reads=[("ps", bk2), "smallc"], writes=["zp"])
                  op("dve", lambda: dve.memset(hcar[:], 0.0), writes=["hcar"])
                  op("dve", lambda: dve.memset(Acar[:], 1.0), writes=["Acar"])

                  with contextlib.ExitStack() as pb_:
                      sets = []
                      for i in range(2):
                          d_ = {"gr": sbt(pb_, "gr%d" % i, [128, 2, 256]), "xc": sbt(pb_, "xc%d" % i, [128, 2, 256]),
                                "xcb": sbt(pb_, "xcb%d" % i, [128, 2, 256], BF16)}
                          d_["wk"] = W["wk"] if i == 0 else [sbt(pb_, "wkB%d" % q_, [128, 2, 272]) for q_ in range(4)]
                          d_["xr"] = xr_ext if i == 0 else sbt(pb_, "xr_extB", [128, 2, 259])
                          sets.append(d_)

                      def pre(bi):
                          c0 = 256 * bi
                          N = 256
                          i = bi % 2
                          sx = str(i)
                          S_ = sets[i]
                          xr_i, gr_i, xc_i = S_["xr"], S_["gr"], S_["xc"]
                          for (cb, dst, key) in ((768, xr_i[:, :, 3:259], "xr" + sx), (1024, gr_i[:, :, :], "gr" + sx)):
                              b = C.bank()
                              for c in range(2):
                                  for k in range(8):
                                      op("pe", lambda: pe.matmul(PS(b, N, o=256 * c), lhsT=winb[:, k, cb + 128 * c:cb + 128 * c + 128], rhs=xb[:, k, c0:c0 + N],
                                                                 start=(k == 0), stop=(k == 7)),
                                         reads=["winb"], writes=[("ps", b)], signal=(k == 7 and c == 1))
                              if key.startswith("gr"):
                                  op("act", lambda: act.activation(out=dst, in_=ps[:, b, :].rearrange("p (e n) -> p e n", n=256), func=AF.Copy),
                                     reads=[("ps", b)], writes=[key])
                              else:
                                  op("dve", lambda: dve.tensor_copy(out=dst, in_=ps[:, b, :].rearrange("p (e n) -> p e n", n=256)),
                                     reads=[("ps", b)], writes=[key])
                          if bi > 0:
                              xr_p = sets[1 - i]["xr"]
                              op("dve", lambda: dve.tensor_copy(out=xr_i[:, :, 0:3], in_=xr_p[:, :, 256:259]), reads=["xr" + str(1 - i)], writes=["xr" + sx])
                          yield

                          def h_apply(a_, bb, hs):
                              for c in range(2):
                                  op("dve", lambda: dve.tensor_tensor_scan(out=hs[:, c, :], data0=a_[:, c, :], data1=bb[:, c, :], initial=hcar[:, c:c + 1],
                                                                           op0=ALU.mult, op1=ALU.add), reads=["wk2" + sx, "wk1" + sx, "hcar"], writes=["wk3" + sx])
                              op("dve", lambda: dve.tensor_copy(out=hcar[:, :], in_=hs[:, :, 255]), reads=["wk3" + sx], writes=["hcar"])
                              for c in range(2):
                                  op("dve", lambda: dve.tensor_tensor_scan(out=bb[:, c, :], data0=a_[:, c, :], data1=cneg[:, 0:256], initial=Acar[:, c:c + 1],
                                                                           op0=ALU.mult, op1=ALU.max), reads=["wk2" + sx, "cneg", "Acar"], writes=["wk1" + sx])
                              op("dve", lambda: dve.tensor_copy(out=Acar[:, :], in_=bb[:, :, 255]), reads=["wk1" + sx], writes=["Acar"])

                          def fin(hs, ge, Acum):
                              op("dve", lambda: dve.scalar_tensor_tensor(out=rec0[:, :, c0:c0 + 256], in0=hs, scalar=0.5, in1=ge, op0=ALU.mult, op1=ALU.mult),
                                 reads=["wk3" + sx, "wk0" + sx], writes=["rec0"])
                              op("dve", lambda: dve.scalar_tensor_tensor(out=corr[:, :, c0:c0 + 256], in0=Acum, scalar=0.5, in1=ge, op0=ALU.mult, op1=ALU.mult),
                                 reads=["wk1" + sx, "wk0" + sx], writes=["corr"])
                          yield from lru_part_g(S_, l, N, lambda c, tap: xr_i[:, c, tap:tap + 256], xc_i[:], gr_i[:], h_apply, fin, sfx=sx)

                      pend = [pre(bi) for bi in range(8)]
                      active = []
                      while pend or active:
                          while len(active) < 2 and pend:
                              active.append(pend.pop(0))
                          for g_ in list(active):
                              try:
                                  next(g_)
                              except StopIteration:
                                  active.remove(g_)
                  C.barrier()
                  with nc.allow_non_contiguous_dma(reason="tiny h state"):
                      dma("sp", (cc2_in[l, 0, :].rearrange("(c p) -> p c", p=128), hcar[:, :]), reads=["hcar"], writes=["cc2in"], key="hst")
                  C.collective(cc2_in[l], cc2_out[l], reads=["cc2in"], writes=["cc2out"])
                  with nc.allow_non_contiguous_dma(reason="tiny h state"):
                      dma("sp", (hst[:, :], cc2_out[l, 0, :].rearrange("(c p) -> p c", p=128)), reads=["cc2out"], writes=["hst"], key="hst2")
                  op("dve", lambda: dve.tensor_scalar(out=hst[:], in0=hst[:], scalar1=flag[:, 0:1], scalar2=None, op0=ALU.mult), reads=["hst", "smallc"], writes=["hst"])
                  op("dve", lambda: dve.tensor_tensor(out=hfin[:], in0=Acar[:], in1=hst[:], op=ALU.mult), reads=["Acar", "hst"], writes=["hfin"])
                  op("dve", lambda: dve.tensor_tensor(out=hfin[:], in0=hfin[:], in1=hcar[:], op=ALU.add), reads=["hfin", "hcar"], writes=["hfin"])
                  with nc.allow_non_contiguous_dma(reason="tiny h state"):
                      dma("sp", (st_out[l, 146, :].rearrange("(c p) -> p c", p=128), hfin[:, :]), reads=["hfin"], key="hst3")

                  with contextlib.ExitStack() as pc:
                      qT = sbt(pc, "qT", [64, 8, 256], BF16)
                      attnT = sbt(pc, "attnT", [128, 4, 256], BF16)
                      mixb = sbt(pc, "mixb", [128, 4, 256], BF16)
                      tPy = [sbt(pc, "tPy%d" % i, [128, 1024]) for i in range(2)]
                      tP = [[tPy[i][:, 0:512], tPy[i][:, 512:1024]] for i in range(2)]
                      PT = [[sbt(pc, "PT%d%d" % (i, j_), [128, 512], BF16) for j_ in range(2)] for i in range(2)]
                      W["ybf"] = sbt(pc, "ybfL", [128, 4, 256], BF16)
                      W["ybfk"] = ["ybfL"]
                      stL = sbt(pc, "stL", [128, 3, 256])
                      W["st"] = [stL[:, 0, :], stL[:, 1, :], stL[:, 2, :], stL[:, 1, :]]
                      W["stk"] = ["stL0", "stL1", "stL2", "stL1"]
                      dd = wkt[3][:].rearrange("p c n -> p (c n)")[:, 0:256]
                      S2 = wkt[0][:, :, 0:270]; S4 = wkt[1][:, :, 0:268]
                      def front(bi):
                          c0 = 256 * bi
                          N = 256
                          xkey = "xP%d" % bi
                          xin = [xkey + "b"]

                          def rhs_x(k):
                              return xb[:, k, c0:c0 + N]
                          for j in range(4):
                              b = C.bank()
                              for e in range(2):
                                  h = 2 * j + e
                                  for k in range(8):
                                      op("pe", lambda: pe.matmul(PS(b, N, 0, 64, 256 * e), lhsT=winb[:, k, 64 * h:64 * h + 64], rhs=rhs_x(k),
                                                                 start=(k == 0), stop=(k == 7)),
                                         reads=["winb"] + xin, writes=[("ps", b)], signal=(k == 7 and e == 1))
                              op("act", lambda: act.activation(out=qT[:, 2 * j:2 * j + 2, :], in_=ps[0:64, b, :].rearrange("p (e n) -> p e n", n=256), func=AF.Copy),
                                 reads=[("ps", b)], writes=["qT"])
                              if j % 2 == 1:
                                  yield
                          b = C.bank()
                          for kv in range(2):
                              for k in range(8):
                                  op("pe", lambda: pe.matmul(PS(b, N, 0, 64, 256 * kv), lhsT=winb[:, k, 512 + 64 * kv:512 + 64 * kv + 64], rhs=rhs_x(k),
                                                             start=(k == 0), stop=(k == 7)),
                                     reads=["winb"] + xin, writes=[("ps", b)], signal=(k == 7 and kv == 1))
                          op("act", lambda: act.activation(out=kTb[:, :, 128:384], in_=ps[0:64, b, :].rearrange("p (e n) -> p e n", n=256), func=AF.Copy),
                             reads=[("ps", b)], writes=["kTb"])
                          yield
                          b = C.bank()
                          for c in range(2):
                              for k in range(8):
                                  op("pe", lambda: pe.matmul(PS(b, N, o=256 * c), lhsT=winb[:, k, 1280 + 128 * c:1280 + 128 * c + 128], rhs=rhs_x(k),
                                                             start=(k == 0), stop=(k == 7)),
                                     reads=["winb"] + xin, writes=[("ps", b)], signal=(k == 7 and c == 1))
                          op("dve", lambda: dve.tensor_copy(out=zp_ext[:, :, 15:271], in_=ps[:, b, :].rearrange("p (e n) -> p e n", n=256)),
                             reads=[("ps", b)], writes=["zp"])
                          yield
                          b = C.bank()
                          for i in range(2):
                              for k in range(8):
                                  op("pe", lambda: pe.matmul(PS(b, 128, o=128 * i), lhsT=xb[:, k, c0 + 128 * i:c0 + 128 * i + 128], rhs=winb[:, k, 640:768],
                                                             start=(k == 0), stop=(k == 7)),
                                     reads=["winb"] + xin, writes=[("ps", b)], signal=(k == 7 and i == 1))
                          op("act", lambda: act.activation(out=Vb[:, 1:3, :], in_=ps[:, b, 0:256].rearrange("p (e n) -> p e n", n=128), func=AF.Copy),
                             reads=[("ps", b)], writes=["Vb"])
                          yield
                          iters = [(qi, kv) for qi in range(2) for kv in range(2)]

                          def s1(it):
                              qi, kv = iters[it]
                              sx = it % 2
                              bs = [C.bank(), C.bank()]
                              for pc_ in range(2):
                                  ko = 128 * qi + 128 * pc_
                                  tk, pk = "tP%d%d" % (sx, pc_), "PT%d%d" % (sx, pc_)
                                  op("pe", lambda: pe.matmul(PS(bs[pc_]), lhsT=kTb[:, kv, ko:ko + 128], rhs=qT[:, 4 * kv:4 * kv + 4, 128 * qi:128 * qi + 128],
                                                             start=True, stop=True),
                                     reads=["kTb", "qT"], writes=[("ps", bs[pc_])])
                                  bias_t = (bprev if pc_ == 0 else bcur)[:, kv, :]
                                  op("dve", lambda: dve.scalar_tensor_tensor(out=tP[sx][pc_], in0=PS(bs[pc_]), scalar=0.125, in1=bias_t, op0=ALU.mult, op1=ALU.add),
                                     reads=[("ps", bs[pc_]), "bcur", "bprev"], writes=[tk])
                                  if pc_ == 0 and bi == 0 and qi == 0:
                                      op("act", lambda: act.activation(out=PT[sx][pc_][:], in_=tP[sx][pc_], func=AF.Exp, bias=nm0[:, 0:1]),
                                         reads=[tk, "smallc"], writes=[pk])
                                  else:
                                      op("act", lambda: act.activation(out=PT[sx][pc_][:], in_=tP[sx][pc_], func=AF.Exp),
                                         reads=[tk], writes=[pk])

                          def s2(it):
                              qi, kv = iters[it]
                              sx = it % 2
                              bo, bd = C.bank(), C.bank()
                              for e in range(2):
                                  for pc_ in range(2):
                                      rhs_ = PT[sx][pc_][:].rearrange("p (gg e n) -> p e gg n", e=2, n=128)[:, e, :, :]
                                      op("pe", lambda: pe.matmul(PS(bo, 256, 64 * e, 64 * e + 64), lhsT=Vb[:, qi + pc_, 64 * kv:64 * kv + 64], rhs=rhs_,
                                                                 start=(pc_ == 0), stop=(pc_ == 1)),
                                         reads=["Vb", "PT%d%d" % (sx, pc_)], writes=[("ps", bo)], signal=(pc_ == 1 and e == 1))
                              for e in range(2):
                                  for pc_ in range(2):
                                      rhs_ = PT[sx][pc_][:].rearrange("p (gg e n) -> p e gg n", e=2, n=128)[:, e, :, :]
                                      op("pe", lambda: pe.matmul(PS(bd, 256, 64 * e, 64 * e + 64), lhsT=ones[:, 0:64], rhs=rhs_, start=(pc_ == 0), stop=(pc_ == 1)),
                                         reads=["ones", "PT%d%d" % (sx, pc_)], writes=[("ps", bd)], signal=(pc_ == 1 and e == 1))
                              for e in range(2):
                                  for gg in range(2):
                                      hcol = 8 * l + 4 * kv + 2 * gg + e
                                      op("act", lambda: act.activation(out=dd[64 * e:64 * e + 64, 128 * gg:128 * gg + 128], in_=ps[64 * e:64 * e + 64, bd, 128 * gg:128 * gg + 128],
                                                                       func=AF.Ln, bias=es64[64 * e:64 * e + 64, hcol:hcol + 1]),
                                         reads=[("ps", bd), "es64"], writes=["wk3"])
                              op("act", lambda: act.activation(out=dd, in_=dd, func=AF.Exp, scale=-1.0), reads=["wk3"], writes=["wk3"])
                              op("dve", lambda: dve.tensor_tensor(out=attnT[:, 2 * kv:2 * kv + 2, 128 * qi:128 * qi + 128],
                                                                  in0=ps[:, bo, 0:256].rearrange("p (g n) -> p g n", n=128),
                                                                  in1=dd.rearrange("p (g n) -> p g n", n=128), op=ALU.mult),
                                 reads=[("ps", bo), "wk3"], writes=["attnT"])

                          s1(0)
                          for it in range(4):
                              if it + 1 < 4:
                                  s1(it + 1)
                              s2(it)
                              yield
                          op("dve", lambda: dve.tensor_tensor(out=S2, in0=zp_ext[:, :, 1:271], in1=zp_ext[:, :, 0:270], op=ALU.add), reads=["zp"], writes=["wk0"])
                          op("dve", lambda: dve.tensor_tensor(out=S4, in0=S2[:, :, 2:270], in1=S2[:, :, 0:268], op=ALU.add), reads=["wk0"], writes=["wk1"])
                          S8a = wkt[2][:, 0, 0:264]; S8b = wkt[2][:, 1, 0:264]
                          op("dve", lambda: dve.tensor_tensor(out=S8a, in0=S4[:, 1, 4:268], in1=S4[:, 1, 0:264], op=ALU.add),
                             reads=["wk1"], writes=["wk2"])
                          op("dve", lambda: dve.tensor_tensor(out=S8b[:, 0:256], in0=S8a[:, 8:264], in1=S8a[:, 0:256], op=ALU.add),
                             reads=["wk2"], writes=["wk2"])
                          zwin = [(0, 64, 0, S2[0:64, 0, 14:270], 2, "wk0"), (64, 128, 0, S4[64:128, 0, 12:268], 4, "wk1"),
                                  (0, 64, 1, S8a[0:64, 8:264], 8, "wk2"), (64, 128, 1, S8b[64:128, 0:256], 16, "wk2")]

                          def first_corr(p0, p1, c, win, wkey, bi=bi):
                              if bi != 0:
                                  return
                              w16 = win[:, 0:16]
                              op("dve", lambda: dve.tensor_tensor(out=w16, in0=w16, in1=pcorr[p0:p1, c, :], op=ALU.mult),
                                 reads=[wkey, "smallc"], writes=[wkey])
                          yield
                          pool_part(W, l, N, zwin, lambda c, p0, p1: zp_ext[p0:p1, c, 15:271], mixb[:], first_corr)
                          yield
                          for c in range(2):
                              op("dve", lambda: dve.scalar_tensor_tensor(out=mixb[:, c, :], in0=corr[:, c, c0:c0 + N], scalar=hst[:, c:c + 1], in1=rec0[:, c, c0:c0 + N],
                                                                         op0=ALU.mult, op1=ALU.add), reads=["corr", "rec0", "hst"], writes=["mixb"])
                          yield
                          yield from wout_g(W, l, c0, N, attnT, mixb, xkey)
                          yield
                          if bi < 7:
                              op("dve", lambda: dve.tensor_copy(out=zp_ext[:, :, 0:15], in_=zp_ext[:, :, 256:271]), reads=["zp"], writes=["zp"])
                              op("act", lambda: act.activation(out=kTb[:, :, 0:128], in_=kTb[:, :, 256:384], func=AF.Copy), reads=["kTb"], writes=["kTb"])
                              op("act", lambda: act.activation(out=Vb[:, 0, :], in_=Vb[:, 2, :], func=AF.Copy), reads=["Vb"], writes=["Vb"])

                      def drive(gA, gB, delay=0):
                          a_alive, b_alive = gA is not None, gB is not None
                          rnd = 0
                          while a_alive or b_alive:
                              if a_alive:
                                  try:
                                      next(gA)
                                  except StopIteration:
                                      a_alive = False
                              rnd += 1
                              if b_alive and (rnd > delay or not a_alive):
                                  try:
                                      next(gB)
                                  except StopIteration:
                                      b_alive = False

                      C.bank_pool = list(range(6))
                      drive(front(0), None)
                      for bi in range(8):
                          drive(front(bi + 1) if bi + 1 < 8 else None, layer_norm_g(W, l, 1, 256 * bi, 256, "xP%d" % bi, banks=(6, 7)), delay=4)
                      C.bank_pool = list(range(8))

              C.barrier()
              if STOP == 'prompt%d' % l: return

              with contextlib.ExitStack() as pp:
                  N = NS
                  c0 = T
                  xkey = "xS"
                  W["ybf"] = sbt(pp, "ybfS", [128, 8, NS], BF16); W["ybfk"] = ["ybf"]
                  W["xcb"] = sbt(pp, "xcbS", [128, 2, NS], BF16)
                  W["st"], W["stk"] = st_ph, stk_ph
                  xin = ["xSb"]
                  qTs = sbt(pp, "qTs", [64, 8, NS]); kTs = sbt(pp, "kTs", [64, 2, NS]); vTs = sbt(pp, "vTs", [64, 2, NS])
                  xrs = sbt(pp, "xrs", [128, 2, NS, 4]); grs = sbt(pp, "grs", [128, 2, NS]); zps = sbt(pp, "zps", [128, 2, NS, 16])
                  xcs = sbt(pp, "xcs", [128, 2, NS]); h0s = sbt(pp, "h0s", [128, 2, NS])
                  attnTs = sbt(pp, "attnTs", [128, 4, NS], BF16); accw = sbt(pp, "accw", [128, 128], BF16); mixbs = sbt(pp, "mixbs", [128, 4, NS], BF16)
                  sts = sbt(pp, "sts", [NS, 768]); scs = sbt(pp, "scs", [48, 256]); sps = sbt(pp, "sps", [120, 2, 256]); shs = sbt(pp, "shs", [NS, 256])
                  sths = sbt(pp, "sths", [NS, 256])
                  qs128 = sbt(pp, "qs128", [128, 64]); kn128 = sbt(pp, "kn128", [128, 64]); vn128 = sbt(pp, "vn128", [128, 64])
                  krep = sbt(pp, "krep", [64, 128]); vrep = sbt(pp, "vrep", [64, 128])
                  Kcs = [sbt(pp, "Kc%d" % i, [128, 16, 64]) for i in range(2)]; Vcs = [sbt(pp, "Vc%d" % i, [128, 16, 64]) for i in range(2)]
                  tmpc = sbt(pp, "tmpc", [128, 16, 64])

                  def load_kv(buf, src, ch, key):
                      pairs = [(buf[16 * h:16 * h + 16, :, :], src[l, :, 16 * ch:16 * ch + 16, 64 * (h // 4):64 * (h // 4) + 64]) for h in range(8)]
                      dma("sp", pairs, writes=[key], key=key)
                  scr = sbt(pp, "scr", [128, 128]); Pm = sbt(pp, "Pm", [128, 128]); sm = sbt(pp, "sm", [128, 8])
                  acc = sbt(pp, "acc", [128, 64]); part = sbt(pp, "part", [128, 64]); wins = sbt(pp, "wins", [128, 2, NS])

                  def rhs_x(k):
                      return xb[:, k, c0:c0 + N]
                  dma("sp", [(scs[:], sc[l]), (sps[:], spool[l].rearrange("(i r) n -> r i n", i=2)), (shs[:], sh[l])], writes=["sst"], key="sst")
                  for ch in range(2):
                      load_kv(Kcs[ch], ck, ch, "Kc%d" % ch)
                  for ch in range(2):
                      load_kv(Vcs[ch], cv, ch, "Vc%d" % ch)
                  b = C.bank()
                  for c in range(2):
                      op("pe", lambda: pe.transpose(PS(b, 48, o=64 * c), scs[:, 128 * c:128 * c + 128], ident[0:48, 0:48]),
                         reads=["sst", "ident"], writes=[("ps", b)], signal=(c == 1))
                  for c in range(2):
                      op("dve", lambda: dve.tensor_copy(out=xrs[:, c, :, 0:3], in_=ps[:, b, 64 * c:64 * c + 48].rearrange("p (b t) -> p b t", t=3)),
                         reads=[("ps", b)], writes=["xr"])
                  b = C.bank()
                  for i in range(2):
                      for c in range(2):
                          op("pe", lambda: pe.transpose(PS(b, 120, o=120 * (2 * i + c)), sps[:, i, 128 * c:128 * c + 128], ident[0:120, 0:120]),
                             reads=["sst", "ident"], writes=[("ps", b)], signal=(i == 1 and c == 1))
                  for i in range(2):
                      for c in range(2):
                          op("dve", lambda: dve.tensor_copy(out=zps[:, c, 8 * i:8 * i + 8, 0:15],
                                                            in_=ps[:, b, 120 * (2 * i + c):120 * (2 * i + c) + 120].rearrange("p (b t) -> p b t", t=15)),
                             reads=[("ps", b)], writes=["zp"])
                  b = C.bank()
                  for c in range(2):
                      op("pe", lambda: pe.transpose(PS(b, NS, o=NS * c), shs[:, 128 * c:128 * c + 128], ident[0:NS, 0:NS]),
                         reads=["sst", "ident"], writes=[("ps", b)], signal=(c == 1))
                  op("dve", lambda: dve.tensor_copy(out=h0s[:], in_=ps[:, b, 0:2 * NS].rearrange("p (c n) -> p c n", n=NS)), reads=[("ps", b)], writes=["h0s"])
                  b = C.bank()
                  for h in range(8):
                      for k in range(8):
                          op("pe", lambda: pe.matmul(PS(b, N, 0, 64, NS * h), lhsT=winb[:, k, 64 * h:64 * h + 64], rhs=rhs_x(k), start=(k == 0), stop=(k == 7)),
                             reads=["winb"] + xin, writes=[("ps", b)], signal=(k == 7 and h == 7))
                  op("dve", lambda: dve.tensor_copy(out=qTs[:], in_=ps[0:64, b, 0:8 * NS].rearrange("p (h n) -> p h n", n=NS)), reads=[("ps", b)], writes=["qTs"])
                  b = C.bank()
                  for e in range(4):
                      for k in range(8):
                          op("pe", lambda: pe.matmul(PS(b, N, 0, 64, NS * e), lhsT=winb[:, k, 512 + 64 * e:512 + 64 * e + 64], rhs=rhs_x(k), start=(k == 0), stop=(k == 7)),
                             reads=["winb"] + xin, writes=[("ps", b)], signal=(k == 7 and e == 3))
                  op("dve", lambda: dve.tensor_copy(out=kTs[:], in_=ps[0:64, b, 0:2 * NS].rearrange("p (h n) -> p h n", n=NS)), reads=[("ps", b)], writes=["kTs"])
                  op("dve", lambda: dve.tensor_copy(out=vTs[:], in_=ps[0:64, b, 2 * NS:4 * NS].rearrange("p (h n) -> p h n", n=NS)), reads=[("ps", b)], writes=["vTs"])
                  b = C.bank()
                  for e in range(6):
                      for k in range(8):
                          op("pe", lambda: pe.matmul(PS(b, N, o=NS * e), lhsT=winb[:, k, 768 + 128 * e:768 + 128 * e + 128], rhs=rhs_x(k), start=(k == 0), stop=(k == 7)),
                             reads=["winb"] + xin, writes=[("ps", b)], signal=(k == 7 and e == 5))
                  op("dve", lambda: dve.tensor_copy(out=xrs[:, :, :, 3], in_=ps[:, b, 0:2 * NS].rearrange("p (c n) -> p c n", n=NS)), reads=[("ps", b)], writes=["xr"])
                  op("dve", lambda: dve.tensor_copy(out=grs[:], in_=ps[:, b, 2 * NS:4 * NS].rearrange("p (c n) -> p c n", n=NS)), reads=[("ps", b)], writes=["gr"])
                  op("dve", lambda: dve.tensor_copy(out=zps[:, :, :, 15], in_=ps[:, b, 4 * NS:6 * NS].rearrange("p (c n) -> p c n", n=NS)), reads=[("ps", b)], writes=["zp"])
                  b1, b2 = C.bank(), C.bank()
                  for k in range(8):
                      op("pe", lambda: pe.matmul(PS(b1, 512, 0, NS), lhsT=xb[:, k, c0:c0 + NS], rhs=winb[:, k, 512:1024], start=(k == 0), stop=(k == 7)),
                         reads=["winb"] + xin, writes=[("ps", b1)], signal=(k == 7))
                  for k in range(8):
                      op("pe", lambda: pe.matmul(PS(b2, 256, 0, NS), lhsT=xb[:, k, c0:c0 + NS], rhs=winb[:, k, 1280:1536], start=(k == 0), stop=(k == 7)),
                         reads=["winb"] + xin, writes=[("ps", b2)], signal=(k == 7))
                  op("dve", lambda: dve.tensor_copy(out=sts[:, 0:512], in_=PS(b1, 512, 0, NS)), reads=[("ps", b1)], writes=["sts"])
                  op("dve", lambda: dve.tensor_copy(out=sts[:, 512:768], in_=PS(b2, 256, 0, NS)), reads=[("ps", b2)], writes=["sts"])
                  dma("sp", [(s_k[l, :, 127, :], sts[:, 0:128]), (s_v[l, :, 127, :], sts[:, 128:256]),
                             (s_conv[l, :, 2, :], sts[:, 256:512]), (s_pool[l, :, 14, :], sts[:, 512:768])], reads=["sts"], key="sts")
                  b = C.bank()
                  op("pe", lambda: pe.transpose(PS(b, 64), qTs[:].rearrange("p h n -> p (h n)"), ident[0:64, 0:64]), reads=["qTs", "ident"], writes=[("ps", b)])
                  op("dve", lambda: dve.tensor_copy(out=qs128[:], in_=PS(b, 64)), reads=[("ps", b)], writes=["qs128"])
                  op("dve", lambda: dve.tensor_copy(out=krep[:].rearrange("p (k g n) -> p k g n", k=2, g=4),
                                                    in_=kTs[:].unsqueeze(2).to_broadcast([64, 2, 4, NS])), reads=["kTs"], writes=["krep"])
                  op("dve", lambda: dve.tensor_copy(out=vrep[:].rearrange("p (k g n) -> p k g n", k=2, g=4),
                                                    in_=vTs[:].unsqueeze(2).to_broadcast([64, 2, 4, NS])), reads=["vTs"], writes=["vrep"])
                  b = C.bank()
                  op("pe", lambda: pe.transpose(PS(b, 64), krep[:], ident[0:64, 0:64]), reads=["krep", "ident"], writes=[("ps", b)], signal=False)
                  op("pe", lambda: pe.transpose(PS(b, 64, o=64), vrep[:], ident[0:64, 0:64]), reads=["vrep", "ident"], writes=[("ps", b)])
                  op("dve", lambda: dve.tensor_copy(out=kn128[:], in_=PS(b, 64)), reads=[("ps", b)], writes=["kn128"])
                  op("dve", lambda: dve.tensor_copy(out=vn128[:], in_=PS(b, 64, o=64)), reads=[("ps", b)], writes=["vn128"])
                  for ch in range(8):
                      Kc = Kcs[ch % 2]
                      op("dve", lambda: dve.tensor_tensor(out=tmpc[:], in0=Kc[:], in1=qs128[:].unsqueeze(1).to_broadcast([128, 16, 64]), op=ALU.mult),
                         reads=["Kc%d" % (ch % 2), "qs128"], writes=["tmpc"])
                      op("dve", lambda: dve.tensor_reduce(out=scr[:, 16 * ch:16 * ch + 16], in_=tmpc[:], op=ALU.add, axis=AX.X), reads=["tmpc"], writes=["scr"])
                      if ch + 2 < 8:
                          load_kv(Kcs[ch % 2], ck, ch + 2, "Kc%d" % (ch % 2))
                  op("dve", lambda: dve.tensor_tensor(out=part[:], in0=kn128[:], in1=qs128[:], op=ALU.mult), reads=["kn128", "qs128"], writes=["part"])
                  op("dve", lambda: dve.tensor_reduce(out=sm[:, 0:1], in_=part[:], op=ALU.add, axis=AX.X), reads=["part"], writes=["sm0"])
                  op("dve", lambda: dve.scalar_tensor_tensor(out=scr[:], in0=scr[:], scalar=0.125, in1=sbias[:], op0=ALU.mult, op1=ALU.add),
                     reads=["scr", "smallc"], writes=["scr"])
                  op("act", lambda: act.activation(out=Pm[:], in_=scr[:], func=AF.Exp), reads=["scr"], writes=["Pm"])
                  op("act", lambda: act.activation(out=sm[:, 1:2], in_=sm[:, 0:1], func=AF.Exp, scale=0.125), reads=["sm0"], writes=["sm1"])
                  op("dve", lambda: dve.tensor_reduce(out=sm[:, 2:3], in_=Pm[:], op=ALU.add, axis=AX.X), reads=["Pm"], writes=["sm2"])
                  op("dve", lambda: dve.tensor_tensor(out=sm[:, 2:3], in0=sm[:, 2:3], in1=sm[:, 1:2], op=ALU.add), reads=["sm2", "sm1"], writes=["sm2"])
                  op("dve", lambda: dve.tensor_tensor(out=sm[:, 2:3], in0=sm[:, 2:3], in1=essamp[:, l:l + 1], op=ALU.add), reads=["sm2", "essamp"], writes=["sm2"])
                  op("dve", lambda: dve.reciprocal(out=sm[:, 3:4], in_=sm[:, 2:3]), reads=["sm2"], writes=["sm3"])
                  op("dve", lambda: dve.tensor_scalar(out=acc[:], in0=vn128[:], scalar1=sm[:, 1:2], scalar2=None, op0=ALU.mult), reads=["vn128", "sm1"], writes=["acc"])
                  for ch in range(8):
                      Vc = Vcs[ch % 2]
                      op("dve", lambda: dve.tensor_tensor(out=tmpc[:], in0=Vc[:], in1=Pm[:, 16 * ch:16 * ch + 16].unsqueeze(2).to_broadcast([128, 16, 64]), op=ALU.mult),
                         reads=["Vc%d" % (ch % 2), "Pm"], writes=["tmpc"])
                      op("dve", lambda: dve.tensor_reduce(out=part[:], in_=tmpc[:].rearrange("p s d -> p d s"), op=ALU.add, axis=AX.X), reads=["tmpc"], writes=["part"])
                      op("dve", lambda: dve.tensor_tensor(out=acc[:], in0=acc[:], in1=part[:], op=ALU.add), reads=["acc", "part"], writes=["acc"])
                      if ch + 2 < 8:
                          load_kv(Vcs[ch % 2], cv, ch + 2, "Vc%d" % (ch % 2))
                  op("dve", lambda: dve.tensor_scalar(out=acc[:], in0=acc[:], scalar1=sm[:, 3:4], scalar2=None, op0=ALU.mult), reads=["acc", "sm3"], writes=["acc"])
                  b = C.bank()
                  for e in range(2):
                      op("dve", lambda: dve.tensor_scalar(out=accw[:, 64 * e:64 * e + 64], in0=acc[:], scalar1=emask[:, e:e + 1], scalar2=None, op0=ALU.mult),
                         reads=["acc", "emask"], writes=["accw"])
                  op("pe", lambda: pe.matmul(PS(b, 64), lhsT=accw[:], rhs=selm[:], start=True, stop=True), reads=["accw", "selm"], writes=[("ps", b)])
                  op("dve", lambda: dve.tensor_copy(out=attnTs[:], in_=ps[:, b, 0:64].rearrange("p (j n) -> p j n", n=NS)), reads=[("ps", b)], writes=["attnT"])

                  def h_apply_s(a_, bb, hs):
                      op("dve", lambda: dve.tensor_tensor(out=hs, in0=a_, in1=h0s[:], op=ALU.mult), reads=["wk2", "h0s"], writes=["wk3"])
                      op("dve", lambda: dve.tensor_tensor(out=hs, in0=hs, in1=bb, op=ALU.add), reads=["wk3", "wk1"], writes=["wk3"])
                      bt = C.bank()
                      for c in range(2):
                          op("pe", lambda: pe.transpose(PS(bt, 128, 0, NS, 128 * c), hs[:, c, :], ident[:, :]), reads=["wk3", "ident"], writes=[("ps", bt)], signal=(c == 1))
                      op("dve", lambda: dve.tensor_copy(out=sths[:], in_=PS(bt, 256, 0, NS)), reads=[("ps", bt)], writes=["sths"])
                      dma("sp", (s_h[l], sths[:]), reads=["sths"], key="sths")

                  zwin = []
                  for gi, (p0, p1, c, w) in enumerate([(0, 64, 0, 2), (64, 128, 0, 4), (0, 64, 1, 8), (64, 128, 1, 16)]):
                      op("dve", lambda: dve.tensor_reduce(out=wins[p0:p1, c, :], in_=zps[p0:p1, c, :, 16 - w:16], op=ALU.add, axis=AX.X), reads=["zp"], writes=["zw"])
                      zwin.append((p0, p1, c, wins[p0:p1, c, :], w, "zw"))
                  lru_pool_common(W, l, N, lambda c, tap: xrs[:, c, :, tap], xcs[:], grs[:], zwin,
                                  lambda c, p0, p1: zps[p0:p1, c, :, 15], h_apply_s, mixbs[:])
                  wout_ln(W, l, c0, N, attnTs, mixbs, xkey)
              C.barrier()
              if STOP == 'sample%d' % l: return

          with contextlib.ExitStack() as ph:
              W = {}
              NSLOT = 3
              w1s = [sbt(ph, "w1s%d" % i, [128, 8, 512], BF16) for i in range(NSLOT)]
              w2s = [sbt(ph, "w2s%d" % i, [128, 4, 1024], BF16) for i in range(NSLOT)]
              hT = [sbt(ph, "hT%d" % i, [128, 4, 512], BF16) for i in range(2)]
              rt = [sbt(ph, "rt0", [128, 512])] * 2
              Wl = []
              for q_ in range(2):
                  d_ = {"ybf": sbt(ph, "ybfF%d" % q_, [128, 4, 256], BF16), "ybfk": ["ybfF%d" % q_]}
                  st_ = sbt(ph, "stF%d" % q_, [128, 3, 256])
                  d_["st"] = [st_[:, 0, :], st_[:, 1, :], st_[:, 2, :], st_[:, 1, :]]
                  d_["stk"] = ["stF%d_0" % q_, "stF%d_1" % q_, "stF%d_2" % q_, "stF%d_1" % q_]
                  d_["banks"] = (6, 7) if q_ == 0 else (4, 5)
                  Wl.append(d_)
              ost = [sbt(ph, "ost%d" % i, [128, 1024]) for i in range(2)]

              def load_slice(j):
                  s = j % NSLOT
                  for i_ in range(4):
                      dma("pool", (w1s[s][:, :, 128 * i_:128 * i_ + 128], w_ff1[l, :, 512 * j + 128 * i_:512 * j + 128 * i_ + 128].rearrange("(k p) n -> p k n", p=128)),
                          writes=["w1s%d_%d" % (s, i_)], key="w1s%d_%d" % (s, i_))
                  for h_ in range(2):
                      dma("pool", (w2s[s][:, 2 * h_:2 * h_ + 2, :], w_ff2[l, 512 * j + 256 * h_:512 * j + 256 * h_ + 256, :].rearrange("(i p) n -> p i n", p=128)),
                          writes=["w2s%d_%d" % (s, h_)], key="w2s%d_%d" % (s, h_))

              for j in range(NSLOT):
                  load_slice(j)
              blocks = [(512 * i, 512) for i in range(4)]
              items = [(j, c0, N) for j in range(5) for (c0, N) in blocks]
              items += [(j, c0, N) for (c0, N) in blocks for j in (5, 6, 7)]
              last_of_slice = {}
              for ii, (j_, _c, _n) in enumerate(items):
                  last_of_slice[j_] = ii

              hTs = [sbt(ph, "hTs%d" % i, [128, 4, NS], BF16) for i in range(2)]
              rts = sbt(ph, "rts", [128, NS])

              def ff1(idx):
                  j, c0, N = items[idx]
                  s = j % NSLOT
                  hb = idx % 2
                  xk = "xF%d" % c0
                  mrg = (c0 == 1536)
                  for i in range(4):
                      b = C.bank()
                      b2 = C.bank() if mrg else None
                      for k in range(8):
                          op("pe", lambda: pe.matmul(PS(b, N), lhsT=w1s[s][:, k, 128 * i:128 * i + 128], rhs=xb[:, k, c0:c0 + N], start=(k == 0), stop=(k == 7)),
                             reads=["w1s%d_%d" % (s, i), xk + "_0b", xk + "_1b"], writes=[("ps", b)], signal=(k == 7))
                          if mrg:
                              op("pe", lambda: pe.matmul(PS(b2, NS), lhsT=w1s[s][:, k, 128 * i:128 * i + 128], rhs=xb[:, k, T:T + NS], start=(k == 0), stop=(k == 7)),
                                 reads=["w1s%d_%d" % (s, i), "xF2048_0b"], writes=[("ps", b2)], signal=(k == 7))
                      op("act", lambda: act.activation(out=rt[0][:, 0:N], in_=PS(b, N), func=AF.Relu), reads=[("ps", b)], writes=["rt0"])
                      op("act", lambda: act.activation(out=hT[hb][:, i, 0:N], in_=rt[0][:, 0:N], func=AF.Square), reads=["rt0"], writes=["hT%d" % hb])
                      if mrg:
                          op("act", lambda: act.activation(out=rts[:, :], in_=PS(b2, NS), func=AF.Relu), reads=[("ps", b2)], writes=["rts"])
                          op("act", lambda: act.activation(out=hTs[hb][:, i, :], in_=rts[:, :], func=AF.Square), reads=["rts"], writes=["hTs%d" % hb])
                      yield

              def ff2(idx):
                  j, c0, N = items[idx]
                  s = j % NSLOT
                  hb = idx % 2
                  xk = "xF%d" % c0
                  mrg = (c0 == 1536)
                  for m in range(8):
                      b = C.bank()
                      b2 = C.bank() if mrg else None
                      for i in range(4):
                          op("pe", lambda: pe.matmul(PS(b, N), lhsT=w2s[s][:, i, 128 * m:128 * m + 128], rhs=hT[hb][:, i, 0:N], start=(i == 0), stop=(i == 3)),
                             reads=["w2s%d_%d" % (s, i // 2), "hT%d" % hb], writes=[("ps", b)], signal=(i == 3))
                          if mrg:
                              op("pe", lambda: pe.matmul(PS(b2, NS), lhsT=w2s[s][:, i, 128 * m:128 * m + 128], rhs=hTs[hb][:, i, :], start=(i == 0), stop=(i == 3)),
                                 reads=["w2s%d_%d" % (s, i // 2), "hTs%d" % hb], writes=[("ps", b2)], signal=(i == 3))
                      for (bq, cq, nq, kq) in ([(b, c0, N, [xk + "_0", xk + "_1"])] + ([(b2, T, NS, ["xF2048_0"])] if mrg else [])):
                          xs_ = xf[:, m, cq:cq + nq]
                          if j == 0:
                              op("dve", lambda: dve.scalar_tensor_tensor(out=xs_, in0=xs_, scalar=ALPHA, in1=PS(bq, nq), op0=ALU.mult, op1=ALU.add),
                                 reads=[("ps", bq)] + kq, writes=kq)
                          else:
                              op("dve", lambda: dve.tensor_tensor(out=xs_, in0=xs_, in1=PS(bq, nq), op=ALU.add), reads=[("ps", bq)] + kq, writes=kq)
                      yield

              otile = [0]

              def ln2_sub(cc, nn, xk, Wq):
                  yield from layer_norm_g(Wq, l, 2, cc, nn, xk, banks=Wq["banks"], slack=1)
                  if l == L - 1:
                      ntile = 2 if nn == 256 else 1
                      for ti in range(ntile):
                          n = 128 if nn == 256 else NS
                          t0 = cc + 128 * ti
                          o = ost[otile[0] % 2]
                          ok = "ost%d" % (otile[0] % 2)
                          otile[0] += 1
                          for hb in range(2):
                              b = C.bank()
                              for mm in range(4):
                                  m = 4 * hb + mm
                                  op("pe", lambda: pe.transpose(PS(b, 128, 0, n, 128 * mm), xf[:, m, t0:t0 + n], ident[:, :]),
                                     reads=[xk, "ident"], writes=[("ps", b)], signal=(mm == 3))
                              if hb == 0:
                                  op("act", lambda: act.activation(out=o[0:n, 0:512], in_=PS(b, 512, 0, n), func=AF.Copy), reads=[("ps", b)], writes=[ok])
                              else:
                                  op("dve", lambda: dve.tensor_copy(out=o[0:n, 512:1024], in_=PS(b, 512, 0, n)), reads=[("ps", b)], writes=[ok])
                          dst = yp[t0:t0 + 128, :] if nn == 256 else ys[:, :]
                          dma("sp", (dst, o[0:n, :]), reads=[ok], key=ok)
                          yield

              d_ = {"ybf": w1s[2][:].rearrange("p k n -> p (k n)")[:, 0:1024].rearrange("p (m n) -> p m n", n=256),
                    "ybfk": ["w1s2_0", "w1s2_1", "w1s2_2", "w1s2_3"]}
              st_ = w2s[2][:].rearrange("p k n -> p (k n)")[:, 0:1536].bitcast(F32).rearrange("p (m n) -> p m n", n=256)
              d_["st"] = [st_[:, 0, :], st_[:, 1, :], st_[:, 2, :], st_[:, 1, :]]
              d_["stk"] = ["w2s2_0", "w2s2_0", "w2s2_0", "w2s2_0"]
              d_["banks"] = (2, 3)
              Wl.append(d_)

              lnq = []
              nln = [0]

              def delayed(g_, k_):
                  for _ in range(k_):
                      yield
                  yield from g_

              def ffn_main():
                  yield from ff1(0)
                  for idx in range(len(items)):
                      if idx + 1 < len(items):
                          yield from ff1(idx + 1)
                      yield from ff2(idx)
                      j, c0, N = items[idx]
                      if idx + 3 < len(items) and items[idx + 3][0] == 7 and items[idx + 2][0] == 6 and items[idx + 1][0] == 5 and items[idx][0] == 4:
                          C.bank_pool = list(range(4))
                      if j == 7:
                          lnq.append((c0, 256, "xF%d_0" % c0))
                          lnq.append((c0 + 256, 256, "xF%d_1" % c0))
                          if c0 == 1536:
                              lnq.append((T, NS, "xF2048_0"))
                      if last_of_slice[j] == idx and j + NSLOT < 8:
                          load_slice(j + NSLOT)
                      if j == 3 and last_of_slice[j] == idx and l + 1 < L:
                          load_winb(l + 1)

              C.bank_pool = list(range(8))
              gmain = ffn_main()
              alive = True
              slots = [None, None, None]
              while alive or lnq or any(g_ is not None for g_ in slots):
                  if alive:
                      try:
                          next(gmain)
                      except StopIteration:
                          alive = False
                          C.bank_pool = [0, 1]
                  nslots = 2 if alive else 3
                  for q_ in range(nslots):
                      if slots[q_] is None and lnq:
                          cc_, nn_, xk_ = lnq.pop(0)
                          slots[q_] = delayed(ln2_sub(cc_, nn_, xk_, Wl[q_]), 3 if alive else 0)
                      if slots[q_] is not None:
                          try:
                              next(slots[q_])
                          except StopIteration:
                              slots[q_] = None
              C.bank_pool = list(range(8))
          C.barrier()
          if STOP == 'ln2%d' % l: return
    run_layers()
    C.finish()
    top.close()
    return nc


def _consts():
    slopes = np.exp2(-8.0 * (np.arange(8, dtype=np.float32) + 1.0) / 8).astype(np.float32)
    s = np.arange(128)[:, None]
    q = np.arange(128)[None, :]
    bcur = np.full((2, 128, 4, 128), NEG, np.float32)
    bprev = np.full((2, 128, 4, 128), NEG, np.float32)
    for kv in range(2):
        for g in range(4):
            sl = slopes[4 * kv + g]
            d = (q - s).astype(np.float32)
            bcur[kv, :, g, :] = np.where(s <= q, -sl * d, NEG)
            d2 = (q - s + 128).astype(np.float32)
            bprev[kv, :, g, :] = np.where(s >= q, -sl * d2, NEG)
    sbias = np.zeros((128, 128), np.float32)
    for h in range(8):
        sbias[16 * h:16 * h + 16, :] = -slopes[h] * (128 - np.arange(128, dtype=np.float32))[None, :]
    return bcur.reshape(2, 128, 512), bprev.reshape(2, 128, 512), sbias


_NC = None


def kernel(x_prompt, x_sample, cache_k, cache_v, state_h, state_conv, state_pool,
           w_in, attn_sinks, conv_w, conv_b, gate_a_w, gate_a_b, gate_x_w, gate_x_b, lru_lambda,
           pool_w, pool_scale, w_out, ln1_g, ln1_b, w_ff1, w_ff2, ln2_g, ln2_b):
    global _NC
    f = lambda a: np.ascontiguousarray(np.asarray(a, dtype=np.float32))
    bcur, bprev, sbias = _consts()
    ident = np.eye(128, dtype=np.float32)
    shared = dict(w_in=f(w_in), sinks=f(attn_sinks), conv_w=f(conv_w), conv_b=f(conv_b), ga_w=f(gate_a_w), ga_b=f(gate_a_b),
                  gx_w=f(gate_x_w), gx_b=f(gate_x_b), lam=f(lru_lambda), pool_w=f(pool_w), pool_s=f(pool_scale),
                  w_out=f(w_out), ln1_g=f(ln1_g), ln1_b=f(ln1_b), w_ff1=f(w_ff1), w_ff2=f(w_ff2), ln2_g=f(ln2_g), ln2_b=f(ln2_b),
                  ident=ident, bcur=bcur, bprev=bprev, sbias=sbias)
    emask = np.zeros((128, 2), np.float32); sel = np.zeros((128, 64), np.float32)
    for p in range(128):
        h, b_ = p // 16, p % 16
        emask[p, h % 2] = 1.0
        sel[p, (h // 2) * 16 + b_] = 1.0
    xpr = f(x_prompt); xsa = f(x_sample)
    ckk = f(cache_k).reshape(L, 128, 128, 128); cvv = f(cache_v).reshape(L, 128, 128, 128)
    shh = f(state_h); scc = f(state_conv); spp = f(state_pool)
    in_maps = []
    for c in range(NCORES):
        seq, half = c // 2, c % 2
        b0 = NS * c
        pcorr = np.ones((128, 2, 16), np.float32)
        if half == 0:
            for gi, w in enumerate((2, 4, 8, 16)):
                cc, p0 = gi // 2, 64 * (gi % 2)
                t = np.arange(16, dtype=np.float32)
                pcorr[p0:p0 + 64, cc, :] = (w / np.minimum(t + 1.0, float(w)))[None, :]
        m = dict(shared)
        m.update(xp=np.ascontiguousarray(xpr[seq, T * half:T * half + T]), xs=np.ascontiguousarray(xsa[b0:b0 + NS, 0]),
                 ck=np.ascontiguousarray(ckk[:, b0:b0 + NS]), cv=np.ascontiguousarray(cvv[:, b0:b0 + NS]),
                 sh=np.ascontiguousarray(shh[:, b0:b0 + NS]),
                 sc=np.ascontiguousarray(scc[:, b0:b0 + NS].reshape(L, NS * 3, 256)),
                 spool=np.ascontiguousarray(spp[:, b0:b0 + NS].reshape(L, NS * 15, 256)),
                 nm0=np.full((128, 1), NEG if half == 0 else 0.0, np.float32),
                 flag=np.full((128, 1), 0.0 if half == 0 else 1.0, np.float32),
                 pcorr=pcorr, st_in=np.zeros((L, 147, 256), np.float32), emask=emask, sel=sel)
        in_maps.append(m)
    if _NC is None:
        _NC = build()
    res = run_bass_kernel_spmd(_NC, in_maps, core_ids=list(range(NCORES))).results
    y_prompt = np.zeros((4, 4096, 1024), np.float32)
    for c in range(NCORES):
        y_prompt[c // 2, T * (c % 2):T * (c % 2) + T] = res[c]["yp"]
    y_sample = np.concatenate([res[c]["ys"] for c in range(NCORES)], 0).reshape(128, 1, 1024)
    sto = np.stack([res[2 * s + 1]["st_out"] for s in range(4)], 1)
    p_k = np.ascontiguousarray(sto[:, :, 0:128, 0:128]).reshape(L, 4, 128, 2, 64)
    p_v = np.ascontiguousarray(sto[:, :, 0:128, 128:256]).reshape(L, 4, 128, 2, 64)
    p_conv = np.ascontiguousarray(sto[:, :, 128:131, :])
    p_pool = np.ascontiguousarray(sto[:, :, 131:146, :])
    p_h = np.ascontiguousarray(sto[:, :, 146, :])
    cat = lambda k: np.concatenate([res[c][k] for c in range(NCORES)], 1)
    s_k = cat("s_k").reshape(L, 128, 128, 2, 64); s_v = cat("s_v").reshape(L, 128, 128, 2, 64)
    return (y_prompt, y_sample, p_k, p_v, p_h, p_conv, p_pool, s_k, s_v, cat("s_h"), cat("s_conv"), cat("s_pool"))
```

```python
import contextlib
import numpy as np
import concourse.bass as bass
import concourse.mybir as mybir
from concourse.bass_utils import run_bass_kernel_spmd

F32 = mybir.dt.float32
BF16 = mybir.dt.bfloat16
AF = mybir.ActivationFunctionType
ALU = mybir.AluOpType
AX = mybir.AxisListType

NCORES = 8
T = 2048
NS = 16
TT = T + NS
L = 2
ALPHA = float((2.0 * L) ** 0.25)
EPS = 1e-5
NEG = -1e30
NPV = 50
USE_CC = False
STOP = None
NOSELF = False
XMODE = 2


class _Stop(Exception):
    pass


class Ctx:
    def __init__(self, nc):
        self.nc = nc
        self.engs = {"pe": nc.tensor, "act": nc.scalar, "dve": nc.vector, "pool": nc.gpsimd, "sp": nc.sync}
        self.esem = {e: nc.alloc_semaphore(name="sem_" + e) for e in ["pe", "act", "dve", "pool"]}
        self.ecnt = {e: 0 for e in self.esem}
        self.seen = {e: {} for e in self.engs}
        self.res = {}
        self.dsem = {}
        self.nbank = 0
        self.bank_pool = list(range(8))
        self.ccs = []

    def bank(self):
        b = self.bank_pool[self.nbank % len(self.bank_pool)]
        self.nbank += 1
        return b

    def _wait(self, eng, tok):
        sem, val, src = tok
        if src == "pe" and eng == "pe":
            return
        if NOSELF and src == eng:
            return
        d = self.seen[eng]
        if d.get(id(sem), 0) >= val:
            return
        self.engs[eng].wait_ge(sem, val)
        d[id(sem)] = val

    def _deps(self, eng, reads, writes):
        for k in reads:
            st = self.res.get(k)
            if st and st["w"]:
                self._wait(eng, st["w"])
        for k in writes:
            st = self.res.get(k)
            if st:
                if st["w"]:
                    self._wait(eng, st["w"])
                for r in st["r"]:
                    self._wait(eng, r)

    def _reg(self, tok, reads, writes):
        for k in reads:
            self.res.setdefault(k, {"w": None, "r": []})["r"].append(tok)
        for k in writes:
            self.res[k] = {"w": tok, "r": []}

    def op(self, eng, fn, reads=(), writes=(), signal=True):
        psr = [k for k in reads if isinstance(k, tuple) and k[0] == "ps"]
        if psr:
            reads = [k for k in reads if k not in psr]
            writes = list(writes) + psr
        self._deps(eng, reads, writes)
        ins = fn()
        if signal:
            self.ecnt[eng] += 1
            ins.then_inc(self.esem[eng], 1)
            val = self.ecnt[eng]
        else:
            val = self.ecnt[eng] + 1
        self._reg((self.esem[eng], val, eng), reads, writes)

    def dma(self, q, pairs, reads=(), writes=(), key=None, **kw):
        if not isinstance(pairs, list):
            pairs = [pairs]
        self._deps(q, reads, writes)
        if key not in self.dsem:
            self.dsem[key] = [self.nc.alloc_semaphore(name="d_" + str(key)), 0]
        ent = self.dsem[key]
        if ent[1] > 0:
            self._wait(q, (ent[0], ent[1], "dma"))
        for out, in_ in pairs:
            self.engs[q].dma_start(out=out, in_=in_, **kw).then_inc(ent[0], 16)
            ent[1] += 16
        self._reg((ent[0], ent[1], "dma"), reads, writes)

    def collective(self, in_ap, out_ap, reads=(), writes=()):
        self._deps("pool", reads, writes)
        sem = self.nc.alloc_semaphore(name="cc%d" % len(self.ccs))
        self.ccs.append(sem)
        self.nc.gpsimd.collective_compute("AllGather", ALU.bypass, replica_groups=[[0, 1], [2, 3], [4, 5], [6, 7]],
                                          ins=[in_ap], outs=[out_ap]).then_inc(sem, 1)
        self._reg((sem, 1, "cc"), reads, writes)

    def barrier(self):
        toks = [(self.esem[e], self.ecnt[e], e) for e in self.esem if self.ecnt[e] > 0]
        toks += [(s, c, "dma") for s, c in self.dsem.values() if c > 0]
        for e in self.engs:
            for t in toks:
                if t[2] == e:
                    continue
                self._wait(e, t)
        self.res = {}

    def finish(self):
        for s, c in self.dsem.values():
            if c > 0:
                self._wait("sp", (s, c, "dma"))
        for e in self.esem:
            if self.ecnt[e] > 0:
                self._wait("sp", (self.esem[e], self.ecnt[e], e))


def build():
    nc = bass.Bass("TRN2", target_bir_lowering=False)
    C = Ctx(nc)

    def din(name, shape):
        return nc.dram_tensor(name, shape, F32, kind="ExternalInput").ap()

    def dout(name, shape):
        return nc.dram_tensor(name, shape, F32, kind="ExternalOutput").ap()

    xp = din("xp", [T, 1024]); xs = din("xs", [NS, 1024])
    ck = din("ck", [L, NS, 128, 128]); cv = din("cv", [L, NS, 128, 128])
    sh = din("sh", [L, NS, 256]); sc = din("sc", [L, NS * 3, 256]); spool = din("spool", [L, NS * 15, 256])
    w_in = din("w_in", [L, 1024, 1536]); sinks = din("sinks", [L, 8])
    conv_w = din("conv_w", [L, 4, 256]); conv_b = din("conv_b", [L, 256])
    ga_w = din("ga_w", [L, 4, 64, 64]); ga_b = din("ga_b", [L, 256])
    gx_w = din("gx_w", [L, 4, 64, 64]); gx_b = din("gx_b", [L, 256])
    lam = din("lam", [L, 256]); pool_w = din("pool_w", [L, 4, 64, 64]); pool_s = din("pool_s", [L, 256])
    w_out = din("w_out", [L, 1024, 1024]); ln1_g = din("ln1_g", [L, 1024]); ln1_b = din("ln1_b", [L, 1024])
    w_ff1 = din("w_ff1", [L, 1024, 4096]); w_ff2 = din("w_ff2", [L, 4096, 1024])
    ln2_g = din("ln2_g", [L, 1024]); ln2_b = din("ln2_b", [L, 1024])
    ident_d = din("ident", [128, 128]); bcur_d = din("bcur", [2, 128, 512]); bprev_d = din("bprev", [2, 128, 512])
    sbias_d = din("sbias", [128, 128]); nm0_d = din("nm0", [128, 1]); flag_d = din("flag", [128, 1])
    pcorr_d = din("pcorr", [128, 2, 16]); st_in = din("st_in", [L, 147, 256])
    emask_d = din("emask", [128, 2]); sel_d = din("sel", [128, 64])

    cc1_in = nc.dram_tensor("cc1_in", [L, 146, 256], F32, kind="Internal").ap()
    cc1_out = nc.dram_tensor("cc1_out", [L, 292, 256], F32, kind="Internal").ap()
    cc2_in = nc.dram_tensor("cc2_in", [L, 1, 256], F32, kind="Internal").ap()
    cc2_out = nc.dram_tensor("cc2_out", [L, 2, 256], F32, kind="Internal").ap()
    yp = dout("yp", [T, 1024]); ys = dout("ys", [NS, 1024]); st_out = dout("st_out", [L, 147, 256])
    s_k = dout("s_k", [L, NS, 128, 128]); s_v = dout("s_v", [L, NS, 128, 128])
    s_h = dout("s_h", [L, NS, 256]); s_conv = dout("s_conv", [L, NS, 3, 256]); s_pool = dout("s_pool", [L, NS, 15, 256])

    op = C.op
    dma = C.dma
    pe, act, dve, pool = nc.tensor, nc.scalar, nc.vector, nc.gpsimd

    top = contextlib.ExitStack()

    uid = [0]

    def sbt(stack, name, shape, dt=F32):
        uid[0] += 1
        return stack.enter_context(nc.sbuf_tensor("sb%d_%s" % (uid[0], name), shape, dt))

    ps = top.enter_context(nc.psum_tensor("psum_all", [128, 8, 512], F32))
    xf = sbt(top, "xf", [128, 8, TT]); xb = sbt(top, "xb", [128, 8, TT], BF16)
    ident = sbt(top, "ident", [128, 128]); ones = sbt(top, "ones", [128, 128], BF16)
    bcur = sbt(top, "bcur", [128, 2, 512], BF16); bprev = sbt(top, "bprev", [128, 2, 512], BF16)
    sbias = sbt(top, "sbias", [128, 128]); nm0 = sbt(top, "nm0", [128, 1]); flag = sbt(top, "flag", [128, 1])
    pcorr = sbt(top, "pcorr", [128, 2, 16]); pv = sbt(top, "pv", [128, 2 * NPV]); der = sbt(top, "der", [128, L, 8])
    es64 = sbt(top, "es64", [128, 16]); essamp = sbt(top, "essamp", [128, 2])
    wbd = sbt(top, "wbd", [128, L, 6, 128], BF16)
    cneg = sbt(top, "cneg", [128, 256], BF16)
    emask = sbt(top, "emask", [128, 2]); selm = sbt(top, "selm", [128, 64], BF16)
    winb = sbt(top, "winb", [128, 8, 1536], BF16)

    def load_winb(l):
        for kk in range(4):
            dma("pool", (winb[:, 2 * kk:2 * kk + 2, :], w_in[l, 256 * kk:256 * kk + 256, :].rearrange("(k p) n -> p k n", p=128)),
                writes=["winb"], key="winb%d" % kk)

    def PS(b, n=512, p0=0, p1=128, o=0):
        return ps[p0:p1, b, o:o + n]

    dma("sp", (ident[:], ident_d[:, :]), writes=["ident"], key="c0")
    dma("pool", (bcur[:], bcur_d.rearrange("k s n -> s k n")), writes=["bcur"], key="c1")
    dma("pool", (bprev[:], bprev_d.rearrange("k s n -> s k n")), writes=["bprev"], key="c2")
    dma("sp", [(sbias[:], sbias_d[:, :]), (nm0[:], nm0_d[:, :]), (flag[:], flag_d[:, :]), (pcorr[:], pcorr_d[:, :, :])],
        writes=["smallc"], key="c3")
    dma("sp", (emask[:], emask_d[:, :]), writes=["emask"], key="c8")
    dma("pool", (selm[:], sel_d[:, :]), writes=["selm"], key="c9")
    op("dve", lambda: dve.memset(ones[:], 1.0), writes=["ones"])
    op("dve", lambda: dve.memset(cneg[:], -0.5), writes=["cneg"])
    op("dve", lambda: dve.memset(wbd[:], 0.0), writes=["wbd"])
    if STOP == 'i1':
        C.finish(); top.close(); return nc
    with contextlib.ExitStack() as ph:
        prow = sbt(ph, "prow", [2 * NPV, 128])
        plist = [(conv_w, 0, 8), (conv_b, 8, 2), (ga_b, 10, 2), (gx_b, 12, 2), (lam, 14, 2), (pool_s, 16, 2),
                 (ln1_g, 18, 8), (ln1_b, 26, 8), (ln2_g, 34, 8), (ln2_b, 42, 8)]
        pairs = []
        for l in range(L):
            for (t, off, n) in plist:
                if t is conv_w:
                    src = t[l].rearrange("t (c p) -> (t c) p", p=128)
                else:
                    src = t[l].rearrange("(c p) -> c p", p=128)
                pairs.append((prow[NPV * l + off:NPV * l + off + n, :], src))
        dma("sp", pairs, writes=["prow"], key="c4")
        b0 = C.bank()
        op("pe", lambda: pe.transpose(PS(b0, 2 * NPV), prow[:], ident[0:2 * NPV, 0:2 * NPV]),
           reads=["prow", "ident"], writes=[("ps", b0)])
        op("dve", lambda: dve.tensor_copy(out=pv[:], in_=PS(b0, 2 * NPV)), reads=[("ps", b0)], writes=["pv"])
        if STOP == 'i2':
            C.finish(); ph.close(); top.close(); return nc
        tl = sbt(ph, "tl", [128, 2])
        for l in range(L):
            pb = NPV * l
            op("act", lambda: act.activation(out=tl[:], in_=pv[:, pb + 14:pb + 16], func=AF.Exp, scale=-1.0),
               reads=["pv"], writes=["tl"])
            op("dve", lambda: dve.tensor_scalar_add(tl[:], tl[:], 1.0), reads=["tl"], writes=["tl"])
            op("act", lambda: act.activation(out=tl[:], in_=tl[:], func=AF.Ln), reads=["tl"], writes=["tl"])
            op("dve", lambda: dve.tensor_scalar_mul(der[:, l, 0:2], tl[:], -4.0), reads=["tl"], writes=["der"])
            op("dve", lambda: dve.tensor_scalar_mul(der[:, l, 2:4], tl[:], -8.0), reads=["tl"], writes=["der"])
            op("dve", lambda: dve.tensor_scalar_mul(der[:, l, 4:6], pv[:, pb + 10:pb + 12], 0.5), reads=["pv"], writes=["der"])
            op("dve", lambda: dve.tensor_scalar_mul(der[:, l, 6:8], pv[:, pb + 12:pb + 14], 0.5), reads=["pv"], writes=["der"])
        if STOP == 'i3':
            C.finish(); ph.close(); top.close(); return nc
        dma("sp", (es64[:], sinks.rearrange("l h -> (l h)").partition_broadcast(128)), writes=["es64"], key="c5")
        if STOP == 'i3a':
            C.finish(); ph.close(); top.close(); return nc
        op("act", lambda: act.activation(out=es64[:], in_=es64[:], func=AF.Exp), reads=["es64"], writes=["es64"])
        pairs = []
        for l in range(L):
            for h in range(8):
                pairs.append((essamp[16 * h:16 * h + 16, l:l + 1], sinks[l, h:h + 1].partition_broadcast(16)))
        dma("sp", pairs, writes=["essamp"], key="c6")
        op("act", lambda: act.activation(out=essamp[:], in_=essamp[:], func=AF.Exp), reads=["essamp"], writes=["essamp"])
        if STOP == 'i4':
            C.finish(); ph.close(); top.close(); return nc
        pairs = []
        for l in range(L):
            for n in range(4):
                c, e = n // 2, n % 2
                for si, wt in ((0, ga_w), (2, gx_w), (4, pool_w)):
                    pairs.append((wbd[64 * e:64 * e + 64, l, si + c, 64 * e:64 * e + 64], wt[l, n]))
        dma("pool", pairs, writes=["wbd"], key="c7")
        if STOP == 'i5':
            C.finish(); ph.close(); top.close(); return nc

        if STOP == 'i6':
            C.finish(); ph.close(); top.close(); return nc
        if STOP == 'i6b':
            C.barrier(); C.finish(); ph.close(); top.close(); return nc

        load_winb(0)
        xst = [sbt(ph, "xst%d" % i, [128, 1024]) for i in range(2)]
        for ti in range(16 if STOP == 'initA' else (1 if STOP == 'initB' else 17)):
            st = xst[ti % 2]
            sk = "xst%d" % (ti % 2)
            n = 128 if ti < 16 else NS
            src = xp[128 * ti:128 * ti + 128, :] if ti < 16 else xs[:, :]
            dma("sp", (st[0:n, :], src), writes=[sk], key=sk)
            for hb in range(0 if XMODE == 0 else 2):
                b = C.bank()
                for mm in range(4):
                    m = 4 * hb + mm
                    op("pe", lambda: pe.transpose(PS(b, n, o=n * mm), st[0:n, 128 * m:128 * m + 128], ident[0:n, 0:n]),
                       reads=[sk, "ident"], writes=[("ps", b)], signal=(mm == 3))
                src_ps = ps[:, b, 0:4 * n].rearrange("p (m n) -> p m n", n=n)
                op("dve", lambda: dve.tensor_copy(out=xf[:, 4 * hb:4 * hb + 4, 128 * ti:128 * ti + n], in_=src_ps),
                   reads=[("ps", b)], writes=[("xf", ti)])
                if XMODE >= 2:
                    op("act", lambda: act.activation(out=xb[:, 4 * hb:4 * hb + 4, 128 * ti:128 * ti + n], in_=src_ps, func=AF.Copy),
                       reads=[("ps", b)], writes=[("xb", ti)])
    C.barrier()
    for l in range(L):
        dma("sp", [(s_k[l, :, 0:127, :], ck[l, :, 1:128, :]), (s_v[l, :, 0:127, :], cv[l, :, 1:128, :]),
                   (s_conv[l, :, 0:2, :], sc[l].rearrange("(b t) n -> b t n", t=3)[:, 1:3, :]),
                   (s_pool[l, :, 0:14, :], spool[l].rearrange("(b t) n -> b t n", t=15)[:, 1:15, :])],
            key="dd%d" % l)
    if STOP in ('init', 'initA', 'initB'):
        C.finish(); top.close(); return nc

    def layer_norm(W, l, which, c0, N, xkey):
        for _ in layer_norm_g(W, l, which, c0, N, xkey):
            pass

    def layer_norm_g(W, l, which, c0, N, xkey, banks=None, slack=0):
        gcol = NPV * l + (18 if which == 1 else 34)
        bcol = gcol + 8
        ybf, ybk = W["ybf"], W["ybfk"]
        k0_, k1_, k2_, k3_ = W["stk"]
        cap = ybf.shape[1]
        b1, b2 = banks if banks is not None else (C.bank(), C.bank())
        for (bb_, fn_) in ((b1, AF.Copy), (b2, AF.Square)):
            for h0 in range(0, 8, cap):
                op("act", lambda: act.activation(out=ybf[:, 0:cap, 0:N], in_=xf[:, h0:h0 + cap, c0:c0 + N], func=fn_), reads=[xkey], writes=ybk)
                yield
                for m in range(h0, h0 + cap):
                    op("pe", lambda: pe.matmul(PS(bb_, N), lhsT=ones[:], rhs=ybf[:, m - h0, 0:N], start=(m == 0), stop=(m == 7)),
                       reads=ybk + ["ones"], writes=[("ps", bb_)], signal=(m == h0 + cap - 1))
            yield
        for _ in range(slack):
            yield
        mean, msq, rstd, nmr = [a_[:, 0:N] for a_ in W["st"]]
        op("dve", lambda: dve.tensor_scalar_mul(mean, PS(b1, N), 1.0 / 1024), reads=[("ps", b1)], writes=[k0_])
        op("dve", lambda: dve.tensor_tensor(out=msq, in0=mean, in1=mean, op=ALU.mult), reads=[k0_], writes=[k1_])
        op("dve", lambda: dve.scalar_tensor_tensor(out=msq, in0=PS(b2, N), scalar=1.0 / 1024, in1=msq, op0=ALU.mult, op1=ALU.subtract),
           reads=[("ps", b2), k1_], writes=[k1_])
        op("dve", lambda: dve.tensor_scalar_add(msq, msq, EPS), reads=[k1_], writes=[k1_])
        op("act", lambda: act.activation(out=rstd, in_=msq, func=AF.Ln), reads=[k1_], writes=[k2_])
        op("act", lambda: act.activation(out=rstd, in_=rstd, func=AF.Exp, scale=-0.5), reads=[k2_], writes=[k2_])
        yield
        for _ in range(slack):
            yield
        op("dve", lambda: dve.scalar_tensor_tensor(out=nmr, in0=mean, scalar=-1.0, in1=rstd, op0=ALU.mult, op1=ALU.mult),
           reads=[k0_, k2_], writes=[k3_])
        xblk = xf[:, :, c0:c0 + N]
        op("dve", lambda: dve.tensor_tensor(out=xblk, in0=xblk, in1=rstd.unsqueeze(1).to_broadcast([128, 8, N]), op=ALU.mult), reads=[xkey, k2_], writes=[xkey])
        yield
        op("dve", lambda: dve.tensor_tensor(out=xblk, in0=xblk, in1=nmr.unsqueeze(1).to_broadcast([128, 8, N]), op=ALU.add), reads=[xkey, k3_], writes=[xkey])
        yield
        for m in range(8):
            xs_ = xf[:, m, c0:c0 + N]
            op("dve", lambda: dve.tensor_scalar(out=xs_, in0=xs_, scalar1=pv[:, gcol + m:gcol + m + 1], scalar2=pv[:, bcol + m:bcol + m + 1],
                                                op0=ALU.mult, op1=ALU.add), reads=[xkey, "pv"], writes=[xkey])
            if m % 4 == 3:
                yield
        op("act", lambda: act.activation(out=xb[:, :, c0:c0 + N], in_=xblk, func=AF.Copy), reads=[xkey], writes=[xkey + "b"])

    def pool_part(W, l, N, zwin, zcur, mixb, first_corr=None):
        pb = NPV * l
        diffb = W["diffb"]
        for (p0, p1, c, win, w, wkey) in zwin:
            if first_corr is not None:
                first_corr(p0, p1, c, win, wkey)
            op("dve", lambda: dve.scalar_tensor_tensor(out=diffb[p0:p1, c, 0:N], in0=win, scalar=1.0 / w, in1=zcur(c, p0, p1),
                                                       op0=ALU.mult, op1=ALU.subtract), reads=[wkey, "zp"], writes=["diffb"])
        bp = C.bank()
        for c in range(2):
            op("pe", lambda: pe.matmul(PS(bp, N, o=256 * c), lhsT=wbd[:, l, 4 + c, :], rhs=diffb[:, c, 0:N], start=True, stop=True),
               reads=["diffb", "wbd"], writes=[("ps", bp)])
        for c in range(2):
            op("act", lambda: act.activation(out=mixb[:, 2 + c, :], in_=PS(bp, N, o=256 * c), func=AF.Identity, scale=pv[:, pb + 16 + c:pb + 17 + c]),
               reads=[("ps", bp), "pv"], writes=["mixb"])

    def lru_part_g(W, l, N, xc_taps, xc, gr, h_apply, finish, sfx=""):
        pb = NPV * l
        wk = W["wk"]
        xcb = W["xcb"]
        for c in range(2):
            op("act", lambda: act.activation(out=xc[:, c, :], in_=xc_taps(c, 0), func=AF.Identity, scale=pv[:, pb + c:pb + c + 1],
                                             bias=pv[:, pb + 8 + c:pb + 9 + c]),
               reads=["xr" + sfx, "pv"], writes=["xc" + sfx])
            for tap in range(1, 4):
                op("dve", lambda: dve.scalar_tensor_tensor(out=xc[:, c, :], in0=xc_taps(c, tap), scalar=pv[:, pb + 2 * tap + c:pb + 2 * tap + c + 1],
                                                           in1=xc[:, c, :], op0=ALU.mult, op1=ALU.add),
                   reads=["xr" + sfx, "pv", "xc" + sfx], writes=["xc" + sfx])
        yield
        op("act", lambda: act.activation(out=xcb[:, :, 0:N], in_=xc, func=AF.Copy), reads=["xc" + sfx], writes=["xcb" + sfx])
        bg, bh = C.bank(), C.bank()
        for c in range(2):
            op("pe", lambda: pe.matmul(PS(bg, N, o=256 * c), lhsT=wbd[:, l, 0 + c, :], rhs=xcb[:, c, 0:N], start=True, stop=True),
               reads=["xcb" + sfx, "wbd"], writes=[("ps", bg)])
            op("pe", lambda: pe.matmul(PS(bh, N, o=256 * c), lhsT=wbd[:, l, 2 + c, :], rhs=xcb[:, c, 0:N], start=True, stop=True),
               reads=["xcb" + sfx, "wbd"], writes=[("ps", bh)])
        tha, thx, a_, a2 = [wk[i][:, :, 0:N] for i in range(4)]
        hs = a2
        for c in range(2):
            op("act", lambda: act.activation(out=tha[:, c, :], in_=PS(bg, N, o=256 * c), func=AF.Tanh, scale=0.5, bias=der[:, l, 4 + c:5 + c]),
               reads=[("ps", bg), "der"], writes=["wk0" + sfx])
            op("act", lambda: act.activation(out=thx[:, c, :], in_=PS(bh, N, o=256 * c), func=AF.Tanh, scale=0.5, bias=der[:, l, 6 + c:7 + c]),
               reads=[("ps", bh), "der"], writes=["wk1" + sfx])
            op("act", lambda: act.activation(out=a_[:, c, :], in_=tha[:, c, :], func=AF.Exp, scale=der[:, l, c:c + 1], bias=der[:, l, c:c + 1]),
               reads=["wk0" + sfx, "der"], writes=["wk2" + sfx])
            op("act", lambda: act.activation(out=a2[:, c, :], in_=tha[:, c, :], func=AF.Exp, scale=der[:, l, 2 + c:3 + c], bias=der[:, l, 2 + c:3 + c]),
               reads=["wk0" + sfx, "der"], writes=["wk3" + sfx])
        yield
        op("dve", lambda: dve.tensor_scalar(out=a2, in0=a2, scalar1=-1.0, scalar2=1.0, op0=ALU.mult, op1=ALU.add), reads=["wk3" + sfx], writes=["wk3" + sfx])
        op("act", lambda: act.activation(out=tha, in_=a2, func=AF.Ln), reads=["wk3" + sfx], writes=["wk0" + sfx])
        op("act", lambda: act.activation(out=tha, in_=tha, func=AF.Exp, scale=0.5), reads=["wk0" + sfx], writes=["wk0" + sfx])
        op("dve", lambda: dve.scalar_tensor_tensor(out=thx, in0=thx, scalar=1.0, in1=xc, op0=ALU.add, op1=ALU.mult),
           reads=["wk1" + sfx, "xc" + sfx], writes=["wk1" + sfx])
        op("dve", lambda: dve.scalar_tensor_tensor(out=thx, in0=thx, scalar=0.5, in1=tha, op0=ALU.mult, op1=ALU.mult),
           reads=["wk1" + sfx, "wk0" + sfx], writes=["wk1" + sfx])
        yield
        h_apply(a_, thx, hs)
        yield
        op("act", lambda: act.activation(out=tha, in_=gr, func=AF.Square), reads=["gr" + sfx], writes=["wk0" + sfx])
        op("dve", lambda: dve.tensor_scalar(out=tha, in0=tha, scalar1=0.044715, scalar2=1.0, op0=ALU.mult, op1=ALU.add), reads=["wk0" + sfx], writes=["wk0" + sfx])
        op("dve", lambda: dve.tensor_tensor(out=tha, in0=tha, in1=gr, op=ALU.mult), reads=["wk0" + sfx, "gr" + sfx], writes=["wk0" + sfx])
        op("act", lambda: act.activation(out=tha, in_=tha, func=AF.Tanh, scale=0.7978845608028654), reads=["wk0" + sfx], writes=["wk0" + sfx])
        op("dve", lambda: dve.scalar_tensor_tensor(out=tha, in0=tha, scalar=1.0, in1=gr, op0=ALU.add, op1=ALU.mult), reads=["wk0" + sfx, "gr" + sfx], writes=["wk0" + sfx])
        yield
        finish(hs, tha, thx)

    def lru_part(W, l, N, xc_taps, xc, gr, h_apply, finish):
        for _ in lru_part_g(W, l, N, xc_taps, xc, gr, h_apply, finish):
            pass

    def lru_pool_common(W, l, N, xc_taps, xc, gr, zwin, zcur, h_apply, mixb, first_corr=None):
        pool_part(W, l, N, zwin, zcur, mixb, first_corr)

        def fin(hs, ge, _):
            op("dve", lambda: dve.scalar_tensor_tensor(out=mixb[:, 0:2, :], in0=hs, scalar=0.5, in1=ge, op0=ALU.mult, op1=ALU.mult),
               reads=["wk3", "wk0"], writes=["mixb"])
        lru_part(W, l, N, xc_taps, xc, gr, h_apply, fin)

    def wout_ln(W, l, c0, N, attn, mixb, xkey, ln=True):
        for _ in wout_g(W, l, c0, N, attn, mixb, xkey):
            pass
        if ln:
            layer_norm(W, l, 1, c0, N, xkey)

    def wout_g(W, l, c0, N, attn, mixb, xkey):
        woa, wob = W["woa"], W["wob"]
        for m in range(8):
            b = C.bank()
            for h in range(4):
                op("pe", lambda: pe.matmul(PS(b, N), lhsT=woa[:, h, 128 * m:128 * m + 128], rhs=attn[:, h, :], start=(h == 0), stop=False),
                   reads=["attnT", "woa"], writes=[("ps", b)], signal=False)
            for j in range(4):
                op("pe", lambda: pe.matmul(PS(b, N), lhsT=wob[:, j, 128 * m:128 * m + 128], rhs=mixb[:, j, :], start=False, stop=(j == 3)),
                   reads=["mixb", "wob"], writes=[("ps", b)], signal=(j == 3))
            op("dve", lambda: dve.scalar_tensor_tensor(out=xf[:, m, c0:c0 + N], in0=xf[:, m, c0:c0 + N], scalar=ALPHA, in1=PS(b, N),
                                                       op0=ALU.mult, op1=ALU.add), reads=[("ps", b), xkey], writes=[xkey])
            if m % 2 == 1:
                yield

    def chk(stage):
        if STOP == stage:
            raise _Stop()

    def run_layers():
      for l in range(L):
          pb = NPV * l
          with contextlib.ExitStack() as ph:
              W = {}
              W["woa"] = woa = sbt(ph, "woa", [128, 4, 1024], BF16)
              W["wob"] = wob = sbt(ph, "wob", [128, 4, 1024], BF16)
              W["wk"] = [sbt(ph, "wk%d" % i, [128, 2, 272]) for i in range(4)]
              W["st"] = [W["wk"][2][:, 0, 0:256], W["wk"][2][:, 1, 0:256], W["wk"][3][:, 0, 0:256], W["wk"][3][:, 1, 0:256]]
              W["stk"] = ["wk2", "wk2", "wk3", "wk3"]
              st_ph, stk_ph = W["st"], W["stk"]
              W["diffb"] = sbt(ph, "diffb", [128, 2, 256], BF16)
              W["tA"] = W["wk"][0][:, 0, 0:256]; W["tAk"] = "wk0"
              W["tB"] = W["wk"][1][:, 0, 0:256]; W["tBk"] = "wk1"
              dma("pool", (woa[:], w_out[l, 0:512, :].rearrange("(j p) n -> p j n", p=128)), writes=["woa"], key="woa")
              dma("pool", (wob[:], w_out[l, 512:1024, :].rearrange("(j p) n -> p j n", p=128)), writes=["wob"], key="wob")

              with contextlib.ExitStack() as pp:
                  rec0 = sbt(pp, "rec0", [128, 2, T], BF16)
                  corr = sbt(pp, "corr", [128, 2, T], BF16)
                  kTb = sbt(pp, "kTb", [64, 2, 384], BF16)
                  Vb = sbt(pp, "Vb", [128, 3, 128], BF16)
                  xr_ext = sbt(pp, "xr_ext", [128, 2, 259])
                  zp_ext = sbt(pp, "zp_ext", [128, 2, 271])
                  hcar = sbt(pp, "hcar", [128, 2]); Acar = sbt(pp, "Acar", [128, 2]); hst = sbt(pp, "hst", [128, 2]); hfin = sbt(pp, "hfin", [128, 2])
                  wkt = W["wk"]

                  with contextlib.ExitStack() as pa:
                      stq = rec0[:].rearrange("p c n -> p (c n)")[:, 0:1536].bitcast(F32)
                      cview = corr[:].rearrange("p c n -> p (c n)")[:, 0:1024].bitcast(F32)
                      sth = cview[:, 0:256]; stc = cview[0:18, 256:512]
                      b1, b2 = C.bank(), C.bank()
                      for k in range(8):
                          op("pe", lambda: pe.matmul(PS(b1), lhsT=xb[:, k, T - 128:T], rhs=winb[:, k, 512:1024], start=(k == 0), stop=(k == 7)),
                             reads=["winb"], writes=[("ps", b1)], signal=(k == 7))
                      for k in range(8):
                          op("pe", lambda: pe.matmul(PS(b2, 256), lhsT=xb[:, k, T - 128:T], rhs=winb[:, k, 1280:1536], start=(k == 0), stop=(k == 7)),
                             reads=["winb"], writes=[("ps", b2)], signal=(k == 7))
                      op("dve", lambda: dve.tensor_copy(out=stq[:, 0:512], in_=PS(b1)), reads=[("ps", b1)], writes=["rec0"])
                      op("dve", lambda: dve.tensor_copy(out=stq[:, 512:768], in_=PS(b2, 256)), reads=[("ps", b2)], writes=["rec0"])
                      dma("sp", [(st_out[l, 0:128, :], stq[:, 0:256]), (st_out[l, 128:131, :], stq[125:128, 256:512]),
                                 (st_out[l, 131:146, :], stq[113:128, 512:768])], reads=["rec0"], key="stq")
                      dma("sp", [(cc1_in[l, 0:128, :], stq[:, 0:256]), (cc1_in[l, 128:131, :], stq[125:128, 256:512]),
                                 (cc1_in[l, 131:146, :], stq[113:128, 512:768])], reads=["rec0"], writes=["cc1in"], key="stq2")
                      C.collective(cc1_in[l], cc1_out[l], reads=["cc1in"], writes=["cc1out"])
                      dma("sp", [(sth[:], cc1_out[l, 0:128, :]), (stc[:], cc1_out[l, 128:146, :])], reads=["cc1out"], writes=["corr"], key="sth")
                      bk = C.bank()
                      for kv in range(2):
                          op("pe", lambda: pe.transpose(PS(bk, 128, 0, 64, 128 * kv), sth[:, 64 * kv:64 * kv + 64], ident[:, :]),
                             reads=["corr", "ident"], writes=[("ps", bk)], signal=(kv == 1))
                      op("dve", lambda: dve.tensor_scalar(out=kTb[:, :, 0:128], in0=ps[0:64, bk, 0:256].rearrange("p (k n) -> p k n", n=128),
                                                          scalar1=flag[0:64, 0:1], scalar2=None, op0=ALU.mult),
                         reads=[("ps", bk), "smallc"], writes=["kTb"])
                      op("dve", lambda: dve.tensor_scalar(out=Vb[:, 0, :], in0=sth[:, 128:256], scalar1=flag[:, 0:1], scalar2=None, op0=ALU.mult),
                         reads=["corr", "smallc"], writes=["Vb"])
                      bk2 = C.bank()
                      for c in range(2):
                          op("pe", lambda: pe.transpose(PS(bk2, 18, o=32 * c), stc[0:18, 128 * c:128 * c + 128], ident[0:18, 0:18]),
                             reads=["corr", "ident"], writes=[("ps", bk2)], signal=(c == 1))
                      for c in range(2):
                          op("dve", lambda: dve.tensor_scalar(out=xr_ext[:, c, 0:3], in0=PS(bk2, 3, o=32 * c), scalar1=flag[:, 0:1], scalar2=None, op0=ALU.mult),
                             reads=[("ps", bk2), "smallc"], writes=["xr0"])
                          op("dve", lambda: dve.tensor_scalar(out=zp_ext[:, c, 0:15], in0=PS(bk2, 15, o=32 * c + 3), scalar1=flag[:, 0:1], scalar2=None, op0=ALU.mult),
                             reads=[("ps", bk2), "smallc"], writes=["zp"])
                  op("dve", lambda: dve.memset(hcar[:], 0.0), writes=["hcar"])
                  op("dve", lambda: dve.memset(Acar[:], 1.0), writes=["Acar"])

                  with contextlib.ExitStack() as pb_:
                      sets = []
                      for i in range(2):
                          d_ = {"gr": sbt(pb_, "gr%d" % i, [128, 2, 256]), "xc": sbt(pb_, "xc%d" % i, [128, 2, 256]),
                                "xcb": sbt(pb_, "xcb%d" % i, [128, 2, 256], BF16)}
                          d_["wk"] = W["wk"] if i == 0 else [sbt(pb_, "wkB%d" % q_, [128, 2, 272]) for q_ in range(4)]
                          d_["xr"] = xr_ext if i == 0 else sbt(pb_, "xr_extB", [128, 2, 259])
                          sets.append(d_)

                      def pre(bi):
                          c0 = 256 * bi
                          N = 256
                          i = bi % 2
                          sx = str(i)
                          S_ = sets[i]
                          xr_i, gr_i, xc_i = S_["xr"], S_["gr"], S_["xc"]
                          for (cb, dst, key) in ((768, xr_i[:, :, 3:259], "xr" + sx), (1024, gr_i[:, :, :], "gr" + sx)):
                              b = C.bank()
                              for c in range(2):
                                  for k in range(8):
                                      op("pe", lambda: pe.matmul(PS(b, N, o=256 * c), lhsT=winb[:, k, cb + 128 * c:cb + 128 * c + 128], rhs=xb[:, k, c0:c0 + N],
                                                                 start=(k == 0), stop=(k == 7)),
                                         reads=["winb"], writes=[("ps", b)], signal=(k == 7 and c == 1))
                              if key.startswith("gr"):
                                  op("act", lambda: act.activation(out=dst, in_=ps[:, b, :].rearrange("p (e n) -> p e n", n=256), func=AF.Copy),
                                     reads=[("ps", b)], writes=[key])
                              else:
                                  op("dve", lambda: dve.tensor_copy(out=dst, in_=ps[:, b, :].rearrange("p (e n) -> p e n", n=256)),
                                     reads=[("ps", b)], writes=[key])
                          if bi > 0:
                              xr_p = sets[1 - i]["xr"]
                              op("dve", lambda: dve.tensor_copy(out=xr_i[:, :, 0:3], in_=xr_p[:, :, 256:259]), reads=["xr" + str(1 - i)], writes=["xr" + sx])
                          yield

                          def h_apply(a_, bb, hs):
                              for c in range(2):
                                  op("dve", lambda: dve.tensor_tensor_scan(out=hs[:, c, :], data0=a_[:, c, :], data1=bb[:, c, :], initial=hcar[:, c:c + 1],
                                                                           op0=ALU.mult, op1=ALU.add), reads=["wk2" + sx, "wk1" + sx, "hcar"], writes=["wk3" + sx])
                              op("dve", lambda: dve.tensor_copy(out=hcar[:, :], in_=hs[:, :, 255]), reads=["wk3" + sx], writes=["hcar"])
                              for c in range(2):
                                  op("dve", lambda: dve.tensor_tensor_scan(out=bb[:, c, :], data0=a_[:, c, :], data1=cneg[:, 0:256], initial=Acar[:, c:c + 1],
                                                                           op0=ALU.mult, op1=ALU.max), reads=["wk2" + sx, "cneg", "Acar"], writes=["wk1" + sx])
                              op("dve", lambda: dve.tensor_copy(out=Acar[:, :], in_=bb[:, :, 255]), reads=["wk1" + sx], writes=["Acar"])

                          def fin(hs, ge, Acum):
                              op("dve", lambda: dve.scalar_tensor_tensor(out=rec0[:, :, c0:c0 + 256], in0=hs, scalar=0.5, in1=ge, op0=ALU.mult, op1=ALU.mult),
                                 reads=["wk3" + sx, "wk0" + sx], writes=["rec0"])
                              op("dve", lambda: dve.scalar_tensor_tensor(out=corr[:, :, c0:c0 + 256], in0=Acum, scalar=0.5, in1=ge, op0=ALU.mult, op1=ALU.mult),
                                 reads=["wk1" + sx, "wk0" + sx], writes=["corr"])
                          yield from lru_part_g(S_, l, N, lambda c, tap: xr_i[:, c, tap:tap + 256], xc_i[:], gr_i[:], h_apply, fin, sfx=sx)

                      pend = [pre(bi) for bi in range(8)]
                      active = []
                      while pend or active:
                          while len(active) < 2 and pend:
                              active.append(pend.pop(0))
                          for g_ in list(active):
                              try:
                                  next(g_)
                              except StopIteration:
                                  active.remove(g_)
                  C.barrier()
                  with nc.allow_non_contiguous_dma(reason="tiny h state"):
                      dma("sp", (cc2_in[l, 0, :].rearrange("(c p) -> p c", p=128), hcar[:, :]), reads=["hcar"], writes=["cc2in"], key="hst")
                  C.collective(cc2_in[l], cc2_out[l], reads=["cc2in"], writes=["cc2out"])
                  with nc.allow_non_contiguous_dma(reason="tiny h state"):
                      dma("sp", (hst[:, :], cc2_out[l, 0, :].rearrange("(c p) -> p c", p=128)), reads=["cc2out"], writes=["hst"], key="hst2")
                  op("dve", lambda: dve.tensor_scalar(out=hst[:], in0=hst[:], scalar1=flag[:, 0:1], scalar2=None, op0=ALU.mult), reads=["hst", "smallc"], writes=["hst"])
                  op("dve", lambda: dve.tensor_tensor(out=hfin[:], in0=Acar[:], in1=hst[:], op=ALU.mult), reads=["Acar", "hst"], writes=["hfin"])
                  op("dve", lambda: dve.tensor_tensor(out=hfin[:], in0=hfin[:], in1=hcar[:], op=ALU.add), reads=["hfin", "hcar"], writes=["hfin"])
                  with nc.allow_non_contiguous_dma(reason="tiny h state"):
                      dma("sp", (st_out[l, 146, :].rearrange("(c p) -> p c", p=128), hfin[:, :]), reads=["hfin"], key="hst3")

                  with contextlib.ExitStack() as pc:
                      qT = sbt(pc, "qT", [64, 8, 256], BF16)
                      attnT = sbt(pc, "attnT", [128, 4, 256], BF16)
                      mixb = sbt(pc, "mixb", [128, 4, 256], BF16)
                      tPy = [sbt(pc, "tPy%d" % i, [128, 1024]) for i in range(2)]
                      tP = [[tPy[i][:, 0:512], tPy[i][:, 512:1024]] for i in range(2)]
                      PT = [[sbt(pc, "PT%d%d" % (i, j_), [128, 512], BF16) for j_ in range(2)] for i in range(2)]
                      W["ybf"] = sbt(pc, "ybfL", [128, 4, 256], BF16)
                      W["ybfk"] = ["ybfL"]
                      stL = sbt(pc, "stL", [128, 3, 256])
                      W["st"] = [stL[:, 0, :], stL[:, 1, :], stL[:, 2, :], stL[:, 1, :]]
                      W["stk"] = ["stL0", "stL1", "stL2", "stL1"]
                      dd = wkt[3][:].rearrange("p c n -> p (c n)")[:, 0:256]
                      S2 = wkt[0][:, :, 0:270]; S4 = wkt[1][:, :, 0:268]
                      def front(bi):
                          c0 = 256 * bi
                          N = 256
                          xkey = "xP%d" % bi
                          xin = [xkey + "b"]

                          def rhs_x(k):
                              return xb[:, k, c0:c0 + N]
                          for j in range(4):
                              b = C.bank()
                              for e in range(2):
                                  h = 2 * j + e
                                  for k in range(8):
                                      op("pe", lambda: pe.matmul(PS(b, N, 0, 64, 256 * e), lhsT=winb[:, k, 64 * h:64 * h + 64], rhs=rhs_x(k),
                                                                 start=(k == 0), stop=(k == 7)),
                                         reads=["winb"] + xin, writes=[("ps", b)], signal=(k == 7 and e == 1))
                              op("act", lambda: act.activation(out=qT[:, 2 * j:2 * j + 2, :], in_=ps[0:64, b, :].rearrange("p (e n) -> p e n", n=256), func=AF.Copy),
                                 reads=[("ps", b)], writes=["qT"])
                              if j % 2 == 1:
                                  yield
                          b = C.bank()
                          for kv in range(2):
                              for k in range(8):
                                  op("pe", lambda: pe.matmul(PS(b, N, 0, 64, 256 * kv), lhsT=winb[:, k, 512 + 64 * kv:512 + 64 * kv + 64], rhs=rhs_x(k),
                                                             start=(k == 0), stop=(k == 7)),
                                     reads=["winb"] + xin, writes=[("ps", b)], signal=(k == 7 and kv == 1))
                          op("act", lambda: act.activation(out=kTb[:, :, 128:384], in_=ps[0:64, b, :].rearrange("p (e n) -> p e n", n=256), func=AF.Copy),
                             reads=[("ps", b)], writes=["kTb"])
                          yield
                          b = C.bank()
                          for c in range(2):
                              for k in range(8):
                                  op("pe", lambda: pe.matmul(PS(b, N, o=256 * c), lhsT=winb[:, k, 1280 + 128 * c:1280 + 128 * c + 128], rhs=rhs_x(k),
                                                             start=(k == 0), stop=(k == 7)),
                                     reads=["winb"] + xin, writes=[("ps", b)], signal=(k == 7 and c == 1))
                          op("dve", lambda: dve.tensor_copy(out=zp_ext[:, :, 15:271], in_=ps[:, b, :].rearrange("p (e n) -> p e n", n=256)),
                             reads=[("ps", b)], writes=["zp"])
                          yield
                          b = C.bank()
                          for i in range(2):
                              for k in range(8):
                                  op("pe", lambda: pe.matmul(PS(b, 128, o=128 * i), lhsT=xb[:, k, c0 + 128 * i:c0 + 128 * i + 128], rhs=winb[:, k, 640:768],
                                                             start=(k == 0), stop=(k == 7)),
                                     reads=["winb"] + xin, writes=[("ps", b)], signal=(k == 7 and i == 1))
                          op("act", lambda: act.activation(out=Vb[:, 1:3, :], in_=ps[:, b, 0:256].rearrange("p (e n) -> p e n", n=128), func=AF.Copy),
                             reads=[("ps", b)], writes=["Vb"])
                          yield
                          iters = [(qi, kv) for qi in range(2) for kv in range(2)]

                          def s1(it):
                              qi, kv = iters[it]
                              sx = it % 2
                              bs = [C.bank(), C.bank()]
                              for pc_ in range(2):
                                  ko = 128 * qi + 128 * pc_
                                  tk, pk = "tP%d%d" % (sx, pc_), "PT%d%d" % (sx, pc_)
                                  op("pe", lambda: pe.matmul(PS(bs[pc_]), lhsT=kTb[:, kv, ko:ko + 128], rhs=qT[:, 4 * kv:4 * kv + 4, 128 * qi:128 * qi + 128],
                                                             start=True, stop=True),
                                     reads=["kTb", "qT"], writes=[("ps", bs[pc_])])
                                  bias_t = (bprev if pc_ == 0 else bcur)[:, kv, :]
                                  op("dve", lambda: dve.scalar_tensor_tensor(out=tP[sx][pc_], in0=PS(bs[pc_]), scalar=0.125, in1=bias_t, op0=ALU.mult, op1=ALU.add),
                                     reads=[("ps", bs[pc_]), "bcur", "bprev"], writes=[tk])
                                  if pc_ == 0 and bi == 0 and qi == 0:
                                      op("act", lambda: act.activation(out=PT[sx][pc_][:], in_=tP[sx][pc_], func=AF.Exp, bias=nm0[:, 0:1]),
                                         reads=[tk, "smallc"], writes=[pk])
                                  else:
                                      op("act", lambda: act.activation(out=PT[sx][pc_][:], in_=tP[sx][pc_], func=AF.Exp),
                                         reads=[tk], writes=[pk])

                          def s2(it):
                              qi, kv = iters[it]
                              sx = it % 2
                              bo, bd = C.bank(), C.bank()
                              for e in range(2):
                                  for pc_ in range(2):
                                      rhs_ = PT[sx][pc_][:].rearrange("p (gg e n) -> p e gg n", e=2, n=128)[:, e, :, :]
                                      op("pe", lambda: pe.matmul(PS(bo, 256, 64 * e, 64 * e + 64), lhsT=Vb[:, qi + pc_, 64 * kv:64 * kv + 64], rhs=rhs_,
                                                                 start=(pc_ == 0), stop=(pc_ == 1)),
                                         reads=["Vb", "PT%d%d" % (sx, pc_)], writes=[("ps", bo)], signal=(pc_ == 1 and e == 1))
                              for e in range(2):
                                  for pc_ in range(2):
                                      rhs_ = PT[sx][pc_][:].rearrange("p (gg e n) -> p e gg n", e=2, n=128)[:, e, :, :]
                                      op("pe", lambda: pe.matmul(PS(bd, 256, 64 * e, 64 * e + 64), lhsT=ones[:, 0:64], rhs=rhs_, start=(pc_ == 0), stop=(pc_ == 1)),
                                         reads=["ones", "PT%d%d" % (sx, pc_)], writes=[("ps", bd)], signal=(pc_ == 1 and e == 1))
                              for e in range(2):
                                  for gg in range(2):
                                      hcol = 8 * l + 4 * kv + 2 * gg + e
                                      op("act", lambda: act.activation(out=dd[64 * e:64 * e + 64, 128 * gg:128 * gg + 128], in_=ps[64 * e:64 * e + 64, bd, 128 * gg:128 * gg + 128],
                                                                       func=AF.Ln, bias=es64[64 * e:64 * e + 64, hcol:hcol + 1]),
                                         reads=[("ps", bd), "es64"], writes=["wk3"])
                              op("act", lambda: act.activation(out=dd, in_=dd, func=AF.Exp, scale=-1.0), reads=["wk3"], writes=["wk3"])
                              op("dve", lambda: dve.tensor_tensor(out=attnT[:, 2 * kv:2 * kv + 2, 128 * qi:128 * qi + 128],
                                                                  in0=ps[:, bo, 0:256].rearrange("p (g n) -> p g n", n=128),
                                                                  in1=dd.rearrange("p (g n) -> p g n", n=128), op=ALU.mult),
                                 reads=[("ps", bo), "wk3"], writes=["attnT"])

                          s1(0)
                          for it in range(4):
                              if it + 1 < 4:
                                  s1(it + 1)
                              s2(it)
                              yield
                          op("dve", lambda: dve.tensor_tensor(out=S2, in0=zp_ext[:, :, 1:271], in1=zp_ext[:, :, 0:270], op=ALU.add), reads=["zp"], writes=["wk0"])
                          op("dve", lambda: dve.tensor_tensor(out=S4, in0=S2[:, :, 2:270], in1=S2[:, :, 0:268], op=ALU.add), reads=["wk0"], writes=["wk1"])
                          S8a = wkt[2][:, 0, 0:264]; S8b = wkt[2][:, 1, 0:264]
                          op("dve", lambda: dve.tensor_tensor(out=S8a, in0=S4[:, 1, 4:268], in1=S4[:, 1, 0:264], op=ALU.add),
                             reads=["wk1"], writes=["wk2"])
                          op("dve", lambda: dve.tensor_tensor(out=S8b[:, 0:256], in0=S8a[:, 8:264], in1=S8a[:, 0:256], op=ALU.add),
                             reads=["wk2"], writes=["wk2"])
                          zwin = [(0, 64, 0, S2[0:64, 0, 14:270], 2, "wk0"), (64, 128, 0, S4[64:128, 0, 12:268], 4, "wk1"),
                                  (0, 64, 1, S8a[0:64, 8:264], 8, "wk2"), (64, 128, 1, S8b[64:128, 0:256], 16, "wk2")]

                          def first_corr(p0, p1, c, win, wkey, bi=bi):
                              if bi != 0:
                                  return
                              w16 = win[:, 0:16]
                              op("dve", lambda: dve.tensor_tensor(out=w16, in0=w16, in1=pcorr[p0:p1, c, :], op=ALU.mult),
                                 reads=[wkey, "smallc"], writes=[wkey])
                          yield
                          pool_part(W, l, N, zwin, lambda c, p0, p1: zp_ext[p0:p1, c, 15:271], mixb[:], first_corr)
                          yield
                          for c in range(2):
                              op("dve", lambda: dve.scalar_tensor_tensor(out=mixb[:, c, :], in0=corr[:, c, c0:c0 + N], scalar=hst[:, c:c + 1], in1=rec0[:, c, c0:c0 + N],
                                                                         op0=ALU.mult, op1=ALU.add), reads=["corr", "rec0", "hst"], writes=["mixb"])
                          yield
                          yield from wout_g(W, l, c0, N, attnT, mixb, xkey)
                          yield
                          if bi < 7:
                              op("dve", lambda: dve.tensor_copy(out=zp_ext[:, :, 0:15], in_=zp_ext[:, :, 256:271]), reads=["zp"], writes=["zp"])
                              op("act", lambda: act.activation(out=kTb[:, :, 0:128], in_=kTb[:, :, 256:384], func=AF.Copy), reads=["kTb"], writes=["kTb"])
                              op("act", lambda: act.activation(out=Vb[:, 0, :], in_=Vb[:, 2, :], func=AF.Copy), reads=["Vb"], writes=["Vb"])

                      def drive(gA, gB, delay=0):
                          a_alive, b_alive = gA is not None, gB is not None
                          rnd = 0
                          while a_alive or b_alive:
                              if a_alive:
                                  try:
                                      next(gA)
                                  except StopIteration:
                                      a_alive = False
                              rnd += 1
                              if b_alive and (rnd > delay or not a_alive):
                                  try:
                                      next(gB)
                                  except StopIteration:
                                      b_alive = False

                      C.bank_pool = list(range(6))
                      drive(front(0), None)
                      for bi in range(8):
                          drive(front(bi + 1) if bi + 1 < 8 else None, layer_norm_g(W, l, 1, 256 * bi, 256, "xP%d" % bi, banks=(6, 7)), delay=4)
                      C.bank_pool = list(range(8))

              C.barrier()
              if STOP == 'prompt%d' % l: return

              with contextlib.ExitStack() as pp:
                  N = NS
                  c0 = T
                  xkey = "xS"
                  W["ybf"] = sbt(pp, "ybfS", [128, 8, NS], BF16); W["ybfk"] = ["ybf"]
                  W["xcb"] = sbt(pp, "xcbS", [128, 2, NS], BF16)
                  W["st"], W["stk"] = st_ph, stk_ph
                  xin = ["xSb"]
                  qTs = sbt(pp, "qTs", [64, 8, NS]); kTs = sbt(pp, "kTs", [64, 2, NS]); vTs = sbt(pp, "vTs", [64, 2, NS])
                  xrs = sbt(pp, "xrs", [128, 2, NS, 4]); grs = sbt(pp, "grs", [128, 2, NS]); zps = sbt(pp, "zps", [128, 2, NS, 16])
                  xcs = sbt(pp, "xcs", [128, 2, NS]); h0s = sbt(pp, "h0s", [128, 2, NS])
                  attnTs = sbt(pp, "attnTs", [128, 4, NS], BF16); accw = sbt(pp, "accw", [128, 128], BF16); mixbs = sbt(pp, "mixbs", [128, 4, NS], BF16)
                  sts = sbt(pp, "sts", [NS, 768]); scs = sbt(pp, "scs", [48, 256]); sps = sbt(pp, "sps", [120, 2, 256]); shs = sbt(pp, "shs", [NS, 256])
                  sths = sbt(pp, "sths", [NS, 256])
                  qs128 = sbt(pp, "qs128", [128, 64]); kn128 = sbt(pp, "kn128", [128, 64]); vn128 = sbt(pp, "vn128", [128, 64])
                  krep = sbt(pp, "krep", [64, 128]); vrep = sbt(pp, "vrep", [64, 128])
                  Kcs = [sbt(pp, "Kc%d" % i, [128, 16, 64]) for i in range(2)]; Vcs = [sbt(pp, "Vc%d" % i, [128, 16, 64]) for i in range(2)]
                  tmpc = sbt(pp, "tmpc", [128, 16, 64])

                  def load_kv(buf, src, ch, key):
                      pairs = [(buf[16 * h:16 * h + 16, :, :], src[l, :, 16 * ch:16 * ch + 16, 64 * (h // 4):64 * (h // 4) + 64]) for h in range(8)]
                      dma("sp", pairs, writes=[key], key=key)
                  scr = sbt(pp, "scr", [128, 128]); Pm = sbt(pp, "Pm", [128, 128]); sm = sbt(pp, "sm", [128, 8])
                  acc = sbt(pp, "acc", [128, 64]); part = sbt(pp, "part", [128, 64]); wins = sbt(pp, "wins", [128, 2, NS])

                  def rhs_x(k):
                      return xb[:, k, c0:c0 + N]
                  dma("sp", [(scs[:], sc[l]), (sps[:], spool[l].rearrange("(i r) n -> r i n", i=2)), (shs[:], sh[l])], writes=["sst"], key="sst")
                  for ch in range(2):
                      load_kv(Kcs[ch], ck, ch, "Kc%d" % ch)
                  for ch in range(2):
                      load_kv(Vcs[ch], cv, ch, "Vc%d" % ch)
                  b = C.bank()
                  for c in range(2):
                      op("pe", lambda: pe.transpose(PS(b, 48, o=64 * c), scs[:, 128 * c:128 * c + 128], ident[0:48, 0:48]),
                         reads=["sst", "ident"], writes=[("ps", b)], signal=(c == 1))
                  for c in range(2):
                      op("dve", lambda: dve.tensor_copy(out=xrs[:, c, :, 0:3], in_=ps[:, b, 64 * c:64 * c + 48].rearrange("p (b t) -> p b t", t=3)),
                         reads=[("ps", b)], writes=["xr"])
                  b = C.bank()
                  for i in range(2):
                      for c in range(2):
                          op("pe", lambda: pe.transpose(PS(b, 120, o=120 * (2 * i + c)), sps[:, i, 128 * c:128 * c + 128], ident[0:120, 0:120]),
                             reads=["sst", "ident"], writes=[("ps", b)], signal=(i == 1 and c == 1))
                  for i in range(2):
                      for c in range(2):
                          op("dve", lambda: dve.tensor_copy(out=zps[:, c, 8 * i:8 * i + 8, 0:15],
                                                            in_=ps[:, b, 120 * (2 * i + c):120 * (2 * i + c) + 120].rearrange("p (b t) -> p b t", t=15)),
                             reads=[("ps", b)], writes=["zp"])
                  b = C.bank()
                  for c in range(2):
                      op("pe", lambda: pe.transpose(PS(b, NS, o=NS * c), shs[:, 128 * c:128 * c + 128], ident[0:NS, 0:NS]),
                         reads=["sst", "ident"], writes=[("ps", b)], signal=(c == 1))
                  op("dve", lambda: dve.tensor_copy(out=h0s[:], in_=ps[:, b, 0:2 * NS].rearrange("p (c n) -> p c n", n=NS)), reads=[("ps", b)], writes=["h0s"])
                  b = C.bank()
                  for h in range(8):
                      for k in range(8):
                          op("pe", lambda: pe.matmul(PS(b, N, 0, 64, NS * h), lhsT=winb[:, k, 64 * h:64 * h + 64], rhs=rhs_x(k), start=(k == 0), stop=(k == 7)),
                             reads=["winb"] + xin, writes=[("ps", b)], signal=(k == 7 and h == 7))
                  op("dve", lambda: dve.tensor_copy(out=qTs[:], in_=ps[0:64, b, 0:8 * NS].rearrange("p (h n) -> p h n", n=NS)), reads=[("ps", b)], writes=["qTs"])
                  b = C.bank()
                  for e in range(4):
                      for k in range(8):
                          op("pe", lambda: pe.matmul(PS(b, N, 0, 64, NS * e), lhsT=winb[:, k, 512 + 64 * e:512 + 64 * e + 64], rhs=rhs_x(k), start=(k == 0), stop=(k == 7)),
                             reads=["winb"] + xin, writes=[("ps", b)], signal=(k == 7 and e == 3))
                  op("dve", lambda: dve.tensor_copy(out=kTs[:], in_=ps[0:64, b, 0:2 * NS].rearrange("p (h n) -> p h n", n=NS)), reads=[("ps", b)], writes=["kTs"])
                  op("dve", lambda: dve.tensor_copy(out=vTs[:], in_=ps[0:64, b, 2 * NS:4 * NS].rearrange("p (h n) -> p h n", n=NS)), reads=[("ps", b)], writes=["vTs"])
                  b = C.bank()
                  for e in range(6):
                      for k in range(8):
                          op("pe", lambda: pe.matmul(PS(b, N, o=NS * e), lhsT=winb[:, k, 768 + 128 * e:768 + 128 * e + 128], rhs=rhs_x(k), start=(k == 0), stop=(k == 7)),
                             reads=["winb"] + xin, writes=[("ps", b)], signal=(k == 7 and e == 5))
                  op("dve", lambda: dve.tensor_copy(out=xrs[:, :, :, 3], in_=ps[:, b, 0:2 * NS].rearrange("p (c n) -> p c n", n=NS)), reads=[("ps", b)], writes=["xr"])
                  op("dve", lambda: dve.tensor_copy(out=grs[:], in_=ps[:, b, 2 * NS:4 * NS].rearrange("p (c n) -> p c n", n=NS)), reads=[("ps", b)], writes=["gr"])
                  op("dve", lambda: dve.tensor_copy(out=zps[:, :, :, 15], in_=ps[:, b, 4 * NS:6 * NS].rearrange("p (c n) -> p c n", n=NS)), reads=[("ps", b)], writes=["zp"])
                  b1, b2 = C.bank(), C.bank()
                  for k in range(8):
                      op("pe", lambda: pe.matmul(PS(b1, 512, 0, NS), lhsT=xb[:, k, c0:c0 + NS], rhs=winb[:, k, 512:1024], start=(k == 0), stop=(k == 7)),
                         reads=["winb"] + xin, writes=[("ps", b1)], signal=(k == 7))
                  for k in range(8):
                      op("pe", lambda: pe.matmul(PS(b2, 256, 0, NS), lhsT=xb[:, k, c0:c0 + NS], rhs=winb[:, k, 1280:1536], start=(k == 0), stop=(k == 7)),
                         reads=["winb"] + xin, writes=[("ps", b2)], signal=(k == 7))
                  op("dve", lambda: dve.tensor_copy(out=sts[:, 0:512], in_=PS(b1, 512, 0, NS)), reads=[("ps", b1)], writes=["sts"])
                  op("dve", lambda: dve.tensor_copy(out=sts[:, 512:768], in_=PS(b2, 256, 0, NS)), reads=[("ps", b2)], writes=["sts"])
                  dma("sp", [(s_k[l, :, 127, :], sts[:, 0:128]), (s_v[l, :, 127, :], sts[:, 128:256]),
                             (s_conv[l, :, 2, :], sts[:, 256:512]), (s_pool[l, :, 14, :], sts[:, 512:768])], reads=["sts"], key="sts")
                  b = C.bank()
                  op("pe", lambda: pe.transpose(PS(b, 64), qTs[:].rearrange("p h n -> p (h n)"), ident[0:64, 0:64]), reads=["qTs", "ident"], writes=[("ps", b)])
                  op("dve", lambda: dve.tensor_copy(out=qs128[:], in_=PS(b, 64)), reads=[("ps", b)], writes=["qs128"])
                  op("dve", lambda: dve.tensor_copy(out=krep[:].rearrange("p (k g n) -> p k g n", k=2, g=4),
                                                    in_=kTs[:].unsqueeze(2).to_broadcast([64, 2, 4, NS])), reads=["kTs"], writes=["krep"])
                  op("dve", lambda: dve.tensor_copy(out=vrep[:].rearrange("p (k g n) -> p k g n", k=2, g=4),
                                                    in_=vTs[:].unsqueeze(2).to_broadcast([64, 2, 4, NS])), reads=["vTs"], writes=["vrep"])
                  b = C.bank()
                  op("pe", lambda: pe.transpose(PS(b, 64), krep[:], ident[0:64, 0:64]), reads=["krep", "ident"], writes=[("ps", b)], signal=False)
                  op("pe", lambda: pe.transpose(PS(b, 64, o=64), vrep[:], ident[0:64, 0:64]), reads=["vrep", "ident"], writes=[("ps", b)])
                  op("dve", lambda: dve.tensor_copy(out=kn128[:], in_=PS(b, 64)), reads=[("ps", b)], writes=["kn128"])
                  op("dve", lambda: dve.tensor_copy(out=vn128[:], in_=PS(b, 64, o=64)), reads=[("ps", b)], writes=["vn128"])
                  for ch in range(8):
                      Kc = Kcs[ch % 2]
                      op("dve", lambda: dve.tensor_tensor(out=tmpc[:], in0=Kc[:], in1=qs128[:].unsqueeze(1).to_broadcast([128, 16, 64]), op=ALU.mult),
                         reads=["Kc%d" % (ch % 2), "qs128"], writes=["tmpc"])
                      op("dve", lambda: dve.tensor_reduce(out=scr[:, 16 * ch:16 * ch + 16], in_=tmpc[:], op=ALU.add, axis=AX.X), reads=["tmpc"], writes=["scr"])
                      if ch + 2 < 8:
                          load_kv(Kcs[ch % 2], ck, ch + 2, "Kc%d" % (ch % 2))
                  op("dve", lambda: dve.tensor_tensor(out=part[:], in0=kn128[:], in1=qs128[:], op=ALU.mult), reads=["kn128", "qs128"], writes=["part"])
                  op("dve", lambda: dve.tensor_reduce(out=sm[:, 0:1], in_=part[:], op=ALU.add, axis=AX.X), reads=["part"], writes=["sm0"])
                  op("dve", lambda: dve.scalar_tensor_tensor(out=scr[:], in0=scr[:], scalar=0.125, in1=sbias[:], op0=ALU.mult, op1=ALU.add),
                     reads=["scr", "smallc"], writes=["scr"])
                  op("act", lambda: act.activation(out=Pm[:], in_=scr[:], func=AF.Exp), reads=["scr"], writes=["Pm"])
                  op("act", lambda: act.activation(out=sm[:, 1:2], in_=sm[:, 0:1], func=AF.Exp, scale=0.125), reads=["sm0"], writes=["sm1"])
                  op("dve", lambda: dve.tensor_reduce(out=sm[:, 2:3], in_=Pm[:], op=ALU.add, axis=AX.X), reads=["Pm"], writes=["sm2"])
                  op("dve", lambda: dve.tensor_tensor(out=sm[:, 2:3], in0=sm[:, 2:3], in1=sm[:, 1:2], op=ALU.add), reads=["sm2", "sm1"], writes=["sm2"])
                  op("dve", lambda: dve.tensor_tensor(out=sm[:, 2:3], in0=sm[:, 2:3], in1=essamp[:, l:l + 1], op=ALU.add), reads=["sm2", "essamp"], writes=["sm2"])
                  op("dve", lambda: dve.reciprocal(out=sm[:, 3:4], in_=sm[:, 2:3]), reads=["sm2"], writes=["sm3"])
                  op("dve", lambda: dve.tensor_scalar(out=acc[:], in0=vn128[:], scalar1=sm[:, 1:2], scalar2=None, op0=ALU.mult), reads=["vn128", "sm1"], writes=["acc"])
                  for ch in range(8):
                      Vc = Vcs[ch % 2]
                      op("dve", lambda: dve.tensor_tensor(out=tmpc[:], in0=Vc[:], in1=Pm[:, 16 * ch:16 * ch + 16].unsqueeze(2).to_broadcast([128, 16, 64]), op=ALU.mult),
                         reads=["Vc%d" % (ch % 2), "Pm"], writes=["tmpc"])
                      op("dve", lambda: dve.tensor_reduce(out=part[:], in_=tmpc[:].rearrange("p s d -> p d s"), op=ALU.add, axis=AX.X), reads=["tmpc"], writes=["part"])
                      op("dve", lambda: dve.tensor_tensor(out=acc[:], in0=acc[:], in1=part[:], op=ALU.add), reads=["acc", "part"], writes=["acc"])
                      if ch + 2 < 8:
                          load_kv(Vcs[ch % 2], cv, ch + 2, "Vc%d" % (ch % 2))
                  op("dve", lambda: dve.tensor_scalar(out=acc[:], in0=acc[:], scalar1=sm[:, 3:4], scalar2=None, op0=ALU.mult), reads=["acc", "sm3"], writes=["acc"])
                  b = C.bank()
                  for e in range(2):
                      op("dve", lambda: dve.tensor_scalar(out=accw[:, 64 * e:64 * e + 64], in0=acc[:], scalar1=emask[:, e:e + 1], scalar2=None, op0=ALU.mult),
                         reads=["acc", "emask"], writes=["accw"])
                  op("pe", lambda: pe.matmul(PS(b, 64), lhsT=accw[:], rhs=selm[:], start=True, stop=True), reads=["accw", "selm"], writes=[("ps", b)])
                  op("dve", lambda: dve.tensor_copy(out=attnTs[:], in_=ps[:, b, 0:64].rearrange("p (j n) -> p j n", n=NS)), reads=[("ps", b)], writes=["attnT"])

                  def h_apply_s(a_, bb, hs):
                      op("dve", lambda: dve.tensor_tensor(out=hs, in0=a_, in1=h0s[:], op=ALU.mult), reads=["wk2", "h0s"], writes=["wk3"])
                      op("dve", lambda: dve.tensor_tensor(out=hs, in0=hs, in1=bb, op=ALU.add), reads=["wk3", "wk1"], writes=["wk3"])
                      bt = C.bank()
                      for c in range(2):
                          op("pe", lambda: pe.transpose(PS(bt, 128, 0, NS, 128 * c), hs[:, c, :], ident[:, :]), reads=["wk3", "ident"], writes=[("ps", bt)], signal=(c == 1))
                      op("dve", lambda: dve.tensor_copy(out=sths[:], in_=PS(bt, 256, 0, NS)), reads=[("ps", bt)], writes=["sths"])
                      dma("sp", (s_h[l], sths[:]), reads=["sths"], key="sths")

                  zwin = []
                  for gi, (p0, p1, c, w) in enumerate([(0, 64, 0, 2), (64, 128, 0, 4), (0, 64, 1, 8), (64, 128, 1, 16)]):
                      op("dve", lambda: dve.tensor_reduce(out=wins[p0:p1, c, :], in_=zps[p0:p1, c, :, 16 - w:16], op=ALU.add, axis=AX.X), reads=["zp"], writes=["zw"])
                      zwin.append((p0, p1, c, wins[p0:p1, c, :], w, "zw"))
                  lru_pool_common(W, l, N, lambda c, tap: xrs[:, c, :, tap], xcs[:], grs[:], zwin,
                                  lambda c, p0, p1: zps[p0:p1, c, :, 15], h_apply_s, mixbs[:])
                  wout_ln(W, l, c0, N, attnTs, mixbs, xkey)
              C.barrier()
              if STOP == 'sample%d' % l: return

          with contextlib.ExitStack() as ph:
              W = {}
              NSLOT = 3
              w1s = [sbt(ph, "w1s%d" % i, [128, 8, 512], BF16) for i in range(NSLOT)]
              w2s = [sbt(ph, "w2s%d" % i, [128, 4, 1024], BF16) for i in range(NSLOT)]
              hT = [sbt(ph, "hT%d" % i, [128, 4, 512], BF16) for i in range(2)]
              rt = [sbt(ph, "rt0", [128, 512])] * 2
              Wl = []
              for q_ in range(2):
                  d_ = {"ybf": sbt(ph, "ybfF%d" % q_, [128, 4, 256], BF16), "ybfk": ["ybfF%d" % q_]}
                  st_ = sbt(ph, "stF%d" % q_, [128, 3, 256])
                  d_["st"] = [st_[:, 0, :], st_[:, 1, :], st_[:, 2, :], st_[:, 1, :]]
                  d_["stk"] = ["stF%d_0" % q_, "stF%d_1" % q_, "stF%d_2" % q_, "stF%d_1" % q_]
                  d_["banks"] = (6, 7) if q_ == 0 else (4, 5)
                  Wl.append(d_)
              ost = [sbt(ph, "ost%d" % i, [128, 1024]) for i in range(2)]

              def load_slice(j):
                  s = j % NSLOT
                  for i_ in range(4):
                      dma("pool", (w1s[s][:, :, 128 * i_:128 * i_ + 128], w_ff1[l, :, 512 * j + 128 * i_:512 * j + 128 * i_ + 128].rearrange("(k p) n -> p k n", p=128)),
                          writes=["w1s%d_%d" % (s, i_)], key="w1s%d_%d" % (s, i_))
                  for h_ in range(2):
                      dma("pool", (w2s[s][:, 2 * h_:2 * h_ + 2, :], w_ff2[l, 512 * j + 256 * h_:512 * j + 256 * h_ + 256, :].rearrange("(i p) n -> p i n", p=128)),
                          writes=["w2s%d_%d" % (s, h_)], key="w2s%d_%d" % (s, h_))

              for j in range(NSLOT):
                  load_slice(j)
              blocks = [(512 * i, 512) for i in range(4)]
              items = [(j, c0, N) for j in range(5) for (c0, N) in blocks]
              items += [(j, c0, N) for (c0, N) in blocks for j in (5, 6, 7)]
              last_of_slice = {}
              for ii, (j_, _c, _n) in enumerate(items):
                  last_of_slice[j_] = ii

              hTs = [sbt(ph, "hTs%d" % i, [128, 4, NS], BF16) for i in range(2)]
              rts = sbt(ph, "rts", [128, NS])

              def ff1(idx):
                  j, c0, N = items[idx]
                  s = j % NSLOT
                  hb = idx % 2
                  xk = "xF%d" % c0
                  mrg = (c0 == 1536)
                  for i in range(4):
                      b = C.bank()
                      b2 = C.bank() if mrg else None
                      for k in range(8):
                          op("pe", lambda: pe.matmul(PS(b, N), lhsT=w1s[s][:, k, 128 * i:128 * i + 128], rhs=xb[:, k, c0:c0 + N], start=(k == 0), stop=(k == 7)),
                             reads=["w1s%d_%d" % (s, i), xk + "_0b", xk + "_1b"], writes=[("ps", b)], signal=(k == 7))
                          if mrg:
                              op("pe", lambda: pe.matmul(PS(b2, NS), lhsT=w1s[s][:, k, 128 * i:128 * i + 128], rhs=xb[:, k, T:T + NS], start=(k == 0), stop=(k == 7)),
                                 reads=["w1s%d_%d" % (s, i), "xF2048_0b"], writes=[("ps", b2)], signal=(k == 7))
                      op("act", lambda: act.activation(out=rt[0][:, 0:N], in_=PS(b, N), func=AF.Relu), reads=[("ps", b)], writes=["rt0"])
                      op("act", lambda: act.activation(out=hT[hb][:, i, 0:N], in_=rt[0][:, 0:N], func=AF.Square), reads=["rt0"], writes=["hT%d" % hb])
                      if mrg:
                          op("act", lambda: act.activation(out=rts[:, :], in_=PS(b2, NS), func=AF.Relu), reads=[("ps", b2)], writes=["rts"])
                          op("act", lambda: act.activation(out=hTs[hb][:, i, :], in_=rts[:, :], func=AF.Square), reads=["rts"], writes=["hTs%d" % hb])
                      yield

              def ff2(idx):
                  j, c0, N = items[idx]
                  s = j % NSLOT
                  hb = idx % 2
                  xk = "xF%d" % c0
                  mrg = (c0 == 1536)
                  for m in range(8):
                      b = C.bank()
                      b2 = C.bank() if mrg else None
                      for i in range(4):
                          op("pe", lambda: pe.matmul(PS(b, N), lhsT=w2s[s][:, i, 128 * m:128 * m + 128], rhs=hT[hb][:, i, 0:N], start=(i == 0), stop=(i == 3)),
                             reads=["w2s%d_%d" % (s, i // 2), "hT%d" % hb], writes=[("ps", b)], signal=(i == 3))
                          if mrg:
                              op("pe", lambda: pe.matmul(PS(b2, NS), lhsT=w2s[s][:, i, 128 * m:128 * m + 128], rhs=hTs[hb][:, i, :], start=(i == 0), stop=(i == 3)),
                                 reads=["w2s%d_%d" % (s, i // 2), "hTs%d" % hb], writes=[("ps", b2)], signal=(i == 3))
                      for (bq, cq, nq, kq) in ([(b, c0, N, [xk + "_0", xk + "_1"])] + ([(b2, T, NS, ["xF2048_0"])] if mrg else [])):
                          xs_ = xf[:, m, cq:cq + nq]
                          if j == 0:
                              op("dve", lambda: dve.scalar_tensor_tensor(out=xs_, in0=xs_, scalar=ALPHA, in1=PS(bq, nq), op0=ALU.mult, op1=ALU.add),
                                 reads=[("ps", bq)] + kq, writes=kq)
                          else:
                              op("dve", lambda: dve.tensor_tensor(out=xs_, in0=xs_, in1=PS(bq, nq), op=ALU.add), reads=[("ps", bq)] + kq, writes=kq)
                      yield

              otile = [0]

              def ln2_sub(cc, nn, xk, Wq):
                  yield from layer_norm_g(Wq, l, 2, cc, nn, xk, banks=Wq["banks"], slack=1)
                  if l == L - 1:
                      ntile = 2 if nn == 256 else 1
                      for ti in range(ntile):
                          n = 128 if nn == 256 else NS
                          t0 = cc + 128 * ti
                          o = ost[otile[0] % 2]
                          ok = "ost%d" % (otile[0] % 2)
                          otile[0] += 1
                          for hb in range(2):
                              b = C.bank()
                              for mm in range(4):
                                  m = 4 * hb + mm
                                  op("pe", lambda: pe.transpose(PS(b, 128, 0, n, 128 * mm), xf[:, m, t0:t0 + n], ident[:, :]),
                                     reads=[xk, "ident"], writes=[("ps", b)], signal=(mm == 3))
                              if hb == 0:
                                  op("act", lambda: act.activation(out=o[0:n, 0:512], in_=PS(b, 512, 0, n), func=AF.Copy), reads=[("ps", b)], writes=[ok])
                              else:
                                  op("dve", lambda: dve.tensor_copy(out=o[0:n, 512:1024], in_=PS(b, 512, 0, n)), reads=[("ps", b)], writes=[ok])
                          dst = yp[t0:t0 + 128, :] if nn == 256 else ys[:, :]
                          dma("sp", (dst, o[0:n, :]), reads=[ok], key=ok)
                          yield

              d_ = {"ybf": w1s[2][:].rearrange("p k n -> p (k n)")[:, 0:1024].rearrange("p (m n) -> p m n", n=256),
                    "ybfk": ["w1s2_0", "w1s2_1", "w1s2_2", "w1s2_3"]}
              st_ = w2s[2][:].rearrange("p k n -> p (k n)")[:, 0:1536].bitcast(F32).rearrange("p (m n) -> p m n", n=256)
              d_["st"] = [st_[:, 0, :], st_[:, 1, :], st_[:, 2, :], st_[:, 1, :]]
              d_["stk"] = ["w2s2_0", "w2s2_0", "w2s2_0", "w2s2_0"]
              d_["banks"] = (2, 3)
              Wl.append(d_)

              lnq = []
              nln = [0]

              def delayed(g_, k_):
                  for _ in range(k_):
                      yield
                  yield from g_

              def ffn_main():
                  yield from ff1(0)
                  for idx in range(len(items)):
                      if idx + 1 < len(items):
                          yield from ff1(idx + 1)
                      yield from ff2(idx)
                      j, c0, N = items[idx]
                      if idx + 3 < len(items) and items[idx + 3][0] == 7 and items[idx + 2][0] == 6 and items[idx + 1][0] == 5 and items[idx][0] == 4:
                          C.bank_pool = list(range(4))
                      if j == 7:
                          lnq.append((c0, 256, "xF%d_0" % c0))
                          lnq.append((c0 + 256, 256, "xF%d_1" % c0))
                          if c0 == 1536:
                              lnq.append((T, NS, "xF2048_0"))
                      if last_of_slice[j] == idx and j + NSLOT < 8:
                          load_slice(j + NSLOT)
                      if j == 3 and last_of_slice[j] == idx and l + 1 < L:
                          load_winb(l + 1)

              C.bank_pool = list(range(8))
              gmain = ffn_main()
              alive = True
              slots = [None, None, None]
              while alive or lnq or any(g_ is not None for g_ in slots):
                  if alive:
                      try:
                          next(gmain)
                      except StopIteration:
                          alive = False
                          C.bank_pool = [0, 1]
                  nslots = 2 if alive else 3
                  for q_ in range(nslots):
                      if slots[q_] is None and lnq:
                          cc_, nn_, xk_ = lnq.pop(0)
                          slots[q_] = delayed(ln2_sub(cc_, nn_, xk_, Wl[q_]), 3 if alive else 0)
                      if slots[q_] is not None:
                          try:
                              next(slots[q_])
                          except StopIteration:
                              slots[q_] = None
              C.bank_pool = list(range(8))
          C.barrier()
          if STOP == 'ln2%d' % l: return
    run_layers()
    C.finish()
    top.close()
    return nc


def _consts():
    slopes = np.exp2(-8.0 * (np.arange(8, dtype=np.float32) + 1.0) / 8).astype(np.float32)
    s = np.arange(128)[:, None]
    q = np.arange(128)[None, :]
    bcur = np.full((2, 128, 4, 128), NEG, np.float32)
    bprev = np.full((2, 128, 4, 128), NEG, np.float32)
    for kv in range(2):
        for g in range(4):
            sl = slopes[4 * kv + g]
            d = (q - s).astype(np.float32)
            bcur[kv, :, g, :] = np.where(s <= q, -sl * d, NEG)
            d2 = (q - s + 128).astype(np.float32)
            bprev[kv, :, g, :] = np.where(s >= q, -sl * d2, NEG)
    sbias = np.zeros((128, 128), np.float32)
    for h in range(8):
        sbias[16 * h:16 * h + 16, :] = -slopes[h] * (128 - np.arange(128, dtype=np.float32))[None, :]
    return bcur.reshape(2, 128, 512), bprev.reshape(2, 128, 512), sbias


_NC = None


def kernel(x_prompt, x_sample, cache_k, cache_v, state_h, state_conv, state_pool,
           w_in, attn_sinks, conv_w, conv_b, gate_a_w, gate_a_b, gate_x_w, gate_x_b, lru_lambda,
           pool_w, pool_scale, w_out, ln1_g, ln1_b, w_ff1, w_ff2, ln2_g, ln2_b):
    global _NC
    f = lambda a: np.ascontiguousarray(np.asarray(a, dtype=np.float32))
    bcur, bprev, sbias = _consts()
    ident = np.eye(128, dtype=np.float32)
    shared = dict(w_in=f(w_in), sinks=f(attn_sinks), conv_w=f(conv_w), conv_b=f(conv_b), ga_w=f(gate_a_w), ga_b=f(gate_a_b),
                  gx_w=f(gate_x_w), gx_b=f(gate_x_b), lam=f(lru_lambda), pool_w=f(pool_w), pool_s=f(pool_scale),
                  w_out=f(w_out), ln1_g=f(ln1_g), ln1_b=f(ln1_b), w_ff1=f(w_ff1), w_ff2=f(w_ff2), ln2_g=f(ln2_g), ln2_b=f(ln2_b),
                  ident=ident, bcur=bcur, bprev=bprev, sbias=sbias)
    emask = np.zeros((128, 2), np.float32); sel = np.zeros((128, 64), np.float32)
    for p in range(128):
        h, b_ = p // 16, p % 16
        emask[p, h % 2] = 1.0
        sel[p, (h // 2) * 16 + b_] = 1.0
    xpr = f(x_prompt); xsa = f(x_sample)
    ckk = f(cache_k).reshape(L, 128, 128, 128); cvv = f(cache_v).reshape(L, 128, 128, 128)
    shh = f(state_h); scc = f(state_conv); spp = f(state_pool)
    in_maps = []
    for c in range(NCORES):
        seq, half = c // 2, c % 2
        b0 = NS * c
        pcorr = np.ones((128, 2, 16), np.float32)
        if half == 0:
            for gi, w in enumerate((2, 4, 8, 16)):
                cc, p0 = gi // 2, 64 * (gi % 2)
                t = np.arange(16, dtype=np.float32)
                pcorr[p0:p0 + 64, cc, :] = (w / np.minimum(t + 1.0, float(w)))[None, :]
        m = dict(shared)
        m.update(xp=np.ascontiguousarray(xpr[seq, T * half:T * half + T]), xs=np.ascontiguousarray(xsa[b0:b0 + NS, 0]),
                 ck=np.ascontiguousarray(ckk[:, b0:b0 + NS]), cv=np.ascontiguousarray(cvv[:, b0:b0 + NS]),
                 sh=np.ascontiguousarray(shh[:, b0:b0 + NS]),
                 sc=np.ascontiguousarray(scc[:, b0:b0 + NS].reshape(L, NS * 3, 256)),
                 spool=np.ascontiguousarray(spp[:, b0:b0 + NS].reshape(L, NS * 15, 256)),
                 nm0=np.full((128, 1), NEG if half == 0 else 0.0, np.float32),
                 flag=np.full((128, 1), 0.0 if half == 0 else 1.0, np.float32),
                 pcorr=pcorr, st_in=np.zeros((L, 147, 256), np.float32), emask=emask, sel=sel)
        in_maps.append(m)
    if _NC is None:
        _NC = build()
    res = run_bass_kernel_spmd(_NC, in_maps, core_ids=list(range(NCORES))).results
    y_prompt = np.zeros((4, 4096, 1024), np.float32)
    for c in range(NCORES):
        y_prompt[c // 2, T * (c % 2):T * (c % 2) + T] = res[c]["yp"]
    y_sample = np.concatenate([res[c]["ys"] for c in range(NCORES)], 0).reshape(128, 1, 1024)
    sto = np.stack([res[2 * s + 1]["st_out"] for s in range(4)], 1)
    p_k = np.ascontiguousarray(sto[:, :, 0:128, 0:128]).reshape(L, 4, 128, 2, 64)
    p_v = np.ascontiguousarray(sto[:, :, 0:128, 128:256]).reshape(L, 4, 128, 2, 64)
    p_conv = np.ascontiguousarray(sto[:, :, 128:131, :])
    p_pool = np.ascontiguousarray(sto[:, :, 131:146, :])
    p_h = np.ascontiguousarray(sto[:, :, 146, :])
    cat = lambda k: np.concatenate([res[c][k] for c in range(NCORES)], 1)
    s_k = cat("s_k").reshape(L, 128, 128, 2, 64); s_v = cat("s_v").reshape(L, 128, 128, 2, 64)
    return (y_prompt, y_sample, p_k, p_v, p_h, p_conv, p_pool, s_k, s_v, cat("s_h"), cat("s_conv"), cat("s_pool"))
```

```python
import contextlib
import numpy as np
import concourse.bass as bass
import concourse.mybir as mybir
from concourse.bass_utils import run_bass_kernel_spmd

F32 = mybir.dt.float32
BF16 = mybir.dt.bfloat16
AF = mybir.ActivationFunctionType
ALU = mybir.AluOpType
AX = mybir.AxisListType

NCORES = 8
T = 2048
NS = 16
TT = T + NS
L = 2
ALPHA = float((2.0 * L) ** 0.25)
EPS = 1e-5
NEG = -1e30
NPV = 50
USE_CC = False
STOP = None
NOSELF = False
XMODE = 2


class _Stop(Exception):
    pass


class Ctx:
    def __init__(self, nc):
        self.nc = nc
        self.engs = {"pe": nc.tensor, "act": nc.scalar, "dve": nc.vector, "pool": nc.gpsimd, "sp": nc.sync}
        self.esem = {e: nc.alloc_semaphore(name="sem_" + e) for e in ["pe", "act", "dve", "pool"]}
        self.ecnt = {e: 0 for e in self.esem}
        self.seen = {e: {} for e in self.engs}
        self.res = {}
        self.dsem = {}
        self.nbank = 0
        self.bank_pool = list(range(8))
        self.ccs = []

    def bank(self):
        b = self.bank_pool[self.nbank % len(self.bank_pool)]
        self.nbank += 1
        return b

    def _wait(self, eng, tok):
        sem, val, src = tok
        if src == "pe" and eng == "pe":
            return
        if NOSELF and src == eng:
            return
        d = self.seen[eng]
        if d.get(id(sem), 0) >= val:
            return
        self.engs[eng].wait_ge(sem, val)
        d[id(sem)] = val

    def _deps(self, eng, reads, writes):
        for k in reads:
            st = self.res.get(k)
            if st and st["w"]:
                self._wait(eng, st["w"])
        for k in writes:
            st = self.res.get(k)
            if st:
                if st["w"]:
                    self._wait(eng, st["w"])
                for r in st["r"]:
                    self._wait(eng, r)

    def _reg(self, tok, reads, writes):
        for k in reads:
            self.res.setdefault(k, {"w": None, "r": []})["r"].append(tok)
        for k in writes:
            self.res[k] = {"w": tok, "r": []}

    def op(self, eng, fn, reads=(), writes=(), signal=True):
        psr = [k for k in reads if isinstance(k, tuple) and k[0] == "ps"]
        if psr:
            reads = [k for k in reads if k not in psr]
            writes = list(writes) + psr
        self._deps(eng, reads, writes)
        ins = fn()
        if signal:
            self.ecnt[eng] += 1
            ins.then_inc(self.esem[eng], 1)
            val = self.ecnt[eng]
        else:
            val = self.ecnt[eng] + 1
        self._reg((self.esem[eng], val, eng), reads, writes)

    def dma(self, q, pairs, reads=(), writes=(), key=None, **kw):
        if not isinstance(pairs, list):
            pairs = [pairs]
        self._deps(q, reads, writes)
        if key not in self.dsem:
            self.dsem[key] = [self.nc.alloc_semaphore(name="d_" + str(key)), 0]
        ent = self.dsem[key]
        if ent[1] > 0:
            self._wait(q, (ent[0], ent[1], "dma"))
        for out, in_ in pairs:
            self.engs[q].dma_start(out=out, in_=in_, **kw).then_inc(ent[0], 16)
            ent[1] += 16
        self._reg((ent[0], ent[1], "dma"), reads, writes)

    def collective(self, in_ap, out_ap, reads=(), writes=()):
        self._deps("pool", reads, writes)
        sem = self.nc.alloc_semaphore(name="cc%d" % len(self.ccs))
        self.ccs.append(sem)
        self.nc.gpsimd.collective_compute("AllGather", ALU.bypass, replica_groups=[[0, 1], [2, 3], [4, 5], [6, 7]],
                                          ins=[in_ap], outs=[out_ap]).then_inc(sem, 1)
        self._reg((sem, 1, "cc"), reads, writes)

    def barrier(self):
        toks = [(self.esem[e], self.ecnt[e], e) for e in self.esem if self.ecnt[e] > 0]
        toks += [(s, c, "dma") for s, c in self.dsem.values() if c > 0]
        for e in self.engs:
            for t in toks:
                if t[2] == e:
                    continue
                self._wait(e, t)
        self.res = {}

    def finish(self):
        for s, c in self.dsem.values():
            if c > 0:
                self._wait("sp", (s, c, "dma"))
        for e in self.esem:
            if self.ecnt[e] > 0:
                self._wait("sp", (self.esem[e], self.ecnt[e], e))


def build():
    nc = bass.Bass("TRN2", target_bir_lowering=False)
    C = Ctx(nc)

    def din(name, shape):
        return nc.dram_tensor(name, shape, F32, kind="ExternalInput").ap()

    def dout(name, shape):
        return nc.dram_tensor(name, shape, F32, kind="ExternalOutput").ap()

    xp = din("xp", [T, 1024]); xs = din("xs", [NS, 1024])
    ck = din("ck", [L, NS, 128, 128]); cv = din("cv", [L, NS, 128, 128])
    sh = din("sh", [L, NS, 256]); sc = din("sc", [L, NS * 3, 256]); spool = din("spool", [L, NS * 15, 256])
    w_in = din("w_in", [L, 1024, 1536]); sinks = din("sinks", [L, 8])
    conv_w = din("conv_w", [L, 4, 256]); conv_b = din("conv_b", [L, 256])
    ga_w = din("ga_w", [L, 4, 64, 64]); ga_b = din("ga_b", [L, 256])
    gx_w = din("gx_w", [L, 4, 64, 64]); gx_b = din("gx_b", [L, 256])
    lam = din("lam", [L, 256]); pool_w = din("pool_w", [L, 4, 64, 64]); pool_s = din("pool_s", [L, 256])
    w_out = din("w_out", [L, 1024, 1024]); ln1_g = din("ln1_g", [L, 1024]); ln1_b = din("ln1_b", [L, 1024])
    w_ff1 = din("w_ff1", [L, 1024, 4096]); w_ff2 = din("w_ff2", [L, 4096, 1024])
    ln2_g = din("ln2_g", [L, 1024]); ln2_b = din("ln2_b", [L, 1024])
    ident_d = din("ident", [128, 128]); bcur_d = din("bcur", [2, 128, 512]); bprev_d = din("bprev", [2, 128, 512])
    sbias_d = din("sbias", [128, 128]); nm0_d = din("nm0", [128, 1]); flag_d = din("flag", [128, 1])
    pcorr_d = din("pcorr", [128, 2, 16]); st_in = din("st_in", [L, 147, 256])
    emask_d = din("emask", [128, 2]); sel_d = din("sel", [128, 64])

    cc1_in = nc.dram_tensor("cc1_in", [L, 146, 256], F32, kind="Internal").ap()
    cc1_out = nc.dram_tensor("cc1_out", [L, 292, 256], F32, kind="Internal").ap()
    cc2_in = nc.dram_tensor("cc2_in", [L, 1, 256], F32, kind="Internal").ap()
    cc2_out = nc.dram_tensor("cc2_out", [L, 2, 256], F32, kind="Internal").ap()
    yp = dout("yp", [T, 1024]); ys = dout("ys", [NS, 1024]); st_out = dout("st_out", [L, 147, 256])
    s_k = dout("s_k", [L, NS, 128, 128]); s_v = dout("s_v", [L, NS, 128, 128])
    s_h = dout("s_h", [L, NS, 256]); s_conv = dout("s_conv", [L, NS, 3, 256]); s_pool = dout("s_pool", [L, NS, 15, 256])

    op = C.op
    dma = C.dma
    pe, act, dve, pool = nc.tensor, nc.scalar, nc.vector, nc.gpsimd

    top = contextlib.ExitStack()

    uid = [0]

    def sbt(stack, name, shape, dt=F32):
        uid[0] += 1
        return stack.enter_context(nc.sbuf_tensor("sb%d_%s" % (uid[0], name), shape, dt))

    ps = top.enter_context(nc.psum_tensor("psum_all", [128, 8, 512], F32))
    xf = sbt(top, "xf", [128, 8, TT]); xb = sbt(top, "xb", [128, 8, TT], BF16)
    ident = sbt(top, "ident", [128, 128]); ones = sbt(top, "ones", [128, 128], BF16)
    bcur = sbt(top, "bcur", [128, 2, 512], BF16); bprev = sbt(top, "bprev", [128, 2, 512], BF16)
    sbias = sbt(top, "sbias", [128, 128]); nm0 = sbt(top, "nm0", [128, 1]); flag = sbt(top, "flag", [128, 1])
    pcorr = sbt(top, "pcorr", [128, 2, 16]); pv = sbt(top, "pv", [128, 2 * NPV]); der = sbt(top, "der", [128, L, 8])
    es64 = sbt(top, "es64", [128, 16]); essamp = sbt(top, "essamp", [128, 2])
    wbd = sbt(top, "wbd", [128, L, 6, 128], BF16)
    cneg = sbt(top, "cneg", [128, 256], BF16)
    emask = sbt(top, "emask", [128, 2]); selm = sbt(top, "selm", [128, 64], BF16)
    winb = sbt(top, "winb", [128, 8, 1536], BF16)

    def load_winb(l):
        for kk in range(4):
            dma("pool", (winb[:, 2 * kk:2 * kk + 2, :], w_in[l, 256 * kk:256 * kk + 256, :].rearrange("(k p) n -> p k n", p=128)),
                writes=["winb"], key="winb%d" % kk)

    def PS(b, n=512, p0=0, p1=128, o=0):
        return ps[p0:p1, b, o:o + n]

    dma("sp", (ident[:], ident_d[:, :]), writes=["ident"], key="c0")
    dma("pool", (bcur[:], bcur_d.rearrange("k s n -> s k n")), writes=["bcur"], key="c1")
    dma("pool", (bprev[:], bprev_d.rearrange("k s n -> s k n")), writes=["bprev"], key="c2")
    dma("sp", [(sbias[:], sbias_d[:, :]), (nm0[:], nm0_d[:, :]), (flag[:], flag_d[:, :]), (pcorr[:], pcorr_d[:, :, :])],
        writes=["smallc"], key="c3")
    dma("sp", (emask[:], emask_d[:, :]), writes=["emask"], key="c8")
    dma("pool", (selm[:], sel_d[:, :]), writes=["selm"], key="c9")
    op("dve", lambda: dve.memset(ones[:], 1.0), writes=["ones"])
    op("dve", lambda: dve.memset(cneg[:], -0.5), writes=["cneg"])
    op("dve", lambda: dve.memset(wbd[:], 0.0), writes=["wbd"])
    if STOP == 'i1':
        C.finish(); top.close(); return nc
    with contextlib.ExitStack() as ph:
        prow = sbt(ph, "prow", [2 * NPV, 128])
        plist = [(conv_w, 0, 8), (conv_b, 8, 2), (ga_b, 10, 2), (gx_b, 12, 2), (lam, 14, 2), (pool_s, 16, 2),
                 (ln1_g, 18, 8), (ln1_b, 26, 8), (ln2_g, 34, 8), (ln2_b, 42, 8)]
        pairs = []
        for l in range(L):
            for (t, off, n) in plist:
                if t is conv_w:
                    src = t[l].rearrange("t (c p) -> (t c) p", p=128)
                else:
                    src = t[l].rearrange("(c p) -> c p", p=128)
                pairs.append((prow[NPV * l + off:NPV * l + off + n, :], src))
        dma("sp", pairs, writes=["prow"], key="c4")
        b0 = C.bank()
        op("pe", lambda: pe.transpose(PS(b0, 2 * NPV), prow[:], ident[0:2 * NPV, 0:2 * NPV]),
           reads=["prow", "ident"], writes=[("ps", b0)])
        op("dve", lambda: dve.tensor_copy(out=pv[:], in_=PS(b0, 2 * NPV)), reads=[("ps", b0)], writes=["pv"])
        if STOP == 'i2':
            C.finish(); ph.close(); top.close(); return nc
        tl = sbt(ph, "tl", [128, 2])
        for l in range(L):
            pb = NPV * l
            op("act", lambda: act.activation(out=tl[:], in_=pv[:, pb + 14:pb + 16], func=AF.Exp, scale=-1.0),
               reads=["pv"], writes=["tl"])
            op("dve", lambda: dve.tensor_scalar_add(tl[:], tl[:], 1.0), reads=["tl"], writes=["tl"])
            op("act", lambda: act.activation(out=tl[:], in_=tl[:], func=AF.Ln), reads=["tl"], writes=["tl"])
            op("dve", lambda: dve.tensor_scalar_mul(der[:, l, 0:2], tl[:], -4.0), reads=["tl"], writes=["der"])
            op("dve", lambda: dve.tensor_scalar_mul(der[:, l, 2:4], tl[:], -8.0), reads=["tl"], writes=["der"])
            op("dve", lambda: dve.tensor_scalar_mul(der[:, l, 4:6], pv[:, pb + 10:pb + 12], 0.5), reads=["pv"], writes=["der"])
            op("dve", lambda: dve.tensor_scalar_mul(der[:, l, 6:8], pv[:, pb + 12:pb + 14], 0.5), reads=["pv"], writes=["der"])
        if STOP == 'i3':
            C.finish(); ph.close(); top.close(); return nc
        dma("sp", (es64[:], sinks.rearrange("l h -> (l h)").partition_broadcast(128)), writes=["es64"], key="c5")
        if STOP == 'i3a':
            C.finish(); ph.close(); top.close(); return nc
        op("act", lambda: act.activation(out=es64[:], in_=es64[:], func=AF.Exp), reads=["es64"], writes=["es64"])
        pairs = []
        for l in range(L):
            for h in range(8):
                pairs.append((essamp[16 * h:16 * h + 16, l:l + 1], sinks[l, h:h + 1].partition_broadcast(16)))
        dma("sp", pairs, writes=["essamp"], key="c6")
        op("act", lambda: act.activation(out=essamp[:], in_=essamp[:], func=AF.Exp), reads=["essamp"], writes=["essamp"])
        if STOP == 'i4':
            C.finish(); ph.close(); top.close(); return nc
        pairs = []
        for l in range(L):
            for n in range(4):
                c, e = n // 2, n % 2
                for si, wt in ((0, ga_w), (2, gx_w), (4, pool_w)):
                    pairs.append((wbd[64 * e:64 * e + 64, l, si + c, 64 * e:64 * e + 64], wt[l, n]))
        dma("pool", pairs, writes=["wbd"], key="c7")
        if STOP == 'i5':
            C.finish(); ph.close(); top.close(); return nc

        if STOP == 'i6':
            C.finish(); ph.close(); top.close(); return nc
        if STOP == 'i6b':
            C.barrier(); C.finish(); ph.close(); top.close(); return nc

        load_winb(0)
        xst = [sbt(ph, "xst%d" % i, [128, 1024]) for i in range(3)]
        for ti in range(16 if STOP == 'initA' else (1 if STOP == 'initB' else 17)):
            st = xst[ti % 3]
            sk = "xst%d" % (ti % 3)
            n = 128 if ti < 16 else NS
            src = xp[128 * ti:128 * ti + 128, :] if ti < 16 else xs[:, :]
            dma("sp", (st[0:n, :], src), writes=[sk], key=sk)
            for hb in range(0 if XMODE == 0 else 2):
                b = C.bank()
                for mm in range(4):
                    m = 4 * hb + mm
                    op("pe", lambda: pe.transpose(PS(b, n, o=n * mm), st[0:n, 128 * m:128 * m + 128], ident[0:n, 0:n]),
                       reads=[sk, "ident"], writes=[("ps", b)], signal=(mm == 3))
                src_ps = ps[:, b, 0:4 * n].rearrange("p (m n) -> p m n", n=n)
                op("dve", lambda: dve.tensor_copy(out=xf[:, 4 * hb:4 * hb + 4, 128 * ti:128 * ti + n], in_=src_ps),
                   reads=[("ps", b)], writes=[("xf", ti)])
                if XMODE >= 2:
                    op("act", lambda: act.activation(out=xb[:, 4 * hb:4 * hb + 4, 128 * ti:128 * ti + n], in_=src_ps, func=AF.Copy),
                       reads=[("ps", b)], writes=[("xb", ti)])
    C.barrier()
    for l in range(L):
        dma("sp", [(s_k[l, :, 0:127, :], ck[l, :, 1:128, :]), (s_v[l, :, 0:127, :], cv[l, :, 1:128, :]),
                   (s_conv[l, :, 0:2, :], sc[l].rearrange("(b t) n -> b t n", t=3)[:, 1:3, :]),
                   (s_pool[l, :, 0:14, :], spool[l].rearrange("(b t) n -> b t n", t=15)[:, 1:15, :])],
            key="dd%d" % l)
    if STOP in ('init', 'initA', 'initB'):
        C.finish(); top.close(); return nc

    def layer_norm(W, l, which, c0, N, xkey):
        for _ in layer_norm_g(W, l, which, c0, N, xkey):
            pass

    def layer_norm_g(W, l, which, c0, N, xkey, banks=None, slack=0):
        gcol = NPV * l + (18 if which == 1 else 34)
        bcol = gcol + 8
        ybf, ybk = W["ybf"], W["ybfk"]
        k0_, k1_, k2_, k3_ = W["stk"]
        cap = ybf.shape[1]
        b1, b2 = banks if banks is not None else (C.bank(), C.bank())
        for (bb_, fn_) in ((b1, AF.Copy), (b2, AF.Square)):
            for h0 in range(0, 8, cap):
                op("act", lambda: act.activation(out=ybf[:, 0:cap, 0:N], in_=xf[:, h0:h0 + cap, c0:c0 + N], func=fn_), reads=[xkey], writes=ybk)
                yield
                for m in range(h0, h0 + cap):
                    op("pe", lambda: pe.matmul(PS(bb_, N), lhsT=ones[:], rhs=ybf[:, m - h0, 0:N], start=(m == 0), stop=(m == 7)),
                       reads=ybk + ["ones"], writes=[("ps", bb_)], signal=(m == h0 + cap - 1))
            yield
        for _ in range(slack):
            yield
        mean, msq, rstd, nmr = [a_[:, 0:N] for a_ in W["st"]]
        op("dve", lambda: dve.tensor_scalar_mul(mean, PS(b1, N), 1.0 / 1024), reads=[("ps", b1)], writes=[k0_])
        op("dve", lambda: dve.tensor_tensor(out=msq, in0=mean, in1=mean, op=ALU.mult), reads=[k0_], writes=[k1_])
        op("dve", lambda: dve.scalar_tensor_tensor(out=msq, in0=PS(b2, N), scalar=1.0 / 1024, in1=msq, op0=ALU.mult, op1=ALU.subtract),
           reads=[("ps", b2), k1_], writes=[k1_])
        op("dve", lambda: dve.tensor_scalar_add(msq, msq, EPS), reads=[k1_], writes=[k1_])
        op("act", lambda: act.activation(out=rstd, in_=msq, func=AF.Ln), reads=[k1_], writes=[k2_])
        op("act", lambda: act.activation(out=rstd, in_=rstd, func=AF.Exp, scale=-0.5), reads=[k2_], writes=[k2_])
        yield
        for _ in range(slack):
            yield
        op("dve", lambda: dve.scalar_tensor_tensor(out=nmr, in0=mean, scalar=-1.0, in1=rstd, op0=ALU.mult, op1=ALU.mult),
           reads=[k0_, k2_], writes=[k3_])
        xblk = xf[:, :, c0:c0 + N]
        op("dve", lambda: dve.tensor_tensor(out=xblk, in0=xblk, in1=rstd.unsqueeze(1).to_broadcast([128, 8, N]), op=ALU.mult), reads=[xkey, k2_], writes=[xkey])
        yield
        op("dve", lambda: dve.tensor_tensor(out=xblk, in0=xblk, in1=nmr.unsqueeze(1).to_broadcast([128, 8, N]), op=ALU.add), reads=[xkey, k3_], writes=[xkey])
        yield
        for m in range(8):
            xs_ = xf[:, m, c0:c0 + N]
            op("dve", lambda: dve.tensor_scalar(out=xs_, in0=xs_, scalar1=pv[:, gcol + m:gcol + m + 1], scalar2=pv[:, bcol + m:bcol + m + 1],
                                                op0=ALU.mult, op1=ALU.add), reads=[xkey, "pv"], writes=[xkey])
            if m % 4 == 3:
                yield
        op("act", lambda: act.activation(out=xb[:, :, c0:c0 + N], in_=xblk, func=AF.Copy), reads=[xkey], writes=[xkey + "b"])

    def pool_part(W, l, N, zwin, zcur, mixb, first_corr=None):
        pb = NPV * l
        diffb = W["diffb"]
        for (p0, p1, c, win, w, wkey) in zwin:
            if first_corr is not None:
                first_corr(p0, p1, c, win, wkey)
            op("dve", lambda: dve.scalar_tensor_tensor(out=diffb[p0:p1, c, 0:N], in0=win, scalar=1.0 / w, in1=zcur(c, p0, p1),
                                                       op0=ALU.mult, op1=ALU.subtract), reads=[wkey, "zp"], writes=["diffb"])
        bp = C.bank()
        for c in range(2):
            op("pe", lambda: pe.matmul(PS(bp, N, o=256 * c), lhsT=wbd[:, l, 4 + c, :], rhs=diffb[:, c, 0:N], start=True, stop=True),
               reads=["diffb", "wbd"], writes=[("ps", bp)])
        for c in range(2):
            op("act", lambda: act.activation(out=mixb[:, 2 + c, :], in_=PS(bp, N, o=256 * c), func=AF.Identity, scale=pv[:, pb + 16 + c:pb + 17 + c]),
               reads=[("ps", bp), "pv"], writes=["mixb"])

    def lru_part_g(W, l, N, xc_taps, xc, gr, h_apply, finish, sfx=""):
        pb = NPV * l
        wk = W["wk"]
        xcb = W["xcb"]
        for c in range(2):
            op("act", lambda: act.activation(out=xc[:, c, :], in_=xc_taps(c, 0), func=AF.Identity, scale=pv[:, pb + c:pb + c + 1],
                                             bias=pv[:, pb + 8 + c:pb + 9 + c]),
               reads=["xr" + sfx, "pv"], writes=["xc" + sfx])
            for tap in range(1, 4):
                op("dve", lambda: dve.scalar_tensor_tensor(out=xc[:, c, :], in0=xc_taps(c, tap), scalar=pv[:, pb + 2 * tap + c:pb + 2 * tap + c + 1],
                                                           in1=xc[:, c, :], op0=ALU.mult, op1=ALU.add),
                   reads=["xr" + sfx, "pv", "xc" + sfx], writes=["xc" + sfx])
        yield
        op("act", lambda: act.activation(out=xcb[:, :, 0:N], in_=xc, func=AF.Copy), reads=["xc" + sfx], writes=["xcb" + sfx])
        bg, bh = C.bank(), C.bank()
        for c in range(2):
            op("pe", lambda: pe.matmul(PS(bg, N, o=256 * c), lhsT=wbd[:, l, 0 + c, :], rhs=xcb[:, c, 0:N], start=True, stop=True),
               reads=["xcb" + sfx, "wbd"], writes=[("ps", bg)])
            op("pe", lambda: pe.matmul(PS(bh, N, o=256 * c), lhsT=wbd[:, l, 2 + c, :], rhs=xcb[:, c, 0:N], start=True, stop=True),
               reads=["xcb" + sfx, "wbd"], writes=[("ps", bh)])
        tha, thx, a_, a2 = [wk[i][:, :, 0:N] for i in range(4)]
        hs = a2
        for c in range(2):
            op("act", lambda: act.activation(out=tha[:, c, :], in_=PS(bg, N, o=256 * c), func=AF.Tanh, scale=0.5, bias=der[:, l, 4 + c:5 + c]),
               reads=[("ps", bg), "der"], writes=["wk0" + sfx])
            op("act", lambda: act.activation(out=thx[:, c, :], in_=PS(bh, N, o=256 * c), func=AF.Tanh, scale=0.5, bias=der[:, l, 6 + c:7 + c]),
               reads=[("ps", bh), "der"], writes=["wk1" + sfx])
            op("act", lambda: act.activation(out=a_[:, c, :], in_=tha[:, c, :], func=AF.Exp, scale=der[:, l, c:c + 1], bias=der[:, l, c:c + 1]),
               reads=["wk0" + sfx, "der"], writes=["wk2" + sfx])
            op("act", lambda: act.activation(out=a2[:, c, :], in_=tha[:, c, :], func=AF.Exp, scale=der[:, l, 2 + c:3 + c], bias=der[:, l, 2 + c:3 + c]),
               reads=["wk0" + sfx, "der"], writes=["wk3" + sfx])
        yield
        op("dve", lambda: dve.tensor_scalar(out=a2, in0=a2, scalar1=-1.0, scalar2=1.0, op0=ALU.mult, op1=ALU.add), reads=["wk3" + sfx], writes=["wk3" + sfx])
        op("act", lambda: act.activation(out=tha, in_=a2, func=AF.Ln), reads=["wk3" + sfx], writes=["wk0" + sfx])
        op("act", lambda: act.activation(out=tha, in_=tha, func=AF.Exp, scale=0.5), reads=["wk0" + sfx], writes=["wk0" + sfx])
        op("dve", lambda: dve.scalar_tensor_tensor(out=thx, in0=thx, scalar=1.0, in1=xc, op0=ALU.add, op1=ALU.mult),
           reads=["wk1" + sfx, "xc" + sfx], writes=["wk1" + sfx])
        op("dve", lambda: dve.scalar_tensor_tensor(out=thx, in0=thx, scalar=0.5, in1=tha, op0=ALU.mult, op1=ALU.mult),
           reads=["wk1" + sfx, "wk0" + sfx], writes=["wk1" + sfx])
        yield
        h_apply(a_, thx, hs)
        yield
        op("act", lambda: act.activation(out=tha, in_=gr, func=AF.Square), reads=["gr" + sfx], writes=["wk0" + sfx])
        op("dve", lambda: dve.tensor_scalar(out=tha, in0=tha, scalar1=0.044715, scalar2=1.0, op0=ALU.mult, op1=ALU.add), reads=["wk0" + sfx], writes=["wk0" + sfx])
        op("dve", lambda: dve.tensor_tensor(out=tha, in0=tha, in1=gr, op=ALU.mult), reads=["wk0" + sfx, "gr" + sfx], writes=["wk0" + sfx])
        op("act", lambda: act.activation(out=tha, in_=tha, func=AF.Tanh, scale=0.7978845608028654), reads=["wk0" + sfx], writes=["wk0" + sfx])
        op("dve", lambda: dve.scalar_tensor_tensor(out=tha, in0=tha, scalar=1.0, in1=gr, op0=ALU.add, op1=ALU.mult), reads=["wk0" + sfx, "gr" + sfx], writes=["wk0" + sfx])
        yield
        finish(hs, tha, thx)

    def lru_part(W, l, N, xc_taps, xc, gr, h_apply, finish):
        for _ in lru_part_g(W, l, N, xc_taps, xc, gr, h_apply, finish):
            pass

    def lru_pool_common(W, l, N, xc_taps, xc, gr, zwin, zcur, h_apply, mixb, first_corr=None):
        pool_part(W, l, N, zwin, zcur, mixb, first_corr)

        def fin(hs, ge, _):
            op("dve", lambda: dve.scalar_tensor_tensor(out=mixb[:, 0:2, :], in0=hs, scalar=0.5, in1=ge, op0=ALU.mult, op1=ALU.mult),
               reads=["wk3", "wk0"], writes=["mixb"])
        lru_part(W, l, N, xc_taps, xc, gr, h_apply, fin)

    def wout_ln(W, l, c0, N, attn, mixb, xkey, ln=True):
        for _ in wout_g(W, l, c0, N, attn, mixb, xkey):
            pass
        if ln:
            layer_norm(W, l, 1, c0, N, xkey)

    def wout_g(W, l, c0, N, attn, mixb, xkey):
        woa, wob = W["woa"], W["wob"]
        for m in range(8):
            b = C.bank()
            for h in range(4):
                op("pe", lambda: pe.matmul(PS(b, N), lhsT=woa[:, h, 128 * m:128 * m + 128], rhs=attn[:, h, :], start=(h == 0), stop=False),
                   reads=["attnT", "woa"], writes=[("ps", b)], signal=False)
            for j in range(4):
                op("pe", lambda: pe.matmul(PS(b, N), lhsT=wob[:, j, 128 * m:128 * m + 128], rhs=mixb[:, j, :], start=False, stop=(j == 3)),
                   reads=["mixb", "wob"], writes=[("ps", b)], signal=(j == 3))
            op("dve", lambda: dve.scalar_tensor_tensor(out=xf[:, m, c0:c0 + N], in0=xf[:, m, c0:c0 + N], scalar=ALPHA, in1=PS(b, N),
                                                       op0=ALU.mult, op1=ALU.add), reads=[("ps", b), xkey], writes=[xkey])
            if m % 2 == 1:
                yield

    def chk(stage):
        if STOP == stage:
            raise _Stop()

    def run_layers():
      for l in range(L):
          pb = NPV * l
          with contextlib.ExitStack() as ph:
              W = {}
              W["woa"] = woa = sbt(ph, "woa", [128, 4, 1024], BF16)
              W["wob"] = wob = sbt(ph, "wob", [128, 4, 1024], BF16)
              W["wk"] = [sbt(ph, "wk%d" % i, [128, 2, 272]) for i in range(4)]
              W["st"] = [W["wk"][2][:, 0, 0:256], W["wk"][2][:, 1, 0:256], W["wk"][3][:, 0, 0:256], W["wk"][3][:, 1, 0:256]]
              W["stk"] = ["wk2", "wk2", "wk3", "wk3"]
              st_ph, stk_ph = W["st"], W["stk"]
              W["diffb"] = sbt(ph, "diffb", [128, 2, 256], BF16)
              W["tA"] = W["wk"][0][:, 0, 0:256]; W["tAk"] = "wk0"
              W["tB"] = W["wk"][1][:, 0, 0:256]; W["tBk"] = "wk1"
              dma("pool", (woa[:], w_out[l, 0:512, :].rearrange("(j p) n -> p j n", p=128)), writes=["woa"], key="woa")
              dma("pool", (wob[:], w_out[l, 512:1024, :].rearrange("(j p) n -> p j n", p=128)), writes=["wob"], key="wob")

              with contextlib.ExitStack() as pp:
                  rec0 = sbt(pp, "rec0", [128, 2, T], BF16)
                  corr = sbt(pp, "corr", [128, 2, T], BF16)
                  kTb = sbt(pp, "kTb", [64, 2, 384], BF16)
                  Vb = sbt(pp, "Vb", [128, 3, 128], BF16)
                  xr_ext = sbt(pp, "xr_ext", [128, 2, 259])
                  zp_ext = sbt(pp, "zp_ext", [128, 2, 271])
                  hcar = sbt(pp, "hcar", [128, 2]); Acar = sbt(pp, "Acar", [128, 2]); hst = sbt(pp, "hst", [128, 2]); hfin = sbt(pp, "hfin", [128, 2])
                  wkt = W["wk"]

                  with contextlib.ExitStack() as pa:
                      stq = rec0[:].rearrange("p c n -> p (c n)")[:, 0:1536].bitcast(F32)
                      cview = corr[:].rearrange("p c n -> p (c n)")[:, 0:1024].bitcast(F32)
                      sth = cview[:, 0:256]; stc = cview[0:18, 256:512]
                      b1, b2 = C.bank(), C.bank()
                      for k in range(8):
                          op("pe", lambda: pe.matmul(PS(b1), lhsT=xb[:, k, T - 128:T], rhs=winb[:, k, 512:1024], start=(k == 0), stop=(k == 7)),
                             reads=["winb"], writes=[("ps", b1)], signal=(k == 7))
                      for k in range(8):
                          op("pe", lambda: pe.matmul(PS(b2, 256), lhsT=xb[:, k, T - 128:T], rhs=winb[:, k, 1280:1536], start=(k == 0), stop=(k == 7)),
                             reads=["winb"], writes=[("ps", b2)], signal=(k == 7))
                      op("dve", lambda: dve.tensor_copy(out=stq[:, 0:512], in_=PS(b1)), reads=[("ps", b1)], writes=["rec0"])
                      op("dve", lambda: dve.tensor_copy(out=stq[:, 512:768], in_=PS(b2, 256)), reads=[("ps", b2)], writes=["rec0"])
                      dma("sp", [(st_out[l, 0:128, :], stq[:, 0:256]), (st_out[l, 128:131, :], stq[125:128, 256:512]),
                                 (st_out[l, 131:146, :], stq[113:128, 512:768])], reads=["rec0"], key="stq")
                      dma("sp", [(cc1_in[l, 0:128, :], stq[:, 0:256]), (cc1_in[l, 128:131, :], stq[125:128, 256:512]),
                                 (cc1_in[l, 131:146, :], stq[113:128, 512:768])], reads=["rec0"], writes=["cc1in"], key="stq2")
                      C.collective(cc1_in[l], cc1_out[l], reads=["cc1in"], writes=["cc1out"])
                      dma("sp", [(sth[:], cc1_out[l, 0:128, :]), (stc[:], cc1_out[l, 128:146, :])], reads=["cc1out"], writes=["corr"], key="sth")
                      bk = C.bank()
                      for kv in range(2):
                          op("pe", lambda: pe.transpose(PS(bk, 128, 0, 64, 128 * kv), sth[:, 64 * kv:64 * kv + 64], ident[:, :]),
                             reads=["corr", "ident"], writes=[("ps", bk)], signal=(kv == 1))
                      op("dve", lambda: dve.tensor_scalar(out=kTb[:, :, 0:128], in0=ps[0:64, bk, 0:256].rearrange("p (k n) -> p k n", n=128),
                                                          scalar1=flag[0:64, 0:1], scalar2=None, op0=ALU.mult),
                         reads=[("ps", bk), "smallc"], writes=["kTb"])
                      op("dve", lambda: dve.tensor_scalar(out=Vb[:, 0, :], in0=sth[:, 128:256], scalar1=flag[:, 0:1], scalar2=None, op0=ALU.mult),
                         reads=["corr", "smallc"], writes=["Vb"])
                      bk2 = C.bank()
                      for c in range(2):
                          op("pe", lambda: pe.transpose(PS(bk2, 18, o=32 * c), stc[0:18, 128 * c:128 * c + 128], ident[0:18, 0:18]),
                             reads=["corr", "ident"], writes=[("ps", bk2)], signal=(c == 1))
                      for c in range(2):
                          op("dve", lambda: dve.tensor_scalar(out=xr_ext[:, c, 0:3], in0=PS(bk2, 3, o=32 * c), scalar1=flag[:, 0:1], scalar2=None, op0=ALU.mult),
                             reads=[("ps", bk2), "smallc"], writes=["xr0"])
                          op("dve", lambda: dve.tensor_scalar(out=zp_ext[:, c, 0:15], in0=PS(bk2, 15, o=32 * c + 3), scalar1=flag[:, 0:1], scalar2=None, op0=ALU.mult),
                             reads=[("ps", bk2), "smallc"], writes=["zp"])
                  op("dve", lambda: dve.memset(hcar[:], 0.0), writes=["hcar"])
                  op("dve", lambda: dve.memset(Acar[:], 1.0), writes=["Acar"])

                  with contextlib.ExitStack() as pb_:
                      sets = []
                      for i in range(2):
                          d_ = {"gr": sbt(pb_, "gr%d" % i, [128, 2, 256]), "xc": sbt(pb_, "xc%d" % i, [128, 2, 256]),
                                "xcb": sbt(pb_, "xcb%d" % i, [128, 2, 256], BF16)}
                          d_["wk"] = W["wk"] if i == 0 else [sbt(pb_, "wkB%d" % q_, [128, 2, 272]) for q_ in range(4)]
                          d_["xr"] = xr_ext if i == 0 else sbt(pb_, "xr_extB", [128, 2, 259])
                          sets.append(d_)

                      def pre(bi):
                          c0 = 256 * bi
                          N = 256
                          i = bi % 2
                          sx = str(i)
                          S_ = sets[i]
                          xr_i, gr_i, xc_i = S_["xr"], S_["gr"], S_["xc"]
                          for (cb, dst, key) in ((768, xr_i[:, :, 3:259], "xr" + sx), (1024, gr_i[:, :, :], "gr" + sx)):
                              b = C.bank()
                              for c in range(2):
                                  for k in range(8):
                                      op("pe", lambda: pe.matmul(PS(b, N, o=256 * c), lhsT=winb[:, k, cb + 128 * c:cb + 128 * c + 128], rhs=xb[:, k, c0:c0 + N],
                                                                 start=(k == 0), stop=(k == 7)),
                                         reads=["winb"], writes=[("ps", b)], signal=(k == 7 and c == 1))
                              if key.startswith("gr"):
                                  op("act", lambda: act.activation(out=dst, in_=ps[:, b, :].rearrange("p (e n) -> p e n", n=256), func=AF.Copy),
                                     reads=[("ps", b)], writes=[key])
                              else:
                                  op("dve", lambda: dve.tensor_copy(out=dst, in_=ps[:, b, :].rearrange("p (e n) -> p e n", n=256)),
                                     reads=[("ps", b)], writes=[key])
                          if bi > 0:
                              xr_p = sets[1 - i]["xr"]
                              op("dve", lambda: dve.tensor_copy(out=xr_i[:, :, 0:3], in_=xr_p[:, :, 256:259]), reads=["xr" + str(1 - i)], writes=["xr" + sx])
                          yield

                          def h_apply(a_, bb, hs):
                              for c in range(2):
                                  op("dve", lambda: dve.tensor_tensor_scan(out=hs[:, c, :], data0=a_[:, c, :], data1=bb[:, c, :], initial=hcar[:, c:c + 1],
                                                                           op0=ALU.mult, op1=ALU.add), reads=["wk2" + sx, "wk1" + sx, "hcar"], writes=["wk3" + sx])
                              op("dve", lambda: dve.tensor_copy(out=hcar[:, :], in_=hs[:, :, 255]), reads=["wk3" + sx], writes=["hcar"])
                              for c in range(2):
                                  op("dve", lambda: dve.tensor_tensor_scan(out=bb[:, c, :], data0=a_[:, c, :], data1=cneg[:, 0:256], initial=Acar[:, c:c + 1],
                                                                           op0=ALU.mult, op1=ALU.max), reads=["wk2" + sx, "cneg", "Acar"], writes=["wk1" + sx])
                              op("dve", lambda: dve.tensor_copy(out=Acar[:, :], in_=bb[:, :, 255]), reads=["wk1" + sx], writes=["Acar"])

                          def fin(hs, ge, Acum):
                              op("dve", lambda: dve.scalar_tensor_tensor(out=rec0[:, :, c0:c0 + 256], in0=hs, scalar=0.5, in1=ge, op0=ALU.mult, op1=ALU.mult),
                                 reads=["wk3" + sx, "wk0" + sx], writes=["rec0"])
                              op("dve", lambda: dve.scalar_tensor_tensor(out=corr[:, :, c0:c0 + 256], in0=Acum, scalar=0.5, in1=ge, op0=ALU.mult, op1=ALU.mult),
                                 reads=["wk1" + sx, "wk0" + sx], writes=["corr"])
                          yield from lru_part_g(S_, l, N, lambda c, tap: xr_i[:, c, tap:tap + 256], xc_i[:], gr_i[:], h_apply, fin, sfx=sx)

                      pend = [pre(bi) for bi in range(8)]
                      active = []
                      while pend or active:
                          while len(active) < 2 and pend:
                              active.append(pend.pop(0))
                          for g_ in list(active):
                              try:
                                  next(g_)
                              except StopIteration:
                                  active.remove(g_)
                  C.barrier()
                  with nc.allow_non_contiguous_dma(reason="tiny h state"):
                      dma("sp", (cc2_in[l, 0, :].rearrange("(c p) -> p c", p=128), hcar[:, :]), reads=["hcar"], writes=["cc2in"], key="hst")
                  C.collective(cc2_in[l], cc2_out[l], reads=["cc2in"], writes=["cc2out"])
                  with nc.allow_non_contiguous_dma(reason="tiny h state"):
                      dma("sp", (hst[:, :], cc2_out[l, 0, :].rearrange("(c p) -> p c", p=128)), reads=["cc2out"], writes=["hst"], key="hst2")
                  op("dve", lambda: dve.tensor_scalar(out=hst[:], in0=hst[:], scalar1=flag[:, 0:1], scalar2=None, op0=ALU.mult), reads=["hst", "smallc"], writes=["hst"])
                  op("dve", lambda: dve.tensor_tensor(out=hfin[:], in0=Acar[:], in1=hst[:], op=ALU.mult), reads=["Acar", "hst"], writes=["hfin"])
                  op("dve", lambda: dve.tensor_tensor(out=hfin[:], in0=hfin[:], in1=hcar[:], op=ALU.add), reads=["hfin", "hcar"], writes=["hfin"])
                  with nc.allow_non_contiguous_dma(reason="tiny h state"):
                      dma("sp", (st_out[l, 146, :].rearrange("(c p) -> p c", p=128), hfin[:, :]), reads=["hfin"], key="hst3")

                  with contextlib.ExitStack() as pc:
                      qT = sbt(pc, "qT", [64, 8, 256], BF16)
                      attnT = sbt(pc, "attnT", [128, 4, 256], BF16)
                      mixb = sbt(pc, "mixb", [128, 4, 256], BF16)
                      tPy = [sbt(pc, "tPy%d" % i, [128, 1024]) for i in range(2)]
                      tP = [[tPy[i][:, 0:512], tPy[i][:, 512:1024]] for i in range(2)]
                      PT = [[sbt(pc, "PT%d%d" % (i, j_), [128, 512], BF16) for j_ in range(2)] for i in range(2)]
                      W["ybf"] = sbt(pc, "ybfL", [128, 4, 256], BF16)
                      W["ybfk"] = ["ybfL"]
                      stL = sbt(pc, "stL", [128, 3, 256])
                      W["st"] = [stL[:, 0, :], stL[:, 1, :], stL[:, 2, :], stL[:, 1, :]]
                      W["stk"] = ["stL0", "stL1", "stL2", "stL1"]
                      dd = wkt[3][:].rearrange("p c n -> p (c n)")[:, 0:256]
                      S2 = wkt[0][:, :, 0:270]; S4 = wkt[1][:, :, 0:268]
                      def front(bi):
                          c0 = 256 * bi
                          N = 256
                          xkey = "xP%d" % bi
                          xin = [xkey + "b"]

                          def rhs_x(k):
                              return xb[:, k, c0:c0 + N]
                          for j in range(4):
                              b = C.bank()
                              for e in range(2):
                                  h = 2 * j + e
                                  for k in range(8):
                                      op("pe", lambda: pe.matmul(PS(b, N, 0, 64, 256 * e), lhsT=winb[:, k, 64 * h:64 * h + 64], rhs=rhs_x(k),
                                                                 start=(k == 0), stop=(k == 7)),
                                         reads=["winb"] + xin, writes=[("ps", b)], signal=(k == 7 and e == 1))
                              op("act", lambda: act.activation(out=qT[:, 2 * j:2 * j + 2, :], in_=ps[0:64, b, :].rearrange("p (e n) -> p e n", n=256), func=AF.Copy),
                                 reads=[("ps", b)], writes=["qT"])
                              if j % 2 == 1:
                                  yield
                          b = C.bank()
                          for kv in range(2):
                              for k in range(8):
                                  op("pe", lambda: pe.matmul(PS(b, N, 0, 64, 256 * kv), lhsT=winb[:, k, 512 + 64 * kv:512 + 64 * kv + 64], rhs=rhs_x(k),
                                                             start=(k == 0), stop=(k == 7)),
                                     reads=["winb"] + xin, writes=[("ps", b)], signal=(k == 7 and kv == 1))
                          op("act", lambda: act.activation(out=kTb[:, :, 128:384], in_=ps[0:64, b, :].rearrange("p (e n) -> p e n", n=256), func=AF.Copy),
                             reads=[("ps", b)], writes=["kTb"])
                          yield
                          b = C.bank()
                          for c in range(2):
                              for k in range(8):
                                  op("pe", lambda: pe.matmul(PS(b, N, o=256 * c), lhsT=winb[:, k, 1280 + 128 * c:1280 + 128 * c + 128], rhs=rhs_x(k),
                                                             start=(k == 0), stop=(k == 7)),
                                     reads=["winb"] + xin, writes=[("ps", b)], signal=(k == 7 and c == 1))
                          op("dve", lambda: dve.tensor_copy(out=zp_ext[:, :, 15:271], in_=ps[:, b, :].rearrange("p (e n) -> p e n", n=256)),
                             reads=[("ps", b)], writes=["zp"])
                          yield
                          b = C.bank()
                          for i in range(2):
                              for k in range(8):
                                  op("pe", lambda: pe.matmul(PS(b, 128, o=128 * i), lhsT=xb[:, k, c0 + 128 * i:c0 + 128 * i + 128], rhs=winb[:, k, 640:768],
                                                             start=(k == 0), stop=(k == 7)),
                                     reads=["winb"] + xin, writes=[("ps", b)], signal=(k == 7 and i == 1))
                          op("act", lambda: act.activation(out=Vb[:, 1:3, :], in_=ps[:, b, 0:256].rearrange("p (e n) -> p e n", n=128), func=AF.Copy),
                             reads=[("ps", b)], writes=["Vb"])
                          yield
                          iters = [(qi, kv) for qi in range(2) for kv in range(2)]

                          def s1(it):
                              qi, kv = iters[it]
                              sx = it % 2
                              bs = [C.bank(), C.bank()]
                              for pc_ in range(2):
                                  ko = 128 * qi + 128 * pc_
                                  tk, pk = "tP%d%d" % (sx, pc_), "PT%d%d" % (sx, pc_)
                                  op("pe", lambda: pe.matmul(PS(bs[pc_]), lhsT=kTb[:, kv, ko:ko + 128], rhs=qT[:, 4 * kv:4 * kv + 4, 128 * qi:128 * qi + 128],
                                                             start=True, stop=True),
                                     reads=["kTb", "qT"], writes=[("ps", bs[pc_])])
                                  bias_t = (bprev if pc_ == 0 else bcur)[:, kv, :]
                                  op("dve", lambda: dve.scalar_tensor_tensor(out=tP[sx][pc_], in0=PS(bs[pc_]), scalar=0.125, in1=bias_t, op0=ALU.mult, op1=ALU.add),
                                     reads=[("ps", bs[pc_]), "bcur", "bprev"], writes=[tk])
                                  if pc_ == 0 and bi == 0 and qi == 0:
                                      op("act", lambda: act.activation(out=PT[sx][pc_][:], in_=tP[sx][pc_], func=AF.Exp, bias=nm0[:, 0:1]),
                                         reads=[tk, "smallc"], writes=[pk])
                                  else:
                                      op("act", lambda: act.activation(out=PT[sx][pc_][:], in_=tP[sx][pc_], func=AF.Exp),
                                         reads=[tk], writes=[pk])

                          def s2(it):
                              qi, kv = iters[it]
                              sx = it % 2
                              bo, bd = C.bank(), C.bank()
                              for e in range(2):
                                  for pc_ in range(2):
                                      rhs_ = PT[sx][pc_][:].rearrange("p (gg e n) -> p e gg n", e=2, n=128)[:, e, :, :]
                                      op("pe", lambda: pe.matmul(PS(bo, 256, 64 * e, 64 * e + 64), lhsT=Vb[:, qi + pc_, 64 * kv:64 * kv + 64], rhs=rhs_,
                                                                 start=(pc_ == 0), stop=(pc_ == 1)),
                                         reads=["Vb", "PT%d%d" % (sx, pc_)], writes=[("ps", bo)], signal=(pc_ == 1 and e == 1))
                              for e in range(2):
                                  for pc_ in range(2):
                                      rhs_ = PT[sx][pc_][:].rearrange("p (gg e n) -> p e gg n", e=2, n=128)[:, e, :, :]
                                      op("pe", lambda: pe.matmul(PS(bd, 256, 64 * e, 64 * e + 64), lhsT=ones[:, 0:64], rhs=rhs_, start=(pc_ == 0), stop=(pc_ == 1)),
                                         reads=["ones", "PT%d%d" % (sx, pc_)], writes=[("ps", bd)], signal=(pc_ == 1 and e == 1))
                              for e in range(2):
                                  for gg in range(2):
                                      hcol = 8 * l + 4 * kv + 2 * gg + e
                                      op("act", lambda: act.activation(out=dd[64 * e:64 * e + 64, 128 * gg:128 * gg + 128], in_=ps[64 * e:64 * e + 64, bd, 128 * gg:128 * gg + 128],
                                                                       func=AF.Ln, bias=es64[64 * e:64 * e + 64, hcol:hcol + 1]),
                                         reads=[("ps", bd), "es64"], writes=["wk3"])
                              op("act", lambda: act.activation(out=dd, in_=dd, func=AF.Exp, scale=-1.0), reads=["wk3"], writes=["wk3"])
                              op("dve", lambda: dve.tensor_tensor(out=attnT[:, 2 * kv:2 * kv + 2, 128 * qi:128 * qi + 128],
                                                                  in0=ps[:, bo, 0:256].rearrange("p (g n) -> p g n", n=128),
                                                                  in1=dd.rearrange("p (g n) -> p g n", n=128), op=ALU.mult),
                                 reads=[("ps", bo), "wk3"], writes=["attnT"])

                          s1(0)
                          for it in range(4):
                              if it + 1 < 4:
                                  s1(it + 1)
                              s2(it)
                              yield
                          op("dve", lambda: dve.tensor_tensor(out=S2, in0=zp_ext[:, :, 1:271], in1=zp_ext[:, :, 0:270], op=ALU.add), reads=["zp"], writes=["wk0"])
                          op("dve", lambda: dve.tensor_tensor(out=S4, in0=S2[:, :, 2:270], in1=S2[:, :, 0:268], op=ALU.add), reads=["wk0"], writes=["wk1"])
                          S8a = wkt[2][:, 0, 0:264]; S8b = wkt[2][:, 1, 0:264]
                          op("dve", lambda: dve.tensor_tensor(out=S8a, in0=S4[:, 1, 4:268], in1=S4[:, 1, 0:264], op=ALU.add),
                             reads=["wk1"], writes=["wk2"])
                          op("dve", lambda: dve.tensor_tensor(out=S8b[:, 0:256], in0=S8a[:, 8:264], in1=S8a[:, 0:256], op=ALU.add),
                             reads=["wk2"], writes=["wk2"])
                          zwin = [(0, 64, 0, S2[0:64, 0, 14:270], 2, "wk0"), (64, 128, 0, S4[64:128, 0, 12:268], 4, "wk1"),
                                  (0, 64, 1, S8a[0:64, 8:264], 8, "wk2"), (64, 128, 1, S8b[64:128, 0:256], 16, "wk2")]

                          def first_corr(p0, p1, c, win, wkey, bi=bi):
                              if bi != 0:
                                  return
                              w16 = win[:, 0:16]
                              op("dve", lambda: dve.tensor_tensor(out=w16, in0=w16, in1=pcorr[p0:p1, c, :], op=ALU.mult),
                                 reads=[wkey, "smallc"], writes=[wkey])
                          yield
                          pool_part(W, l, N, zwin, lambda c, p0, p1: zp_ext[p0:p1, c, 15:271], mixb[:], first_corr)
                          yield
                          for c in range(2):
                              op("dve", lambda: dve.scalar_tensor_tensor(out=mixb[:, c, :], in0=corr[:, c, c0:c0 + N], scalar=hst[:, c:c + 1], in1=rec0[:, c, c0:c0 + N],
                                                                         op0=ALU.mult, op1=ALU.add), reads=["corr", "rec0", "hst"], writes=["mixb"])
                          yield
                          yield from wout_g(W, l, c0, N, attnT, mixb, xkey)
                          yield
                          if bi < 7:
                              op("dve", lambda: dve.tensor_copy(out=zp_ext[:, :, 0:15], in_=zp_ext[:, :, 256:271]), reads=["zp"], writes=["zp"])
                              op("act", lambda: act.activation(out=kTb[:, :, 0:128], in_=kTb[:, :, 256:384], func=AF.Copy), reads=["kTb"], writes=["kTb"])
                              op("act", lambda: act.activation(out=Vb[:, 0, :], in_=Vb[:, 2, :], func=AF.Copy), reads=["Vb"], writes=["Vb"])

                      def drive(gA, gB, delay=0):
                          a_alive, b_alive = gA is not None, gB is not None
                          rnd = 0
                          while a_alive or b_alive:
                              if a_alive:
                                  try:
                                      next(gA)
                                  except StopIteration:
                                      a_alive = False
                              rnd += 1
                              if b_alive and (rnd > delay or not a_alive):
                                  try:
                                      next(gB)
                                  except StopIteration:
                                      b_alive = False

                      C.bank_pool = list(range(6))
                      drive(front(0), None)
                      for bi in range(8):
                          drive(front(bi + 1) if bi + 1 < 8 else None, layer_norm_g(W, l, 1, 256 * bi, 256, "xP%d" % bi, banks=(6, 7)), delay=4)
                      C.bank_pool = list(range(8))

              C.barrier()
              if STOP == 'prompt%d' % l: return

              with contextlib.ExitStack() as pp:
                  N = NS
                  c0 = T
                  xkey = "xS"
                  W["ybf"] = sbt(pp, "ybfS", [128, 8, NS], BF16); W["ybfk"] = ["ybf"]
                  W["xcb"] = sbt(pp, "xcbS", [128, 2, NS], BF16)
                  W["st"], W["stk"] = st_ph, stk_ph
                  xin = ["xSb"]
                  qTs = sbt(pp, "qTs", [64, 8, NS]); kTs = sbt(pp, "kTs", [64, 2, NS]); vTs = sbt(pp, "vTs", [64, 2, NS])
                  xrs = sbt(pp, "xrs", [128, 2, NS, 4]); grs = sbt(pp, "grs", [128, 2, NS]); zps = sbt(pp, "zps", [128, 2, NS, 16])
                  xcs = sbt(pp, "xcs", [128, 2, NS]); h0s = sbt(pp, "h0s", [128, 2, NS])
                  attnTs = sbt(pp, "attnTs", [128, 4, NS], BF16); accw = sbt(pp, "accw", [128, 128], BF16); mixbs = sbt(pp, "mixbs", [128, 4, NS], BF16)
                  sts = sbt(pp, "sts", [NS, 768]); scs = sbt(pp, "scs", [48, 256]); sps = sbt(pp, "sps", [120, 2, 256]); shs = sbt(pp, "shs", [NS, 256])
                  sths = sbt(pp, "sths", [NS, 256])
                  qs128 = sbt(pp, "qs128", [128, 64]); kn128 = sbt(pp, "kn128", [128, 64]); vn128 = sbt(pp, "vn128", [128, 64])
                  krep = sbt(pp, "krep", [64, 128]); vrep = sbt(pp, "vrep", [64, 128])
                  Kcs = [sbt(pp, "Kc%d" % i, [128, 16, 64]) for i in range(2)]; Vcs = [sbt(pp, "Vc%d" % i, [128, 16, 64]) for i in range(2)]
                  tmpc = sbt(pp, "tmpc", [128, 16, 64])

                  def load_kv(buf, src, ch, key):
                      pairs = [(buf[16 * h:16 * h + 16, :, :], src[l, :, 16 * ch:16 * ch + 16, 64 * (h // 4):64 * (h // 4) + 64]) for h in range(8)]
                      dma("sp", pairs, writes=[key], key=key)
                  scr = sbt(pp, "scr", [128, 128]); Pm = sbt(pp, "Pm", [128, 128]); sm = sbt(pp, "sm", [128, 8])
                  acc = sbt(pp, "acc", [128, 64]); part = sbt(pp, "part", [128, 64]); wins = sbt(pp, "wins", [128, 2, NS])

                  def rhs_x(k):
                      return xb[:, k, c0:c0 + N]
                  dma("sp", [(scs[:], sc[l]), (sps[:], spool[l].rearrange("(i r) n -> r i n", i=2)), (shs[:], sh[l])], writes=["sst"], key="sst")
                  for ch in range(2):
                      load_kv(Kcs[ch], ck, ch, "Kc%d" % ch)
                  for ch in range(2):
                      load_kv(Vcs[ch], cv, ch, "Vc%d" % ch)
                  b = C.bank()
                  for c in range(2):
                      op("pe", lambda: pe.transpose(PS(b, 48, o=64 * c), scs[:, 128 * c:128 * c + 128], ident[0:48, 0:48]),
                         reads=["sst", "ident"], writes=[("ps", b)], signal=(c == 1))
                  for c in range(2):
                      op("dve", lambda: dve.tensor_copy(out=xrs[:, c, :, 0:3], in_=ps[:, b, 64 * c:64 * c + 48].rearrange("p (b t) -> p b t", t=3)),
                         reads=[("ps", b)], writes=["xr"])
                  b = C.bank()
                  for i in range(2):
                      for c in range(2):
                          op("pe", lambda: pe.transpose(PS(b, 120, o=120 * (2 * i + c)), sps[:, i, 128 * c:128 * c + 128], ident[0:120, 0:120]),
                             reads=["sst", "ident"], writes=[("ps", b)], signal=(i == 1 and c == 1))
                  for i in range(2):
                      for c in range(2):
                          op("dve", lambda: dve.tensor_copy(out=zps[:, c, 8 * i:8 * i + 8, 0:15],
                                                            in_=ps[:, b, 120 * (2 * i + c):120 * (2 * i + c) + 120].rearrange("p (b t) -> p b t", t=15)),
                             reads=[("ps", b)], writes=["zp"])
                  b = C.bank()
                  for c in range(2):
                      op("pe", lambda: pe.transpose(PS(b, NS, o=NS * c), shs[:, 128 * c:128 * c + 128], ident[0:NS, 0:NS]),
                         reads=["sst", "ident"], writes=[("ps", b)], signal=(c == 1))
                  op("dve", lambda: dve.tensor_copy(out=h0s[:], in_=ps[:, b, 0:2 * NS].rearrange("p (c n) -> p c n", n=NS)), reads=[("ps", b)], writes=["h0s"])
                  b = C.bank()
                  for h in range(8):
                      for k in range(8):
                          op("pe", lambda: pe.matmul(PS(b, N, 0, 64, NS * h), lhsT=winb[:, k, 64 * h:64 * h + 64], rhs=rhs_x(k), start=(k == 0), stop=(k == 7)),
                             reads=["winb"] + xin, writes=[("ps", b)], signal=(k == 7 and h == 7))
                  op("dve", lambda: dve.tensor_copy(out=qTs[:], in_=ps[0:64, b, 0:8 * NS].rearrange("p (h n) -> p h n", n=NS)), reads=[("ps", b)], writes=["qTs"])
                  b = C.bank()
                  for e in range(4):
                      for k in range(8):
                          op("pe", lambda: pe.matmul(PS(b, N, 0, 64, NS * e), lhsT=winb[:, k, 512 + 64 * e:512 + 64 * e + 64], rhs=rhs_x(k), start=(k == 0), stop=(k == 7)),
                             reads=["winb"] + xin, writes=[("ps", b)], signal=(k == 7 and e == 3))
                  op("dve", lambda: dve.tensor_copy(out=kTs[:], in_=ps[0:64, b, 0:2 * NS].rearrange("p (h n) -> p h n", n=NS)), reads=[("ps", b)], writes=["kTs"])
                  op("dve", lambda: dve.tensor_copy(out=vTs[:], in_=ps[0:64, b, 2 * NS:4 * NS].rearrange("p (h n) -> p h n", n=NS)), reads=[("ps", b)], writes=["vTs"])
                  b = C.bank()
                  for e in range(6):
                      for k in range(8):
                          op("pe", lambda: pe.matmul(PS(b, N, o=NS * e), lhsT=winb[:, k, 768 + 128 * e:768 + 128 * e + 128], rhs=rhs_x(k), start=(k == 0), stop=(k == 7)),
                             reads=["winb"] + xin, writes=[("ps", b)], signal=(k == 7 and e == 5))
                  op("dve", lambda: dve.tensor_copy(out=xrs[:, :, :, 3], in_=ps[:, b, 0:2 * NS].rearrange("p (c n) -> p c n", n=NS)), reads=[("ps", b)], writes=["xr"])
                  op("dve", lambda: dve.tensor_copy(out=grs[:], in_=ps[:, b, 2 * NS:4 * NS].rearrange("p (c n) -> p c n", n=NS)), reads=[("ps", b)], writes=["gr"])
                  op("dve", lambda: dve.tensor_copy(out=zps[:, :, :, 15], in_=ps[:, b, 4 * NS:6 * NS].rearrange("p (c n) -> p c n", n=NS)), reads=[("ps", b)], writes=["zp"])
                  b1, b2 = C.bank(), C.bank()
                  for k in range(8):
                      op("pe", lambda: pe.matmul(PS(b1, 512, 0, NS), lhsT=xb[:, k, c0:c0 + NS], rhs=winb[:, k, 512:1024], start=(k == 0), stop=(k == 7)),
                         reads=["winb"] + xin, writes=[("ps", b1)], signal=(k == 7))
                  for k in range(8):
                      op("pe", lambda: pe.matmul(PS(b2, 256, 0, NS), lhsT=xb[:, k, c0:c0 + NS], rhs=winb[:, k, 1280:1536], start=(k == 0), stop=(k == 7)),
                         reads=["winb"] + xin, writes=[("ps", b2)], signal=(k == 7))
                  op("dve", lambda: dve.tensor_copy(out=sts[:, 0:512], in_=PS(b1, 512, 0, NS)), reads=[("ps", b1)], writes=["sts"])
                  op("dve", lambda: dve.tensor_copy(out=sts[:, 512:768], in_=PS(b2, 256, 0, NS)), reads=[("ps", b2)], writes=["sts"])
                  dma("sp", [(s_k[l, :, 127, :], sts[:, 0:128]), (s_v[l, :, 127, :], sts[:, 128:256]),
                             (s_conv[l, :, 2, :], sts[:, 256:512]), (s_pool[l, :, 14, :], sts[:, 512:768])], reads=["sts"], key="sts")
                  b = C.bank()
                  op("pe", lambda: pe.transpose(PS(b, 64), qTs[:].rearrange("p h n -> p (h n)"), ident[0:64, 0:64]), reads=["qTs", "ident"], writes=[("ps", b)])
                  op("dve", lambda: dve.tensor_copy(out=qs128[:], in_=PS(b, 64)), reads=[("ps", b)], writes=["qs128"])
                  op("dve", lambda: dve.tensor_copy(out=krep[:].rearrange("p (k g n) -> p k g n", k=2, g=4),
                                                    in_=kTs[:].unsqueeze(2).to_broadcast([64, 2, 4, NS])), reads=["kTs"], writes=["krep"])
                  op("dve", lambda: dve.tensor_copy(out=vrep[:].rearrange("p (k g n) -> p k g n", k=2, g=4),
                                                    in_=vTs[:].unsqueeze(2).to_broadcast([64, 2, 4, NS])), reads=["vTs"], writes=["vrep"])
                  b = C.bank()
                  op("pe", lambda: pe.transpose(PS(b, 64), krep[:], ident[0:64, 0:64]), reads=["krep", "ident"], writes=[("ps", b)], signal=False)
                  op("pe", lambda: pe.transpose(PS(b, 64, o=64), vrep[:], ident[0:64, 0:64]), reads=["vrep", "ident"], writes=[("ps", b)])
                  op("dve", lambda: dve.tensor_copy(out=kn128[:], in_=PS(b, 64)), reads=[("ps", b)], writes=["kn128"])
                  op("dve", lambda: dve.tensor_copy(out=vn128[:], in_=PS(b, 64, o=64)), reads=[("ps", b)], writes=["vn128"])
                  for ch in range(8):
                      Kc = Kcs[ch % 2]
                      op("dve", lambda: dve.tensor_tensor(out=tmpc[:], in0=Kc[:], in1=qs128[:].unsqueeze(1).to_broadcast([128, 16, 64]), op=ALU.mult),
                         reads=["Kc%d" % (ch % 2), "qs128"], writes=["tmpc"])
                      op("dve", lambda: dve.tensor_reduce(out=scr[:, 16 * ch:16 * ch + 16], in_=tmpc[:], op=ALU.add, axis=AX.X), reads=["tmpc"], writes=["scr"])
                      if ch + 2 < 8:
                          load_kv(Kcs[ch % 2], ck, ch + 2, "Kc%d" % (ch % 2))
                  op("dve", lambda: dve.tensor_tensor(out=part[:], in0=kn128[:], in1=qs128[:], op=ALU.mult), reads=["kn128", "qs128"], writes=["part"])
                  op("dve", lambda: dve.tensor_reduce(out=sm[:, 0:1], in_=part[:], op=ALU.add, axis=AX.X), reads=["part"], writes=["sm0"])
                  op("dve", lambda: dve.scalar_tensor_tensor(out=scr[:], in0=scr[:], scalar=0.125, in1=sbias[:], op0=ALU.mult, op1=ALU.add),
                     reads=["scr", "smallc"], writes=["scr"])
                  op("act", lambda: act.activation(out=Pm[:], in_=scr[:], func=AF.Exp), reads=["scr"], writes=["Pm"])
                  op("act", lambda: act.activation(out=sm[:, 1:2], in_=sm[:, 0:1], func=AF.Exp, scale=0.125), reads=["sm0"], writes=["sm1"])
                  op("dve", lambda: dve.tensor_reduce(out=sm[:, 2:3], in_=Pm[:], op=ALU.add, axis=AX.X), reads=["Pm"], writes=["sm2"])
                  op("dve", lambda: dve.tensor_tensor(out=sm[:, 2:3], in0=sm[:, 2:3], in1=sm[:, 1:2], op=ALU.add), reads=["sm2", "sm1"], writes=["sm2"])
                  op("dve", lambda: dve.tensor_tensor(out=sm[:, 2:3], in0=sm[:, 2:3], in1=essamp[:, l:l + 1], op=ALU.add), reads=["sm2", "essamp"], writes=["sm2"])
                  op("dve", lambda: dve.reciprocal(out=sm[:, 3:4], in_=sm[:, 2:3]), reads=["sm2"], writes=["sm3"])
                  op("dve", lambda: dve.tensor_scalar(out=acc[:], in0=vn128[:], scalar1=sm[:, 1:2], scalar2=None, op0=ALU.mult), reads=["vn128", "sm1"], writes=["acc"])
                  for ch in range(8):
                      Vc = Vcs[ch % 2]
                      op("dve", lambda: dve.tensor_tensor(out=tmpc[:], in0=Vc[:], in1=Pm[:, 16 * ch:16 * ch + 16].unsqueeze(2).to_broadcast([128, 16, 64]), op=ALU.mult),
                         reads=["Vc%d" % (ch % 2), "Pm"], writes=["tmpc"])
                      op("dve", lambda: dve.tensor_reduce(out=part[:], in_=tmpc[:].rearrange("p s d -> p d s"), op=ALU.add, axis=AX.X), reads=["tmpc"], writes=["part"])
                      op("dve", lambda: dve.tensor_tensor(out=acc[:], in0=acc[:], in1=part[:], op=ALU.add), reads=["acc", "part"], writes=["acc"])
                      if ch + 2 < 8:
                          load_kv(Vcs[ch % 2], cv, ch + 2, "Vc%d" % (ch % 2))
                  op("dve", lambda: dve.tensor_scalar(out=acc[:], in0=acc[:], scalar1=sm[:, 3:4], scalar2=None, op0=ALU.mult), reads=["acc", "sm3"], writes=["acc"])
                  b = C.bank()
                  for e in range(2):
                      op("dve", lambda: dve.tensor_scalar(out=accw[:, 64 * e:64 * e + 64], in0=acc[:], scalar1=emask[:, e:e + 1], scalar2=None, op0=ALU.mult),
                         reads=["acc", "emask"], writes=["accw"])
                  op("pe", lambda: pe.matmul(PS(b, 64), lhsT=accw[:], rhs=selm[:], start=True, stop=True), reads=["accw", "selm"], writes=[("ps", b)])
                  op("dve", lambda: dve.tensor_copy(out=attnTs[:], in_=ps[:, b, 0:64].rearrange("p (j n) -> p j n", n=NS)), reads=[("ps", b)], writes=["attnT"])

                  def h_apply_s(a_, bb, hs):
                      op("dve", lambda: dve.tensor_tensor(out=hs, in0=a_, in1=h0s[:], op=ALU.mult), reads=["wk2", "h0s"], writes=["wk3"])
                      op("dve", lambda: dve.tensor_tensor(out=hs, in0=hs, in1=bb, op=ALU.add), reads=["wk3", "wk1"], writes=["wk3"])
                      bt = C.bank()
                      for c in range(2):
                          op("pe", lambda: pe.transpose(PS(bt, 128, 0, NS, 128 * c), hs[:, c, :], ident[:, :]), reads=["wk3", "ident"], writes=[("ps", bt)], signal=(c == 1))
                      op("dve", lambda: dve.tensor_copy(out=sths[:], in_=PS(bt, 256, 0, NS)), reads=[("ps", bt)], writes=["sths"])
                      dma("sp", (s_h[l], sths[:]), reads=["sths"], key="sths")

                  zwin = []
                  for gi, (p0, p1, c, w) in enumerate([(0, 64, 0, 2), (64, 128, 0, 4), (0, 64, 1, 8), (64, 128, 1, 16)]):
                      op("dve", lambda: dve.tensor_reduce(out=wins[p0:p1, c, :], in_=zps[p0:p1, c, :, 16 - w:16], op=ALU.add, axis=AX.X), reads=["zp"], writes=["zw"])
                      zwin.append((p0, p1, c, wins[p0:p1, c, :], w, "zw"))
                  lru_pool_common(W, l, N, lambda c, tap: xrs[:, c, :, tap], xcs[:], grs[:], zwin,
                                  lambda c, p0, p1: zps[p0:p1, c, :, 15], h_apply_s, mixbs[:])
                  wout_ln(W, l, c0, N, attnTs, mixbs, xkey)
              C.barrier()
              if STOP == 'sample%d' % l: return

          with contextlib.ExitStack() as ph:
              W = {}
              NSLOT = 3
              w1s = [sbt(ph, "w1s%d" % i, [128, 8, 512], BF16) for i in range(NSLOT)]
              w2s = [sbt(ph, "w2s%d" % i, [128, 4, 1024], BF16) for i in range(NSLOT)]
              hT = [sbt(ph, "hT%d" % i, [128, 4, 512], BF16) for i in range(2)]
              rt = [sbt(ph, "rt0", [128, 512])] * 2
              Wl = []
              for q_ in range(2):
                  d_ = {"ybf": sbt(ph, "ybfF%d" % q_, [128, 4, 256], BF16), "ybfk": ["ybfF%d" % q_]}
                  st_ = sbt(ph, "stF%d" % q_, [128, 3, 256])
                  d_["st"] = [st_[:, 0, :], st_[:, 1, :], st_[:, 2, :], st_[:, 1, :]]
                  d_["stk"] = ["stF%d_0" % q_, "stF%d_1" % q_, "stF%d_2" % q_, "stF%d_1" % q_]
                  d_["banks"] = (6, 7) if q_ == 0 else (4, 5)
                  Wl.append(d_)
              ost = [sbt(ph, "ost%d" % i, [128, 1024]) for i in range(2)]

              def load_slice(j):
                  s = j % NSLOT
                  for i_ in range(4):
                      dma("pool", (w1s[s][:, :, 128 * i_:128 * i_ + 128], w_ff1[l, :, 512 * j + 128 * i_:512 * j + 128 * i_ + 128].rearrange("(k p) n -> p k n", p=128)),
                          writes=["w1s%d_%d" % (s, i_)], key="w1s%d_%d" % (s, i_))
                  for h_ in range(2):
                      dma("pool", (w2s[s][:, 2 * h_:2 * h_ + 2, :], w_ff2[l, 512 * j + 256 * h_:512 * j + 256 * h_ + 256, :].rearrange("(i p) n -> p i n", p=128)),
                          writes=["w2s%d_%d" % (s, h_)], key="w2s%d_%d" % (s, h_))

              for j in range(NSLOT):
                  load_slice(j)
              blocks = [(512 * i, 512) for i in range(4)]
              items = [(j, c0, N) for j in range(5) for (c0, N) in blocks]
              items += [(j, c0, N) for (c0, N) in blocks for j in (5, 6, 7)]
              last_of_slice = {}
              for ii, (j_, _c, _n) in enumerate(items):
                  last_of_slice[j_] = ii

              hTs = [sbt(ph, "hTs%d" % i, [128, 4, NS], BF16) for i in range(2)]
              rts = sbt(ph, "rts", [128, NS])

              def ff1(idx):
                  j, c0, N = items[idx]
                  s = j % NSLOT
                  hb = idx % 2
                  xk = "xF%d" % c0
                  mrg = (c0 == 1536)
                  for i in range(4):
                      b = C.bank()
                      b2 = C.bank() if mrg else None
                      for k in range(8):
                          op("pe", lambda: pe.matmul(PS(b, N), lhsT=w1s[s][:, k, 128 * i:128 * i + 128], rhs=xb[:, k, c0:c0 + N], start=(k == 0), stop=(k == 7)),
                             reads=["w1s%d_%d" % (s, i), xk + "_0b", xk + "_1b"], writes=[("ps", b)], signal=(k == 7))
                          if mrg:
                              op("pe", lambda: pe.matmul(PS(b2, NS), lhsT=w1s[s][:, k, 128 * i:128 * i + 128], rhs=xb[:, k, T:T + NS], start=(k == 0), stop=(k == 7)),
                                 reads=["w1s%d_%d" % (s, i), "xF2048_0b"], writes=[("ps", b2)], signal=(k == 7))
                      op("act", lambda: act.activation(out=rt[0][:, 0:N], in_=PS(b, N), func=AF.Relu), reads=[("ps", b)], writes=["rt0"])
                      op("act", lambda: act.activation(out=hT[hb][:, i, 0:N], in_=rt[0][:, 0:N], func=AF.Square), reads=["rt0"], writes=["hT%d" % hb])
                      if mrg:
                          op("act", lambda: act.activation(out=rts[:, :], in_=PS(b2, NS), func=AF.Relu), reads=[("ps", b2)], writes=["rts"])
                          op("act", lambda: act.activation(out=hTs[hb][:, i, :], in_=rts[:, :], func=AF.Square), reads=["rts"], writes=["hTs%d" % hb])
                      yield

              def ff2(idx):
                  j, c0, N = items[idx]
                  s = j % NSLOT
                  hb = idx % 2
                  xk = "xF%d" % c0
                  mrg = (c0 == 1536)
                  for m in range(8):
                      b = C.bank()
                      b2 = C.bank() if mrg else None
                      for i in range(4):
                          op("pe", lambda: pe.matmul(PS(b, N), lhsT=w2s[s][:, i, 128 * m:128 * m + 128], rhs=hT[hb][:, i, 0:N], start=(i == 0), stop=(i == 3)),
                             reads=["w2s%d_%d" % (s, i // 2), "hT%d" % hb], writes=[("ps", b)], signal=(i == 3))
                          if mrg:
                              op("pe", lambda: pe.matmul(PS(b2, NS), lhsT=w2s[s][:, i, 128 * m:128 * m + 128], rhs=hTs[hb][:, i, :], start=(i == 0), stop=(i == 3)),
                                 reads=["w2s%d_%d" % (s, i // 2), "hTs%d" % hb], writes=[("ps", b2)], signal=(i == 3))
                      for (bq, cq, nq, kq) in ([(b, c0, N, [xk + "_0", xk + "_1"])] + ([(b2, T, NS, ["xF2048_0"])] if mrg else [])):
                          xs_ = xf[:, m, cq:cq + nq]
                          if j == 0:
                              op("dve", lambda: dve.scalar_tensor_tensor(out=xs_, in0=xs_, scalar=ALPHA, in1=PS(bq, nq), op0=ALU.mult, op1=ALU.add),
                                 reads=[("ps", bq)] + kq, writes=kq)
                          else:
                              op("dve", lambda: dve.tensor_tensor(out=xs_, in0=xs_, in1=PS(bq, nq), op=ALU.add), reads=[("ps", bq)] + kq, writes=kq)
                      yield

              otile = [0]

              def ln2_sub(cc, nn, xk, Wq):
                  yield from layer_norm_g(Wq, l, 2, cc, nn, xk, banks=Wq["banks"], slack=1)
                  if l == L - 1:
                      ntile = 2 if nn == 256 else 1
                      for ti in range(ntile):
                          n = 128 if nn == 256 else NS
                          t0 = cc + 128 * ti
                          o = ost[otile[0] % 2]
                          ok = "ost%d" % (otile[0] % 2)
                          otile[0] += 1
                          for hb in range(2):
                              b = C.bank()
                              for mm in range(4):
                                  m = 4 * hb + mm
                                  op("pe", lambda: pe.transpose(PS(b, 128, 0, n, 128 * mm), xf[:, m, t0:t0 + n], ident[:, :]),
                                     reads=[xk, "ident"], writes=[("ps", b)], signal=(mm == 3))
                              if hb == 0:
                                  op("act", lambda: act.activation(out=o[0:n, 0:512], in_=PS(b, 512, 0, n), func=AF.Copy), reads=[("ps", b)], writes=[ok])
                              else:
                                  op("dve", lambda: dve.tensor_copy(out=o[0:n, 512:1024], in_=PS(b, 512, 0, n)), reads=[("ps", b)], writes=[ok])
                          dst = yp[t0:t0 + 128, :] if nn == 256 else ys[:, :]
                          dma("sp", (dst, o[0:n, :]), reads=[ok], key=ok)
                          yield

              d_ = {"ybf": w1s[2][:].rearrange("p k n -> p (k n)")[:, 0:1024].rearrange("p (m n) -> p m n", n=256),
                    "ybfk": ["w1s2_0", "w1s2_1", "w1s2_2", "w1s2_3"]}
              st_ = w2s[2][:].rearrange("p k n -> p (k n)")[:, 0:1536].bitcast(F32).rearrange("p (m n) -> p m n", n=256)
              d_["st"] = [st_[:, 0, :], st_[:, 1, :], st_[:, 2, :], st_[:, 1, :]]
              d_["stk"] = ["w2s2_0", "w2s2_0", "w2s2_0", "w2s2_0"]
              d_["banks"] = (2, 3)
              Wl.append(d_)

              lnq = []
              nln = [0]

              def delayed(g_, k_):
                  for _ in range(k_):
                      yield
                  yield from g_

              def ffn_main():
                  yield from ff1(0)
                  for idx in range(len(items)):
                      if idx + 1 < len(items):
                          yield from ff1(idx + 1)
                      yield from ff2(idx)
                      j, c0, N = items[idx]
                      if idx + 3 < len(items) and items[idx + 3][0] == 7 and items[idx + 2][0] == 6 and items[idx + 1][0] == 5 and items[idx][0] == 4:
                          C.bank_pool = list(range(4))
                      if j == 7:
                          lnq.append((c0, 256, "xF%d_0" % c0))
                          lnq.append((c0 + 256, 256, "xF%d_1" % c0))
                          if c0 == 1536:
                              lnq.append((T, NS, "xF2048_0"))
                      if last_of_slice[j] == idx and j + NSLOT < 8:
                          load_slice(j + NSLOT)
                      if j == 3 and last_of_slice[j] == idx and l + 1 < L:
                          load_winb(l + 1)

              C.bank_pool = list(range(8))
              gmain = ffn_main()
              alive = True
              slots = [None, None, None]
              while alive or lnq or any(g_ is not None for g_ in slots):
                  if alive:
                      try:
                          next(gmain)
                      except StopIteration:
                          alive = False
                          C.bank_pool = [0, 1]
                  nslots = 2 if alive else 3
                  for q_ in range(nslots):
                      if slots[q_] is None and lnq:
                          cc_, nn_, xk_ = lnq.pop(0)
                          slots[q_] = delayed(ln2_sub(cc_, nn_, xk_, Wl[q_]), 3 if alive else 0)
                      if slots[q_] is not None:
                          try:
                              next(slots[q_])
                          except StopIteration:
                              slots[q_] = None
              C.bank_pool = list(range(8))
          C.barrier()
          if STOP == 'ln2%d' % l: return
    run_layers()
    C.finish()
    top.close()
    return nc


def _consts():
    slopes = np.exp2(-8.0 * (np.arange(8, dtype=np.float32) + 1.0) / 8).astype(np.float32)
    s = np.arange(128)[:, None]
    q = np.arange(128)[None, :]
    bcur = np.full((2, 128, 4, 128), NEG, np.float32)
    bprev = np.full((2, 128, 4, 128), NEG, np.float32)
    for kv in range(2):
        for g in range(4):
            sl = slopes[4 * kv + g]
            d = (q - s).astype(np.float32)
            bcur[kv, :, g, :] = np.where(s <= q, -sl * d, NEG)
            d2 = (q - s + 128).astype(np.float32)
            bprev[kv, :, g, :] = np.where(s >= q, -sl * d2, NEG)
    sbias = np.zeros((128, 128), np.float32)
    for h in range(8):
        sbias[16 * h:16 * h + 16, :] = -slopes[h] * (128 - np.arange(128, dtype=np.float32))[None, :]
    return bcur.reshape(2, 128, 512), bprev.reshape(2, 128, 512), sbias


_NC = None


def kernel(x_prompt, x_sample, cache_k, cache_v, state_h, state_conv, state_pool,
           w_in, attn_sinks, conv_w, conv_b, gate_a_w, gate_a_b, gate_x_w, gate_x_b, lru_lambda,
           pool_w, pool_scale, w_out, ln1_g, ln1_b, w_ff1, w_ff2, ln2_g, ln2_b):
    global _NC
    f = lambda a: np.ascontiguousarray(np.asarray(a, dtype=np.float32))
    bcur, bprev, sbias = _consts()
    ident = np.eye(128, dtype=np.float32)
    shared = dict(w_in=f(w_in), sinks=f(attn_sinks), conv_w=f(conv_w), conv_b=f(conv_b), ga_w=f(gate_a_w), ga_b=f(gate_a_b),
                  gx_w=f(gate_x_w), gx_b=f(gate_x_b), lam=f(lru_lambda), pool_w=f(pool_w), pool_s=f(pool_scale),
                  w_out=f(w_out), ln1_g=f(ln1_g), ln1_b=f(ln1_b), w_ff1=f(w_ff1), w_ff2=f(w_ff2), ln2_g=f(ln2_g), ln2_b=f(ln2_b),
                  ident=ident, bcur=bcur, bprev=bprev, sbias=sbias)
    emask = np.zeros((128, 2), np.float32); sel = np.zeros((128, 64), np.float32)
    for p in range(128):
        h, b_ = p // 16, p % 16
        emask[p, h % 2] = 1.0
        sel[p, (h // 2) * 16 + b_] = 1.0
    xpr = f(x_prompt); xsa = f(x_sample)
    ckk = f(cache_k).reshape(L, 128, 128, 128); cvv = f(cache_v).reshape(L, 128, 128, 128)
    shh = f(state_h); scc = f(state_conv); spp = f(state_pool)
    in_maps = []
    for c in range(NCORES):
        seq, half = c // 2, c % 2
        b0 = NS * c
        pcorr = np.ones((128, 2, 16), np.float32)
        if half == 0:
            for gi, w in enumerate((2, 4, 8, 16)):
                cc, p0 = gi // 2, 64 * (gi % 2)
                t = np.arange(16, dtype=np.float32)
                pcorr[p0:p0 + 64, cc, :] = (w / np.minimum(t + 1.0, float(w)))[None, :]
        m = dict(shared)
        m.update(xp=np.ascontiguousarray(xpr[seq, T * half:T * half + T]), xs=np.ascontiguousarray(xsa[b0:b0 + NS, 0]),
                 ck=np.ascontiguousarray(ckk[:, b0:b0 + NS]), cv=np.ascontiguousarray(cvv[:, b0:b0 + NS]),
                 sh=np.ascontiguousarray(shh[:, b0:b0 + NS]),
                 sc=np.ascontiguousarray(scc[:, b0:b0 + NS].reshape(L, NS * 3, 256)),
                 spool=np.ascontiguousarray(spp[:, b0:b0 + NS].reshape(L, NS * 15, 256)),
                 nm0=np.full((128, 1), NEG if half == 0 else 0.0, np.float32),
                 flag=np.full((128, 1), 0.0 if half == 0 else 1.0, np.float32),
                 pcorr=pcorr, st_in=np.zeros((L, 147, 256), np.float32), emask=emask, sel=sel)
        in_maps.append(m)
    if _NC is None:
        _NC = build()
    res = run_bass_kernel_spmd(_NC, in_maps, core_ids=list(range(NCORES))).results
    y_prompt = np.zeros((4, 4096, 1024), np.float32)
    for c in range(NCORES):
        y_prompt[c // 2, T * (c % 2):T * (c % 2) + T] = res[c]["yp"]
    y_sample = np.concatenate([res[c]["ys"] for c in range(NCORES)], 0).reshape(128, 1, 1024)
    sto = np.stack([res[2 * s + 1]["st_out"] for s in range(4)], 1)
    p_k = np.ascontiguousarray(sto[:, :, 0:128, 0:128]).reshape(L, 4, 128, 2, 64)
    p_v = np.ascontiguousarray(sto[:, :, 0:128, 128:256]).reshape(L, 4, 128, 2, 64)
    p_conv = np.ascontiguousarray(sto[:, :, 128:131, :])
    p_pool = np.ascontiguousarray(sto[:, :, 131:146, :])
    p_h = np.ascontiguousarray(sto[:, :, 146, :])
    cat = lambda k: np.concatenate([res[c][k] for c in range(NCORES)], 1)
    s_k = cat("s_k").reshape(L, 128, 128, 2, 64); s_v = cat("s_v").reshape(L, 128, 128, 2, 64)
    return (y_prompt, y_sample, p_k, p_v, p_h, p_conv, p_pool, s_k, s_v, cat("s_h"), cat("s_conv"), cat("s_pool"))
```

```python
import contextlib
import numpy as np
import concourse.bass as bass
import concourse.mybir as mybir
from concourse.bass_utils import run_bass_kernel_spmd

F32 = mybir.dt.float32
BF16 = mybir.dt.bfloat16
AF = mybir.ActivationFunctionType
ALU = mybir.AluOpType
AX = mybir.AxisListType

NCORES = 8
T = 2048
NS = 16
TT = T + NS
L = 2
ALPHA = float((2.0 * L) ** 0.25)
EPS = 1e-5
NEG = -1e30
NPV = 50
USE_CC = False
STOP = None
NOSELF = False
XMODE = 2


class _Stop(Exception):
    pass


class Ctx:
    def __init__(self, nc):
        self.nc = nc
        self.engs = {"pe": nc.tensor, "act": nc.scalar, "dve": nc.vector, "pool": nc.gpsimd, "sp": nc.sync}
        self.esem = {e: nc.alloc_semaphore(name="sem_" + e) for e in ["pe", "act", "dve", "pool"]}
        self.ecnt = {e: 0 for e in self.esem}
        self.seen = {e: {} for e in self.engs}
        self.res = {}
        self.dsem = {}
        self.nbank = 0
        self.bank_pool = list(range(8))
        self.ccs = []

    def bank(self):
        b = self.bank_pool[self.nbank % len(self.bank_pool)]
        self.nbank += 1
        return b

    def _wait(self, eng, tok):
        sem, val, src = tok
        if src == "pe" and eng == "pe":
            return
        if NOSELF and src == eng:
            return
        d = self.seen[eng]
        if d.get(id(sem), 0) >= val:
            return
        self.engs[eng].wait_ge(sem, val)
        d[id(sem)] = val

    def _deps(self, eng, reads, writes):
        for k in reads:
            st = self.res.get(k)
            if st and st["w"]:
                self._wait(eng, st["w"])
        for k in writes:
            st = self.res.get(k)
            if st:
                if st["w"]:
                    self._wait(eng, st["w"])
                for r in st["r"]:
                    self._wait(eng, r)

    def _reg(self, tok, reads, writes):
        for k in reads:
            self.res.setdefault(k, {"w": None, "r": []})["r"].append(tok)
        for k in writes:
            self.res[k] = {"w": tok, "r": []}

    def op(self, eng, fn, reads=(), writes=(), signal=True):
        psr = [k for k in reads if isinstance(k, tuple) and k[0] == "ps"]
        if psr:
            reads = [k for k in reads if k not in psr]
            writes = list(writes) + psr
        self._deps(eng, reads, writes)
        ins = fn()
        if signal:
            self.ecnt[eng] += 1
            ins.then_inc(self.esem[eng], 1)
            val = self.ecnt[eng]
        else:
            val = self.ecnt[eng] + 1
        self._reg((self.esem[eng], val, eng), reads, writes)

    def dma(self, q, pairs, reads=(), writes=(), key=None, **kw):
        if not isinstance(pairs, list):
            pairs = [pairs]
        self._deps(q, reads, writes)
        if key not in self.dsem:
            self.dsem[key] = [self.nc.alloc_semaphore(name="d_" + str(key)), 0]
        ent = self.dsem[key]
        if ent[1] > 0:
            self._wait(q, (ent[0], ent[1], "dma"))
        for out, in_ in pairs:
            self.engs[q].dma_start(out=out, in_=in_, **kw).then_inc(ent[0], 16)
            ent[1] += 16
        self._reg((ent[0], ent[1], "dma"), reads, writes)

    def collective(self, in_ap, out_ap, reads=(), writes=()):
        self._deps("pool", reads, writes)
        sem = self.nc.alloc_semaphore(name="cc%d" % len(self.ccs))
        self.ccs.append(sem)
        self.nc.gpsimd.collective_compute("AllGather", ALU.bypass, replica_groups=[[0, 1], [2, 3], [4, 5], [6, 7]],
                                          ins=[in_ap], outs=[out_ap]).then_inc(sem, 1)
        self._reg((sem, 1, "cc"), reads, writes)

    def barrier(self):
        toks = [(self.esem[e], self.ecnt[e], e) for e in self.esem if self.ecnt[e] > 0]
        toks += [(s, c, "dma") for s, c in self.dsem.values() if c > 0]
        for e in self.engs:
            for t in toks:
                if t[2] == e:
                    continue
                self._wait(e, t)
        self.res = {}

    def finish(self):
        for s, c in self.dsem.values():
            if c > 0:
                self._wait("sp", (s, c, "dma"))
        for e in self.esem:
            if self.ecnt[e] > 0:
                self._wait("sp", (self.esem[e], self.ecnt[e], e))


def build():
    nc = bass.Bass("TRN2", target_bir_lowering=False)
    C = Ctx(nc)

    def din(name, shape):
        return nc.dram_tensor(name, shape, F32, kind="ExternalInput").ap()

    def dout(name, shape):
        return nc.dram_tensor(name, shape, F32, kind="ExternalOutput").ap()

    xp = din("xp", [T, 1024]); xs = din("xs", [NS, 1024])
    ck = din("ck", [L, NS, 128, 128]); cv = din("cv", [L, NS, 128, 128])
    sh = din("sh", [L, NS, 256]); sc = din("sc", [L, NS * 3, 256]); spool = din("spool", [L, NS * 15, 256])
    w_in = din("w_in", [L, 1024, 1536]); sinks = din("sinks", [L, 8])
    conv_w = din("conv_w", [L, 4, 256]); conv_b = din("conv_b", [L, 256])
    ga_w = din("ga_w", [L, 4, 64, 64]); ga_b = din("ga_b", [L, 256])
    gx_w = din("gx_w", [L, 4, 64, 64]); gx_b = din("gx_b", [L, 256])
    lam = din("lam", [L, 256]); pool_w = din("pool_w", [L, 4, 64, 64]); pool_s = din("pool_s", [L, 256])
    w_out = din("w_out", [L, 1024, 1024]); ln1_g = din("ln1_g", [L, 1024]); ln1_b = din("ln1_b", [L, 1024])
    w_ff1 = din("w_ff1", [L, 1024, 4096]); w_ff2 = din("w_ff2", [L, 4096, 1024])
    ln2_g = din("ln2_g", [L, 1024]); ln2_b = din("ln2_b", [L, 1024])
    ident_d = din("ident", [128, 128]); bcur_d = din("bcur", [2, 128, 512]); bprev_d = din("bprev", [2, 128, 512])
    sbias_d = din("sbias", [128, 128]); nm0_d = din("nm0", [128, 1]); flag_d = din("flag", [128, 1])
    pcorr_d = din("pcorr", [128, 2, 16]); st_in = din("st_in", [L, 147, 256])
    emask_d = din("emask", [128, 2]); sel_d = din("sel", [128, 64])

    cc1_in = nc.dram_tensor("cc1_in", [L, 146, 256], F32, kind="Internal").ap()
    cc1_out = nc.dram_tensor("cc1_out", [L, 292, 256], F32, kind="Internal").ap()
    cc2_in = nc.dram_tensor("cc2_in", [L, 1, 256], F32, kind="Internal").ap()
    cc2_out = nc.dram_tensor("cc2_out", [L, 2, 256], F32, kind="Internal").ap()
    yp = dout("yp", [T, 1024]); ys = dout("ys", [NS, 1024]); st_out = dout("st_out", [L, 147, 256])
    s_k = dout("s_k", [L, NS, 128, 128]); s_v = dout("s_v", [L, NS, 128, 128])
    s_h = dout("s_h", [L, NS, 256]); s_conv = dout("s_conv", [L, NS, 3, 256]); s_pool = dout("s_pool", [L, NS, 15, 256])

    op = C.op
    dma = C.dma
    pe, act, dve, pool = nc.tensor, nc.scalar, nc.vector, nc.gpsimd

    top = contextlib.ExitStack()

    uid = [0]

    def sbt(stack, name, shape, dt=F32):
        uid[0] += 1
        return stack.enter_context(nc.sbuf_tensor("sb%d_%s" % (uid[0], name), shape, dt))

    ps = top.enter_context(nc.psum_tensor("psum_all", [128, 8, 512], F32))
    xf = sbt(top, "xf", [128, 8, TT]); xb = sbt(top, "xb", [128, 8, TT], BF16)
    ident = sbt(top, "ident", [128, 128]); ones = sbt(top, "ones", [128, 128], BF16)
    bcur = sbt(top, "bcur", [128, 2, 512], BF16); bprev = sbt(top, "bprev", [128, 2, 512], BF16)
    sbias = sbt(top, "sbias", [128, 128]); nm0 = sbt(top, "nm0", [128, 1]); flag = sbt(top, "flag", [128, 1])
    pcorr = sbt(top, "pcorr", [128, 2, 16]); pv = sbt(top, "pv", [128, 2 * NPV]); der = sbt(top, "der", [128, L, 8])
    es64 = sbt(top, "es64", [128, 16]); essamp = sbt(top, "essamp", [128, 2])
    wbd = sbt(top, "wbd", [128, L, 6, 128], BF16)
    cneg = sbt(top, "cneg", [128, 256], BF16)
    emask = sbt(top, "emask", [128, 2]); selm = sbt(top, "selm", [128, 64], BF16)
    winb = sbt(top, "winb", [128, 8, 1536], BF16)

    def load_winb(l):
        for kk in range(4):
            dma("pool", (winb[:, 2 * kk:2 * kk + 2, :], w_in[l, 256 * kk:256 * kk + 256, :].rearrange("(k p) n -> p k n", p=128)),
                writes=["winb"], key="winb%d" % kk)

    def PS(b, n=512, p0=0, p1=128, o=0):
        return ps[p0:p1, b, o:o + n]

    dma("sp", (ident[:], ident_d[:, :]), writes=["ident"], key="c0")
    dma("pool", (bcur[:], bcur_d.rearrange("k s n -> s k n")), writes=["bcur"], key="c1")
    dma("pool", (bprev[:], bprev_d.rearrange("k s n -> s k n")), writes=["bprev"], key="c2")
    dma("sp", [(sbias[:], sbias_d[:, :]), (nm0[:], nm0_d[:, :]), (flag[:], flag_d[:, :]), (pcorr[:], pcorr_d[:, :, :])],
        writes=["smallc"], key="c3")
    dma("sp", (emask[:], emask_d[:, :]), writes=["emask"], key="c8")
    dma("pool", (selm[:], sel_d[:, :]), writes=["selm"], key="c9")
    op("dve", lambda: dve.memset(ones[:], 1.0), writes=["ones"])
    op("dve", lambda: dve.memset(cneg[:], -0.5), writes=["cneg"])
    op("dve", lambda: dve.memset(wbd[:], 0.0), writes=["wbd"])
    if STOP == 'i1':
        C.finish(); top.close(); return nc
    with contextlib.ExitStack() as ph:
        prow = sbt(ph, "prow", [2 * NPV, 128])
        plist = [(conv_w, 0, 8), (conv_b, 8, 2), (ga_b, 10, 2), (gx_b, 12, 2), (lam, 14, 2), (pool_s, 16, 2),
                 (ln1_g, 18, 8), (ln1_b, 26, 8), (ln2_g, 34, 8), (ln2_b, 42, 8)]
        pairs = []
        for l in range(L):
            for (t, off, n) in plist:
                if t is conv_w:
                    src = t[l].rearrange("t (c p) -> (t c) p", p=128)
                else:
                    src = t[l].rearrange("(c p) -> c p", p=128)
                pairs.append((prow[NPV * l + off:NPV * l + off + n, :], src))
        dma("sp", pairs, writes=["prow"], key="c4")
        b0 = C.bank()
        op("pe", lambda: pe.transpose(PS(b0, 2 * NPV), prow[:], ident[0:2 * NPV, 0:2 * NPV]),
           reads=["prow", "ident"], writes=[("ps", b0)])
        op("dve", lambda: dve.tensor_copy(out=pv[:], in_=PS(b0, 2 * NPV)), reads=[("ps", b0)], writes=["pv"])
        if STOP == 'i2':
            C.finish(); ph.close(); top.close(); return nc
        tl = sbt(ph, "tl", [128, 2])
        for l in range(L):
            pb = NPV * l
            op("act", lambda: act.activation(out=tl[:], in_=pv[:, pb + 14:pb + 16], func=AF.Exp, scale=-1.0),
               reads=["pv"], writes=["tl"])
            op("dve", lambda: dve.tensor_scalar_add(tl[:], tl[:], 1.0), reads=["tl"], writes=["tl"])
            op("act", lambda: act.activation(out=tl[:], in_=tl[:], func=AF.Ln), reads=["tl"], writes=["tl"])
            op("dve", lambda: dve.tensor_scalar_mul(der[:, l, 0:2], tl[:], -4.0), reads=["tl"], writes=["der"])
            op("dve", lambda: dve.tensor_scalar_mul(der[:, l, 2:4], tl[:], -8.0), reads=["tl"], writes=["der"])
            op("dve", lambda: dve.tensor_scalar_mul(der[:, l, 4:6], pv[:, pb + 10:pb + 12], 0.5), reads=["pv"], writes=["der"])
            op("dve", lambda: dve.tensor_scalar_mul(der[:, l, 6:8], pv[:, pb + 12:pb + 14], 0.5), reads=["pv"], writes=["der"])
        if STOP == 'i3':
            C.finish(); ph.close(); top.close(); return nc
        dma("sp", (es64[:], sinks.rearrange("l h -> (l h)").partition_broadcast(128)), writes=["es64"], key="c5")
        if STOP == 'i3a':
            C.finish(); ph.close(); top.close(); return nc
        op("act", lambda: act.activation(out=es64[:], in_=es64[:], func=AF.Exp), reads=["es64"], writes=["es64"])
        pairs = []
        for l in range(L):
            for h in range(8):
                pairs.append((essamp[16 * h:16 * h + 16, l:l + 1], sinks[l, h:h + 1].partition_broadcast(16)))
        dma("sp", pairs, writes=["essamp"], key="c6")
        op("act", lambda: act.activation(out=essamp[:], in_=essamp[:], func=AF.Exp), reads=["essamp"], writes=["essamp"])
        if STOP == 'i4':
            C.finish(); ph.close(); top.close(); return nc
        pairs = []
        for l in range(L):
            for n in range(4):
                c, e = n // 2, n % 2
                for si, wt in ((0, ga_w), (2, gx_w), (4, pool_w)):
                    pairs.append((wbd[64 * e:64 * e + 64, l, si + c, 64 * e:64 * e + 64], wt[l, n]))
        dma("pool", pairs, writes=["wbd"], key="c7")
        if STOP == 'i5':
            C.finish(); ph.close(); top.close(); return nc

        if STOP == 'i6':
            C.finish(); ph.close(); top.close(); return nc
        if STOP == 'i6b':
            C.barrier(); C.finish(); ph.close(); top.close(); return nc

        load_winb(0)
        xst = [sbt(ph, "xst%d" % i, [128, 1024]) for i in range(4)]
        for ti in range(16 if STOP == 'initA' else (1 if STOP == 'initB' else 17)):
            st = xst[ti % 4]
            sk = "xst%d" % (ti % 4)
            n = 128 if ti < 16 else NS
            src = xp[128 * ti:128 * ti + 128, :] if ti < 16 else xs[:, :]
            dma("sp", (st[0:n, :], src), writes=[sk], key=sk)
            for hb in range(0 if XMODE == 0 else 2):
                b = C.bank()
                for mm in range(4):
                    m = 4 * hb + mm
                    op("pe", lambda: pe.transpose(PS(b, n, o=n * mm), st[0:n, 128 * m:128 * m + 128], ident[0:n, 0:n]),
                       reads=[sk, "ident"], writes=[("ps", b)], signal=(mm == 3))
                src_ps = ps[:, b, 0:4 * n].rearrange("p (m n) -> p m n", n=n)
                op("dve", lambda: dve.tensor_copy(out=xf[:, 4 * hb:4 * hb + 4, 128 * ti:128 * ti + n], in_=src_ps),
                   reads=[("ps", b)], writes=[("xf", ti)])
                if XMODE >= 2:
                    op("act", lambda: act.activation(out=xb[:, 4 * hb:4 * hb + 4, 128 * ti:128 * ti + n], in_=src_ps, func=AF.Copy),
                       reads=[("ps", b)], writes=[("xb", ti)])
    C.barrier()
    for l in range(L):
        dma("sp", [(s_k[l, :, 0:127, :], ck[l, :, 1:128, :]), (s_v[l, :, 0:127, :], cv[l, :, 1:128, :]),
                   (s_conv[l, :, 0:2, :], sc[l].rearrange("(b t) n -> b t n", t=3)[:, 1:3, :]),
                   (s_pool[l, :, 0:14, :], spool[l].rearrange("(b t) n -> b t n", t=15)[:, 1:15, :])],
            key="dd%d" % l)
    if STOP in ('init', 'initA', 'initB'):
        C.finish(); top.close(); return nc

    def layer_norm(W, l, which, c0, N, xkey):
        for _ in layer_norm_g(W, l, which, c0, N, xkey):
            pass

    def layer_norm_g(W, l, which, c0, N, xkey, banks=None, slack=0):
        gcol = NPV * l + (18 if which == 1 else 34)
        bcol = gcol + 8
        ybf, ybk = W["ybf"], W["ybfk"]
        k0_, k1_, k2_, k3_ = W["stk"]
        cap = ybf.shape[1]
        b1, b2 = banks if banks is not None else (C.bank(), C.bank())
        for (bb_, fn_) in ((b1, AF.Copy), (b2, AF.Square)):
            for h0 in range(0, 8, cap):
                op("act", lambda: act.activation(out=ybf[:, 0:cap, 0:N], in_=xf[:, h0:h0 + cap, c0:c0 + N], func=fn_), reads=[xkey], writes=ybk)
                yield
                for m in range(h0, h0 + cap):
                    op("pe", lambda: pe.matmul(PS(bb_, N), lhsT=ones[:], rhs=ybf[:, m - h0, 0:N], start=(m == 0), stop=(m == 7)),
                       reads=ybk + ["ones"], writes=[("ps", bb_)], signal=(m == h0 + cap - 1))
            yield
        for _ in range(slack):
            yield
        mean, msq, rstd, nmr = [a_[:, 0:N] for a_ in W["st"]]
        op("dve", lambda: dve.tensor_scalar_mul(mean, PS(b1, N), 1.0 / 1024), reads=[("ps", b1)], writes=[k0_])
        op("dve", lambda: dve.tensor_tensor(out=msq, in0=mean, in1=mean, op=ALU.mult), reads=[k0_], writes=[k1_])
        op("dve", lambda: dve.scalar_tensor_tensor(out=msq, in0=PS(b2, N), scalar=1.0 / 1024, in1=msq, op0=ALU.mult, op1=ALU.subtract),
           reads=[("ps", b2), k1_], writes=[k1_])
        op("dve", lambda: dve.tensor_scalar_add(msq, msq, EPS), reads=[k1_], writes=[k1_])
        op("act", lambda: act.activation(out=rstd, in_=msq, func=AF.Ln), reads=[k1_], writes=[k2_])
        op("act", lambda: act.activation(out=rstd, in_=rstd, func=AF.Exp, scale=-0.5), reads=[k2_], writes=[k2_])
        yield
        for _ in range(slack):
            yield
        op("dve", lambda: dve.scalar_tensor_tensor(out=nmr, in0=mean, scalar=-1.0, in1=rstd, op0=ALU.mult, op1=ALU.mult),
           reads=[k0_, k2_], writes=[k3_])
        xblk = xf[:, :, c0:c0 + N]
        op("dve", lambda: dve.tensor_tensor(out=xblk, in0=xblk, in1=rstd.unsqueeze(1).to_broadcast([128, 8, N]), op=ALU.mult), reads=[xkey, k2_], writes=[xkey])
        yield
        op("dve", lambda: dve.tensor_tensor(out=xblk, in0=xblk, in1=nmr.unsqueeze(1).to_broadcast([128, 8, N]), op=ALU.add), reads=[xkey, k3_], writes=[xkey])
        yield
        for m in range(8):
            xs_ = xf[:, m, c0:c0 + N]
            op("dve", lambda: dve.tensor_scalar(out=xs_, in0=xs_, scalar1=pv[:, gcol + m:gcol + m + 1], scalar2=pv[:, bcol + m:bcol + m + 1],
                                                op0=ALU.mult, op1=ALU.add), reads=[xkey, "pv"], writes=[xkey])
            if m % 4 == 3:
                yield
        op("act", lambda: act.activation(out=xb[:, :, c0:c0 + N], in_=xblk, func=AF.Copy), reads=[xkey], writes=[xkey + "b"])

    def pool_part(W, l, N, zwin, zcur, mixb, first_corr=None):
        pb = NPV * l
        diffb = W["diffb"]
        for (p0, p1, c, win, w, wkey) in zwin:
            if first_corr is not None:
                first_corr(p0, p1, c, win, wkey)
            op("dve", lambda: dve.scalar_tensor_tensor(out=diffb[p0:p1, c, 0:N], in0=win, scalar=1.0 / w, in1=zcur(c, p0, p1),
                                                       op0=ALU.mult, op1=ALU.subtract), reads=[wkey, "zp"], writes=["diffb"])
        bp = C.bank()
        for c in range(2):
            op("pe", lambda: pe.matmul(PS(bp, N, o=256 * c), lhsT=wbd[:, l, 4 + c, :], rhs=diffb[:, c, 0:N], start=True, stop=True),
               reads=["diffb", "wbd"], writes=[("ps", bp)])
        for c in range(2):
            op("act", lambda: act.activation(out=mixb[:, 2 + c, :], in_=PS(bp, N, o=256 * c), func=AF.Identity, scale=pv[:, pb + 16 + c:pb + 17 + c]),
               reads=[("ps", bp), "pv"], writes=["mixb"])

    def lru_part_g(W, l, N, xc_taps, xc, gr, h_apply, finish, sfx=""):
        pb = NPV * l
        wk = W["wk"]
        xcb = W["xcb"]
        for c in range(2):
            op("act", lambda: act.activation(out=xc[:, c, :], in_=xc_taps(c, 0), func=AF.Identity, scale=pv[:, pb + c:pb + c + 1],
                                             bias=pv[:, pb + 8 + c:pb + 9 + c]),
               reads=["xr" + sfx, "pv"], writes=["xc" + sfx])
            for tap in range(1, 4):
                op("dve", lambda: dve.scalar_tensor_tensor(out=xc[:, c, :], in0=xc_taps(c, tap), scalar=pv[:, pb + 2 * tap + c:pb + 2 * tap + c + 1],
                                                           in1=xc[:, c, :], op0=ALU.mult, op1=ALU.add),
                   reads=["xr" + sfx, "pv", "xc" + sfx], writes=["xc" + sfx])
        yield
        op("act", lambda: act.activation(out=xcb[:, :, 0:N], in_=xc, func=AF.Copy), reads=["xc" + sfx], writes=["xcb" + sfx])
        bg, bh = C.bank(), C.bank()
        for c in range(2):
            op("pe", lambda: pe.matmul(PS(bg, N, o=256 * c), lhsT=wbd[:, l, 0 + c, :], rhs=xcb[:, c, 0:N], start=True, stop=True),
               reads=["xcb" + sfx, "wbd"], writes=[("ps", bg)])
            op("pe", lambda: pe.matmul(PS(bh, N, o=256 * c), lhsT=wbd[:, l, 2 + c, :], rhs=xcb[:, c, 0:N], start=True, stop=True),
               reads=["xcb" + sfx, "wbd"], writes=[("ps", bh)])
        tha, thx, a_, a2 = [wk[i][:, :, 0:N] for i in range(4)]
        hs = a2
        for c in range(2):
            op("act", lambda: act.activation(out=tha[:, c, :], in_=PS(bg, N, o=256 * c), func=AF.Tanh, scale=0.5, bias=der[:, l, 4 + c:5 + c]),
               reads=[("ps", bg), "der"], writes=["wk0" + sfx])
            op("act", lambda: act.activation(out=thx[:, c, :], in_=PS(bh, N, o=256 * c), func=AF.Tanh, scale=0.5, bias=der[:, l, 6 + c:7 + c]),
               reads=[("ps", bh), "der"], writes=["wk1" + sfx])
            op("act", lambda: act.activation(out=a_[:, c, :], in_=tha[:, c, :], func=AF.Exp, scale=der[:, l, c:c + 1], bias=der[:, l, c:c + 1]),
               reads=["wk0" + sfx, "der"], writes=["wk2" + sfx])
            op("act", lambda: act.activation(out=a2[:, c, :], in_=tha[:, c, :], func=AF.Exp, scale=der[:, l, 2 + c:3 + c], bias=der[:, l, 2 + c:3 + c]),
               reads=["wk0" + sfx, "der"], writes=["wk3" + sfx])
        yield
        op("dve", lambda: dve.tensor_scalar(out=a2, in0=a2, scalar1=-1.0, scalar2=1.0, op0=ALU.mult, op1=ALU.add), reads=["wk3" + sfx], writes=["wk3" + sfx])
        op("act", lambda: act.activation(out=tha, in_=a2, func=AF.Ln), reads=["wk3" + sfx], writes=["wk0" + sfx])
        op("act", lambda: act.activation(out=tha, in_=tha, func=AF.Exp, scale=0.5), reads=["wk0" + sfx], writes=["wk0" + sfx])
        op("dve", lambda: dve.scalar_tensor_tensor(out=thx, in0=thx, scalar=1.0, in1=xc, op0=ALU.add, op1=ALU.mult),
           reads=["wk1" + sfx, "xc" + sfx], writes=["wk1" + sfx])
        op("dve", lambda: dve.scalar_tensor_tensor(out=thx, in0=thx, scalar=0.5, in1=tha, op0=ALU.mult, op1=ALU.mult),
           reads=["wk1" + sfx, "wk0" + sfx], writes=["wk1" + sfx])
        yield
        h_apply(a_, thx, hs)
        yield
        op("act", lambda: act.activation(out=tha, in_=gr, func=AF.Square), reads=["gr" + sfx], writes=["wk0" + sfx])
        op("dve", lambda: dve.tensor_scalar(out=tha, in0=tha, scalar1=0.044715, scalar2=1.0, op0=ALU.mult, op1=ALU.add), reads=["wk0" + sfx], writes=["wk0" + sfx])
        op("dve", lambda: dve.tensor_tensor(out=tha, in0=tha, in1=gr, op=ALU.mult), reads=["wk0" + sfx, "gr" + sfx], writes=["wk0" + sfx])
        op("act", lambda: act.activation(out=tha, in_=tha, func=AF.Tanh, scale=0.7978845608028654), reads=["wk0" + sfx], writes=["wk0" + sfx])
        op("dve", lambda: dve.scalar_tensor_tensor(out=tha, in0=tha, scalar=1.0, in1=gr, op0=ALU.add, op1=ALU.mult), reads=["wk0" + sfx, "gr" + sfx], writes=["wk0" + sfx])
        yield
        finish(hs, tha, thx)

    def lru_part(W, l, N, xc_taps, xc, gr, h_apply, finish):
        for _ in lru_part_g(W, l, N, xc_taps, xc, gr, h_apply, finish):
            pass

    def lru_pool_common(W, l, N, xc_taps, xc, gr, zwin, zcur, h_apply, mixb, first_corr=None):
        pool_part(W, l, N, zwin, zcur, mixb, first_corr)

        def fin(hs, ge, _):
            op("dve", lambda: dve.scalar_tensor_tensor(out=mixb[:, 0:2, :], in0=hs, scalar=0.5, in1=ge, op0=ALU.mult, op1=ALU.mult),
               reads=["wk3", "wk0"], writes=["mixb"])
        lru_part(W, l, N, xc_taps, xc, gr, h_apply, fin)

    def wout_ln(W, l, c0, N, attn, mixb, xkey, ln=True):
        for _ in wout_g(W, l, c0, N, attn, mixb, xkey):
            pass
        if ln:
            layer_norm(W, l, 1, c0, N, xkey)

    def wout_g(W, l, c0, N, attn, mixb, xkey):
        woa, wob = W["woa"], W["wob"]
        for m in range(8):
            b = C.bank()
            for h in range(4):
                op("pe", lambda: pe.matmul(PS(b, N), lhsT=woa[:, h, 128 * m:128 * m + 128], rhs=attn[:, h, :], start=(h == 0), stop=False),
                   reads=["attnT", "woa"], writes=[("ps", b)], signal=False)
            for j in range(4):
                op("pe", lambda: pe.matmul(PS(b, N), lhsT=wob[:, j, 128 * m:128 * m + 128], rhs=mixb[:, j, :], start=False, stop=(j == 3)),
                   reads=["mixb", "wob"], writes=[("ps", b)], signal=(j == 3))
            op("dve", lambda: dve.scalar_tensor_tensor(out=xf[:, m, c0:c0 + N], in0=xf[:, m, c0:c0 + N], scalar=ALPHA, in1=PS(b, N),
                                                       op0=ALU.mult, op1=ALU.add), reads=[("ps", b), xkey], writes=[xkey])
            if m % 2 == 1:
                yield

    def chk(stage):
        if STOP == stage:
            raise _Stop()

    def run_layers():
      for l in range(L):
          pb = NPV * l
          with contextlib.ExitStack() as ph:
              W = {}
              W["woa"] = woa = sbt(ph, "woa", [128, 4, 1024], BF16)
              W["wob"] = wob = sbt(ph, "wob", [128, 4, 1024], BF16)
              W["wk"] = [sbt(ph, "wk%d" % i, [128, 2, 272]) for i in range(4)]
              W["st"] = [W["wk"][2][:, 0, 0:256], W["wk"][2][:, 1, 0:256], W["wk"][3][:, 0, 0:256], W["wk"][3][:, 1, 0:256]]
              W["stk"] = ["wk2", "wk2", "wk3", "wk3"]
              st_ph, stk_ph = W["st"], W["stk"]
              W["diffb"] = sbt(ph, "diffb", [128, 2, 256], BF16)
              W["tA"] = W["wk"][0][:, 0, 0:256]; W["tAk"] = "wk0"
              W["tB"] = W["wk"][1][:, 0, 0:256]; W["tBk"] = "wk1"
              dma("pool", (woa[:], w_out[l, 0:512, :].rearrange("(j p) n -> p j n", p=128)), writes=["woa"], key="woa")
              dma("pool", (wob[:], w_out[l, 512:1024, :].rearrange("(j p) n -> p j n", p=128)), writes=["wob"], key="wob")

              with contextlib.ExitStack() as pp:
                  rec0 = sbt(pp, "rec0", [128, 2, T], BF16)
                  corr = sbt(pp, "corr", [128, 2, T], BF16)
                  kTb = sbt(pp, "kTb", [64, 2, 384], BF16)
                  Vb = sbt(pp, "Vb", [128, 3, 128], BF16)
                  xr_ext = sbt(pp, "xr_ext", [128, 2, 259])
                  zp_ext = sbt(pp, "zp_ext", [128, 2, 271])
                  hcar = sbt(pp, "hcar", [128, 2]); Acar = sbt(pp, "Acar", [128, 2]); hst = sbt(pp, "hst", [128, 2]); hfin = sbt(pp, "hfin", [128, 2])
                  wkt = W["wk"]

                  with contextlib.ExitStack() as pa:
                      stq = rec0[:].rearrange("p c n -> p (c n)")[:, 0:1536].bitcast(F32)
                      cview = corr[:].rearrange("p c n -> p (c n)")[:, 0:1024].bitcast(F32)
                      sth = cview[:, 0:256]; stc = cview[0:18, 256:512]
                      b1, b2 = C.bank(), C.bank()
                      for k in range(8):
                          op("pe", lambda: pe.matmul(PS(b1), lhsT=xb[:, k, T - 128:T], rhs=winb[:, k, 512:1024], start=(k == 0), stop=(k == 7)),
                             reads=["winb"], writes=[("ps", b1)], signal=(k == 7))
                      for k in range(8):
                          op("pe", lambda: pe.matmul(PS(b2, 256), lhsT=xb[:, k, T - 128:T], rhs=winb[:, k, 1280:1536], start=(k == 0), stop=(k == 7)),
                             reads=["winb"], writes=[("ps", b2)], signal=(k == 7))
                      op("dve", lambda: dve.tensor_copy(out=stq[:, 0:512], in_=PS(b1)), reads=[("ps", b1)], writes=["rec0"])
                      op("dve", lambda: dve.tensor_copy(out=stq[:, 512:768], in_=PS(b2, 256)), reads=[("ps", b2)], writes=["rec0"])
                      dma("sp", [(st_out[l, 0:128, :], stq[:, 0:256]), (st_out[l, 128:131, :], stq[125:128, 256:512]),
                                 (st_out[l, 131:146, :], stq[113:128, 512:768])], reads=["rec0"], key="stq")
                      dma("sp", [(cc1_in[l, 0:128, :], stq[:, 0:256]), (cc1_in[l, 128:131, :], stq[125:128, 256:512]),
                                 (cc1_in[l, 131:146, :], stq[113:128, 512:768])], reads=["rec0"], writes=["cc1in"], key="stq2")
                      C.collective(cc1_in[l], cc1_out[l], reads=["cc1in"], writes=["cc1out"])
                      dma("sp", [(sth[:], cc1_out[l, 0:128, :]), (stc[:], cc1_out[l, 128:146, :])], reads=["cc1out"], writes=["corr"], key="sth")
                      bk = C.bank()
                      for kv in range(2):
                          op("pe", lambda: pe.transpose(PS(bk, 128, 0, 64, 128 * kv), sth[:, 64 * kv:64 * kv + 64], ident[:, :]),
                             reads=["corr", "ident"], writes=[("ps", bk)], signal=(kv == 1))
                      op("dve", lambda: dve.tensor_scalar(out=kTb[:, :, 0:128], in0=ps[0:64, bk, 0:256].rearrange("p (k n) -> p k n", n=128),
                                                          scalar1=flag[0:64, 0:1], scalar2=None, op0=ALU.mult),
                         reads=[("ps", bk), "smallc"], writes=["kTb"])
                      op("dve", lambda: dve.tensor_scalar(out=Vb[:, 0, :], in0=sth[:, 128:256], scalar1=flag[:, 0:1], scalar2=None, op0=ALU.mult),
                         reads=["corr", "smallc"], writes=["Vb"])
                      bk2 = C.bank()
                      for c in range(2):
                          op("pe", lambda: pe.transpose(PS(bk2, 18, o=32 * c), stc[0:18, 128 * c:128 * c + 128], ident[0:18, 0:18]),
                             reads=["corr", "ident"], writes=[("ps", bk2)], signal=(c == 1))
                      for c in range(2):
                          op("dve", lambda: dve.tensor_scalar(out=xr_ext[:, c, 0:3], in0=PS(bk2, 3, o=32 * c), scalar1=flag[:, 0:1], scalar2=None, op0=ALU.mult),
                             reads=[("ps", bk2), "smallc"], writes=["xr0"])
                          op("dve", lambda: dve.tensor_scalar(out=zp_ext[:, c, 0:15], in0=PS(bk2, 15, o=32 * c + 3), scalar1=flag[:, 0:1], scalar2=None, op0=ALU.mult),
                             reads=[("ps", bk2), "smallc"], writes=["zp"])
                  op("dve", lambda: dve.memset(hcar[:], 0.0), writes=["hcar"])
                  op("dve", lambda: dve.memset(Acar[:], 1.0), writes=["Acar"])

                  with contextlib.ExitStack() as pb_:
                      sets = []
                      for i in range(2):
                          d_ = {"gr": sbt(pb_, "gr%d" % i, [128, 2, 256]), "xc": sbt(pb_, "xc%d" % i, [128, 2, 256]),
                                "xcb": sbt(pb_, "xcb%d" % i, [128, 2, 256], BF16)}
                          d_["wk"] = W["wk"] if i == 0 else [sbt(pb_, "wkB%d" % q_, [128, 2, 272]) for q_ in range(4)]
                          d_["xr"] = xr_ext if i == 0 else sbt(pb_, "xr_extB", [128, 2, 259])
                          sets.append(d_)

                      def pre(bi):
                          c0 = 256 * bi
                          N = 256
                          i = bi % 2
                          sx = str(i)
                          S_ = sets[i]
                          xr_i, gr_i, xc_i = S_["xr"], S_["gr"], S_["xc"]
                          for (cb, dst, key) in ((768, xr_i[:, :, 3:259], "xr" + sx), (1024, gr_i[:, :, :], "gr" + sx)):
                              b = C.bank()
                              for c in range(2):
                                  for k in range(8):
                                      op("pe", lambda: pe.matmul(PS(b, N, o=256 * c), lhsT=winb[:, k, cb + 128 * c:cb + 128 * c + 128], rhs=xb[:, k, c0:c0 + N],
                                                                 start=(k == 0), stop=(k == 7)),
                                         reads=["winb"], writes=[("ps", b)], signal=(k == 7 and c == 1))
                              if key.startswith("gr"):
                                  op("act", lambda: act.activation(out=dst, in_=ps[:, b, :].rearrange("p (e n) -> p e n", n=256), func=AF.Copy),
                                     reads=[("ps", b)], writes=[key])
                              else:
                                  op("dve", lambda: dve.tensor_copy(out=dst, in_=ps[:, b, :].rearrange("p (e n) -> p e n", n=256)),
                                     reads=[("ps", b)], writes=[key])
                          if bi > 0:
                              xr_p = sets[1 - i]["xr"]
                              op("dve", lambda: dve.tensor_copy(out=xr_i[:, :, 0:3], in_=xr_p[:, :, 256:259]), reads=["xr" + str(1 - i)], writes=["xr" + sx])
                          yield

                          def h_apply(a_, bb, hs):
                              for c in range(2):
                                  op("dve", lambda: dve.tensor_tensor_scan(out=hs[:, c, :], data0=a_[:, c, :], data1=bb[:, c, :], initial=hcar[:, c:c + 1],
                                                                           op0=ALU.mult, op1=ALU.add), reads=["wk2" + sx, "wk1" + sx, "hcar"], writes=["wk3" + sx])
                              op("dve", lambda: dve.tensor_copy(out=hcar[:, :], in_=hs[:, :, 255]), reads=["wk3" + sx], writes=["hcar"])
                              for c in range(2):
                                  op("dve", lambda: dve.tensor_tensor_scan(out=bb[:, c, :], data0=a_[:, c, :], data1=cneg[:, 0:256], initial=Acar[:, c:c + 1],
                                                                           op0=ALU.mult, op1=ALU.max), reads=["wk2" + sx, "cneg", "Acar"], writes=["wk1" + sx])
                              op("dve", lambda: dve.tensor_copy(out=Acar[:, :], in_=bb[:, :, 255]), reads=["wk1" + sx], writes=["Acar"])

                          def fin(hs, ge, Acum):
                              op("dve", lambda: dve.scalar_tensor_tensor(out=rec0[:, :, c0:c0 + 256], in0=hs, scalar=0.5, in1=ge, op0=ALU.mult, op1=ALU.mult),
                                 reads=["wk3" + sx, "wk0" + sx], writes=["rec0"])
                              op("dve", lambda: dve.scalar_tensor_tensor(out=corr[:, :, c0:c0 + 256], in0=Acum, scalar=0.5, in1=ge, op0=ALU.mult, op1=ALU.mult),
                                 reads=["wk1" + sx, "wk0" + sx], writes=["corr"])
                          yield from lru_part_g(S_, l, N, lambda c, tap: xr_i[:, c, tap:tap + 256], xc_i[:], gr_i[:], h_apply, fin, sfx=sx)

                      pend = [pre(bi) for bi in range(8)]
                      active = []
                      while pend or active:
                          while len(active) < 2 and pend:
                              active.append(pend.pop(0))
                          for g_ in list(active):
                              try:
                                  next(g_)
                              except StopIteration:
                                  active.remove(g_)
                  C.barrier()
                  with nc.allow_non_contiguous_dma(reason="tiny h state"):
                      dma("sp", (cc2_in[l, 0, :].rearrange("(c p) -> p c", p=128), hcar[:, :]), reads=["hcar"], writes=["cc2in"], key="hst")
                  C.collective(cc2_in[l], cc2_out[l], reads=["cc2in"], writes=["cc2out"])
                  with nc.allow_non_contiguous_dma(reason="tiny h state"):
                      dma("sp", (hst[:, :], cc2_out[l, 0, :].rearrange("(c p) -> p c", p=128)), reads=["cc2out"], writes=["hst"], key="hst2")
                  op("dve", lambda: dve.tensor_scalar(out=hst[:], in0=hst[:], scalar1=flag[:, 0:1], scalar2=None, op0=ALU.mult), reads=["hst", "smallc"], writes=["hst"])
                  op("dve", lambda: dve.tensor_tensor(out=hfin[:], in0=Acar[:], in1=hst[:], op=ALU.mult), reads=["Acar", "hst"], writes=["hfin"])
                  op("dve", lambda: dve.tensor_tensor(out=hfin[:], in0=hfin[:], in1=hcar[:], op=ALU.add), reads=["hfin", "hcar"], writes=["hfin"])
                  with nc.allow_non_contiguous_dma(reason="tiny h state"):
                      dma("sp", (st_out[l, 146, :].rearrange("(c p) -> p c", p=128), hfin[:, :]), reads=["hfin"], key="hst3")

                  with contextlib.ExitStack() as pc:
                      qT = sbt(pc, "qT", [64, 8, 256], BF16)
                      attnT = sbt(pc, "attnT", [128, 4, 256], BF16)
                      mixb = sbt(pc, "mixb", [128, 4, 256], BF16)
                      tPy = [sbt(pc, "tPy%d" % i, [128, 1024]) for i in range(2)]
                      tP = [[tPy[i][:, 0:512], tPy[i][:, 512:1024]] for i in range(2)]
                      PT = [[sbt(pc, "PT%d%d" % (i, j_), [128, 512], BF16) for j_ in range(2)] for i in range(2)]
                      W["ybf"] = sbt(pc, "ybfL", [128, 4, 256], BF16)
                      W["ybfk"] = ["ybfL"]
                      stL = sbt(pc, "stL", [128, 3, 256])
                      W["st"] = [stL[:, 0, :], stL[:, 1, :], stL[:, 2, :], stL[:, 1, :]]
                      W["stk"] = ["stL0", "stL1", "stL2", "stL1"]
                      dd = wkt[3][:].rearrange("p c n -> p (c n)")[:, 0:256]
                      S2 = wkt[0][:, :, 0:270]; S4 = wkt[1][:, :, 0:268]
                      def front(bi):
                          c0 = 256 * bi
                          N = 256
                          xkey = "xP%d" % bi
                          xin = [xkey + "b"]

                          def rhs_x(k):
                              return xb[:, k, c0:c0 + N]
                          for j in range(4):
                              b = C.bank()
                              for e in range(2):
                                  h = 2 * j + e
                                  for k in range(8):
                                      op("pe", lambda: pe.matmul(PS(b, N, 0, 64, 256 * e), lhsT=winb[:, k, 64 * h:64 * h + 64], rhs=rhs_x(k),
                                                                 start=(k == 0), stop=(k == 7)),
                                         reads=["winb"] + xin, writes=[("ps", b)], signal=(k == 7 and e == 1))
                              op("act", lambda: act.activation(out=qT[:, 2 * j:2 * j + 2, :], in_=ps[0:64, b, :].rearrange("p (e n) -> p e n", n=256), func=AF.Copy),
                                 reads=[("ps", b)], writes=["qT"])
                              if j % 2 == 1:
                                  yield
                          b = C.bank()
                          for kv in range(2):
                              for k in range(8):
                                  op("pe", lambda: pe.matmul(PS(b, N, 0, 64, 256 * kv), lhsT=winb[:, k, 512 + 64 * kv:512 + 64 * kv + 64], rhs=rhs_x(k),
                                                             start=(k == 0), stop=(k == 7)),
                                     reads=["winb"] + xin, writes=[("ps", b)], signal=(k == 7 and kv == 1))
                          op("act", lambda: act.activation(out=kTb[:, :, 128:384], in_=ps[0:64, b, :].rearrange("p (e n) -> p e n", n=256), func=AF.Copy),
                             reads=[("ps", b)], writes=["kTb"])
                          yield
                          b = C.bank()
                          for c in range(2):
                              for k in range(8):
                                  op("pe", lambda: pe.matmul(PS(b, N, o=256 * c), lhsT=winb[:, k, 1280 + 128 * c:1280 + 128 * c + 128], rhs=rhs_x(k),
                                                             start=(k == 0), stop=(k == 7)),
                                     reads=["winb"] + xin, writes=[("ps", b)], signal=(k == 7 and c == 1))
                          op("dve", lambda: dve.tensor_copy(out=zp_ext[:, :, 15:271], in_=ps[:, b, :].rearrange("p (e n) -> p e n", n=256)),
                             reads=[("ps", b)], writes=["zp"])
                          yield
                          b = C.bank()
                          for i in range(2):
                              for k in range(8):
                                  op("pe", lambda: pe.matmul(PS(b, 128, o=128 * i), lhsT=xb[:, k, c0 + 128 * i:c0 + 128 * i + 128], rhs=winb[:, k, 640:768],
                                                             start=(k == 0), stop=(k == 7)),
                                     reads=["winb"] + xin, writes=[("ps", b)], signal=(k == 7 and i == 1))
                          op("act", lambda: act.activation(out=Vb[:, 1:3, :], in_=ps[:, b, 0:256].rearrange("p (e n) -> p e n", n=128), func=AF.Copy),
                             reads=[("ps", b)], writes=["Vb"])
                          yield
                          iters = [(qi, kv) for qi in range(2) for kv in range(2)]

                          def s1(it):
                              qi, kv = iters[it]
                              sx = it % 2
                              bs = [C.bank(), C.bank()]
                              for pc_ in range(2):
                                  ko = 128 * qi + 128 * pc_
                                  tk, pk = "tP%d%d" % (sx, pc_), "PT%d%d" % (sx, pc_)
                                  op("pe", lambda: pe.matmul(PS(bs[pc_]), lhsT=kTb[:, kv, ko:ko + 128], rhs=qT[:, 4 * kv:4 * kv + 4, 128 * qi:128 * qi + 128],
                                                             start=True, stop=True),
                                     reads=["kTb", "qT"], writes=[("ps", bs[pc_])])
                                  bias_t = (bprev if pc_ == 0 else bcur)[:, kv, :]
                                  op("dve", lambda: dve.scalar_tensor_tensor(out=tP[sx][pc_], in0=PS(bs[pc_]), scalar=0.125, in1=bias_t, op0=ALU.mult, op1=ALU.add),
                                     reads=[("ps", bs[pc_]), "bcur", "bprev"], writes=[tk])
                                  if pc_ == 0 and bi == 0 and qi == 0:
                                      op("act", lambda: act.activation(out=PT[sx][pc_][:], in_=tP[sx][pc_], func=AF.Exp, bias=nm0[:, 0:1]),
                                         reads=[tk, "smallc"], writes=[pk])
                                  else:
                                      op("act", lambda: act.activation(out=PT[sx][pc_][:], in_=tP[sx][pc_], func=AF.Exp),
                                         reads=[tk], writes=[pk])

                          def s2(it):
                              qi, kv = iters[it]
                              sx = it % 2
                              bo, bd = C.bank(), C.bank()
                              for e in range(2):
                                  for pc_ in range(2):
                                      rhs_ = PT[sx][pc_][:].rearrange("p (gg e n) -> p e gg n", e=2, n=128)[:, e, :, :]
                                      op("pe", lambda: pe.matmul(PS(bo, 256, 64 * e, 64 * e + 64), lhsT=Vb[:, qi + pc_, 64 * kv:64 * kv + 64], rhs=rhs_,
                                                                 start=(pc_ == 0), stop=(pc_ == 1)),
                                         reads=["Vb", "PT%d%d" % (sx, pc_)], writes=[("ps", bo)], signal=(pc_ == 1 and e == 1))
                              for e in range(2):
                                  for pc_ in range(2):
                                      rhs_ = PT[sx][pc_][:].rearrange("p (gg e n) -> p e gg n", e=2, n=128)[:, e, :, :]
                                      op("pe", lambda: pe.matmul(PS(bd, 256, 64 * e, 64 * e + 64), lhsT=ones[:, 0:64], rhs=rhs_, start=(pc_ == 0), stop=(pc_ == 1)),
                                         reads=["ones", "PT%d%d" % (sx, pc_)], writes=[("ps", bd)], signal=(pc_ == 1 and e == 1))
                              for e in range(2):
                                  for gg in range(2):
                                      hcol = 8 * l + 4 * kv + 2 * gg + e
                                      op("act", lambda: act.activation(out=dd[64 * e:64 * e + 64, 128 * gg:128 * gg + 128], in_=ps[64 * e:64 * e + 64, bd, 128 * gg:128 * gg + 128],
                                                                       func=AF.Ln, bias=es64[64 * e:64 * e + 64, hcol:hcol + 1]),
                                         reads=[("ps", bd), "es64"], writes=["wk3"])
                              op("act", lambda: act.activation(out=dd, in_=dd, func=AF.Exp, scale=-1.0), reads=["wk3"], writes=["wk3"])
                              op("dve", lambda: dve.tensor_tensor(out=attnT[:, 2 * kv:2 * kv + 2, 128 * qi:128 * qi + 128],
                                                                  in0=ps[:, bo, 0:256].rearrange("p (g n) -> p g n", n=128),
                                                                  in1=dd.rearrange("p (g n) -> p g n", n=128), op=ALU.mult),
                                 reads=[("ps", bo), "wk3"], writes=["attnT"])

                          s1(0)
                          for it in range(4):
                              if it + 1 < 4:
                                  s1(it + 1)
                              s2(it)
                              yield
                          op("dve", lambda: dve.tensor_tensor(out=S2, in0=zp_ext[:, :, 1:271], in1=zp_ext[:, :, 0:270], op=ALU.add), reads=["zp"], writes=["wk0"])
                          op("dve", lambda: dve.tensor_tensor(out=S4, in0=S2[:, :, 2:270], in1=S2[:, :, 0:268], op=ALU.add), reads=["wk0"], writes=["wk1"])
                          S8a = wkt[2][:, 0, 0:264]; S8b = wkt[2][:, 1, 0:264]
                          op("dve", lambda: dve.tensor_tensor(out=S8a, in0=S4[:, 1, 4:268], in1=S4[:, 1, 0:264], op=ALU.add),
                             reads=["wk1"], writes=["wk2"])
                          op("dve", lambda: dve.tensor_tensor(out=S8b[:, 0:256], in0=S8a[:, 8:264], in1=S8a[:, 0:256], op=ALU.add),
                             reads=["wk2"], writes=["wk2"])
                          zwin = [(0, 64, 0, S2[0:64, 0, 14:270], 2, "wk0"), (64, 128, 0, S4[64:128, 0, 12:268], 4, "wk1"),
                                  (0, 64, 1, S8a[0:64, 8:264], 8, "wk2"), (64, 128, 1, S8b[64:128, 0:256], 16, "wk2")]

                          def first_corr(p0, p1, c, win, wkey, bi=bi):
                              if bi != 0:
                                  return
                              w16 = win[:, 0:16]
                              op("dve", lambda: dve.tensor_tensor(out=w16, in0=w16, in1=pcorr[p0:p1, c, :], op=ALU.mult),
                                 reads=[wkey, "smallc"], writes=[wkey])
                          yield
                          pool_part(W, l, N, zwin, lambda c, p0, p1: zp_ext[p0:p1, c, 15:271], mixb[:], first_corr)
                          yield
                          for c in range(2):
                              op("dve", lambda: dve.scalar_tensor_tensor(out=mixb[:, c, :], in0=corr[:, c, c0:c0 + N], scalar=hst[:, c:c + 1], in1=rec0[:, c, c0:c0 + N],
                                                                         op0=ALU.mult, op1=ALU.add), reads=["corr", "rec0", "hst"], writes=["mixb"])
                          yield
                          yield from wout_g(W, l, c0, N, attnT, mixb, xkey)
                          yield
                          if bi < 7:
                              op("dve", lambda: dve.tensor_copy(out=zp_ext[:, :, 0:15], in_=zp_ext[:, :, 256:271]), reads=["zp"], writes=["zp"])
                              op("act", lambda: act.activation(out=kTb[:, :, 0:128], in_=kTb[:, :, 256:384], func=AF.Copy), reads=["kTb"], writes=["kTb"])
                              op("act", lambda: act.activation(out=Vb[:, 0, :], in_=Vb[:, 2, :], func=AF.Copy), reads=["Vb"], writes=["Vb"])

                      def drive(gA, gB, delay=0):
                          a_alive, b_alive = gA is not None, gB is not None
                          rnd = 0
                          while a_alive or b_alive:
                              if a_alive:
                                  try:
                                      next(gA)
                                  except StopIteration:
                                      a_alive = False
                              rnd += 1
                              if b_alive and (rnd > delay or not a_alive):
                                  try:
                                      next(gB)
                                  except StopIteration:
                                      b_alive = False

                      C.bank_pool = list(range(6))
                      drive(front(0), None)
                      for bi in range(8):
                          drive(front(bi + 1) if bi + 1 < 8 else None, layer_norm_g(W, l, 1, 256 * bi, 256, "xP%d" % bi, banks=(6, 7)), delay=4)
                      C.bank_pool = list(range(8))

              C.barrier()
              if STOP == 'prompt%d' % l: return

              with contextlib.ExitStack() as pp:
                  N = NS
                  c0 = T
                  xkey = "xS"
                  W["ybf"] = sbt(pp, "ybfS", [128, 8, NS], BF16); W["ybfk"] = ["ybf"]
                  W["xcb"] = sbt(pp, "xcbS", [128, 2, NS], BF16)
                  W["st"], W["stk"] = st_ph, stk_ph
                  xin = ["xSb"]
                  qTs = sbt(pp, "qTs", [64, 8, NS]); kTs = sbt(pp, "kTs", [64, 2, NS]); vTs = sbt(pp, "vTs", [64, 2, NS])
                  xrs = sbt(pp, "xrs", [128, 2, NS, 4]); grs = sbt(pp, "grs", [128, 2, NS]); zps = sbt(pp, "zps", [128, 2, NS, 16])
                  xcs = sbt(pp, "xcs", [128, 2, NS]); h0s = sbt(pp, "h0s", [128, 2, NS])
                  attnTs = sbt(pp, "attnTs", [128, 4, NS], BF16); accw = sbt(pp, "accw", [128, 128], BF16); mixbs = sbt(pp, "mixbs", [128, 4, NS], BF16)
                  sts = sbt(pp, "sts", [NS, 768]); scs = sbt(pp, "scs", [48, 256]); sps = sbt(pp, "sps", [120, 2, 256]); shs = sbt(pp, "shs", [NS, 256])
                  sths = sbt(pp, "sths", [NS, 256])
                  qs128 = sbt(pp, "qs128", [128, 64]); kn128 = sbt(pp, "kn128", [128, 64]); vn128 = sbt(pp, "vn128", [128, 64])
                  krep = sbt(pp, "krep", [64, 128]); vrep = sbt(pp, "vrep", [64, 128])
                  Kcs = [sbt(pp, "Kc%d" % i, [128, 16, 64]) for i in range(2)]; Vcs = [sbt(pp, "Vc%d" % i, [128, 16, 64]) for i in range(2)]
                  tmpc = sbt(pp, "tmpc", [128, 16, 64])

                  def load_kv(buf, src, ch, key):
                      pairs = [(buf[16 * h:16 * h + 16, :, :], src[l, :, 16 * ch:16 * ch + 16, 64 * (h // 4):64 * (h // 4) + 64]) for h in range(8)]
                      dma("sp", pairs, writes=[key], key=key)
                  scr = sbt(pp, "scr", [128, 128]); Pm = sbt(pp, "Pm", [128, 128]); sm = sbt(pp, "sm", [128, 8])
                  acc = sbt(pp, "acc", [128, 64]); part = sbt(pp, "part", [128, 64]); wins = sbt(pp, "wins", [128, 2, NS])

                  def rhs_x(k):
                      return xb[:, k, c0:c0 + N]
                  dma("sp", [(scs[:], sc[l]), (sps[:], spool[l].rearrange("(i r) n -> r i n", i=2)), (shs[:], sh[l])], writes=["sst"], key="sst")
                  for ch in range(2):
                      load_kv(Kcs[ch], ck, ch, "Kc%d" % ch)
                  for ch in range(2):
                      load_kv(Vcs[ch], cv, ch, "Vc%d" % ch)
                  b = C.bank()
                  for c in range(2):
                      op("pe", lambda: pe.transpose(PS(b, 48, o=64 * c), scs[:, 128 * c:128 * c + 128], ident[0:48, 0:48]),
                         reads=["sst", "ident"], writes=[("ps", b)], signal=(c == 1))
                  for c in range(2):
                      op("dve", lambda: dve.tensor_copy(out=xrs[:, c, :, 0:3], in_=ps[:, b, 64 * c:64 * c + 48].rearrange("p (b t) -> p b t", t=3)),
                         reads=[("ps", b)], writes=["xr"])
                  b = C.bank()
                  for i in range(2):
                      for c in range(2):
                          op("pe", lambda: pe.transpose(PS(b, 120, o=120 * (2 * i + c)), sps[:, i, 128 * c:128 * c + 128], ident[0:120, 0:120]),
                             reads=["sst", "ident"], writes=[("ps", b)], signal=(i == 1 and c == 1))
                  for i in range(2):
                      for c in range(2):
                          op("dve", lambda: dve.tensor_copy(out=zps[:, c, 8 * i:8 * i + 8, 0:15],
                                                            in_=ps[:, b, 120 * (2 * i + c):120 * (2 * i + c) + 120].rearrange("p (b t) -> p b t", t=15)),
                             reads=[("ps", b)], writes=["zp"])
                  b = C.bank()
                  for c in range(2):
                      op("pe", lambda: pe.transpose(PS(b, NS, o=NS * c), shs[:, 128 * c:128 * c + 128], ident[0:NS, 0:NS]),
                         reads=["sst", "ident"], writes=[("ps", b)], signal=(c == 1))
                  op("dve", lambda: dve.tensor_copy(out=h0s[:], in_=ps[:, b, 0:2 * NS].rearrange("p (c n) -> p c n", n=NS)), reads=[("ps", b)], writes=["h0s"])
                  b = C.bank()
                  for h in range(8):
                      for k in range(8):
                          op("pe", lambda: pe.matmul(PS(b, N, 0, 64, NS * h), lhsT=winb[:, k, 64 * h:64 * h + 64], rhs=rhs_x(k), start=(k == 0), stop=(k == 7)),
                             reads=["winb"] + xin, writes=[("ps", b)], signal=(k == 7 and h == 7))
                  op("dve", lambda: dve.tensor_copy(out=qTs[:], in_=ps[0:64, b, 0:8 * NS].rearrange("p (h n) -> p h n", n=NS)), reads=[("ps", b)], writes=["qTs"])
                  b = C.bank()
                  for e in range(4):
                      for k in range(8):
                          op("pe", lambda: pe.matmul(PS(b, N, 0, 64, NS * e), lhsT=winb[:, k, 512 + 64 * e:512 + 64 * e + 64], rhs=rhs_x(k), start=(k == 0), stop=(k == 7)),
                             reads=["winb"] + xin, writes=[("ps", b)], signal=(k == 7 and e == 3))
                  op("dve", lambda: dve.tensor_copy(out=kTs[:], in_=ps[0:64, b, 0:2 * NS].rearrange("p (h n) -> p h n", n=NS)), reads=[("ps", b)], writes=["kTs"])
                  op("dve", lambda: dve.tensor_copy(out=vTs[:], in_=ps[0:64, b, 2 * NS:4 * NS].rearrange("p (h n) -> p h n", n=NS)), reads=[("ps", b)], writes=["vTs"])
                  b = C.bank()
                  for e in range(6):
                      for k in range(8):
                          op("pe", lambda: pe.matmul(PS(b, N, o=NS * e), lhsT=winb[:, k, 768 + 128 * e:768 + 128 * e + 128], rhs=rhs_x(k), start=(k == 0), stop=(k == 7)),
                             reads=["winb"] + xin, writes=[("ps", b)], signal=(k == 7 and e == 5))
                  op("dve", lambda: dve.tensor_copy(out=xrs[:, :, :, 3], in_=ps[:, b, 0:2 * NS].rearrange("p (c n) -> p c n", n=NS)), reads=[("ps", b)], writes=["xr"])
                  op("dve", lambda: dve.tensor_copy(out=grs[:], in_=ps[:, b, 2 * NS:4 * NS].rearrange("p (c n) -> p c n", n=NS)), reads=[("ps", b)], writes=["gr"])
                  op("dve", lambda: dve.tensor_copy(out=zps[:, :, :, 15], in_=ps[:, b, 4 * NS:6 * NS].rearrange("p (c n) -> p c n", n=NS)), reads=[("ps", b)], writes=["zp"])
                  b1, b2 = C.bank(), C.bank()
                  for k in range(8):
                      op("pe", lambda: pe.matmul(PS(b1, 512, 0, NS), lhsT=xb[:, k, c0:c0 + NS], rhs=winb[:, k, 512:1024], start=(k == 0), stop=(k == 7)),
                         reads=["winb"] + xin, writes=[("ps", b1)], signal=(k == 7))
                  for k in range(8):
                      op("pe", lambda: pe.matmul(PS(b2, 256, 0, NS), lhsT=xb[:, k, c0:c0 + NS], rhs=winb[:, k, 1280:1536], start=(k == 0), stop=(k == 7)),
                         reads=["winb"] + xin, writes=[("ps", b2)], signal=(k == 7))
                  op("dve", lambda: dve.tensor_copy(out=sts[:, 0:512], in_=PS(b1, 512, 0, NS)), reads=[("ps", b1)], writes=["sts"])
                  op("dve", lambda: dve.tensor_copy(out=sts[:, 512:768], in_=PS(b2, 256, 0, NS)), reads=[("ps", b2)], writes=["sts"])
                  dma("sp", [(s_k[l, :, 127, :], sts[:, 0:128]), (s_v[l, :, 127, :], sts[:, 128:256]),
                             (s_conv[l, :, 2, :], sts[:, 256:512]), (s_pool[l, :, 14, :], sts[:, 512:768])], reads=["sts"], key="sts")
                  b = C.bank()
                  op("pe", lambda: pe.transpose(PS(b, 64), qTs[:].rearrange("p h n -> p (h n)"), ident[0:64, 0:64]), reads=["qTs", "ident"], writes=[("ps", b)])
                  op("dve", lambda: dve.tensor_copy(out=qs128[:], in_=PS(b, 64)), reads=[("ps", b)], writes=["qs128"])
                  op("dve", lambda: dve.tensor_copy(out=krep[:].rearrange("p (k g n) -> p k g n", k=2, g=4),
                                                    in_=kTs[:].unsqueeze(2).to_broadcast([64, 2, 4, NS])), reads=["kTs"], writes=["krep"])
                  op("dve", lambda: dve.tensor_copy(out=vrep[:].rearrange("p (k g n) -> p k g n", k=2, g=4),
                                                    in_=vTs[:].unsqueeze(2).to_broadcast([64, 2, 4, NS])), reads=["vTs"], writes=["vrep"])
                  b = C.bank()
                  op("pe", lambda: pe.transpose(PS(b, 64), krep[:], ident[0:64, 0:64]), reads=["krep", "ident"], writes=[("ps", b)], signal=False)
                  op("pe", lambda: pe.transpose(PS(b, 64, o=64), vrep[:], ident[0:64, 0:64]), reads=["vrep", "ident"], writes=[("ps", b)])
                  op("dve", lambda: dve.tensor_copy(out=kn128[:], in_=PS(b, 64)), reads=[("ps", b)], writes=["kn128"])
                  op("dve", lambda: dve.tensor_copy(out=vn128[:], in_=PS(b, 64, o=64)), reads=[("ps", b)], writes=["vn128"])
                  for ch in range(8):
                      Kc = Kcs[ch % 2]
                      op("dve", lambda: dve.tensor_tensor(out=tmpc[:], in0=Kc[:], in1=qs128[:].unsqueeze(1).to_broadcast([128, 16, 64]), op=ALU.mult),
                         reads=["Kc%d" % (ch % 2), "qs128"], writes=["tmpc"])
                      op("dve", lambda: dve.tensor_reduce(out=scr[:, 16 * ch:16 * ch + 16], in_=tmpc[:], op=ALU.add, axis=AX.X), reads=["tmpc"], writes=["scr"])
                      if ch + 2 < 8:
                          load_kv(Kcs[ch % 2], ck, ch + 2, "Kc%d" % (ch % 2))
                  op("dve", lambda: dve.tensor_tensor(out=part[:], in0=kn128[:], in1=qs128[:], op=ALU.mult), reads=["kn128", "qs128"], writes=["part"])
                  op("dve", lambda: dve.tensor_reduce(out=sm[:, 0:1], in_=part[:], op=ALU.add, axis=AX.X), reads=["part"], writes=["sm0"])
                  op("dve", lambda: dve.scalar_tensor_tensor(out=scr[:], in0=scr[:], scalar=0.125, in1=sbias[:], op0=ALU.mult, op1=ALU.add),
                     reads=["scr", "smallc"], writes=["scr"])
                  op("act", lambda: act.activation(out=Pm[:], in_=scr[:], func=AF.Exp), reads=["scr"], writes=["Pm"])
                  op("act", lambda: act.activation(out=sm[:, 1:2], in_=sm[:, 0:1], func=AF.Exp, scale=0.125), reads=["sm0"], writes=["sm1"])
                  op("dve", lambda: dve.tensor_reduce(out=sm[:, 2:3], in_=Pm[:], op=ALU.add, axis=AX.X), reads=["Pm"], writes=["sm2"])
                  op("dve", lambda: dve.tensor_tensor(out=sm[:, 2:3], in0=sm[:, 2:3], in1=sm[:, 1:2], op=ALU.add), reads=["sm2", "sm1"], writes=["sm2"])
                  op("dve", lambda: dve.tensor_tensor(out=sm[:, 2:3], in0=sm[:, 2:3], in1=essamp[:, l:l + 1], op=ALU.add), reads=["sm2", "essamp"], writes=["sm2"])
                  op("dve", lambda: dve.reciprocal(out=sm[:, 3:4], in_=sm[:, 2:3]), reads=["sm2"], writes=["sm3"])
                  op("dve", lambda: dve.tensor_scalar(out=acc[:], in0=vn128[:], scalar1=sm[:, 1:2], scalar2=None, op0=ALU.mult), reads=["vn128", "sm1"], writes=["acc"])
                  for ch in range(8):
                      Vc = Vcs[ch % 2]
                      op("dve", lambda: dve.tensor_tensor(out=tmpc[:], in0=Vc[:], in1=Pm[:, 16 * ch:16 * ch + 16].unsqueeze(2).to_broadcast([128, 16, 64]), op=ALU.mult),
                         reads=["Vc%d" % (ch % 2), "Pm"], writes=["tmpc"])
                      op("dve", lambda: dve.tensor_reduce(out=part[:], in_=tmpc[:].rearrange("p s d -> p d s"), op=ALU.add, axis=AX.X), reads=["tmpc"], writes=["part"])
                      op("dve", lambda: dve.tensor_tensor(out=acc[:], in0=acc[:], in1=part[:], op=ALU.add), reads=["acc", "part"], writes=["acc"])
                      if ch + 2 < 8:
                          load_kv(Vcs[ch % 2], cv, ch + 2, "Vc%d" % (ch % 2))
                  op("dve", lambda: dve.tensor_scalar(out=acc[:], in0=acc[:], scalar1=sm[:, 3:4], scalar2=None, op0=ALU.mult), reads=["acc", "sm3"], writes=["acc"])
                  b = C.bank()
                  for e in range(2):
                      op("dve", lambda: dve.tensor_scalar(out=accw[:, 64 * e:64 * e + 64], in0=acc[:], scalar1=emask[:, e:e + 1], scalar2=None, op0=ALU.mult),
                         reads=["acc", "emask"], writes=["accw"])
                  op("pe", lambda: pe.matmul(PS(b, 64), lhsT=accw[:], rhs=selm[:], start=True, stop=True), reads=["accw", "selm"], writes=[("ps", b)])
                  op("dve", lambda: dve.tensor_copy(out=attnTs[:], in_=ps[:, b, 0:64].rearrange("p (j n) -> p j n", n=NS)), reads=[("ps", b)], writes=["attnT"])

                  def h_apply_s(a_, bb, hs):
                      op("dve", lambda: dve.tensor_tensor(out=hs, in0=a_, in1=h0s[:], op=ALU.mult), reads=["wk2", "h0s"], writes=["wk3"])
                      op("dve", lambda: dve.tensor_tensor(out=hs, in0=hs, in1=bb, op=ALU.add), reads=["wk3", "wk1"], writes=["wk3"])
                      bt = C.bank()
                      for c in range(2):
                          op("pe", lambda: pe.transpose(PS(bt, 128, 0, NS, 128 * c), hs[:, c, :], ident[:, :]), reads=["wk3", "ident"], writes=[("ps", bt)], signal=(c == 1))
                      op("dve", lambda: dve.tensor_copy(out=sths[:], in_=PS(bt, 256, 0, NS)), reads=[("ps", bt)], writes=["sths"])
                      dma("sp", (s_h[l], sths[:]), reads=["sths"], key="sths")

                  zwin = []
                  for gi, (p0, p1, c, w) in enumerate([(0, 64, 0, 2), (64, 128, 0, 4), (0, 64, 1, 8), (64, 128, 1, 16)]):
                      op("dve", lambda: dve.tensor_reduce(out=wins[p0:p1, c, :], in_=zps[p0:p1, c, :, 16 - w:16], op=ALU.add, axis=AX.X), reads=["zp"], writes=["zw"])
                      zwin.append((p0, p1, c, wins[p0:p1, c, :], w, "zw"))
                  lru_pool_common(W, l, N, lambda c, tap: xrs[:, c, :, tap], xcs[:], grs[:], zwin,
                                  lambda c, p0, p1: zps[p0:p1, c, :, 15], h_apply_s, mixbs[:])
                  wout_ln(W, l, c0, N, attnTs, mixbs, xkey)
              C.barrier()
              if STOP == 'sample%d' % l: return

          with contextlib.ExitStack() as ph:
              W = {}
              NSLOT = 3
              w1s = [sbt(ph, "w1s%d" % i, [128, 8, 512], BF16) for i in range(NSLOT)]
              w2s = [sbt(ph, "w2s%d" % i, [128, 4, 1024], BF16) for i in range(NSLOT)]
              hT = [sbt(ph, "hT%d" % i, [128, 4, 512], BF16) for i in range(2)]
              rt = [sbt(ph, "rt0", [128, 512])] * 2
              Wl = []
              for q_ in range(2):
                  d_ = {"ybf": sbt(ph, "ybfF%d" % q_, [128, 4, 256], BF16), "ybfk": ["ybfF%d" % q_]}
                  st_ = sbt(ph, "stF%d" % q_, [128, 3, 256])
                  d_["st"] = [st_[:, 0, :], st_[:, 1, :], st_[:, 2, :], st_[:, 1, :]]
                  d_["stk"] = ["stF%d_0" % q_, "stF%d_1" % q_, "stF%d_2" % q_, "stF%d_1" % q_]
                  d_["banks"] = (6, 7) if q_ == 0 else (4, 5)
                  Wl.append(d_)
              ost = [sbt(ph, "ost%d" % i, [128, 1024]) for i in range(2)]

              def load_slice(j):
                  s = j % NSLOT
                  for i_ in range(4):
                      dma("pool", (w1s[s][:, :, 128 * i_:128 * i_ + 128], w_ff1[l, :, 512 * j + 128 * i_:512 * j + 128 * i_ + 128].rearrange("(k p) n -> p k n", p=128)),
                          writes=["w1s%d_%d" % (s, i_)], key="w1s%d_%d" % (s, i_))
                  for h_ in range(2):
                      dma("pool", (w2s[s][:, 2 * h_:2 * h_ + 2, :], w_ff2[l, 512 * j + 256 * h_:512 * j + 256 * h_ + 256, :].rearrange("(i p) n -> p i n", p=128)),
                          writes=["w2s%d_%d" % (s, h_)], key="w2s%d_%d" % (s, h_))

              for j in range(NSLOT):
                  load_slice(j)
              blocks = [(512 * i, 512) for i in range(4)]
              items = [(j, c0, N) for j in range(5) for (c0, N) in blocks]
              items += [(j, c0, N) for (c0, N) in blocks for j in (5, 6, 7)]
              last_of_slice = {}
              for ii, (j_, _c, _n) in enumerate(items):
                  last_of_slice[j_] = ii

              hTs = [sbt(ph, "hTs%d" % i, [128, 4, NS], BF16) for i in range(2)]
              rts = sbt(ph, "rts", [128, NS])

              def ff1(idx):
                  j, c0, N = items[idx]
                  s = j % NSLOT
                  hb = idx % 2
                  xk = "xF%d" % c0
                  mrg = (c0 == 1536)
                  for i in range(4):
                      b = C.bank()
                      b2 = C.bank() if mrg else None
                      for k in range(8):
                          op("pe", lambda: pe.matmul(PS(b, N), lhsT=w1s[s][:, k, 128 * i:128 * i + 128], rhs=xb[:, k, c0:c0 + N], start=(k == 0), stop=(k == 7)),
                             reads=["w1s%d_%d" % (s, i), xk + "_0b", xk + "_1b"], writes=[("ps", b)], signal=(k == 7))
                          if mrg:
                              op("pe", lambda: pe.matmul(PS(b2, NS), lhsT=w1s[s][:, k, 128 * i:128 * i + 128], rhs=xb[:, k, T:T + NS], start=(k == 0), stop=(k == 7)),
                                 reads=["w1s%d_%d" % (s, i), "xF2048_0b"], writes=[("ps", b2)], signal=(k == 7))
                      op("act", lambda: act.activation(out=rt[0][:, 0:N], in_=PS(b, N), func=AF.Relu), reads=[("ps", b)], writes=["rt0"])
                      op("act", lambda: act.activation(out=hT[hb][:, i, 0:N], in_=rt[0][:, 0:N], func=AF.Square), reads=["rt0"], writes=["hT%d" % hb])
                      if mrg:
                          op("act", lambda: act.activation(out=rts[:, :], in_=PS(b2, NS), func=AF.Relu), reads=[("ps", b2)], writes=["rts"])
                          op("act", lambda: act.activation(out=hTs[hb][:, i, :], in_=rts[:, :], func=AF.Square), reads=["rts"], writes=["hTs%d" % hb])
                      yield

              def ff2(idx):
                  j, c0, N = items[idx]
                  s = j % NSLOT
                  hb = idx % 2
                  xk = "xF%d" % c0
                  mrg = (c0 == 1536)
                  for m in range(8):
                      b = C.bank()
                      b2 = C.bank() if mrg else None
                      for i in range(4):
                          op("pe", lambda: pe.matmul(PS(b, N), lhsT=w2s[s][:, i, 128 * m:128 * m + 128], rhs=hT[hb][:, i, 0:N], start=(i == 0), stop=(i == 3)),
                             reads=["w2s%d_%d" % (s, i // 2), "hT%d" % hb], writes=[("ps", b)], signal=(i == 3))
                          if mrg:
                              op("pe", lambda: pe.matmul(PS(b2, NS), lhsT=w2s[s][:, i, 128 * m:128 * m + 128], rhs=hTs[hb][:, i, :], start=(i == 0), stop=(i == 3)),
                                 reads=["w2s%d_%d" % (s, i // 2), "hTs%d" % hb], writes=[("ps", b2)], signal=(i == 3))
                      for (bq, cq, nq, kq) in ([(b, c0, N, [xk + "_0", xk + "_1"])] + ([(b2, T, NS, ["xF2048_0"])] if mrg else [])):
                          xs_ = xf[:, m, cq:cq + nq]
                          if j == 0:
                              op("dve", lambda: dve.scalar_tensor_tensor(out=xs_, in0=xs_, scalar=ALPHA, in1=PS(bq, nq), op0=ALU.mult, op1=ALU.add),
                                 reads=[("ps", bq)] + kq, writes=kq)
                          else:
                              op("dve", lambda: dve.tensor_tensor(out=xs_, in0=xs_, in1=PS(bq, nq), op=ALU.add), reads=[("ps", bq)] + kq, writes=kq)
                      yield

              otile = [0]

              def ln2_sub(cc, nn, xk, Wq):
                  yield from layer_norm_g(Wq, l, 2, cc, nn, xk, banks=Wq["banks"], slack=1)
                  if l == L - 1:
                      ntile = 2 if nn == 256 else 1
                      for ti in range(ntile):
                          n = 128 if nn == 256 else NS
                          t0 = cc + 128 * ti
                          o = ost[otile[0] % 2]
                          ok = "ost%d" % (otile[0] % 2)
                          otile[0] += 1
                          for hb in range(2):
                              b = C.bank()
                              for mm in range(4):
                                  m = 4 * hb + mm
                                  op("pe", lambda: pe.transpose(PS(b, 128, 0, n, 128 * mm), xf[:, m, t0:t0 + n], ident[:, :]),
                                     reads=[xk, "ident"], writes=[("ps", b)], signal=(mm == 3))
                              if hb == 0:
                                  op("act", lambda: act.activation(out=o[0:n, 0:512], in_=PS(b, 512, 0, n), func=AF.Copy), reads=[("ps", b)], writes=[ok])
                              else:
                                  op("dve", lambda: dve.tensor_copy(out=o[0:n, 512:1024], in_=PS(b, 512, 0, n)), reads=[("ps", b)], writes=[ok])
                          dst = yp[t0:t0 + 128, :] if nn == 256 else ys[:, :]
                          dma("sp", (dst, o[0:n, :]), reads=[ok], key=ok)
                          yield

              d_ = {"ybf": w1s[2][:].rearrange("p k n -> p (k n)")[:, 0:1024].rearrange("p (m n) -> p m n", n=256),
                    "ybfk": ["w1s2_0", "w1s2_1", "w1s2_2", "w1s2_3"]}
              st_ = w2s[2][:].rearrange("p k n -> p (k n)")[:, 0:1536].bitcast(F32).rearrange("p (m n) -> p m n", n=256)
              d_["st"] = [st_[:, 0, :], st_[:, 1, :], st_[:, 2, :], st_[:, 1, :]]
              d_["stk"] = ["w2s2_0", "w2s2_0", "w2s2_0", "w2s2_0"]
              d_["banks"] = (2, 3)
              Wl.append(d_)

              lnq = []
              nln = [0]

              def delayed(g_, k_):
                  for _ in range(k_):
                      yield
                  yield from g_

              def ffn_main():
                  yield from ff1(0)
                  for idx in range(len(items)):
                      if idx + 1 < len(items):
                          yield from ff1(idx + 1)
                      yield from ff2(idx)
                      j, c0, N = items[idx]
                      if idx + 3 < len(items) and items[idx + 3][0] == 7 and items[idx + 2][0] == 6 and items[idx + 1][0] == 5 and items[idx][0] == 4:
                          C.bank_pool = list(range(4))
                      if j == 7:
                          lnq.append((c0, 256, "xF%d_0" % c0))
                          lnq.append((c0 + 256, 256, "xF%d_1" % c0))
                          if c0 == 1536:
                              lnq.append((T, NS, "xF2048_0"))
                      if last_of_slice[j] == idx and j + NSLOT < 8:
                          load_slice(j + NSLOT)
                      if j == 3 and last_of_slice[j] == idx and l + 1 < L:
                          load_winb(l + 1)

              C.bank_pool = list(range(8))
              gmain = ffn_main()
              alive = True
              slots = [None, None, None]
              while alive or lnq or any(g_ is not None for g_ in slots):
                  if alive:
                      try:
                          next(gmain)
                      except StopIteration:
                          alive = False
                          C.bank_pool = [0, 1]
                  nslots = 2 if alive else 3
                  for q_ in range(nslots):
                      if slots[q_] is None and lnq:
                          cc_, nn_, xk_ = lnq.pop(0)
                          slots[q_] = delayed(ln2_sub(cc_, nn_, xk_, Wl[q_]), 3 if alive else 0)
                      if slots[q_] is not None:
                          try:
                              next(slots[q_])
                          except StopIteration:
                              slots[q_] = None
              C.bank_pool = list(range(8))
          C.barrier()
          if STOP == 'ln2%d' % l: return
    run_layers()
    C.finish()
    top.close()
    return nc


def _consts():
    slopes = np.exp2(-8.0 * (np.arange(8, dtype=np.float32) + 1.0) / 8).astype(np.float32)
    s = np.arange(128)[:, None]
    q = np.arange(128)[None, :]
    bcur = np.full((2, 128, 4, 128), NEG, np.float32)
    bprev = np.full((2, 128, 4, 128), NEG, np.float32)
    for kv in range(2):
        for g in range(4):
            sl = slopes[4 * kv + g]
            d = (q - s).astype(np.float32)
            bcur[kv, :, g, :] = np.where(s <= q, -sl * d, NEG)
            d2 = (q - s + 128).astype(np.float32)
            bprev[kv, :, g, :] = np.where(s >= q, -sl * d2, NEG)
    sbias = np.zeros((128, 128), np.float32)
    for h in range(8):
        sbias[16 * h:16 * h + 16, :] = -slopes[h] * (128 - np.arange(128, dtype=np.float32))[None, :]
    return bcur.reshape(2, 128, 512), bprev.reshape(2, 128, 512), sbias


_NC = None


def kernel(x_prompt, x_sample, cache_k, cache_v, state_h, state_conv, state_pool,
           w_in, attn_sinks, conv_w, conv_b, gate_a_w, gate_a_b, gate_x_w, gate_x_b, lru_lambda,
           pool_w, pool_scale, w_out, ln1_g, ln1_b, w_ff1, w_ff2, ln2_g, ln2_b):
    global _NC
    f = lambda a: np.ascontiguousarray(np.asarray(a, dtype=np.float32))
    bcur, bprev, sbias = _consts()
    ident = np.eye(128, dtype=np.float32)
    shared = dict(w_in=f(w_in), sinks=f(attn_sinks), conv_w=f(conv_w), conv_b=f(conv_b), ga_w=f(gate_a_w), ga_b=f(gate_a_b),
                  gx_w=f(gate_x_w), gx_b=f(gate_x_b), lam=f(lru_lambda), pool_w=f(pool_w), pool_s=f(pool_scale),
                  w_out=f(w_out), ln1_g=f(ln1_g), ln1_b=f(ln1_b), w_ff1=f(w_ff1), w_ff2=f(w_ff2), ln2_g=f(ln2_g), ln2_b=f(ln2_b),
                  ident=ident, bcur=bcur, bprev=bprev, sbias=sbias)
    emask = np.zeros((128, 2), np.float32); sel = np.zeros((128, 64), np.float32)
    for p in range(128):
        h, b_ = p // 16, p % 16
        emask[p, h % 2] = 1.0
        sel[p, (h // 2) * 16 + b_] = 1.0
    xpr = f(x_prompt); xsa = f(x_sample)
    ckk = f(cache_k).reshape(L, 128, 128, 128); cvv = f(cache_v).reshape(L, 128, 128, 128)
    shh = f(state_h); scc = f(state_conv); spp = f(state_pool)
    in_maps = []
    for c in range(NCORES):
        seq, half = c // 2, c % 2
        b0 = NS * c
        pcorr = np.ones((128, 2, 16), np.float32)
        if half == 0:
            for gi, w in enumerate((2, 4, 8, 16)):
                cc, p0 = gi // 2, 64 * (gi % 2)
                t = np.arange(16, dtype=np.float32)
                pcorr[p0:p0 + 64, cc, :] = (w / np.minimum(t + 1.0, float(w)))[None, :]
        m = dict(shared)
        m.update(xp=np.ascontiguousarray(xpr[seq, T * half:T * half + T]), xs=np.ascontiguousarray(xsa[b0:b0 + NS, 0]),
                 ck=np.ascontiguousarray(ckk[:, b0:b0 + NS]), cv=np.ascontiguousarray(cvv[:, b0:b0 + NS]),
                 sh=np.ascontiguousarray(shh[:, b0:b0 + NS]),
                 sc=np.ascontiguousarray(scc[:, b0:b0 + NS].reshape(L, NS * 3, 256)),
                 spool=np.ascontiguousarray(spp[:, b0:b0 + NS].reshape(L, NS * 15, 256)),
                 nm0=np.full((128, 1), NEG if half == 0 else 0.0, np.float32),
                 flag=np.full((128, 1), 0.0 if half == 0 else 1.0, np.float32),
                 pcorr=pcorr, st_in=np.zeros((L, 147, 256), np.float32), emask=emask, sel=sel)
        in_maps.append(m)
    if _NC is None:
        _NC = build()
    res = run_bass_kernel_spmd(_NC, in_maps, core_ids=list(range(NCORES))).results
    y_prompt = np.zeros((4, 4096, 1024), np.float32)
    for c in range(NCORES):
        y_prompt[c // 2, T * (c % 2):T * (c % 2) + T] = res[c]["yp"]
    y_sample = np.concatenate([res[c]["ys"] for c in range(NCORES)], 0).reshape(128, 1, 1024)
    sto = np.stack([res[2 * s + 1]["st_out"] for s in range(4)], 1)
    p_k = np.ascontiguousarray(sto[:, :, 0:128, 0:128]).reshape(L, 4, 128, 2, 64)
    p_v = np.ascontiguousarray(sto[:, :, 0:128, 128:256]).reshape(L, 4, 128, 2, 64)
    p_conv = np.ascontiguousarray(sto[:, :, 128:131, :])
    p_pool = np.ascontiguousarray(sto[:, :, 131:146, :])
    p_h = np.ascontiguousarray(sto[:, :, 146, :])
    cat = lambda k: np.concatenate([res[c][k] for c in range(NCORES)], 1)
    s_k = cat("s_k").reshape(L, 128, 128, 2, 64); s_v = cat("s_v").reshape(L, 128, 128, 2, 64)
    return (y_prompt, y_sample, p_k, p_v, p_h, p_conv, p_pool, s_k, s_v, cat("s_h"), cat("s_conv"), cat("s_pool"))
```
